# Optimizing a Trainium2 kernel written in Bass

```python
import math
import jax, jax.numpy as jnp
from jax import lax
import numpy as np

D_MODEL = 1024
BATCH = 2
SEQ = 8192
DEPTH = 4

SSM_WIDTH = 512
SSM_GROUP = 16
SSM_GROUPS = SSM_WIDTH // SSM_GROUP
SSM_STATE = 64
MLSTM_HEADS = 4
MLSTM_HEAD_DIM = 128
MLSTM_WIDTH = MLSTM_HEADS * MLSTM_HEAD_DIM
MLSTM_CHUNK = 64
CONV_WIDTH = 4
ATTN_HEADS = 4
ATTN_HEAD_DIM = 128
ATTN_WIDTH = ATTN_HEADS * ATTN_HEAD_DIM
Q_LORA_RANK = 256
IDX_HEADS = 8
IDX_HEAD_DIM = 64
TOPK_MAX = 256
Q_BLOCK = 128
N_BRANCH = 3
BRANCH_WIDTH = 512
EPS = 1e-6

SPLITS = (SSM_WIDTH, SSM_WIDTH,
          MLSTM_WIDTH, MLSTM_WIDTH, MLSTM_WIDTH, MLSTM_HEADS, MLSTM_HEADS, MLSTM_WIDTH,
          Q_LORA_RANK, ATTN_HEAD_DIM, ATTN_HEAD_DIM, IDX_HEAD_DIM, IDX_HEADS, ATTN_WIDTH,
          N_BRANCH * D_MODEL)
IN_WIDTH = sum(SPLITS)

kernel_name = 'hybrid_s5_mlstm_dsa_gated_block'


def rmsnorm(x, g):
    xf = x.astype(jnp.float32)
    y = xf * lax.rsqrt(jnp.mean(xf * xf, axis=-1, keepdims=True) + EPS)
    return (y * g.astype(jnp.float32)).astype(x.dtype)


def layernorm(x, g, b):
    xf = x.astype(jnp.float32)
    mu = jnp.mean(xf, axis=-1, keepdims=True)
    xc = xf - mu
    y = xc * lax.rsqrt(jnp.mean(xc * xc, axis=-1, keepdims=True) + EPS)
    return (y * g.astype(jnp.float32) + b.astype(jnp.float32)).astype(x.dtype)


def causal_depthwise_conv(x, w, b):
    k_w, ch = w.shape
    y = lax.conv_general_dilated(x, w[:, None, :], window_strides=(1,), padding=[(k_w - 1, 0)],
                                 dimension_numbers=('NWC', 'WIO', 'NWC'), feature_group_count=ch)
    return y + b


def s5_mixer(u, a_re, a_im, b_re, b_im, c_re, c_im, log_dt, d_skip, glu_w, glu_b):
    bsz, seq, _ = u.shape
    f32 = jnp.float32
    uf = u.astype(f32).reshape(bsz, seq, SSM_GROUPS, SSM_GROUP)
    dt = jnp.exp(log_dt.astype(f32))[:, None]
    ar = a_re.astype(f32)
    ai = a_im.astype(f32)
    mag = jnp.exp(ar * dt)
    ang = ai * dt
    abar_re = mag * jnp.cos(ang)
    abar_im = mag * jnp.sin(ang)
    den = ar * ar + ai * ai
    fr = ((abar_re - 1.0) * ar + abar_im * ai) / den
    fi = (abar_im * ar - (abar_re - 1.0) * ai) / den
    br = b_re.astype(f32)
    bi = b_im.astype(f32)
    bb_re = fr[..., None] * br - fi[..., None] * bi
    bb_im = fr[..., None] * bi + fi[..., None] * br
    bu_re = jnp.einsum('blgp,gnp->blgn', uf, bb_re)
    bu_im = jnp.einsum('blgp,gnp->blgn', uf, bb_im)
    a_seq_re = jnp.broadcast_to(abar_re, bu_re.shape)
    a_seq_im = jnp.broadcast_to(abar_im, bu_im.shape)

    def combine(e1, e2):
        a1r, a1i, b1r, b1i = e1
        a2r, a2i, b2r, b2i = e2
        return (a2r * a1r - a2i * a1i, a2r * a1i + a2i * a1r,
                a2r * b1r - a2i * b1i + b2r, a2r * b1i + a2i * b1r + b2i)

    _, _, h_re, h_im = lax.associative_scan(combine, (a_seq_re, a_seq_im, bu_re, bu_im), axis=1)
    y = (jnp.einsum('blgn,gpn->blgp', h_re, c_re.astype(f32))
         - jnp.einsum('blgn,gpn->blgp', h_im, c_im.astype(f32)))
    y = y.reshape(bsz, seq, SSM_WIDTH) + d_skip.astype(f32) * u.astype(f32)
    g = jax.nn.gelu(y)
    out = g * jax.nn.sigmoid(g @ glu_w.astype(f32) + glu_b.astype(f32))
    return out.astype(u.dtype)


def mlstm_mixer(q, k, v, i_pre, f_pre, conv_w, conv_b, igate_b, fgate_b, mh_norm_g):
    bsz, seq, _ = q.shape
    f32 = jnp.float32
    qk = jax.nn.silu(causal_depthwise_conv(jnp.concatenate([q, k], axis=-1), conv_w, conv_b))
    q, k = jnp.split(qk, 2, axis=-1)
    nc = seq // MLSTM_CHUNK

    def to_chunks(t):
        t = t.astype(f32).reshape(bsz, nc, MLSTM_CHUNK, MLSTM_HEADS, -1)
        return t.transpose(1, 0, 3, 2, 4)

    def gates_to_chunks(t):
        return t.reshape(bsz, nc, MLSTM_CHUNK, MLSTM_HEADS).transpose(1, 0, 3, 2)

    qc = to_chunks(q)
    kc = to_chunks(k) * (MLSTM_HEAD_DIM ** -0.5)
    vc = to_chunks(v)
    ig = gates_to_chunks(i_pre.astype(f32) + igate_b.astype(f32))
    lf = gates_to_chunks(jax.nn.log_sigmoid(f_pre.astype(f32) + fgate_b.astype(f32)))
    causal = jnp.tril(jnp.ones((MLSTM_CHUNK, MLSTM_CHUNK), dtype=bool))

    def step(carry, xs):
        c_prev, n_prev, m_prev = carry
        qb, kb, vb, ib, fb = xs
        bcum = jnp.cumsum(fb, axis=-1)
        dmat = bcum[..., :, None] - bcum[..., None, :] + ib[..., None, :]
        dmat = jnp.where(causal, dmat, -jnp.inf)
        inter = bcum + m_prev[..., None]
        m = jnp.maximum(inter, jnp.max(dmat, axis=-1))
        dexp = jnp.exp(dmat - m[..., None])
        inter_w = jnp.exp(inter - m)
        s = jnp.einsum('bhtd,bhsd->bhts', qb, kb) * dexp
        num = (jnp.einsum('bhts,bhse->bhte', s, vb)
               + inter_w[..., None] * jnp.einsum('bhtd,bhde->bhte', qb, c_prev))
        den = jnp.sum(s, axis=-1) + inter_w * jnp.einsum('bhtd,bhd->bht', qb, n_prev)
        h = num / jnp.maximum(jnp.abs(den), jnp.exp(-m))[..., None]
        b_last = bcum[..., -1]
        g_end = b_last[..., None] - bcum + ib
        m_next = jnp.maximum(b_last + m_prev, jnp.max(g_end, axis=-1))
        w_s = jnp.exp(g_end - m_next[..., None])
        decay = jnp.exp(b_last + m_prev - m_next)
        c_next = decay[..., None, None] * c_prev + jnp.einsum('bhs,bhsd,bhse->bhde', w_s, kb, vb)
        n_next = decay[..., None] * n_prev + jnp.einsum('bhs,bhsd->bhd', w_s, kb)
        return (c_next, n_next, m_next), h

    init = (jnp.zeros((bsz, MLSTM_HEADS, MLSTM_HEAD_DIM, MLSTM_HEAD_DIM), f32),
            jnp.zeros((bsz, MLSTM_HEADS, MLSTM_HEAD_DIM), f32),
            jnp.zeros((bsz, MLSTM_HEADS), f32))
    _, hs = lax.scan(step, init, (qc, kc, vc, ig, lf))
    h = hs.transpose(1, 0, 3, 2, 4).reshape(bsz, seq, MLSTM_HEADS, MLSTM_HEAD_DIM)
    h = h * lax.rsqrt(jnp.mean(h * h, axis=-1, keepdims=True) + EPS)
    h = h.reshape(bsz, seq, MLSTM_WIDTH) * mh_norm_g.astype(f32)
    return h.astype(q.dtype)


def dsa_mixer(cq, k, v, kidx, widx, q_norm_g, w_uq, w_qidx, kidx_norm_g, kidx_norm_b):
    bsz, seq, _ = cq.shape
    f32 = jnp.float32
    cqn = rmsnorm(cq, q_norm_g)
    q = (cqn @ w_uq).reshape(bsz, seq, ATTN_HEADS, ATTN_HEAD_DIM)
    qi = (cqn @ w_qidx).reshape(bsz, seq, IDX_HEADS, IDX_HEAD_DIM)
    ki = layernorm(kidx, kidx_norm_g, kidx_norm_b).astype(f32)
    wi = widx.astype(f32) * (IDX_HEADS ** -0.5) * (IDX_HEAD_DIM ** -0.5)
    topk = min(TOPK_MAX, seq // 4)
    nb = seq // Q_BLOCK
    q_blk = q.reshape(bsz, nb, Q_BLOCK, ATTN_HEADS, ATTN_HEAD_DIM).transpose(1, 0, 2, 3, 4)
    qi_blk = qi.reshape(bsz, nb, Q_BLOCK, IDX_HEADS, IDX_HEAD_DIM).transpose(1, 0, 2, 3, 4)
    wi_blk = wi.reshape(bsz, nb, Q_BLOCK, IDX_HEADS).transpose(1, 0, 2, 3)
    starts = jnp.arange(nb, dtype=jnp.int32) * Q_BLOCK
    s_pos = jnp.arange(seq, dtype=jnp.int32)
    gather = jax.vmap(lambda kb_, ib_: kb_[ib_])

    def block(args):
        start, qb, qib, wb = args
        t = start + jnp.arange(Q_BLOCK, dtype=jnp.int32)
        rel = jax.nn.relu(jnp.einsum('bthd,bsd->bths', qib.astype(f32), ki))
        score = jnp.einsum('bth,bths->bts', wb, rel)
        causal = s_pos[None, :] <= t[:, None]
        score = jnp.where(causal[None], score, -jnp.inf)
        _, sel = lax.top_k(score, topk)
        valid = sel <= t[None, :, None]
        kg = gather(k, sel).astype(f32)
        vg = gather(v, sel).astype(f32)
        logits = jnp.einsum('bthd,btkd->bthk', qb.astype(f32), kg) * (ATTN_HEAD_DIM ** -0.5)
        logits = jnp.where(valid[:, :, None, :], logits, -jnp.inf)
        p = jax.nn.softmax(logits, axis=-1)
        return jnp.einsum('bthk,btkd->bthd', p, vg)

    o = lax.map(block, (starts, q_blk, qi_blk, wi_blk))
    o = o.transpose(1, 0, 2, 3, 4).reshape(bsz, seq, ATTN_WIDTH)
    return o.astype(cq.dtype)


def hybrid_layer(x, norm_g, w_in, ssm_a_re, ssm_a_im, ssm_b_re, ssm_b_im, ssm_c_re, ssm_c_im,
                 ssm_log_dt, ssm_d, glu_w, glu_b, qk_conv_w, qk_conv_b, igate_b, fgate_b,
                 mh_norm_g, q_norm_g, w_uq, w_qidx, kidx_norm_g, kidx_norm_b, w_branch, w_out):
    bsz, seq, _ = x.shape
    h = rmsnorm(x, norm_g)
    proj = h @ w_in
    idx = np.cumsum(np.array(SPLITS))[:-1].tolist()
    (u_a, z_a, q_b, k_b, v_b, i_b, f_b, z_b,
     cq_c, k_c, v_c, kidx_c, widx_c, z_c, gate_pre) = jnp.split(proj, idx, axis=-1)
    y_a = s5_mixer(u_a, ssm_a_re, ssm_a_im, ssm_b_re, ssm_b_im, ssm_c_re, ssm_c_im,
                   ssm_log_dt, ssm_d, glu_w, glu_b) * jax.nn.silu(z_a)
    y_b = mlstm_mixer(q_b, k_b, v_b, i_b, f_b, qk_conv_w, qk_conv_b, igate_b, fgate_b,
                      mh_norm_g) * jax.nn.silu(z_b)
    y_c = dsa_mixer(cq_c, k_c, v_c, kidx_c, widx_c, q_norm_g, w_uq, w_qidx,
                    kidx_norm_g, kidx_norm_b) * jax.nn.silu(z_c)
    ys = jnp.stack([y_a, y_b, y_c], axis=2)
    branch = jnp.einsum('blnw,nwd->blnd', ys, w_branch)
    gates = jax.nn.sigmoid(gate_pre.reshape(bsz, seq, N_BRANCH, D_MODEL))
    merged = jnp.sum(gates * branch, axis=2)
    return x + merged @ w_out


def setup_inputs(seed: int = 0) -> dict:
    key = jax.random.key(seed)
    ks = jax.random.split(key, 32)
    f32 = jnp.float32
    L_, D_ = DEPTH, D_MODEL
    G, N, P = SSM_GROUPS, SSM_STATE, SSM_GROUP

    def nrm(k, shape, scale):
        return jax.random.normal(k, shape, f32) * scale

    n_idx = jnp.arange(N, dtype=f32)
    inp = {}
    inp['x'] = jax.random.normal(ks[0], (BATCH, SEQ, D_), f32)
    inp['norm_g'] = 1.0 + nrm(ks[1], (L_, D_), 0.02)
    inp['w_in'] = nrm(ks[2], (L_, D_, IN_WIDTH), D_ ** -0.5)
    inp['ssm_a_re'] = -0.5 + nrm(ks[3], (L_, G, N), 0.01)
    inp['ssm_a_im'] = math.pi * n_idx + nrm(ks[4], (L_, G, N), 0.01)
    inp['ssm_b_re'] = nrm(ks[5], (L_, G, N, P), (2.0 * P) ** -0.5)
    inp['ssm_b_im'] = nrm(ks[6], (L_, G, N, P), (2.0 * P) ** -0.5)
    inp['ssm_c_re'] = nrm(ks[7], (L_, G, P, N), (2.0 * N) ** -0.5)
    inp['ssm_c_im'] = nrm(ks[8], (L_, G, P, N), (2.0 * N) ** -0.5)
    inp['ssm_log_dt'] = jax.random.uniform(ks[9], (L_, G), f32, math.log(1e-3), math.log(1e-1))
    inp['ssm_d'] = nrm(ks[10], (L_, SSM_WIDTH), 1.0)
    inp['glu_w'] = nrm(ks[11], (L_, SSM_WIDTH, SSM_WIDTH), SSM_WIDTH ** -0.5)
    inp['glu_b'] = nrm(ks[12], (L_, SSM_WIDTH), 0.02)
    inp['qk_conv_w'] = nrm(ks[13], (L_, CONV_WIDTH, 2 * MLSTM_WIDTH), CONV_WIDTH ** -0.5)
    inp['qk_conv_b'] = nrm(ks[14], (L_, 2 * MLSTM_WIDTH), 0.02)
    inp['igate_b'] = nrm(ks[15], (L_, MLSTM_HEADS), 0.1)
    inp['fgate_b'] = jnp.linspace(3.0, 6.0, MLSTM_HEADS, dtype=f32) + nrm(ks[16], (L_, MLSTM_HEADS), 0.1)
    inp['mh_norm_g'] = 1.0 + nrm(ks[17], (L_, MLSTM_WIDTH), 0.02)
    inp['q_norm_g'] = 1.0 + nrm(ks[18], (L_, Q_LORA_RANK), 0.02)
    inp['w_uq'] = nrm(ks[19], (L_, Q_LORA_RANK, ATTN_WIDTH), Q_LORA_RANK ** -0.5)
    inp['w_qidx'] = nrm(ks[20], (L_, Q_LORA_RANK, IDX_HEADS * IDX_HEAD_DIM), Q_LORA_RANK ** -0.5)
    inp['kidx_norm_g'] = 1.0 + nrm(ks[21], (L_, IDX_HEAD_DIM), 0.02)
    inp['kidx_norm_b'] = nrm(ks[22], (L_, IDX_HEAD_DIM), 0.02)
    inp['w_branch'] = nrm(ks[23], (L_, N_BRANCH, BRANCH_WIDTH, D_), BRANCH_WIDTH ** -0.5)
    inp['w_out'] = nrm(ks[24], (L_, D_, D_), D_ ** -0.5)
    inp['final_norm_g'] = 1.0 + nrm(ks[25], (D_,), 0.02)
    return inp


def reference(x, norm_g, w_in, ssm_a_re, ssm_a_im, ssm_b_re, ssm_b_im, ssm_c_re, ssm_c_im,
              ssm_log_dt, ssm_d, glu_w, glu_b, qk_conv_w, qk_conv_b, igate_b, fgate_b,
              mh_norm_g, q_norm_g, w_uq, w_qidx, kidx_norm_g, kidx_norm_b, w_branch, w_out,
              final_norm_g):
    for l in range(DEPTH):
        x = hybrid_layer(x, norm_g[l], w_in[l], ssm_a_re[l], ssm_a_im[l], ssm_b_re[l], ssm_b_im[l],
                         ssm_c_re[l], ssm_c_im[l], ssm_log_dt[l], ssm_d[l], glu_w[l], glu_b[l],
                         qk_conv_w[l], qk_conv_b[l], igate_b[l], fgate_b[l], mh_norm_g[l],
                         q_norm_g[l], w_uq[l], w_qidx[l], kidx_norm_g[l], kidx_norm_b[l],
                         w_branch[l], w_out[l])
    return rmsnorm(x, final_norm_g)
```

```python
import numpy as np
import concourse.bass as bass
import concourse.mybir as mybir
from concourse.bass_utils import run_bass_kernel_spmd
from contextlib import ExitStack

F32 = mybir.dt.float32
BF16 = mybir.dt.bfloat16
AF = mybir.ActivationFunctionType
ALU = mybir.AluOpType
AX = mybir.AxisListType

L = 8192
D = 1024
NL = 4
NBLK = L // 512
NT = L // 128
EPS = 1e-6
BIG = 1.0e30

C_U = 0; C_ZA = 512; C_QB = 1024; C_KB = 1536; C_VB = 2048; C_IF = 2560; C_ZB = 2568
C_CQ = 3080; C_KC = 3336; C_VC = 3464; C_KIDX = 3592; C_WIDX = 3656; C_ZC = 3664; C_GATE = 4176
INW = 7248
WJ = 1208

SAME_SYNC = True
STQ = 'sp'


class KB:
    def __init__(self, nc, es, ext=None):
        self.nc = nc
        self.es = es
        self.ext = ext or {}
        self.E = {'pe': nc.tensor, 'act': nc.scalar, 'dve': nc.vector, 'pool': nc.gpsimd, 'sp': nc.sync}
        self.sem = {}
        self.cnt = {}
        for e in ['pe', 'act', 'dve', 'pool']:
            self.sem[e] = es.enter_context(nc.semaphore('s_' + e))
            self.cnt[e] = 0
        self.dsem = {}
        self.dcnt = {}
        self.dstream = {}
        self.waited = {e: {} for e in self.E}
        self.lastw = {}
        self.readers = {}
        self.uid = 0
        self.ninstr = 0

    def name(self, base):
        self.uid += 1
        return f"{base}_{self.uid}"

    def dram(self, name, shape, dtype):
        kind = self.ext.get(name, 'Internal')
        return self.nc.dram_tensor(name, list(shape), dtype, kind=kind).ap()

    NS = 6

    def _stream(self, s):
        if s not in self.dstream:
            self.dstream[s] = 0
            for j in range(self.NS):
                self.dsem[(s, j)] = self.es.enter_context(self.nc.semaphore(f'd_{s}{j}'))
                self.dcnt[(s, j)] = 0
        j = self.dstream[s] % self.NS
        self.dstream[s] += 1
        return (s, j)

    def _wait(self, e, src, val):
        if self.waited[e].get(src, 0) >= val:
            return
        sem = self.sem[src[1]] if src[0] == 'e' else self.dsem[src[1]]
        self.E[e].wait_ge(sem, val)
        self.waited[e][src] = val
        self.ninstr += 1

    def _deps(self, e, reads, writes):
        deps = {}

        def add(src, val):
            if deps.get(src, 0) < val:
                deps[src] = val
        for k in reads:
            ev = self.lastw.get(k)
            if ev:
                add(*ev)
        for k in writes:
            ev = self.lastw.get(k)
            if ev:
                add(*ev)
            for src, val in self.readers.get(k, {}).items():
                add(src, val)
        for src, val in deps.items():
            if src == ('e', e) and (e == 'pe' or not SAME_SYNC):
                continue
            self._wait(e, src, val)

    def _commit(self, ev, reads, writes):
        for k in writes:
            self.lastw[k] = ev
            self.readers[k] = {}
        for k in reads:
            r = self.readers.setdefault(k, {})
            if r.get(ev[0], 0) < ev[1]:
                r[ev[0]] = ev[1]

    def op(self, e, fn, r=(), w=(), post=None):
        self._deps(e, r, w)
        ins = fn(self.E[e])
        if post is not None:
            ins = post(self.E[e])
            self.ninstr += 1
        self.cnt[e] += 1
        ins.then_inc(self.sem[e], 1)
        self._commit((('e', e), self.cnt[e]), r, w)
        self.ninstr += 1

    def dma(self, out, in_, r=(), w=(), s='ld', q='sp'):
        sk = self._stream(s)
        if self.dcnt[sk] > 0:
            self._wait(q, ('d', sk), self.dcnt[sk])
        self._deps(q, r, w)
        ins = self.E[q].dma_start(out=out, in_=in_)
        self.dcnt[sk] += 16
        ins.then_inc(self.dsem[sk], 16)
        self._commit((('d', sk), self.dcnt[sk]), r, w)
        self.ninstr += 1

    def barrier(self):
        for e in self.E:
            for p in self.sem:
                if self.cnt[p] > 0 and not (p == e):
                    self._wait(e, ('e', p), self.cnt[p])
            for s in self.dsem:
                if self.dcnt[s] > 0:
                    self._wait(e, ('d', s), self.dcnt[s])
        self.lastw = {}
        self.readers = {}

    def final_wait(self):
        for s in self.dsem:
            if self.dcnt[s] > 0:
                self._wait('sp', ('d', s), self.dcnt[s])


class Alloc:
    def __init__(self, kb, es, tag):
        self.kb = kb
        self.es = es
        self.tag = tag

    def sb(self, name, shape, dt):
        return self.es.enter_context(self.kb.nc.sbuf_tensor(self.kb.name(self.tag + name), list(shape), dt))

    def ps(self, name, shape, dt):
        return self.es.enter_context(self.kb.nc.psum_tensor(self.kb.name(self.tag + name), list(shape), dt))


def wkeys(kc, c0, c1):
    return [('wb', kc, j) for j in range(c0 // WJ, (c1 - 1) // WJ + 1)]


def phase1(kb, G, l, xsrc):
    nc = kb.nc
    cst = G['cst']
    with ExitStack() as es:
        A = Alloc(kb, es, f'p1l{l}')
        wb = A.sb('wb', [128, 8, INW], BF16)
        stg = [A.sb(f'wstg{i}', [128, WJ], F32) for i in range(2)]
        g = A.sb('g', [128, 8], F32)
        ident = A.sb('ident', [128, 128], BF16)
        identf = A.sb('identf', [128, 128], F32)
        kb.dma(g[:], G['norm_g'][l], w=['g'])
        kb.dma(identf[:], cst['ident'], w=['identf'])
        kb.op('dve', lambda e: e.tensor_copy(out=ident[:], in_=identf[:]), r=['identf'], w=['ident'])
        w_in = G['w_in'][l].rearrange("(kc p) n -> p kc n", p=128)
        i = 0
        for kc in range(8):
            for j in range(6):
                s = stg[i % 2]
                kb.dma(s[:], w_in[:, kc, j * WJ:(j + 1) * WJ], w=[('stg', i % 2)])
                if i % 2 == 0:
                    kb.op('dve', lambda e, s=s, kc=kc, j=j: e.tensor_scalar(
                        out=wb[:, kc, j * WJ:(j + 1) * WJ], in0=s[:], scalar1=g[:, kc:kc + 1], scalar2=None,
                        op0=ALU.mult), r=[('stg', 0), 'g'], w=[('wb', kc, j)])
                else:
                    kb.op('act', lambda e, s=s, kc=kc, j=j: e.activation(
                        out=wb[:, kc, j * WJ:(j + 1) * WJ], in_=s[:], func=AF.Copy, scale=g[:, kc:kc + 1]),
                        r=[('stg', 1), 'g'], w=[('wb', kc, j)])
                i += 1

        xt = [A.sb(f'xt{i}', [128, D], F32) for i in range(2)]
        junk = A.sb('junk', [128, D], BF16)
        ss = [A.sb(f'ss{i}', [128, 1], F32) for i in range(2)]
        hb = [A.sb(f'hb{i}', [128, D], BF16) for i in range(2)]
        hT = [A.sb(f'hT{i}', [128, 8, 512], BF16) for i in range(2)]
        pT = [A.ps(f'pT{i}', [128, 8, 128], BF16) for i in range(2)]
        pO = [A.ps(f'pO{i}', [128, 512], F32) for i in range(4)]
        so = [A.sb(f'so{i}', [128, 512], BF16) for i in range(4)]
        sof = [A.sb(f'sof{i}', [128, 256], F32) for i in range(2)]

        fm = [(C_U, 4, AF.Copy, G['uT']), (C_ZA, 4, AF.Silu, G['zaT']), (C_QB, 4, AF.Copy, G['qbT']),
              (C_KB, 4, AF.Copy, G['kbT']), (C_ZB, 4, AF.Silu, G['zbT']), (C_CQ, 2, AF.Copy, G['cqT']),
              (C_KC, 1, AF.Copy, G['kcT']), (C_ZC, 4, AF.Silu, G['zcT']), (C_GATE, 24, AF.Sigmoid, G['gateT'])]
        wst = [A.sb(f'wst{i}', [8, 512], F32) for i in range(2)]
        state = {'ti': 0, 'oi': 0}

        def prep(blk):
            hTb = hT[blk % 2]
            for tt in range(4):
                ti = state['ti']
                t = blk * 4 + tt
                x_t = xt[ti % 2]; s_t = ss[ti % 2]; h_t = hb[ti % 2]; p_t = pT[ti % 2]
                kx = ('xt', ti % 2); ks = ('ss', ti % 2); kh = ('hb', ti % 2); kp = ('pT', ti % 2)
                kb.dma(x_t[:], xsrc[t * 128:(t + 1) * 128, :], r=[('x', t)], w=[kx], s='ldx')
                kb.op('act', lambda e: e.activation(out=junk[:], in_=x_t[:], func=AF.Square, accum_out=s_t[:]),
                      r=[kx], w=['junk', ks])
                kb.op('dve', lambda e: e.tensor_scalar(out=s_t[:], in0=s_t[:], scalar1=1.0 / D, scalar2=EPS,
                                                       op0=ALU.mult, op1=ALU.add), r=[ks], w=[ks])
                kb.op('act', lambda e: e.activation(out=s_t[:], in_=s_t[:], func=AF.Sqrt), r=[ks], w=[ks])
                kb.op('dve', lambda e: e.reciprocal(out=s_t[:], in_=s_t[:]), r=[ks], w=[ks])
                kb.op('dve', lambda e: e.tensor_scalar(out=h_t[:], in0=x_t[:], scalar1=s_t[:, 0:1], scalar2=None,
                                                       op0=ALU.mult), r=[kx, ks], w=[kh])
                for c in range(8):
                    kb.op('pe', lambda e: e.transpose(out=p_t[:, c, :], in_=h_t[:, c * 128:(c + 1) * 128],
                                                      identity=ident[:]), r=[kh, 'ident'], w=[kp])
                kb.op('dve', lambda e: e.tensor_copy(out=hTb[:, :, tt * 128:(tt + 1) * 128], in_=p_t[:]),
                      r=[kp], w=[('hT', blk % 2, tt)])
                state['ti'] += 1

        def tm(blk):
            hTb = hT[blk % 2]
            hkeys = [('hT', blk % 2, tt) for tt in range(4)]
            for tt in range(4):
                t = blk * 4 + tt
                hk = [('hT', blk % 2, tt)]
                lhs = lambda k: hTb[:, k, tt * 128:(tt + 1) * 128]
                oi = state['oi']; po = pO[oi % 4]; kpo = ('pO', oi % 4); st = so[oi % 4]; kst = ('so', oi % 4)
                for k in range(8):
                    kb.op('pe', lambda e: e.matmul(po[:], lhsT=lhs(k), rhs=wb[:, k, C_VB:C_VB + 512],
                                                   start=(k == 0), stop=(k == 7)),
                          r=hk + wkeys(k, C_VB, C_VB + 512), w=[kpo])
                kb.op('dve', lambda e: e.tensor_copy(out=st[:], in_=po[:]), r=[kpo], w=[kst])
                kb.dma(G['vb'][t * 128:(t + 1) * 128, :], st[:], r=[kst], w=[('vb', t)], s='st1')
                state['oi'] += 1
                if 'if' in G.get('cfg', {}).get('tm_skip', ()):
                    continue
                oi = state['oi']; po = pO[oi % 4]; kpo = ('pO', oi % 4); sf = sof[0]
                for k in range(8):
                    kb.op('pe', lambda e: e.matmul(po[:, 0:8], lhsT=lhs(k), rhs=wb[:, k, C_IF:C_IF + 8],
                                                   start=(k == 0), stop=(k == 7)),
                          r=hk + wkeys(k, C_IF, C_IF + 8), w=[kpo])
                kb.op('dve', lambda e: e.tensor_copy(out=sf[:, 0:8], in_=po[:, 0:8]), r=[kpo], w=[('sof', 0)])
                kb.dma(G['ifg'][t * 128:(t + 1) * 128, :], sf[:, 0:8], r=[('sof', 0)], w=[('ifg', t)], s='st1')
                state['oi'] += 1
                if 'vkw' in G.get('cfg', {}).get('tm_skip', ()):
                    continue
                oi = state['oi']; po = pO[oi % 4]; kpo = ('pO', oi % 4); st = so[oi % 4]; kst = ('so', oi % 4); sf = sof[1]
                for k in range(8):
                    kb.op('pe', lambda e: e.matmul(po[:, 0:200], lhsT=lhs(k), rhs=wb[:, k, C_VC:C_VC + 200],
                                                   start=(k == 0), stop=(k == 7)),
                          r=hk + wkeys(k, C_VC, C_VC + 200), w=[kpo])
                kb.op('dve', lambda e: e.tensor_copy(out=st[:, 0:128], in_=po[:, 0:128]), r=[kpo], w=[kst])
                kb.op('dve', lambda e: e.tensor_copy(out=sf[:, 0:72], in_=po[:, 128:200]),
                      r=[kpo], w=[('sof', 1)])
                kb.dma(G['vc'][t * 128:(t + 1) * 128, :], st[:, 0:128], r=[kst], w=[('vc', t)], s='st1')
                kb.dma(G['kidx'][t * 128:(t + 1) * 128, :], sf[:, 0:64], r=[('sof', 1)], w=[('kidx', t)], s='st1')
                kb.dma(G['widx'][t * 128:(t + 1) * 128, :], sf[:, 64:72], r=[('sof', 1)], w=[('widx', t)], s='st1')
                state['oi'] += 1
            if 'wT' in G.get('cfg', {}).get('tm_skip', ()):
                return
            oi = state['oi']; po = pO[oi % 4]; kpo = ('pO', oi % 4); ws = wst[blk % 2]
            for k in range(8):
                kb.op('pe', lambda e: e.matmul(po[:, :], lhsT=wb[:, k, C_WIDX:C_WIDX + 128], rhs=hTb[:, k, :],
                                               start=(k == 0), stop=(k == 7)),
                      r=hkeys + wkeys(k, C_WIDX, C_WIDX + 128), w=[kpo])
            kb.op('dve', lambda e: e.tensor_copy(out=ws[:], in_=po[0:8, :]), r=[kpo], w=[('wst', blk % 2)])
            kb.dma(G['widxT'][:, blk * 512:(blk + 1) * 512], ws[:], r=[('wst', blk % 2)], w=[('widxT', blk)], s='st1')
            state['oi'] += 1

        def fmseg(blk):
            hTb = hT[blk % 2]
            hkeys = [('hT', blk % 2, tt) for tt in range(4)]
            for (c0, nch, func, dst) in fm:
                for ch in range(nch):
                    oi = state['oi']; po = pO[oi % 4]; kpo = ('pO', oi % 4); st = so[oi % 4]; kst = ('so', oi % 4)
                    cc = c0 + ch * 128
                    for k in range(8):
                        kb.op('pe', lambda e: e.matmul(po[:], lhsT=wb[:, k, cc:cc + 128], rhs=hTb[:, k, :],
                                                       start=(k == 0), stop=(k == 7)),
                              r=hkeys + wkeys(k, cc, cc + 128), w=[kpo])
                    kb.op('act', lambda e: e.activation(out=st[:], in_=po[:], func=func), r=[kpo], w=[kst])
                    kb.dma(dst[ch * 128:(ch + 1) * 128, blk * 512:(blk + 1) * 512], st[:], r=[kst],
                           w=[(id(dst), ch, blk)], s='st2', q=STQ)
                    state['oi'] += 1

        stop = G.get('cfg', {}).get('p1_stop', 9)
        if stop >= 1:
            prep(0)
        for blk in range(NBLK):
            if stop >= 2:
                tm(blk)
            if blk + 1 < NBLK and stop >= 1:
                prep(blk + 1)
            if stop >= 3:
                fmseg(blk)
    kb.barrier()


MAGIC = 12582912.0
TWO_PI_S = 6.28318
GC1 = 0.044715
GC2 = 1.5957691216057308


def sincos_turns(kb, A, phi, n, kphi, tag):
    t = A.sb(tag + 't', [128, n], F32)
    k = A.sb(tag + 'k', [128, n], F32)
    sn = A.sb(tag + 'sn', [128, n], F32)
    cs = A.sb(tag + 'cs', [128, n], F32)
    kt, kk, ksn, kcs = tag + 't', tag + 'k', tag + 'sn', tag + 'cs'
    for (off, dst, kd) in ((0.0, sn, ksn), (0.25, cs, kcs)):
        kb.op('dve', lambda e: e.tensor_scalar(out=t[:], in0=phi, scalar1=off, scalar2=MAGIC, op0=ALU.add, op1=ALU.add),
              r=[kphi], w=[kt])
        kb.op('dve', lambda e: e.tensor_scalar(out=k[:], in0=t[:], scalar1=-MAGIC, scalar2=None, op0=ALU.add),
              r=[kt], w=[kk])
        kb.op('dve', lambda e: e.scalar_tensor_tensor(out=t[:], in0=phi, scalar=off, in1=k[:], op0=ALU.add,
                                                      op1=ALU.subtract), r=[kphi, kk], w=[kt])
        kb.op('dve', lambda e: e.tensor_scalar(out=t[:], in0=t[:], scalar1=0.5, scalar2=-0.5, op0=ALU.min, op1=ALU.max),
              r=[kt], w=[kt])
        kb.op('act', lambda e: e.activation(out=dst[:], in_=t[:], func=AF.Sin, scale=TWO_PI_S), r=[kt], w=[kd])
    return sn, cs, ksn, kcs


def phase_s5(kb, G, l):
    nc = kb.nc
    cst = G['cst']
    with ExitStack() as es:
        A = Alloc(kb, es, f's5l{l}')
        Ctab = A.sb('Ctab', [128, 32, 128], F32)
        Stab = A.sb('Stab', [128, 32, 128], F32)
        Rtab = A.sb('Rtab', [128, 32, 128], F32)
        RotL = A.sb('RotL', [128, 32, 128], F32)
        L1 = A.sb('L1', [128, 32, 128], BF16)
        L2 = A.sb('L2', [128, 32, 128], BF16)
        W1 = A.sb('W1', [128, 32, 128], BF16)
        W2 = A.sb('W2', [128, 32, 128], BF16)
        identf = A.sb('identf', [128, 128], F32)
        dsk = A.sb('dsk', [128, 4], F32)
        glub = A.sb('glub', [128, 4], F32)
        gluw = A.sb('gluw', [128, 4, 512], BF16)
        carry = A.sb('carry', [128, 32], F32)
        kb.dma(identf[:], cst['ident'], w=['identf'])
        kb.dma(dsk[:], G['ssm_d'][l], w=['dsk'])
        kb.dma(glub[:], G['glu_b'][l], w=['glub'])
        kb.op('pool', lambda e: e.memset(carry[:], 0.0), w=['carry'])
        with ExitStack() as es2:
            B = Alloc(kb, es2, f's5l{l}t')
            ar = B.sb('ar', [128, 32], F32); ai = B.sb('ai', [128, 32], F32); ldt = B.sb('ldt', [128, 32], F32)
            sgn1 = B.sb('sgn1', [128, 1], F32); swp = B.sb('swp', [128, 128], F32); iot = B.sb('iot', [128, 128], F32)
            X1 = B.sb('X1', [128, 32, 16], F32); X2 = B.sb('X2', [128, 32, 16], F32)
            Cc1 = B.sb('Cc1', [128, 4, 128], F32); Cc2 = B.sb('Cc2', [128, 4, 128], F32)
            gws = B.sb('gws', [128, 4, 512], F32)
            kb.dma(ar[:], G['a_re'][l], w=['ar']); kb.dma(ai[:], G['a_im'][l], w=['ai']); kb.dma(ldt[:], G['log_dt'][l], w=['ldt'])
            kb.dma(sgn1[:], cst['sgn1'], w=['sgn1']); kb.dma(swp[:], cst['swap'], w=['swp']); kb.dma(iot[:], cst['iota'], w=['iot'])
            kb.dma(X1[:], G['X1'][l], w=['X1']); kb.dma(X2[:], G['X2'][l], w=['X2'])
            kb.dma(Cc1[:], G['Cc1'][l].rearrange("c p n -> p c n"), w=['Cc1'])
            kb.dma(Cc2[:], G['Cc2'][l].rearrange("c p n -> p c n"), w=['Cc2'])
            kb.dma(gws[:], G['glu_w'][l].rearrange("(kc p) n -> p kc n", p=128), w=['gws'])
            kb.op('pool', lambda e: e.tensor_copy(out=gluw[:], in_=gws[:]), r=['gws'], w=['gluw'])
            S = {}

            def sm(name):
                S[name] = B.sb(name, [128, 32], F32)
                return S[name]

            def tt(o, a, b, op):
                kb.op('dve', lambda e: e.tensor_tensor(out=S[o][:], in0=S[a][:], in1=S[b][:], op=op), r=[a, b], w=[o])
            S['ar'] = ar; S['ai'] = ai; S['ldt'] = ldt
            for n_ in ['dt', 'mag', 'ang', 'phi1', 'abr', 'abi', 'den', 't1', 'fr', 'fi', 'fis', 'frs', 'tmp', 'phi128', 'ssgn']:
                sm(n_)
            kb.op('act', lambda e: e.activation(out=S['dt'][:], in_=ldt[:], func=AF.Exp), r=['ldt'], w=['dt'])
            tt('tmp', 'ar', 'dt', ALU.mult)
            kb.op('act', lambda e: e.activation(out=S['mag'][:], in_=S['tmp'][:], func=AF.Exp), r=['tmp'], w=['mag'])
            tt('ang', 'ai', 'dt', ALU.mult)
            kb.op('dve', lambda e: e.tensor_scalar(out=S['phi1'][:], in0=S['ang'][:], scalar1=1.0 / (2 * np.pi), scalar2=None,
                                                   op0=ALU.mult), r=['ang'], w=['phi1'])
            s1, c1, ks1, kc1 = sincos_turns(kb, B, S['phi1'][:], 32, 'phi1', 'sc1')
            S['s1'] = s1; S['c1'] = c1
            kb.op('dve', lambda e: e.tensor_tensor(out=S['abr'][:], in0=S['mag'][:], in1=c1[:], op=ALU.mult), r=['mag', kc1], w=['abr'])
            kb.op('dve', lambda e: e.tensor_tensor(out=S['abi'][:], in0=S['mag'][:], in1=s1[:], op=ALU.mult), r=['mag', ks1], w=['abi'])
            tt('den', 'ar', 'ar', ALU.mult)
            tt('tmp', 'ai', 'ai', ALU.mult)
            tt('den', 'den', 'tmp', ALU.add)
            kb.op('dve', lambda e: e.reciprocal(out=S['den'][:], in_=S['den'][:]), r=['den'], w=['den'])
            kb.op('dve', lambda e: e.tensor_scalar(out=S['t1'][:], in0=S['abr'][:], scalar1=-1.0, scalar2=None, op0=ALU.add),
                  r=['abr'], w=['t1'])
            tt('fr', 't1', 'ar', ALU.mult)
            tt('tmp', 'abi', 'ai', ALU.mult)
            tt('fr', 'fr', 'tmp', ALU.add)
            tt('fr', 'fr', 'den', ALU.mult)
            tt('fi', 'abi', 'ar', ALU.mult)
            tt('tmp', 't1', 'ai', ALU.mult)
            tt('fi', 'fi', 'tmp', ALU.subtract)
            tt('fi', 'fi', 'den', ALU.mult)
            kb.op('dve', lambda e: e.tensor_scalar(out=S['frs'][:], in0=S['fr'][:], scalar1=sgn1[:, 0:1], scalar2=None, op0=ALU.mult),
                  r=['fr', 'sgn1'], w=['frs'])
            kb.op('dve', lambda e: e.tensor_scalar(out=S['fis'][:], in0=S['fi'][:], scalar1=sgn1[:, 0:1], scalar2=-1.0, op0=ALU.mult,
                                                   op1=ALU.mult), r=['fi', 'sgn1'], w=['fis'])
            kb.op('dve', lambda e: e.tensor_scalar(out=S['phi128'][:], in0=S['phi1'][:], scalar1=128.0, scalar2=None, op0=ALU.mult),
                  r=['phi1'], w=['phi128'])
            s128, c128, ks128, kc128 = sincos_turns(kb, B, S['phi128'][:], 32, 'phi128', 'sc128')
            kb.op('dve', lambda e: e.tensor_scalar(out=S['ssgn'][:], in0=s128[:], scalar1=sgn1[:, 0:1], scalar2=None, op0=ALU.mult),
                  r=[ks128, 'sgn1'], w=['ssgn'])
            for g in range(32):
                kb.op('dve', lambda e: e.tensor_scalar(out=RotL[:, g, :], in0=identf[:], scalar1=c128[:, g:g + 1], scalar2=None,
                                                       op0=ALU.mult), r=['identf', kc128], w=[('RotL', g)])
                kb.op('dve', lambda e: e.scalar_tensor_tensor(out=RotL[:, g, :], in0=swp[:], scalar=S['ssgn'][:, g:g + 1],
                                                              in1=RotL[:, g, :], op0=ALU.mult, op1=ALU.add),
                      r=['swp', 'ssgn', ('RotL', g)], w=[('RotL', g)])
            es3 = ExitStack()
            B3 = Alloc(kb, es3, f's5l{l}u')
            PHI = B3.sb('PHI', [128, 32 * 128], F32)
            for g in range(32):
                kb.op('pool', lambda e: e.tensor_scalar(out=PHI[:, g * 128:(g + 1) * 128], in0=iot[:], scalar1=S['phi1'][:, g:g + 1],
                                                        scalar2=None, op0=ALU.mult), r=['iot', 'phi1'], w=[('PHI', g)])
                kb.op('pool', lambda e: e.tensor_scalar(out=Rtab[:, g, :], in0=iot[:], scalar1=0.0, scalar2=S['mag'][:, g:g + 1],
                                                        op0=ALU.mult, op1=ALU.add), r=['iot', 'mag'], w=[('Rtab', g)])
            phikeys = [('PHI', g) for g in range(32)]
            tq = B3.sb('tq', [128, 32 * 128], F32)
            kq = B3.sb('kq', [128, 32 * 128], F32)
            for (off, dst, kd) in ((0.0, Stab, 'Stab'), (0.25, Ctab, 'Ctab')):
                kb.op('dve', lambda e: e.tensor_scalar(out=tq[:], in0=PHI[:], scalar1=off, scalar2=MAGIC, op0=ALU.add, op1=ALU.add),
                      r=phikeys, w=['tq'])
                kb.op('dve', lambda e: e.tensor_scalar(out=kq[:], in0=tq[:], scalar1=-MAGIC, scalar2=None, op0=ALU.add),
                      r=['tq'], w=['kq'])
                kb.op('dve', lambda e: e.scalar_tensor_tensor(out=tq[:], in0=PHI[:], scalar=off, in1=kq[:], op0=ALU.add,
                                                              op1=ALU.subtract), r=phikeys + ['kq'], w=['tq'])
                kb.op('dve', lambda e: e.tensor_scalar(out=tq[:], in0=tq[:], scalar1=0.5, scalar2=-0.5, op0=ALU.min, op1=ALU.max),
                      r=['tq'], w=['tq'])
                kb.op('act', lambda e: e.activation(out=dst[:].rearrange("p g j -> p (g j)"), in_=tq[:], func=AF.Sin, scale=TWO_PI_S),
                      r=['tq'], w=[kd])
            kb.barrier()
            es3.close()
            Bp1 = B.sb('Bp1', [128, 32, 128], F32)
            Bp2 = B.sb('Bp2', [128, 32, 128], F32)
            kb.op('pool', lambda e: e.memset(Bp1[:], 0.0), w=['Bp1'])
            kb.op('pool', lambda e: e.memset(Bp2[:], 0.0), w=['Bp2'])
            kb.op('pool', lambda e: e.memset(W1[:], 0.0), w=['W1'])
            kb.op('pool', lambda e: e.memset(W2[:], 0.0), w=['W2'])
            for g in range(32):
                c0 = (g % 8) * 16
                kb.op('dve', lambda e: e.tensor_scalar(out=Bp1[:, g, c0:c0 + 16], in0=X1[:, g, :], scalar1=S['fr'][:, g:g + 1],
                                                       scalar2=None, op0=ALU.mult), r=['X1', 'fr', 'Bp1'], w=[('Bp1', g)])
                kb.op('dve', lambda e: e.scalar_tensor_tensor(out=Bp1[:, g, c0:c0 + 16], in0=X2[:, g, :], scalar=S['fis'][:, g:g + 1],
                                                              in1=Bp1[:, g, c0:c0 + 16], op0=ALU.mult, op1=ALU.add),
                      r=['X2', 'fis', ('Bp1', g)], w=[('Bp1', g)])
                kb.op('dve', lambda e: e.tensor_scalar(out=Bp2[:, g, c0:c0 + 16], in0=X2[:, g, :], scalar1=S['frs'][:, g:g + 1],
                                                       scalar2=None, op0=ALU.mult), r=['X2', 'frs', 'Bp2'], w=[('Bp2', g)])
                kb.op('dve', lambda e: e.scalar_tensor_tensor(out=Bp2[:, g, c0:c0 + 16], in0=X1[:, g, :], scalar=S['fi'][:, g:g + 1],
                                                              in1=Bp2[:, g, c0:c0 + 16], op0=ALU.mult, op1=ALU.add),
                      r=['X1', 'fi', ('Bp2', g)], w=[('Bp2', g)])
            pst = [B.ps(f'pst{i}', [128, 4, 128], F32) for i in range(2)]
            pi_ = 0
            for (Bp, Lx, kn, kl) in ((Bp1, L1, 'Bp1', 'L1'), (Bp2, L2, 'Bp2', 'L2')):
                for q in range(8):
                    pt = pst[pi_ % 2]; kpt = ('pst', pi_ % 2)
                    for j in range(4):
                        g = q * 4 + j
                        kb.op('pe', lambda e: e.transpose(out=pt[:, j, :], in_=Bp[:, g, :], identity=identf[:]),
                              r=[(kn, g), 'identf'], w=[kpt])
                    kb.op('act', lambda e: e.activation(out=Lx[:, q * 4:(q + 1) * 4, :], in_=pt[:], func=AF.Copy),
                          r=[kpt], w=[(kl, q)])
                    pi_ += 1
            for (Cc, Wx, kc_, kw_, neg_all) in ((Cc1, W1, 'Cc1', 'W1', False), (Cc2, W2, 'Cc2', 'W2', True)):
                pt = pst[pi_ % 2]; kpt = ('pst', pi_ % 2)
                for c in range(4):
                    kb.op('pe', lambda e: e.transpose(out=pt[:, c, :], in_=Cc[:, c, :], identity=identf[:]),
                          r=[kc_, 'identf'], w=[kpt])
                for g in range(32):
                    c0 = (g % 8) * 16
                    if neg_all:
                        kb.op('dve', lambda e: e.tensor_scalar(out=Wx[:, g, c0:c0 + 16], in0=pt[:, g // 8, c0:c0 + 16], scalar1=-1.0,
                                                               scalar2=None, op0=ALU.mult), r=[kpt, kw_], w=[(kw_, g)])
                    else:
                        kb.op('dve', lambda e: e.tensor_scalar(out=Wx[:, g, c0:c0 + 16], in0=pt[:, g // 8, c0:c0 + 16],
                                                               scalar1=sgn1[:, 0:1], scalar2=None, op0=ALU.mult),
                              r=[kpt, kw_, 'sgn1'], w=[(kw_, g)])
                pi_ += 1
            kb.barrier()
            if G['cfg'].get('dbg_s5'):
                def dump(name, ap, shape, dt=F32):
                    d = nc.dram_tensor('dbg_' + name, list(shape), dt, kind='ExternalOutput').ap()
                    kb.dma(d, ap, s='dbg')
                mode_ = G['cfg'].get('dbg_s5')
                if mode_ == 'one':
                    dump('dt', S['dt'][:], [128, 32])
                for n_ in ['dt', 'mag', 'ang', 'phi1', 'abr', 'abi', 'fr', 'fi', 'ssgn'] if mode_ is True else []:
                    dump(n_, S[n_][:], [128, 32])
                if mode_ is True:
                  dump('s1', S['s1'][:], [128, 32]); dump('c1', S['c1'][:], [128, 32])
                  dump('Stab', Stab[:], [128, 32, 128]); dump('Ctab', Ctab[:], [128, 32, 128]); dump('Rtab', Rtab[:], [128, 32, 128])
                  dump('RotL', RotL[:], [128, 32, 128]); dump('L1', L1[:], [128, 32, 128], BF16); dump('L2', L2[:], [128, 32, 128], BF16)
                  dump('W1', W1[:], [128, 32, 128], BF16); dump('W2', W2[:], [128, 32, 128], BF16)
                kb.barrier()
        uTb = [A.sb(f'uTb{i}', [128, 4, 512], BF16) for i in range(2)]
        zab = [A.sb(f'zab{i}', [128, 4, 512], BF16) for i in range(2)]
        Dt = [A.sb(f'Dt{i}', [128, 512], F32) for i in range(8)]
        Gt = [A.sb(f'Gt{i}', [128, 512], F32) for i in range(8)]
        tmpb = [A.sb(f'tmpb{i}', [128, 512], F32) for i in range(2)]
        P1 = [A.sb(f'P1{i}', [128, 512], BF16) for i in range(2)]
        P2 = [A.sb(f'P2{i}', [128, 512], BF16) for i in range(2)]
        yv = A.sb('yv', [128, 512], F32)
        yt = A.sb('yt', [128, 512], F32)
        gy = A.sb('gy', [128, 4, 512], BF16)
        sg = A.sb('sg', [128, 512], F32)
        yo = [A.sb(f'yo{i}', [128, 512], BF16) for i in range(2)]
        pb1 = [A.ps(f'pb1{i}', [128, 512], F32) for i in range(2)]
        pb2 = [A.ps(f'pb2{i}', [128, 512], F32) for i in range(2)]
        pY = [A.ps(f'pY{i}', [128, 512], F32) for i in range(2)]
        pc = A.ps('pc', [128, 16], F32)
        cj = A.sb('cj', [128, 8], F32)
        pG = A.ps('pG', [128, 512], F32)

        def load(blk):
            for c in range(4):
                kb.dma(uTb[blk % 2][:, c, :], G['uT'][c * 128:(c + 1) * 128, blk * 512:(blk + 1) * 512],
                       r=[(id(G['uT']), c, blk)], w=[('uTb', blk % 2, c)], s='ldu')
                kb.dma(zab[blk % 2][:, c, :], G['zaT'][c * 128:(c + 1) * 128, blk * 512:(blk + 1) * 512],
                       r=[(id(G['zaT']), c, blk)], w=[('zab', blk % 2, c)], s='ldu')

        bc = lambda tab, g: tab[:, g, :].unsqueeze(1).to_broadcast([128, 4, 128])
        v4 = lambda ap: ap.rearrange("p (s j) -> p s j", j=128)
        load(0)
        bi = 0
        yi = 0
        oi = 0
        for blk in range(NBLK):
            if blk + 1 < NBLK:
                load(blk + 1)
            ub = uTb[blk % 2]
            for c in range(4):
                for gg in range(8):
                    g = 8 * c + gg
                    b1 = pb1[bi % 2]; b2 = pb2[bi % 2]; k1 = ('pb1', bi % 2); k2 = ('pb2', bi % 2)
                    tb = tmpb[bi % 2]; ktb = ('tmpb', bi % 2)
                    kb.op('pe', lambda e: e.matmul(b1[:], lhsT=L1[:, g, :], rhs=ub[:, c, :], start=True, stop=True),
                          r=[('uTb', blk % 2, c)], w=[k1])
                    kb.op('pe', lambda e: e.matmul(b2[:], lhsT=L2[:, g, :], rhs=ub[:, c, :], start=True, stop=True),
                          r=[('uTb', blk % 2, c)], w=[k2])
                    kb.op('dve', lambda e: e.tensor_tensor(out=v4(Dt[gg][:]), in0=v4(b1[:]), in1=bc(Ctab, g), op=ALU.mult),
                          r=[k1], w=[('Dt', gg), k1])
                    kb.op('dve', lambda e: e.tensor_tensor(out=v4(tb[:]), in0=v4(b2[:]), in1=bc(Stab, g), op=ALU.mult),
                          r=[k2], w=[ktb, k2])
                    kb.op('pool', lambda e: e.tensor_tensor(out=Dt[gg][:], in0=Dt[gg][:], in1=tb[:], op=ALU.add),
                          r=[('Dt', gg), ktb], w=[('Dt', gg)])
                    bi += 1
                for seg in range(4):
                    for gg in range(8):
                        g = 8 * c + gg
                        sl = slice(seg * 128, (seg + 1) * 128)
                        kb.op('dve', lambda e: e.tensor_tensor_scan(out=Gt[gg][:, sl], data0=Rtab[:, g, :], data1=Dt[gg][:, sl],
                                                                    initial=carry[:, g:g + 1], op0=ALU.mult, op1=ALU.add),
                              r=[('Dt', gg), ('carry', g)], w=[('Gt', gg, seg)])
                        kb.op('pe', lambda e: e.matmul(pc[:, gg:gg + 1], lhsT=RotL[:, g, :],
                                                       rhs=Gt[gg][:, seg * 128 + 127:seg * 128 + 128], start=True, stop=True),
                              r=[('Gt', gg, seg)], w=[('pc', gg)])
                        kb.op('act', lambda e: e.activation(out=carry[:, g:g + 1], in_=pc[:, gg:gg + 1], func=AF.Copy),
                              r=[('pc', gg)], w=[('carry', g)])
                if G['cfg'].get('s5_cut'):
                    break
                py = pY[yi % 2]; kpy = ('pY', yi % 2)
                for gg in range(8):
                    g = 8 * c + gg
                    p1 = P1[gg % 2]; p2 = P2[gg % 2]
                    gk = [('Gt', gg, sg_) for sg_ in range(4)]
                    kb.op('pool', lambda e: e.tensor_tensor(out=v4(p1[:]), in0=v4(Gt[gg][:]), in1=bc(Ctab, g), op=ALU.mult),
                          r=gk, w=[('P1', gg % 2)])
                    kb.op('pool', lambda e: e.tensor_tensor(out=v4(p2[:]), in0=v4(Gt[gg][:]), in1=bc(Stab, g), op=ALU.mult),
                          r=gk, w=[('P2', gg % 2)])
                    kb.op('pe', lambda e: e.matmul(py[:], lhsT=W1[:, g, :], rhs=p1[:], start=(gg == 0), stop=False),
                          r=[('P1', gg % 2)], w=[kpy])
                    kb.op('pe', lambda e: e.matmul(py[:], lhsT=W2[:, g, :], rhs=p2[:], start=False, stop=(gg == 7)),
                          r=[('P2', gg % 2)], w=[kpy])
                kb.op('dve', lambda e: e.scalar_tensor_tensor(out=yv[:], in0=ub[:, c, :], scalar=dsk[:, c:c + 1], in1=py[:],
                                                              op0=ALU.mult, op1=ALU.add),
                      r=[('uTb', blk % 2, c), 'dsk', kpy], w=['yv', kpy])
                kb.op('dve', lambda e: e.tensor_tensor(out=yt[:], in0=yv[:], in1=yv[:], op=ALU.mult), r=['yv'], w=['yt'])
                kb.op('dve', lambda e: e.tensor_scalar(out=yt[:], in0=yt[:], scalar1=GC1, scalar2=1.0, op0=ALU.mult, op1=ALU.add),
                      r=['yt'], w=['yt'])
                kb.op('dve', lambda e: e.tensor_tensor(out=yt[:], in0=yt[:], in1=yv[:], op=ALU.mult), r=['yt', 'yv'], w=['yt'])
                kb.op('act', lambda e: e.activation(out=yt[:], in_=yt[:], func=AF.Sigmoid, scale=GC2), r=['yt'], w=['yt'])
                kb.op('dve', lambda e: e.tensor_tensor(out=gy[:, c, :], in0=yt[:], in1=yv[:], op=ALU.mult),
                      r=['yt', 'yv'], w=[('gy', c)])
                yi += 1
            gyk = [('gy', c) for c in range(4)]
            for oc in range(4 if not G['cfg'].get('s5_cut') else 0):
                for k in range(4):
                    kb.op('pe', lambda e: e.matmul(pG[:], lhsT=gluw[:, k, oc * 128:(oc + 1) * 128], rhs=gy[:, k, :],
                                                   start=(k == 0), stop=(k == 3)), r=gyk + ['gluw'], w=['pG'])
                kb.op('act', lambda e: e.activation(out=sg[:], in_=pG[:], func=AF.Sigmoid, bias=glub[:, oc:oc + 1]),
                      r=['pG', 'glub'], w=['sg', 'pG'])
                yo_ = yo[oi % 2]; kyo = ('yo', oi % 2)
                kb.op('dve', lambda e: e.tensor_tensor(out=sg[:], in0=sg[:], in1=gy[:, oc, :], op=ALU.mult),
                      r=['sg', ('gy', oc)], w=['sg'])
                kb.op('dve', lambda e: e.tensor_tensor(out=yo_[:], in0=sg[:], in1=zab[blk % 2][:, oc, :], op=ALU.mult),
                      r=['sg', ('zab', blk % 2, oc)], w=[kyo])
                kb.dma(G['ysT'][oc * 128:(oc + 1) * 128, blk * 512:(blk + 1) * 512], yo_[:], r=[kyo],
                       w=[('ysT', oc, blk)], s='st2', q=STQ)
                oi += 1
        if G['cfg'].get('dbg_s5b'):
            kb.barrier()
            def dump2(name, ap, shape, dt=F32):
                d = nc.dram_tensor('dbg_' + name, list(shape), dt, kind='ExternalOutput').ap()
                kb.dma(d, ap, s='dbg')
            for i in range(8):
                dump2(f'Dt{i}', Dt[i][:], [128, 512]); dump2(f'Gt{i}', Gt[i][:], [128, 512])
            dump2('carry', carry[:], [128, 32]); dump2('gy', gy[:], [128, 4, 512], BF16)
            dump2('uTb', uTb[0][:], [128, 4, 512], BF16); dump2('zab', zab[0][:], [128, 4, 512], BF16)
            dump2('gluw', gluw[:], [128, 4, 512], BF16); dump2('sg', sg[:], [128, 512])
            dump2('L1b', L1[:], [128, 32, 128], BF16); dump2('Ctabb', Ctab[:], [128, 32, 128]); dump2('Rtabb', Rtab[:], [128, 32, 128])
            dump2('RotLb', RotL[:], [128, 32, 128]); dump2('W1b', W1[:], [128, 32, 128], BF16)
    kb.barrier()


KSCALE = 128.0 ** -0.5


def phase_ml(kb, G, l):
    nc = kb.nc
    cst = G['cst']
    with ExitStack() as es:
        A = Alloc(kb, es, f'mll{l}')
        identf = A.sb('identf', [128, 128], F32)
        ident = A.sb('ident', [128, 128], BF16)
        U = A.sb('U', [128, 128], F32)
        mneg = A.sb('mneg', [128, 128], F32)
        ones = A.sb('ones', [128, 128], F32)
        cw = A.sb('cw', [128, 8, 4], F32)
        cb = A.sb('cb', [128, 8], F32)
        igb = A.sb('igb', [128, 4], F32)
        fgb = A.sb('fgb', [128, 4], F32)
        mhg = A.sb('mhg', [128, 4], F32)
        C = A.sb('C', [128, 4, 129], F32)
        Cbf = A.sb('Cbf', [128, 4, 129], BF16)
        kb.dma(identf[:], cst['ident'], w=['identf'])
        kb.dma(U[:], cst['utri'], w=['U'])
        kb.dma(mneg[:], cst['mneg'], w=['mneg'])
        kb.dma(cw[:], G['conv_w'][l], w=['cw'])
        kb.dma(cb[:], G['conv_b'][l], w=['cb'])
        kb.dma(igb[:], G['igb'][l], w=['igb'])
        kb.dma(fgb[:], G['fgb'][l], w=['fgb'])
        kb.dma(mhg[:], G['mhg'][l], w=['mhg'])
        kb.op('dve', lambda e: e.tensor_copy(out=ident[:], in_=identf[:]), r=['identf'], w=['ident'])
        kb.op('pool', lambda e: e.memset(ones[:], 1.0), w=['ones'])
        kb.op('pool', lambda e: e.memset(C[:], 0.0), w=[('C', h) for h in range(4)])
        kb.op('pool', lambda e: e.memset(Cbf[:], 0.0), w=[('Cbf', h) for h in range(4)])
        xin = [A.sb(f'xin{i}', [128, 8, 515], BF16) for i in range(2)]
        zb = [A.sb(f'zb{i}', [128, 4, 512], BF16) for i in range(2)]
        vaug = [A.sb(f'vaug{i}', [128, 4, 4, 129], BF16) for i in range(2)]
        gat = [A.sb(f'gat{i}', [128, 4, 8], F32) for i in range(2)]
        acc = A.sb('acc', [128, 512], F32)
        qk = A.sb('qk', [128, 8, 512], BF16)
        ig = A.sb('ig', [128, 4, 4], F32)
        lf = A.sb('lf', [128, 4, 4], F32)
        ybo = [A.sb(f'ybo{i}', [128, 4, 512], BF16) for i in range(2)]
        LF = [A.sb(f'LF{i}', [128, 128], F32) for i in range(2)]
        Am = [A.sb(f'Am{i}', [128, 128], F32) for i in range(2)]
        AT = [A.sb(f'AT{i}', [128, 128], F32) for i in range(2)]
        eb = [A.sb(f'eb{i}', [128, 128], F32) for i in range(2)]
        csc = A.sb('csc', [128, 4], F32)
        wcol = [A.sb(f'wcol{i}', [128, 1], F32) for i in range(2)]
        STm = [A.sb(f'STm{i}', [128, 128], BF16) for i in range(2)]
        qs = [A.sb(f'qs{i}', [128, 128], BF16) for i in range(2)]
        sm = [A.sb(f'sm{i}', [128, 8], F32) for i in range(2)]
        junk = A.sb('junk', [128, 128], BF16)
        hn = [A.sb(f'hn{i}', [128, 128], BF16) for i in range(2)]
        kw = [A.sb(f'kw{i}', [128, 128], BF16) for i in range(2)]
        pB = [A.ps(f'pB{i}', [128, 256], F32) for i in range(2)]
        pS = [A.ps(f'pS{i}', [128, 128], F32) for i in range(2)]
        pN = [A.ps(f'pN{i}', [128, 129], F32) for i in range(2)]
        pTK = A.ps('pTK', [128, 2, 128], BF16)
        pD = A.ps('pD', [128, 129], F32)
        for i in range(2):
            kb.op('pool', lambda e: e.memset(vaug[i][:], 1.0), w=[('vaug', i)])
            kb.op('pool', lambda e: e.memset(xin[i][:, :, 0:3], 0.0), w=[('xin', i, 'halo')])

        def load(blk):
            b = blk % 2
            c0 = blk * 512
            for qk_i, src in ((0, G['qbT']), (1, G['kbT'])):
                for h in range(4):
                    ch = qk_i * 4 + h
                    if blk == 0:
                        kb.dma(xin[b][:, ch, 3:515], src[h * 128:(h + 1) * 128, 0:512], w=[('xin', b, ch)], s='ldm')
                    else:
                        kb.dma(xin[b][:, ch, 0:515], src[h * 128:(h + 1) * 128, c0 - 3:c0 + 512],
                               w=[('xin', b, ch), ('xin', b, 'halo')], s='ldm')
            for h in range(4):
                kb.dma(zb[b][:, h, :], G['zbT'][h * 128:(h + 1) * 128, c0:c0 + 512], w=[('zb', b, h)], s='ldm')
            for ci in range(4):
                t0 = c0 + ci * 128
                kb.dma(vaug[b][:, ci, :, 0:128], G['vb'][t0:t0 + 128, :].rearrange("s (h e) -> s h e", h=4),
                       w=[('vaug', b, ci)], r=[('vaug', b)], s='ldm')
                kb.dma(gat[b][:, ci, :], G['ifg'][t0:t0 + 128, :], w=[('gat', b, ci)], s='ldm')

        load(0)
        it = 0
        for blk in range(NBLK):
            b = blk % 2
            if blk + 1 < NBLK:
                load(blk + 1)
            for ch in range(8):
                xk = [('xin', b, ch), ('xin', b, 'halo')]
                kb.op('dve', lambda e: e.tensor_scalar(out=acc[:], in0=xin[b][:, ch, 3:515], scalar1=cw[:, ch, 3:4],
                                                       scalar2=cb[:, ch:ch + 1], op0=ALU.mult, op1=ALU.add),
                      r=xk + ['cw', 'cb'], w=['acc'])
                for j in range(3):
                    kb.op('dve', lambda e: e.scalar_tensor_tensor(out=acc[:], in0=xin[b][:, ch, j:j + 512], scalar=cw[:, ch, j:j + 1],
                                                                  in1=acc[:], op0=ALU.mult, op1=ALU.add),
                          r=xk + ['cw', 'acc'], w=['acc'])
                kb.op('act', lambda e: e.activation(out=qk[:, ch, :], in_=acc[:], func=AF.Silu), r=['acc'], w=[('qk', ch)])
                if ch >= 4:
                    kb.op('pool', lambda e: e.tensor_scalar(out=qk[:, ch, :], in0=qk[:, ch, :], scalar1=KSCALE, scalar2=1.0,
                                                            op0=ALU.mult, op1=ALU.mult), r=[('qk', ch)], w=[('qk', ch)])
            gk = [('gat', b, ci) for ci in range(4)]
            kb.op('dve', lambda e: e.tensor_tensor(out=ig[:], in0=gat[b][:, :, 0:4], in1=igb[:].unsqueeze(1).to_broadcast([128, 4, 4]),
                                                   op=ALU.add), r=gk + ['igb'], w=['ig'])
            kb.op('dve', lambda e: e.tensor_tensor(out=lf[:], in0=gat[b][:, :, 4:8], in1=fgb[:].unsqueeze(1).to_broadcast([128, 4, 4]),
                                                   op=ALU.add), r=gk + ['fgb'], w=['lf'])
            kb.op('act', lambda e: e.activation(out=lf[:], in_=lf[:], func=AF.Exp, scale=-1.0), r=['lf'], w=['lf'])
            kb.op('act', lambda e: e.activation(out=lf[:], in_=lf[:], func=AF.Ln, bias=1.0), r=['lf'], w=['lf'])
            kb.op('dve', lambda e: e.tensor_scalar(out=lf[:], in0=lf[:], scalar1=-1.0, scalar2=None, op0=ALU.mult), r=['lf'], w=['lf'])
            for ci in range(4):
                cs = slice(ci * 128, (ci + 1) * 128)
                for h in range(4):
                    i2 = it % 2
                    pb = pB[i2]; kpb = ('pB', i2)
                    kb.op('dve', lambda e: e.tensor_scalar(out=LF[i2][:], in0=ones[:], scalar1=lf[:, ci, h:h + 1], scalar2=None,
                                                           op0=ALU.mult), r=['ones', 'lf'], w=[('LF', i2)])
                    kb.op('pe', lambda e: e.matmul(pb[:, 0:128], lhsT=LF[i2][:], rhs=U[:], start=True, stop=True),
                          r=[('LF', i2), 'U'], w=[kpb])
                    kb.op('pe', lambda e: e.matmul(pb[:, 128:132], lhsT=U[:], rhs=lf[:, ci, :], start=True, stop=True),
                          r=['lf', 'U'], w=[kpb])
                    kb.op('dve', lambda e: e.tensor_tensor(out=csc[:], in0=ig[:, ci, :], in1=pb[:, 128:132], op=ALU.subtract),
                          r=['ig', kpb], w=['csc', kpb])
                    kb.op('dve', lambda e: e.tensor_tensor(out=Am[i2][:], in0=pb[:, 0:128], in1=mneg[:], op=ALU.add),
                          r=[kpb, 'mneg'], w=[('Am', i2), kpb])
                    kb.op('act', lambda e: e.activation(out=AT[i2][:], in_=Am[i2][:], func=AF.Exp, bias=csc[:, h:h + 1]),
                          r=[('Am', i2), 'csc'], w=[('AT', i2)])
                    kb.op('act', lambda e: e.activation(out=eb[i2][:], in_=pb[:, 0:128], func=AF.Exp), r=[kpb], w=[('eb', i2), kpb])
                    kb.op('act', lambda e: e.activation(out=wcol[i2][:], in_=pb[:, 127:128], func=AF.Exp, bias=csc[:, h:h + 1]),
                          r=[kpb, 'csc'], w=[('wcol', i2), kpb])
                    ps_ = pS[i2]; kps = ('pS', i2)
                    kb.op('pe', lambda e: e.matmul(ps_[:], lhsT=qk[:, 4 + h, cs], rhs=qk[:, h, cs], start=True, stop=True),
                          r=[('qk', 4 + h), ('qk', h)], w=[kps])
                    kb.op('dve', lambda e: e.tensor_tensor(out=STm[i2][:], in0=ps_[:], in1=AT[i2][:], op=ALU.mult),
                          r=[kps, ('AT', i2)], w=[('STm', i2), kps])
                    kb.op('pool', lambda e: e.tensor_tensor(out=qs[i2][:], in0=qk[:, h, cs], in1=eb[i2][:], op=ALU.mult),
                          r=[('qk', h), ('eb', i2)], w=[('qs', i2)])
                    pn = pN[i2]; kpn = ('pN', i2)
                    kb.op('pe', lambda e: e.matmul(pn[:], lhsT=STm[i2][:], rhs=vaug[b][:, ci, h, :], start=True, stop=False),
                          r=[('STm', i2), ('vaug', b, ci)], w=[kpn])
                    kb.op('pe', lambda e: e.matmul(pn[:], lhsT=qs[i2][:], rhs=Cbf[:, h, :], start=False, stop=True),
                          r=[('qs', i2), ('Cbf', h)], w=[kpn])
                    s_ = sm[i2]; ksm = ('sm', i2)
                    kb.op('dve', lambda e: e.tensor_scalar(out=s_[:, 4:5], in0=pn[:, 128:129], scalar1=-1.0, scalar2=None, op0=ALU.mult),
                          r=[kpn], w=[ksm, kpn])
                    kb.op('dve', lambda e: e.scalar_tensor_tensor(out=s_[:, 0:1], in0=pn[:, 128:129], scalar=1.0, in1=s_[:, 4:5],
                                                                  op0=ALU.max, op1=ALU.max), r=[kpn, ksm], w=[ksm, kpn])
                    kb.op('dve', lambda e: e.reciprocal(out=s_[:, 0:1], in_=s_[:, 0:1]), r=[ksm], w=[ksm])
                    kb.op('act', lambda e: e.activation(out=junk[:], in_=pn[:, 0:128], func=AF.Square, accum_out=s_[:, 1:2]),
                          r=[kpn, ksm], w=['junk', ksm, kpn])
                    kb.op('dve', lambda e: e.tensor_tensor(out=s_[:, 2:3], in0=s_[:, 0:1], in1=s_[:, 0:1], op=ALU.mult), r=[ksm], w=[ksm])
                    kb.op('dve', lambda e: e.tensor_tensor(out=s_[:, 2:3], in0=s_[:, 2:3], in1=s_[:, 1:2], op=ALU.mult), r=[ksm], w=[ksm])
                    kb.op('dve', lambda e: e.tensor_scalar(out=s_[:, 2:3], in0=s_[:, 2:3], scalar1=1.0 / 128, scalar2=EPS, op0=ALU.mult,
                                                           op1=ALU.add), r=[ksm], w=[ksm])
                    kb.op('act', lambda e: e.activation(out=s_[:, 2:3], in_=s_[:, 2:3], func=AF.Sqrt), r=[ksm], w=[ksm])
                    kb.op('dve', lambda e: e.reciprocal(out=s_[:, 2:3], in_=s_[:, 2:3]), r=[ksm], w=[ksm])
                    kb.op('dve', lambda e: e.tensor_tensor(out=s_[:, 3:4], in0=s_[:, 2:3], in1=s_[:, 0:1], op=ALU.mult), r=[ksm], w=[ksm])
                    kb.op('dve', lambda e: e.tensor_scalar(out=hn[i2][:], in0=pn[:, 0:128], scalar1=s_[:, 3:4], scalar2=None, op0=ALU.mult),
                          r=[kpn, ksm], w=[('hn', i2), kpn])
                    kb.op('pe', lambda e: e.transpose(out=pTK[:, 0, :], in_=hn[i2][:], identity=ident[:]),
                          r=[('hn', i2), 'ident'], w=[('pTK', 0)])
                    kb.op('dve', lambda e: e.scalar_tensor_tensor(out=ybo[b][:, h, cs], in0=pTK[:, 0, :], scalar=mhg[:, h:h + 1],
                                                                  in1=zb[b][:, h, cs], op0=ALU.mult, op1=ALU.mult),
                          r=[('pTK', 0), 'mhg', ('zb', b, h)], w=[('ybo', b, h, ci), ('pTK', 0)])
                    kb.op('pe', lambda e: e.transpose(out=pTK[:, 1, :], in_=qk[:, 4 + h, cs], identity=ident[:]),
                          r=[('qk', 4 + h), 'ident'], w=[('pTK', 1)])
                    kb.op('dve', lambda e: e.tensor_scalar(out=kw[i2][:], in0=pTK[:, 1, :], scalar1=wcol[i2][:, 0:1], scalar2=None,
                                                           op0=ALU.mult), r=[('pTK', 1), ('wcol', i2)], w=[('kw', i2), ('pTK', 1)])
                    kb.op('pe', lambda e: e.matmul(pD[:], lhsT=kw[i2][:], rhs=vaug[b][:, ci, h, :], start=True, stop=True),
                          r=[('kw', i2), ('vaug', b, ci)], w=['pD'])
                    kb.op('dve', lambda e: e.scalar_tensor_tensor(out=C[:, h, :], in0=C[:, h, :], scalar=eb[i2][:, 127:128], in1=pD[:],
                                                                  op0=ALU.mult, op1=ALU.add),
                          r=[('C', h), ('eb', i2), 'pD'], w=[('C', h), 'pD'])
                    kb.op('act', lambda e: e.activation(out=Cbf[:, h, :], in_=C[:, h, :], func=AF.Copy), r=[('C', h)], w=[('Cbf', h)])
                    it += 1
            for h in range(4):
                kb.dma(G['ysT'][512 + h * 128:512 + (h + 1) * 128, blk * 512:(blk + 1) * 512], ybo[b][:, h, :],
                       r=[('ybo', b, h, ci) for ci in range(4)], w=[('ysTb', h, blk)], s='st2', q=STQ)
    kb.barrier()


ATT_SCALE = 128.0 ** -0.5
IDX_SCALE = (8.0 ** -0.5) * (64.0 ** -0.5)
TOPK = 256
NBIS = 24


def phase_dsa(kb, G, l):
    nc = kb.nc
    cst = G['cst']
    NQ = L // 128
    with ExitStack() as es:
        A = Alloc(kb, es, f'dsl{l}')
        identf = A.sb('identf', [128, 128], F32)
        ident = A.sb('ident', [128, 128], BF16)
        causn = A.sb('causn', [128, 128], F32)
        kiT2 = A.sb('kiT2', [128, L], BF16)
        kcT = A.sb('kcT', [128, L], BF16)
        vaug = A.sb('vaug', [128, NQ, 129], BF16)
        kb.dma(identf[:], cst['ident'], w=['identf'])
        kb.dma(causn[:], cst['causn'], w=['causn'])
        kb.op('dve', lambda e: e.tensor_copy(out=ident[:], in_=identf[:]), r=['identf'], w=['ident'])
        kb.op('pool', lambda e: e.memset(vaug[:], 1.0), w=['vaug'])
        for blk in range(NBLK):
            kb.dma(kcT[:, blk * 512:(blk + 1) * 512], G['kcT'][:, blk * 512:(blk + 1) * 512], w=[('kcT', blk)], s='ldk')
            kb.dma(vaug[:, blk * 4:(blk + 1) * 4, 0:128], G['vc'][blk * 512:(blk + 1) * 512, :].rearrange("(n s) e -> s n e", s=128),
                   r=['vaug'], w=[('vaug', blk)], s='ldk')
        with ExitStack() as es2:
            B = Alloc(kb, es2, f'dsl{l}t')
            onesf = B.sb('onesf', [128, 128], F32)
            sel8 = B.sb('sel8', [8, 4, 128], F32)
            qng = B.sb('qng', [128, 2], F32)
            kng = B.sb('kng', [128, 64], F32)
            knb = B.sb('knb', [128, 64], F32)
            wuq = B.sb('wuq', [128, 2, 512], BF16)
            wqi = B.sb('wqi', [128, 2, 512], BF16)
            wst = B.sb('wst', [128, 2, 512], F32)
            kb.op('pool', lambda e: e.memset(onesf[:], 1.0), w=['onesf'])
            kb.dma(sel8[:], cst['sel8'], w=['sel8'])
            kb.dma(qng[:], G['qng'][l], w=['qng'])
            kb.dma(kng[:], G['kng'][l], w=['kng'])
            kb.dma(knb[:], G['knb'][l], w=['knb'])
            for (src, dst, kd) in ((G['w_uq'][l], wuq, 'wuq'), (G['w_qidx'][l], wqi, 'wqi')):
                kb.dma(wst[:], src.rearrange("(kc p) n -> p kc n", p=128), w=['wst'], s='ld')
                for kc in range(2):
                    kb.op('dve', lambda e: e.tensor_scalar(out=dst[:, kc, :], in0=wst[:, kc, :], scalar1=qng[:, kc:kc + 1], scalar2=None,
                                                           op0=ALU.mult), r=['wst', 'qng'], w=[(kd, kc)])
            cq = [B.sb(f'cq{i}', [128, 2, 512], BF16) for i in range(2)]
            wT = [B.sb(f'wT{i}', [8, 512], F32) for i in range(2)]
            sq = B.sb('sq', [128, 2, 512], F32)
            rr = B.sb('rr', [128, 512], F32)
            tq = B.sb('tq', [128, 512], F32)
            qo = [B.sb(f'qo{i}', [128, 512], BF16) for i in range(2)]
            kx = [B.sb(f'kx{i}', [128, 64], F32) for i in range(2)]
            kst = [B.sb(f'kst{i}', [128, 4], F32) for i in range(2)]
            kjunk = B.sb('kjunk', [128, 64], F32)
            kn = [B.sb(f'kn{i}', [128, 128], BF16) for i in range(2)]
            pR = B.ps('pR', [128, 512], F32)
            pQ = [B.ps(f'pQ{i}', [128, 512], F32) for i in range(2)]
            pW = B.ps('pW', [128, 512], F32)
            pK = [B.ps(f'pK{i}', [128, 128], BF16) for i in range(2)]
            oi = 0
            ti = 0
            for blk in range(NBLK):
                b = blk % 2
                c0 = blk * 512
                for kc in range(2):
                    kb.dma(cq[b][:, kc, :], G['cqT'][kc * 128:(kc + 1) * 128, c0:c0 + 512], w=[('cq', b, kc)], s='ldc')
                kb.dma(wT[b][:], G['widxT'][:, c0:c0 + 512], w=[('wT', b)], s='ldc')
                cqk = [('cq', b, 0), ('cq', b, 1)]
                kb.op('dve', lambda e: e.tensor_tensor(out=sq[:], in0=cq[b][:], in1=cq[b][:], op=ALU.mult), r=cqk, w=['sq'])
                for kc in range(2):
                    kb.op('pe', lambda e: e.matmul(pR[:], lhsT=onesf[:], rhs=sq[:, kc, :], start=(kc == 0), stop=(kc == 1)),
                          r=['sq', 'onesf'], w=['pR'])
                kb.op('dve', lambda e: e.tensor_scalar(out=rr[:], in0=pR[:], scalar1=1.0 / 256, scalar2=EPS, op0=ALU.mult, op1=ALU.add),
                      r=['pR'], w=['rr', 'pR'])
                kb.op('act', lambda e: e.activation(out=rr[:], in_=rr[:], func=AF.Sqrt), r=['rr'], w=['rr'])
                kb.op('dve', lambda e: e.reciprocal(out=rr[:], in_=rr[:]), r=['rr'], w=['rr'])
                for h in range(4):
                    pq = pQ[oi % 2]; kpq = ('pQ', oi % 2); q_ = qo[oi % 2]; kq = ('qo', oi % 2)
                    for kc in range(2):
                        kb.op('pe', lambda e: e.matmul(pq[:], lhsT=wuq[:, kc, h * 128:(h + 1) * 128], rhs=cq[b][:, kc, :],
                                                       start=(kc == 0), stop=(kc == 1)), r=cqk + [('wuq', kc)], w=[kpq])
                    kb.op('dve', lambda e: e.scalar_tensor_tensor(out=q_[:], in0=pq[:], scalar=ATT_SCALE, in1=rr[:], op0=ALU.mult,
                                                                  op1=ALU.mult), r=[kpq, 'rr'], w=[kq, kpq])
                    kb.dma(G['qT'][h, :, c0:c0 + 512], q_[:], r=[kq], w=[('qT', h, blk)], s='st2', q=STQ)
                    oi += 1
                for pr in range(4):
                    pq = pQ[oi % 2]; kpq = ('pQ', oi % 2); q_ = qo[oi % 2]; kq = ('qo', oi % 2)
                    for kc in range(2):
                        kb.op('pe', lambda e: e.matmul(pq[:], lhsT=wqi[:, kc, pr * 128:(pr + 1) * 128], rhs=cq[b][:, kc, :],
                                                       start=(kc == 0), stop=(kc == 1)), r=cqk + [('wqi', kc)], w=[kpq])
                    kb.op('pe', lambda e: e.matmul(pW[:], lhsT=sel8[:, pr, :], rhs=wT[b][:], start=True, stop=True),
                          r=[('wT', b), 'sel8'], w=['pW'])
                    kb.op('dve', lambda e: e.scalar_tensor_tensor(out=tq[:], in0=pq[:], scalar=IDX_SCALE, in1=rr[:], op0=ALU.mult,
                                                                  op1=ALU.mult), r=[kpq, 'rr'], w=['tq', kpq])
                    kb.op('dve', lambda e: e.tensor_tensor(out=q_[:], in0=tq[:], in1=pW[:], op=ALU.mult), r=['tq', 'pW'], w=[kq, 'pW'])
                    kb.dma(G['qiT'][pr, :, c0:c0 + 512], q_[:], r=[kq], w=[('qiT', pr, blk)], s='st2', q=STQ)
                    oi += 1
                for tt in range(4):
                    t = blk * 4 + tt
                    j = ti % 2
                    x_ = kx[j]; s_ = kst[j]; n_ = kn[j]; kxk = ('kx', j); ksk = ('kst', j); knk = ('kn', j)
                    kb.dma(x_[:], G['kidx'][t * 128:(t + 1) * 128, :], w=[kxk], s='ldc')
                    kb.op('dve', lambda e: e.tensor_reduce(out=s_[:, 0:1], in_=x_[:], axis=AX.X, op=ALU.add), r=[kxk], w=[ksk])
                    kb.op('dve', lambda e: e.tensor_scalar(out=s_[:, 0:1], in0=s_[:, 0:1], scalar1=-1.0 / 64, scalar2=None, op0=ALU.mult),
                          r=[ksk], w=[ksk])
                    kb.op('dve', lambda e: e.tensor_scalar(out=x_[:], in0=x_[:], scalar1=s_[:, 0:1], scalar2=None, op0=ALU.add),
                          r=[kxk, ksk], w=[kxk])
                    kb.op('act', lambda e: e.activation(out=kjunk[:], in_=x_[:], func=AF.Square, accum_out=s_[:, 1:2]),
                          r=[kxk, ksk], w=['kjunk', ksk])
                    kb.op('dve', lambda e: e.tensor_scalar(out=s_[:, 1:2], in0=s_[:, 1:2], scalar1=1.0 / 64, scalar2=EPS, op0=ALU.mult,
                                                           op1=ALU.add), r=[ksk], w=[ksk])
                    kb.op('act', lambda e: e.activation(out=s_[:, 1:2], in_=s_[:, 1:2], func=AF.Sqrt), r=[ksk], w=[ksk])
                    kb.op('dve', lambda e: e.reciprocal(out=s_[:, 1:2], in_=s_[:, 1:2]), r=[ksk], w=[ksk])
                    kb.op('dve', lambda e: e.scalar_tensor_tensor(out=x_[:], in0=x_[:], scalar=s_[:, 1:2], in1=kng[:], op0=ALU.mult,
                                                                  op1=ALU.mult), r=[kxk, ksk, 'kng'], w=[kxk])
                    kb.op('dve', lambda e: e.tensor_tensor(out=n_[:, 0:64], in0=x_[:], in1=knb[:], op=ALU.add), r=[kxk, 'knb'], w=[knk])
                    kb.op('dve', lambda e: e.tensor_tensor(out=n_[:, 64:128], in0=x_[:], in1=knb[:], op=ALU.add), r=[kxk, 'knb', knk], w=[knk])
                    kb.op('pe', lambda e: e.transpose(out=pK[j][:], in_=n_[:], identity=ident[:]), r=[knk, 'ident'], w=[('pK', j)])
                    kb.op('act', lambda e: e.activation(out=kiT2[:, t * 128:(t + 1) * 128], in_=pK[j][:], func=AF.Copy),
                          r=[('pK', j)], w=[('kiT2', t)])
                    ti += 1
            kb.barrier()
        score = A.sb('score', [128, L], F32)
        T8 = A.sb('T8', [128, 8, 512], F32)
        junkb = A.sb('junkb', [128, L], BF16)
        mask = A.sb('mask', [128, L], BF16)
        maskT = A.sb('maskT', [128, NQ, 128], BF16)
        qib = [A.sb(f'qib{i}', [128, 4, 128], BF16) for i in range(2)]
        qtb = [A.sb(f'qtb{i}', [128, 4, 128], BF16) for i in range(2)]
        zcb = [A.sb(f'zcb{i}', [128, 4, 128], BF16) for i in range(2)]
        wdx = [A.sb(f'wdx{i}', [128, 8], F32) for i in range(2)]
        lo = A.sb('lo', [128, 8], F32)
        hi = A.sb('hi', [128, 8], F32)
        bs = A.sb('bs', [128, 8], F32)
        tauc = A.sb('tauc', [128, 1], F32)
        Eb = [A.sb(f'Eb{i}', [128, 512], BF16) for i in range(2)]
        Pb = [A.sb(f'Pb{i}', [128, 512], BF16) for i in range(2)]
        osb = A.sb('osb', [128, 4, 128], BF16)
        rec = A.sb('rec', [128, 4], F32)
        ycb = [A.sb(f'ycb{i}', [128, 4, 128], BF16) for i in range(2)]
        pS = [A.ps(f'pS{i}', [128, 512], F32) for i in range(3)]
        pMT = A.ps('pMT', [128, 4, 128], BF16)
        pL = [A.ps(f'pL{i}', [128, 512], F32) for i in range(2)]
        pO = [A.ps(f'pO{i}', [128, 2, 129], F32) for i in range(2)]
        kb.op('pool', lambda e: e.memset(tauc[:], -1.0e29), w=['tauc'])

        def loadq(i):
            b = i % 2
            t0 = i * 128
            kb.dma(qib[b][:], G['qiT'][:, :, t0:t0 + 128].rearrange("r m t -> m r t"), w=[('qib', b)], s='ldq')
            kb.dma(qtb[b][:], G['qT'][:, :, t0:t0 + 128].rearrange("h d t -> d h t"), w=[('qtb', b)], s='ldq')
            kb.dma(zcb[b][:], G['zcT'][:, t0:t0 + 128].rearrange("(h e) t -> e h t", h=4), w=[('zcb', b)], s='ldq')
            kb.dma(wdx[b][:], G['widx'][t0:t0 + 128, :], w=[('wdx', b)], s='ldq')

        loadq(0)
        si = 0
        li = 0
        for i in range(NQ):
            b = i % 2
            if i + 1 < NQ:
                loadq(i + 1)
            nk = (i + 1) * 128
            nkc = (nk + 511) // 512
            kb.op('dve', lambda e: e.tensor_scalar(out=hi[:], in0=wdx[b][:], scalar1=0.0, scalar2=BIG, op0=ALU.is_ge, op1=ALU.mult),
                  r=[('wdx', b)], w=['hi'])
            kb.op('dve', lambda e: e.tensor_scalar(out=lo[:], in0=hi[:], scalar1=-BIG, scalar2=None, op0=ALU.add), r=['hi'], w=['lo'])
            for kc in range(nkc):
                ncol = min(512, nk - kc * 512)
                for h in range(8):
                    pr = h // 2
                    p0 = 64 * (h % 2)
                    ps_ = pS[si % 3]; kps = ('pS', si % 3)
                    kb.op('pe', lambda e: e.matmul(ps_[:, 0:ncol], lhsT=qib[b][p0:p0 + 64, pr, :], rhs=kiT2[p0:p0 + 64, kc * 512:kc * 512 + ncol],
                                                   start=True, stop=True), r=[('qib', b)], w=[kps])
                    kb.op('dve', lambda e: e.tensor_scalar(out=T8[:, h, 0:ncol], in0=ps_[:, 0:ncol], scalar1=lo[:, h:h + 1],
                                                           scalar2=hi[:, h:h + 1], op0=ALU.max, op1=ALU.min),
                          r=[kps, 'lo', 'hi'], w=[('T8', h), kps])
                    si += 1
                kb.op('dve', lambda e: e.tensor_reduce(out=score[:, kc * 512:kc * 512 + ncol],
                                                       in_=T8[:, :, 0:ncol].rearrange("p h s -> p s h"), axis=AX.X, op=ALU.add),
                      r=[('T8', h) for h in range(8)], w=[('score', kc)])
            sck = [('score', kc) for kc in range(nkc)]
            kb.op('dve', lambda e: e.tensor_tensor(out=score[:, i * 128:nk], in0=score[:, i * 128:nk], in1=causn[:], op=ALU.add),
                  r=sck + ['causn'], w=sck)
            if i >= 2:
                nv = i * 128
                kb.op('dve', lambda e: e.tensor_reduce(out=bs[:, 0:1], in_=score[:, 0:nv], axis=AX.X, op=ALU.min), r=sck, w=['bs'])
                kb.op('dve', lambda e: e.tensor_reduce(out=bs[:, 5:6], in_=score[:, 0:nv], axis=AX.X, op=ALU.max), r=sck + ['bs'], w=['bs'])
                kb.op('dve', lambda e: e.tensor_tensor(out=bs[:, 1:2], in0=bs[:, 5:6], in1=bs[:, 0:1], op=ALU.subtract), r=['bs'], w=['bs'])
                kb.op('dve', lambda e: e.tensor_scalar(out=bs[:, 1:2], in0=bs[:, 1:2], scalar1=1.0001, scalar2=1e-6, op0=ALU.mult, op1=ALU.add),
                      r=['bs'], w=['bs'])
                for k in range(NBIS):
                    f = 2.0 ** -(k + 1)
                    kb.op('dve', lambda e: e.tensor_scalar(out=bs[:, 2:3], in0=bs[:, 1:2], scalar1=f, scalar2=bs[:, 0:1], op0=ALU.mult,
                                                           op1=ALU.add), r=['bs'], w=['bs'])
                    kb.op('dve', lambda e: e.tensor_scalar(out=junkb[:, 0:nk], in0=score[:, 0:nk], scalar1=bs[:, 2:3], scalar2=None,
                                                           op0=ALU.is_ge, op1=ALU.add, accum_out=bs[:, 3:4]), r=sck + ['bs'], w=['bs', 'junkb'])
                    kb.op('dve', lambda e: e.tensor_scalar(out=bs[:, 4:5], in0=bs[:, 3:4], scalar1=TOPK - 0.5, scalar2=bs[:, 1:2],
                                                           op0=ALU.is_ge, op1=ALU.mult), r=['bs'], w=['bs'])
                    kb.op('dve', lambda e: e.scalar_tensor_tensor(out=bs[:, 0:1], in0=bs[:, 4:5], scalar=f, in1=bs[:, 0:1], op0=ALU.mult,
                                                                  op1=ALU.add), r=['bs'], w=['bs'])
                tau = bs[:, 0:1]
                tk = 'bs'
            else:
                tau = tauc[:, 0:1]
                tk = 'tauc'
            kb.op('dve', lambda e: e.tensor_scalar(out=mask[:, 0:nk], in0=score[:, 0:nk], scalar1=tau, scalar2=None, op0=ALU.is_ge),
                  r=sck + [tk], w=['mask'])
            for q4 in range((i + 4) // 4):
                nb = min(4, i + 1 - q4 * 4)
                for j in range(nb):
                    kbk = q4 * 4 + j
                    kb.op('pe', lambda e: e.transpose(out=pMT[:, j, :], in_=mask[:, kbk * 128:(kbk + 1) * 128], identity=ident[:]),
                          r=['mask', 'ident'], w=['pMT'])
                kb.op('act', lambda e: e.activation(out=maskT[:, q4 * 4:q4 * 4 + nb, :], in_=pMT[:, 0:nb, :], func=AF.Copy),
                      r=['pMT'], w=[('maskT', q4)])
            for kbk in range(i + 1):
                pl = pL[li % 2]; kpl = ('pL', li % 2); E_ = Eb[li % 2]; kE = ('Eb', li % 2); P_ = Pb[li % 2]; kP = ('Pb', li % 2)
                kb.op('pe', lambda e: e.matmul(pl[:], lhsT=kcT[:, kbk * 128:(kbk + 1) * 128], rhs=qtb[b][:].rearrange("d h t -> d (h t)"),
                                               start=True, stop=True), r=[('qtb', b)], w=[kpl])
                kb.op('act', lambda e: e.activation(out=E_[:], in_=pl[:], func=AF.Exp), r=[kpl], w=[kE, kpl])
                kb.op('pool', lambda e: e.tensor_tensor(out=P_[:].rearrange("s (h t) -> s h t", h=4), in0=E_[:].rearrange("s (h t) -> s h t", h=4),
                                                        in1=maskT[:, kbk, :].unsqueeze(1).to_broadcast([128, 4, 128]), op=ALU.mult),
                      r=[kE, ('maskT', kbk // 4)], w=[kP])
                for h in range(4):
                    kb.op('pe', lambda e: e.matmul(pO[h // 2][:, h % 2, :], lhsT=P_[:, h * 128:(h + 1) * 128], rhs=vaug[:, kbk, :],
                                                   start=(kbk == 0 and h % 2 == 0), stop=(kbk == i), skip_group_check=True),
                          r=[kP], w=[('pO', h // 2)])
                li += 1
            for hp in range(2):
                kb.op('dve', lambda e: e.reciprocal(out=rec[:, hp * 2:hp * 2 + 2], in_=pO[hp][:, :, 128]), r=[('pO', hp)], w=[('rec', hp), ('pO', hp)])
                for h2 in range(2):
                    h = hp * 2 + h2
                    kb.op('dve', lambda e: e.tensor_scalar(out=osb[:, h, :], in0=pO[hp][:, h2, 0:128], scalar1=rec[:, h:h + 1], scalar2=None,
                                                           op0=ALU.mult), r=[('pO', hp), ('rec', hp)], w=[('osb', h), ('pO', hp)])
            for h in range(4):
                kb.op('pe', lambda e: e.transpose(out=pMT[:, h, :], in_=osb[:, h, :], identity=ident[:]), r=[('osb', h), 'ident'], w=['pMT'])
            kb.op('dve', lambda e: e.tensor_tensor(out=ycb[b][:], in0=pMT[:], in1=zcb[b][:], op=ALU.mult),
                  r=['pMT', ('zcb', b)], w=[('ycb', b), 'pMT'])
            kb.dma(G['ysT'][1024:1536, i * 128:(i + 1) * 128].rearrange("(h e) t -> e h t", h=4), ycb[b][:], r=[('ycb', b)],
                   w=[('ysTc', i)], s='st2', q=STQ)
    kb.barrier()


def phase_mg(kb, G, l, xsrc):
    nc = kb.nc
    with ExitStack() as es:
        A = Alloc(kb, es, f'mgl{l}')
        wbr = A.sb('wbr', [128, 12, 1024], BF16)
        wo = A.sb('wo', [128, 8, 1024], BF16)
        stg = [A.sb(f'stg{i}', [128, 1024], F32) for i in range(2)]
        i = 0
        for n in range(3):
            src = G['w_branch'][l, n].rearrange("(kc p) d -> p kc d", p=128)
            for k in range(4):
                kb.dma(stg[i % 2][:], src[:, k, :], w=[('stg', i % 2)], s='ld')
                kb.op('dve' if i % 2 == 0 else 'act',
                      (lambda e: e.tensor_copy(out=wbr[:, n * 4 + k, :], in_=stg[i % 2][:])) if i % 2 == 0 else
                      (lambda e: e.activation(out=wbr[:, n * 4 + k, :], in_=stg[i % 2][:], func=AF.Copy)),
                      r=[('stg', i % 2)], w=[('wbr', n * 4 + k)])
                i += 1
        src = G['w_out'][l].rearrange("(kc p) d -> p kc d", p=128)
        for k in range(8):
            kb.dma(stg[i % 2][:], src[:, k, :], w=[('stg', i % 2)], s='ld')
            kb.op('dve' if i % 2 == 0 else 'act',
                  (lambda e: e.tensor_copy(out=wo[:, k, :], in_=stg[i % 2][:])) if i % 2 == 0 else
                  (lambda e: e.activation(out=wo[:, k, :], in_=stg[i % 2][:], func=AF.Copy)),
                  r=[('stg', i % 2)], w=[('wo', k)])
            i += 1
        ys = [A.sb(f'ys{i}', [128, 12, 512], BF16) for i in range(2)]
        gt = [A.sb(f'gt{i}', [128, 24, 512], BF16) for i in range(2)]
        mT = A.sb('mT', [128, 8, 512], BF16)
        t0 = A.sb('t0', [128, 512], F32)
        t1 = A.sb('t1', [128, 512], F32)
        t2 = A.sb('t2', [128, 512], F32)
        xt = [A.sb(f'xt{i}', [128, 1024], F32) for i in range(2)]
        pM = [A.ps(f'pM{i}', [128, 512], F32) for i in range(3)]
        pO = [A.ps(f'pO{i}', [128, 512], F32) for i in range(2)]

        def load(blk):
            b = blk % 2
            for c in range(12):
                kb.dma(ys[b][:, c, :], G['ysT'][c * 128:(c + 1) * 128, blk * 512:(blk + 1) * 512], w=[('ys', b, c)], s='ldy')
            for c in range(24):
                kb.dma(gt[b][:, c, :], G['gateT'][c * 128:(c + 1) * 128, blk * 512:(blk + 1) * 512], w=[('gt', b, c)], s='ldg')

        load(0)
        xi = 0
        for blk in range(NBLK):
            b = blk % 2
            if blk + 1 < NBLK:
                load(blk + 1)
            for oc in range(8):
                for n in range(3):
                    for k in range(4):
                        kb.op('pe', lambda e: e.matmul(pM[n][:], lhsT=wbr[:, n * 4 + k, oc * 128:(oc + 1) * 128], rhs=ys[b][:, n * 4 + k, :],
                                                       start=(k == 0), stop=(k == 3)),
                              r=[('wbr', n * 4 + k), ('ys', b, n * 4 + k)], w=[('pM', n)])
                kb.op('dve', lambda e: e.tensor_tensor(out=t0[:], in0=pM[0][:], in1=gt[b][:, oc, :], op=ALU.mult),
                      r=[('pM', 0), ('gt', b, oc)], w=['t0', ('pM', 0)])
                kb.op('dve', lambda e: e.tensor_tensor(out=t1[:], in0=pM[1][:], in1=gt[b][:, 8 + oc, :], op=ALU.mult),
                      r=[('pM', 1), ('gt', b, 8 + oc)], w=['t1', ('pM', 1)])
                kb.op('dve', lambda e: e.tensor_tensor(out=t2[:], in0=pM[2][:], in1=gt[b][:, 16 + oc, :], op=ALU.mult),
                      r=[('pM', 2), ('gt', b, 16 + oc)], w=['t2', ('pM', 2)])
                kb.op('pool', lambda e: e.tensor_tensor(out=t0[:], in0=t0[:], in1=t1[:], op=ALU.add), r=['t0', 't1'], w=['t0'])
                kb.op('pool', lambda e: e.tensor_tensor(out=mT[:, oc, :], in0=t0[:], in1=t2[:], op=ALU.add), r=['t0', 't2'], w=[('mT', oc)])
            mk = [('mT', oc) for oc in range(8)]
            for tt in range(4):
                t = blk * 4 + tt
                x_t = xt[xi % 2]; kx = ('xt', xi % 2)
                kb.dma(x_t[:], xsrc[t * 128:(t + 1) * 128, :], r=[('x', t)], w=[kx], s='ldx')
                for hf in range(2):
                    po = pO[hf]; kpo = ('pO', hf)
                    for k in range(8):
                        kb.op('pe', lambda e: e.matmul(po[:], lhsT=mT[:, k, tt * 128:(tt + 1) * 128], rhs=wo[:, k, hf * 512:(hf + 1) * 512],
                                                       start=(k == 0), stop=(k == 7)), r=mk + [('wo', k)], w=[kpo])
                    kb.op('dve', lambda e: e.tensor_tensor(out=x_t[:, hf * 512:(hf + 1) * 512], in0=po[:], in1=x_t[:, hf * 512:(hf + 1) * 512],
                                                           op=ALU.add), r=[kpo, kx], w=[kx, kpo])
                kb.dma(G['xres'][t * 128:(t + 1) * 128, :], x_t[:], r=[kx], w=[('x', t)], s='stx', q=STQ)
                xi += 1
    kb.barrier()


def phase_final(kb, G, xsrc):
    nc = kb.nc
    with ExitStack() as es:
        A = Alloc(kb, es, 'fin')
        gf = A.sb('gf', [128, D], F32)
        kb.dma(gf[:], G['final_g'], w=['gf'])
        xt = [A.sb(f'xt{i}', [128, D], F32) for i in range(2)]
        junk = A.sb('junk', [128, D], BF16)
        ss = [A.sb(f'ss{i}', [128, 1], F32) for i in range(2)]
        for t in range(NT):
            x_t = xt[t % 2]; s_t = ss[t % 2]; kx = ('xt', t % 2); ks = ('ss', t % 2)
            kb.dma(x_t[:], xsrc[t * 128:(t + 1) * 128, :], r=[('x', t)], w=[kx], s='ldx')
            kb.op('act', lambda e: e.activation(out=junk[:], in_=x_t[:], func=AF.Square, accum_out=s_t[:]), r=[kx], w=['junk', ks])
            kb.op('dve', lambda e: e.tensor_scalar(out=s_t[:], in0=s_t[:], scalar1=1.0 / D, scalar2=EPS, op0=ALU.mult, op1=ALU.add),
                  r=[ks], w=[ks])
            kb.op('act', lambda e: e.activation(out=s_t[:], in_=s_t[:], func=AF.Sqrt), r=[ks], w=[ks])
            kb.op('dve', lambda e: e.reciprocal(out=s_t[:], in_=s_t[:]), r=[ks], w=[ks])
            kb.op('dve', lambda e: e.scalar_tensor_tensor(out=x_t[:], in0=x_t[:], scalar=s_t[:, 0:1], in1=gf[:], op0=ALU.mult, op1=ALU.mult),
                  r=[kx, ks, 'gf'], w=[kx])
            kb.dma(G['out'][t * 128:(t + 1) * 128, :], x_t[:], r=[kx], w=[('out', t)], s='sto', q=STQ)
    kb.barrier()


def host_consts():
    c = {}
    c['ident'] = np.eye(128, dtype=np.float32)
    c['sgn1'] = np.concatenate([np.ones(64), -np.ones(64)]).astype(np.float32).reshape(128, 1)
    sw = np.zeros((128, 128), np.float32)
    sw[np.arange(128), (np.arange(128) + 64) % 128] = 1.0
    c['swap'] = sw
    c['iota'] = np.tile(np.arange(128, dtype=np.float32)[None, :], (128, 1))
    c['utri'] = np.triu(np.ones((128, 128), np.float32))
    c['causn'] = np.where(np.arange(128)[None, :] <= np.arange(128)[:, None], 0.0, -1.0e30).astype(np.float32)
    s8 = np.zeros((8, 4, 128), np.float32)
    for pr in range(4):
        s8[2 * pr, pr, 0:64] = 1.0
        s8[2 * pr + 1, pr, 64:128] = 1.0
    c['sel8'] = s8
    c['mneg'] = np.where(np.arange(128)[None, :] >= np.arange(128)[:, None], 0.0, -30000.0).astype(np.float32)
    return c


def input_shapes():
    return {
        'x': ([L, D], F32),
        'w_in': ([NL, D, INW], F32),
        'norm_g': ([NL, 128, 8], F32),
        'a_re': ([NL, 128, 32], F32), 'a_im': ([NL, 128, 32], F32), 'log_dt': ([NL, 128, 32], F32),
        'X1': ([NL, 128, 32, 16], F32), 'X2': ([NL, 128, 32, 16], F32),
        'Cc1': ([NL, 4, 128, 128], F32), 'Cc2': ([NL, 4, 128, 128], F32),
        'ssm_d': ([NL, 128, 4], F32), 'glu_b': ([NL, 128, 4], F32), 'glu_w': ([NL, 512, 512], F32),
        'conv_w': ([NL, 128, 8, 4], F32), 'conv_b': ([NL, 128, 8], F32), 'igb': ([NL, 128, 4], F32), 'fgb': ([NL, 128, 4], F32),
        'mhg': ([NL, 128, 4], F32),
        'qng': ([NL, 128, 2], F32), 'kng': ([NL, 128, 64], F32), 'knb': ([NL, 128, 64], F32),
        'w_uq': ([NL, 256, 512], F32), 'w_qidx': ([NL, 256, 512], F32),
        'w_branch': ([NL, 3, 512, 1024], F32), 'w_out': ([NL, 1024, 1024], F32), 'final_g': ([128, D], F32),
    }


def set_len(n):
    global L, NBLK, NT
    L = n
    NBLK = L // 512
    NT = L // 128


def declare(kb, cfg):
    nc = kb.nc
    G = {}
    for name, (shape, dt) in input_shapes().items():
        G[name] = nc.dram_tensor(name, list(shape), dt, kind='ExternalInput').ap()
    cst = {}
    for name, arr in host_consts().items():
        cst[name] = nc.dram_tensor('c_' + name, list(arr.shape), F32, kind='ExternalInput').ap()
    G['cst'] = cst
    scratch = {
        'uT': ([512, L], BF16), 'zaT': ([512, L], BF16), 'qbT': ([512, L], BF16), 'kbT': ([512, L], BF16),
        'zbT': ([512, L], BF16), 'cqT': ([256, L], BF16), 'kcT': ([128, L], BF16), 'zcT': ([512, L], BF16),
        'gateT': ([3072, L], BF16), 'widxT': ([8, L], F32),
        'vb': ([L, 512], BF16), 'ifg': ([L, 8], F32), 'vc': ([L, 128], BF16), 'kidx': ([L, 64], F32),
        'widx': ([L, 8], F32), 'ysT': ([1536, L], BF16), 'xres': ([L, D], F32),
        'qT': ([4, 128, L], BF16), 'qiT': ([4, 128, L], BF16),
    }
    for name, (shape, dt) in scratch.items():
        G[name] = kb.dram(name, shape, dt)
    G['out'] = nc.dram_tensor('out', [L, D], F32, kind='ExternalOutput').ap()
    return G


def build(cfg=None):
    cfg = cfg or {}
    nc = bass.Bass("TRN2", target_bir_lowering=False)
    es = ExitStack()
    kb = KB(nc, es, ext=cfg.get('ext', {}))
    G = declare(kb, cfg)
    G['cfg'] = cfg
    phases = cfg.get('phases', None)
    layers = cfg.get('layers', list(range(NL)))
    for l in layers:
        xsrc = G['x'] if l == 0 else G['xres']
        if phases is None or 'p1' in phases:
            phase1(kb, G, l, xsrc)
        if phases is None or 's5' in phases:
            phase_s5(kb, G, l)
        if phases is None or 'ml' in phases:
            phase_ml(kb, G, l)
        if phases is None or 'dsa' in phases:
            phase_dsa(kb, G, l)
        if phases is None or 'mg' in phases:
            phase_mg(kb, G, l, xsrc)
    if phases is None or 'fin' in phases:
        phase_final(kb, G, G['xres'])
    kb.barrier()
    kb.final_wait()
    es.close()
    return nc, kb


def host_layout(inputs, b):
    m = {}
    m['x'] = np.ascontiguousarray(inputs['x'][b][:L])
    m['w_in'] = np.ascontiguousarray(inputs['w_in'])
    m['norm_g'] = np.ascontiguousarray(inputs['norm_g'].reshape(NL, 8, 128).transpose(0, 2, 1))
    ca = np.ascontiguousarray
    art = inputs['ssm_a_re'].transpose(0, 2, 1)
    m['a_re'] = ca(np.concatenate([art, art], axis=1))
    ait = inputs['ssm_a_im'].transpose(0, 2, 1)
    m['a_im'] = ca(np.concatenate([ait, ait], axis=1))
    m['log_dt'] = ca(np.broadcast_to(inputs['ssm_log_dt'][:, None, :], (NL, 128, 32)))
    br = inputs['ssm_b_re'].transpose(0, 2, 1, 3)
    bi_ = inputs['ssm_b_im'].transpose(0, 2, 1, 3)
    m['X1'] = ca(np.concatenate([br, bi_], axis=1))
    m['X2'] = ca(np.concatenate([bi_, br], axis=1))
    cr = inputs['ssm_c_re'].reshape(NL, 4, 128, 64)
    ci = inputs['ssm_c_im'].reshape(NL, 4, 128, 64)
    m['Cc1'] = ca(np.concatenate([cr, ci], axis=3))
    m['Cc2'] = ca(np.concatenate([ci, cr], axis=3))
    m['ssm_d'] = ca(inputs['ssm_d'].reshape(NL, 4, 128).transpose(0, 2, 1))
    m['glu_b'] = ca(inputs['glu_b'].reshape(NL, 4, 128).transpose(0, 2, 1))
    m['glu_w'] = ca(inputs['glu_w'])
    m['conv_w'] = ca(inputs['qk_conv_w'].transpose(0, 2, 1).reshape(NL, 8, 128, 4).transpose(0, 2, 1, 3))
    m['conv_b'] = ca(inputs['qk_conv_b'].reshape(NL, 8, 128).transpose(0, 2, 1))
    m['igb'] = ca(np.broadcast_to(inputs['igate_b'][:, None, :], (NL, 128, 4)))
    m['fgb'] = ca(np.broadcast_to(inputs['fgate_b'][:, None, :], (NL, 128, 4)))
    m['mhg'] = ca(inputs['mh_norm_g'].reshape(NL, 4, 128).transpose(0, 2, 1))
    m['qng'] = ca(inputs['q_norm_g'].reshape(NL, 2, 128).transpose(0, 2, 1))
    m['kng'] = ca(np.broadcast_to(inputs['kidx_norm_g'][:, None, :], (NL, 128, 64)))
    m['knb'] = ca(np.broadcast_to(inputs['kidx_norm_b'][:, None, :], (NL, 128, 64)))
    m['w_uq'] = ca(inputs['w_uq'])
    m['w_qidx'] = ca(inputs['w_qidx'])
    m['w_branch'] = ca(inputs['w_branch'])
    m['w_out'] = ca(inputs['w_out'])
    m['final_g'] = ca(np.broadcast_to(inputs['final_norm_g'][None, :], (128, D)))
    for k, v in host_consts().items():
        m['c_' + k] = v
    return m


N_CORES = 2


def kernel(**inputs):
    set_len(8192)
    nc, kb = build({})
    in_maps = [host_layout(inputs, c % 2) for c in range(N_CORES)]
    res = run_bass_kernel_spmd(nc, in_maps, core_ids=list(range(N_CORES)))
    out = np.stack([np.asarray(res.results[b]['out'], dtype=np.float32) for b in range(2)], axis=0)
    return out
```

```python
import numpy as np
import concourse.bass as bass
import concourse.mybir as mybir
from concourse.bass_utils import run_bass_kernel_spmd
from contextlib import ExitStack

F32 = mybir.dt.float32
BF16 = mybir.dt.bfloat16
AF = mybir.ActivationFunctionType
ALU = mybir.AluOpType
AX = mybir.AxisListType

L = 8192
D = 1024
NL = 4
NBLK = L // 512
NT = L // 128
EPS = 1e-6
BIG = 1.0e30

C_U = 0; C_ZA = 512; C_QB = 1024; C_KB = 1536; C_VB = 2048; C_IF = 2560; C_ZB = 2568
C_CQ = 3080; C_KC = 3336; C_VC = 3464; C_KIDX = 3592; C_WIDX = 3656; C_ZC = 3664; C_GATE = 4176
INW = 7248
WJ = 1208

SAME_SYNC = True
STQ = 'sp'


class KB:
    def __init__(self, nc, es, ext=None):
        self.nc = nc
        self.es = es
        self.ext = ext or {}
        self.E = {'pe': nc.tensor, 'act': nc.scalar, 'dve': nc.vector, 'pool': nc.gpsimd, 'sp': nc.sync}
        self.sem = {}
        self.cnt = {}
        for e in ['pe', 'act', 'dve', 'pool']:
            self.sem[e] = es.enter_context(nc.semaphore('s_' + e))
            self.cnt[e] = 0
        self.dsem = {}
        self.dcnt = {}
        self.dstream = {}
        self.waited = {e: {} for e in self.E}
        self.lastw = {}
        self.readers = {}
        self.uid = 0
        self.ninstr = 0

    def name(self, base):
        self.uid += 1
        return f"{base}_{self.uid}"

    def dram(self, name, shape, dtype):
        kind = self.ext.get(name, 'Internal')
        return self.nc.dram_tensor(name, list(shape), dtype, kind=kind).ap()

    NS = 6

    def _stream(self, s):
        if s not in self.dstream:
            self.dstream[s] = 0
            for j in range(self.NS):
                self.dsem[(s, j)] = self.es.enter_context(self.nc.semaphore(f'd_{s}{j}'))
                self.dcnt[(s, j)] = 0
        j = self.dstream[s] % self.NS
        self.dstream[s] += 1
        return (s, j)

    def _wait(self, e, src, val):
        if self.waited[e].get(src, 0) >= val:
            return
        sem = self.sem[src[1]] if src[0] == 'e' else self.dsem[src[1]]
        self.E[e].wait_ge(sem, val)
        self.waited[e][src] = val
        self.ninstr += 1

    def _deps(self, e, reads, writes):
        deps = {}

        def add(src, val):
            if deps.get(src, 0) < val:
                deps[src] = val
        for k in reads:
            ev = self.lastw.get(k)
            if ev:
                add(*ev)
        for k in writes:
            ev = self.lastw.get(k)
            if ev:
                add(*ev)
            for src, val in self.readers.get(k, {}).items():
                add(src, val)
        for src, val in deps.items():
            if src == ('e', e) and (e == 'pe' or not SAME_SYNC):
                continue
            self._wait(e, src, val)

    def _commit(self, ev, reads, writes):
        for k in writes:
            self.lastw[k] = ev
            self.readers[k] = {}
        for k in reads:
            r = self.readers.setdefault(k, {})
            if r.get(ev[0], 0) < ev[1]:
                r[ev[0]] = ev[1]

    def op(self, e, fn, r=(), w=(), post=None):
        self._deps(e, r, w)
        ins = fn(self.E[e])
        if post is not None:
            ins = post(self.E[e])
            self.ninstr += 1
        self.cnt[e] += 1
        ins.then_inc(self.sem[e], 1)
        self._commit((('e', e), self.cnt[e]), r, w)
        self.ninstr += 1

    def dma(self, out, in_, r=(), w=(), s='ld', q='sp'):
        sk = self._stream(s)
        if self.dcnt[sk] > 0:
            self._wait(q, ('d', sk), self.dcnt[sk])
        self._deps(q, r, w)
        ins = self.E[q].dma_start(out=out, in_=in_)
        self.dcnt[sk] += 16
        ins.then_inc(self.dsem[sk], 16)
        self._commit((('d', sk), self.dcnt[sk]), r, w)
        self.ninstr += 1

    def barrier(self):
        for e in self.E:
            for p in self.sem:
                if self.cnt[p] > 0 and not (p == e):
                    self._wait(e, ('e', p), self.cnt[p])
            for s in self.dsem:
                if self.dcnt[s] > 0:
                    self._wait(e, ('d', s), self.dcnt[s])
        self.lastw = {}
        self.readers = {}

    def final_wait(self):
        for s in self.dsem:
            if self.dcnt[s] > 0:
                self._wait('sp', ('d', s), self.dcnt[s])


class Alloc:
    def __init__(self, kb, es, tag):
        self.kb = kb
        self.es = es
        self.tag = tag

    def sb(self, name, shape, dt):
        return self.es.enter_context(self.kb.nc.sbuf_tensor(self.kb.name(self.tag + name), list(shape), dt))

    def ps(self, name, shape, dt):
        return self.es.enter_context(self.kb.nc.psum_tensor(self.kb.name(self.tag + name), list(shape), dt))


def wkeys(kc, c0, c1):
    return [('wb', kc, j) for j in range(c0 // WJ, (c1 - 1) // WJ + 1)]


def phase1(kb, G, l, xsrc):
    nc = kb.nc
    cst = G['cst']
    with ExitStack() as es:
        A = Alloc(kb, es, f'p1l{l}')
        wb = A.sb('wb', [128, 8, INW], BF16)
        stg = [A.sb(f'wstg{i}', [128, WJ], F32) for i in range(2)]
        g = A.sb('g', [128, 8], F32)
        ident = A.sb('ident', [128, 128], BF16)
        identf = A.sb('identf', [128, 128], F32)
        kb.dma(g[:], G['norm_g'][l], w=['g'])
        kb.dma(identf[:], cst['ident'], w=['identf'])
        kb.op('dve', lambda e: e.tensor_copy(out=ident[:], in_=identf[:]), r=['identf'], w=['ident'])
        w_in = G['w_in'][l].rearrange("(kc p) n -> p kc n", p=128)
        i = 0
        for kc in range(8):
            for j in range(6):
                s = stg[i % 2]
                kb.dma(s[:], w_in[:, kc, j * WJ:(j + 1) * WJ], w=[('stg', i % 2)])
                if i % 2 == 0:
                    kb.op('dve', lambda e, s=s, kc=kc, j=j: e.tensor_scalar(
                        out=wb[:, kc, j * WJ:(j + 1) * WJ], in0=s[:], scalar1=g[:, kc:kc + 1], scalar2=None,
                        op0=ALU.mult), r=[('stg', 0), 'g'], w=[('wb', kc, j)])
                else:
                    kb.op('act', lambda e, s=s, kc=kc, j=j: e.activation(
                        out=wb[:, kc, j * WJ:(j + 1) * WJ], in_=s[:], func=AF.Copy, scale=g[:, kc:kc + 1]),
                        r=[('stg', 1), 'g'], w=[('wb', kc, j)])
                i += 1

        xt = [A.sb(f'xt{i}', [128, D], F32) for i in range(2)]
        junk = A.sb('junk', [128, D], BF16)
        ss = [A.sb(f'ss{i}', [128, 1], F32) for i in range(2)]
        hb = [A.sb(f'hb{i}', [128, D], BF16) for i in range(2)]
        hT = [A.sb(f'hT{i}', [128, 8, 512], BF16) for i in range(2)]
        pT = [A.ps(f'pT{i}', [128, 8, 128], BF16) for i in range(2)]
        pO = [A.ps(f'pO{i}', [128, 512], F32) for i in range(4)]
        so = [A.sb(f'so{i}', [128, 512], BF16) for i in range(4)]
        sof = [A.sb(f'sof{i}', [128, 256], F32) for i in range(2)]

        fm = [(C_U, 4, AF.Copy, G['uT']), (C_ZA, 4, AF.Silu, G['zaT']), (C_QB, 4, AF.Copy, G['qbT']),
              (C_KB, 4, AF.Copy, G['kbT']), (C_ZB, 4, AF.Silu, G['zbT']), (C_CQ, 2, AF.Copy, G['cqT']),
              (C_KC, 1, AF.Copy, G['kcT']), (C_ZC, 4, AF.Silu, G['zcT']), (C_GATE, 24, AF.Sigmoid, G['gateT'])]
        wst = [A.sb(f'wst{i}', [8, 512], F32) for i in range(2)]
        state = {'ti': 0, 'oi': 0}

        def prep(blk):
            hTb = hT[blk % 2]
            for tt in range(4):
                ti = state['ti']
                t = blk * 4 + tt
                x_t = xt[ti % 2]; s_t = ss[ti % 2]; h_t = hb[ti % 2]; p_t = pT[ti % 2]
                kx = ('xt', ti % 2); ks = ('ss', ti % 2); kh = ('hb', ti % 2); kp = ('pT', ti % 2)
                kb.dma(x_t[:], xsrc[t * 128:(t + 1) * 128, :], r=[('x', t)], w=[kx], s='ldx')
                kb.op('act', lambda e: e.activation(out=junk[:], in_=x_t[:], func=AF.Square, accum_out=s_t[:]),
                      r=[kx], w=['junk', ks])
                kb.op('dve', lambda e: e.tensor_scalar(out=s_t[:], in0=s_t[:], scalar1=1.0 / D, scalar2=EPS,
                                                       op0=ALU.mult, op1=ALU.add), r=[ks], w=[ks])
                kb.op('act', lambda e: e.activation(out=s_t[:], in_=s_t[:], func=AF.Sqrt), r=[ks], w=[ks])
                kb.op('dve', lambda e: e.reciprocal(out=s_t[:], in_=s_t[:]), r=[ks], w=[ks])
                kb.op('dve', lambda e: e.tensor_scalar(out=h_t[:], in0=x_t[:], scalar1=s_t[:, 0:1], scalar2=None,
                                                       op0=ALU.mult), r=[kx, ks], w=[kh])
                for c in range(8):
                    kb.op('pe', lambda e: e.transpose(out=p_t[:, c, :], in_=h_t[:, c * 128:(c + 1) * 128],
                                                      identity=ident[:]), r=[kh, 'ident'], w=[kp])
                kb.op('dve', lambda e: e.tensor_copy(out=hTb[:, :, tt * 128:(tt + 1) * 128], in_=p_t[:]),
                      r=[kp], w=[('hT', blk % 2, tt)])
                state['ti'] += 1

        def tm(blk):
            hTb = hT[blk % 2]
            hkeys = [('hT', blk % 2, tt) for tt in range(4)]
            for tt in range(4):
                t = blk * 4 + tt
                hk = [('hT', blk % 2, tt)]
                lhs = lambda k: hTb[:, k, tt * 128:(tt + 1) * 128]
                oi = state['oi']; po = pO[oi % 4]; kpo = ('pO', oi % 4); st = so[oi % 4]; kst = ('so', oi % 4)
                for k in range(8):
                    kb.op('pe', lambda e: e.matmul(po[:], lhsT=lhs(k), rhs=wb[:, k, C_VB:C_VB + 512],
                                                   start=(k == 0), stop=(k == 7)),
                          r=hk + wkeys(k, C_VB, C_VB + 512), w=[kpo])
                kb.op('dve', lambda e: e.tensor_copy(out=st[:], in_=po[:]), r=[kpo], w=[kst])
                kb.dma(G['vb'][t * 128:(t + 1) * 128, :], st[:], r=[kst], w=[('vb', t)], s='st1')
                state['oi'] += 1
                if 'if' in G.get('cfg', {}).get('tm_skip', ()):
                    continue
                oi = state['oi']; po = pO[oi % 4]; kpo = ('pO', oi % 4); sf = sof[0]
                for k in range(8):
                    kb.op('pe', lambda e: e.matmul(po[:, 0:8], lhsT=lhs(k), rhs=wb[:, k, C_IF:C_IF + 8],
                                                   start=(k == 0), stop=(k == 7)),
                          r=hk + wkeys(k, C_IF, C_IF + 8), w=[kpo])
                kb.op('dve', lambda e: e.tensor_copy(out=sf[:, 0:8], in_=po[:, 0:8]), r=[kpo], w=[('sof', 0)])
                kb.dma(G['ifg'][t * 128:(t + 1) * 128, :], sf[:, 0:8], r=[('sof', 0)], w=[('ifg', t)], s='st1')
                state['oi'] += 1
                if 'vkw' in G.get('cfg', {}).get('tm_skip', ()):
                    continue
                oi = state['oi']; po = pO[oi % 4]; kpo = ('pO', oi % 4); st = so[oi % 4]; kst = ('so', oi % 4); sf = sof[1]
                for k in range(8):
                    kb.op('pe', lambda e: e.matmul(po[:, 0:200], lhsT=lhs(k), rhs=wb[:, k, C_VC:C_VC + 200],
                                                   start=(k == 0), stop=(k == 7)),
                          r=hk + wkeys(k, C_VC, C_VC + 200), w=[kpo])
                kb.op('dve', lambda e: e.tensor_copy(out=st[:, 0:128], in_=po[:, 0:128]), r=[kpo], w=[kst])
                kb.op('dve', lambda e: e.tensor_copy(out=sf[:, 0:72], in_=po[:, 128:200]),
                      r=[kpo], w=[('sof', 1)])
                kb.dma(G['vc'][t * 128:(t + 1) * 128, :], st[:, 0:128], r=[kst], w=[('vc', t)], s='st1')
                kb.dma(G['kidx'][t * 128:(t + 1) * 128, :], sf[:, 0:64], r=[('sof', 1)], w=[('kidx', t)], s='st1')
                kb.dma(G['widx'][t * 128:(t + 1) * 128, :], sf[:, 64:72], r=[('sof', 1)], w=[('widx', t)], s='st1')
                state['oi'] += 1
            if 'wT' in G.get('cfg', {}).get('tm_skip', ()):
                return
            oi = state['oi']; po = pO[oi % 4]; kpo = ('pO', oi % 4); ws = wst[blk % 2]
            for k in range(8):
                kb.op('pe', lambda e: e.matmul(po[:, :], lhsT=wb[:, k, C_WIDX:C_WIDX + 128], rhs=hTb[:, k, :],
                                               start=(k == 0), stop=(k == 7)),
                      r=hkeys + wkeys(k, C_WIDX, C_WIDX + 128), w=[kpo])
            kb.op('dve', lambda e: e.tensor_copy(out=ws[:], in_=po[0:8, :]), r=[kpo], w=[('wst', blk % 2)])
            kb.dma(G['widxT'][:, blk * 512:(blk + 1) * 512], ws[:], r=[('wst', blk % 2)], w=[('widxT', blk)], s='st1')
            state['oi'] += 1

        def fmseg(blk):
            hTb = hT[blk % 2]
            hkeys = [('hT', blk % 2, tt) for tt in range(4)]
            for (c0, nch, func, dst) in fm:
                for ch in range(nch):
                    oi = state['oi']; po = pO[oi % 4]; kpo = ('pO', oi % 4); st = so[oi % 4]; kst = ('so', oi % 4)
                    cc = c0 + ch * 128
                    for k in range(8):
                        kb.op('pe', lambda e: e.matmul(po[:], lhsT=wb[:, k, cc:cc + 128], rhs=hTb[:, k, :],
                                                       start=(k == 0), stop=(k == 7)),
                              r=hkeys + wkeys(k, cc, cc + 128), w=[kpo])
                    kb.op('act', lambda e: e.activation(out=st[:], in_=po[:], func=func), r=[kpo], w=[kst])
                    kb.dma(dst[ch * 128:(ch + 1) * 128, blk * 512:(blk + 1) * 512], st[:], r=[kst],
                           w=[(id(dst), ch, blk)], s='st2', q=STQ)
                    state['oi'] += 1

        stop = G.get('cfg', {}).get('p1_stop', 9)
        if stop >= 1:
            prep(0)
        for blk in range(NBLK):
            if stop >= 2:
                tm(blk)
            if blk + 1 < NBLK and stop >= 1:
                prep(blk + 1)
            if stop >= 3:
                fmseg(blk)
    kb.barrier()


MAGIC = 12582912.0
TWO_PI_S = 6.28318
GC1 = 0.044715
GC2 = 1.5957691216057308


def sincos_turns(kb, A, phi, n, kphi, tag):
    t = A.sb(tag + 't', [128, n], F32)
    k = A.sb(tag + 'k', [128, n], F32)
    sn = A.sb(tag + 'sn', [128, n], F32)
    cs = A.sb(tag + 'cs', [128, n], F32)
    kt, kk, ksn, kcs = tag + 't', tag + 'k', tag + 'sn', tag + 'cs'
    for (off, dst, kd) in ((0.0, sn, ksn), (0.25, cs, kcs)):
        kb.op('dve', lambda e: e.tensor_scalar(out=t[:], in0=phi, scalar1=off, scalar2=MAGIC, op0=ALU.add, op1=ALU.add),
              r=[kphi], w=[kt])
        kb.op('dve', lambda e: e.tensor_scalar(out=k[:], in0=t[:], scalar1=-MAGIC, scalar2=None, op0=ALU.add),
              r=[kt], w=[kk])
        kb.op('dve', lambda e: e.scalar_tensor_tensor(out=t[:], in0=phi, scalar=off, in1=k[:], op0=ALU.add,
                                                      op1=ALU.subtract), r=[kphi, kk], w=[kt])
        kb.op('dve', lambda e: e.tensor_scalar(out=t[:], in0=t[:], scalar1=0.5, scalar2=-0.5, op0=ALU.min, op1=ALU.max),
              r=[kt], w=[kt])
        kb.op('act', lambda e: e.activation(out=dst[:], in_=t[:], func=AF.Sin, scale=TWO_PI_S), r=[kt], w=[kd])
    return sn, cs, ksn, kcs


def phase_s5(kb, G, l):
    nc = kb.nc
    cst = G['cst']
    with ExitStack() as es:
        A = Alloc(kb, es, f's5l{l}')
        Ctab = A.sb('Ctab', [128, 32, 128], F32)
        Stab = A.sb('Stab', [128, 32, 128], F32)
        Rtab = A.sb('Rtab', [128, 32, 128], F32)
        RotL = A.sb('RotL', [128, 32, 128], F32)
        L1 = A.sb('L1', [128, 32, 128], BF16)
        L2 = A.sb('L2', [128, 32, 128], BF16)
        W1 = A.sb('W1', [128, 32, 128], BF16)
        W2 = A.sb('W2', [128, 32, 128], BF16)
        identf = A.sb('identf', [128, 128], F32)
        dsk = A.sb('dsk', [128, 4], F32)
        glub = A.sb('glub', [128, 4], F32)
        gluw = A.sb('gluw', [128, 4, 512], BF16)
        carry = A.sb('carry', [128, 32], F32)
        kb.dma(identf[:], cst['ident'], w=['identf'])
        kb.dma(dsk[:], G['ssm_d'][l], w=['dsk'])
        kb.dma(glub[:], G['glu_b'][l], w=['glub'])
        kb.op('pool', lambda e: e.memset(carry[:], 0.0), w=['carry'])
        with ExitStack() as es2:
            B = Alloc(kb, es2, f's5l{l}t')
            ar = B.sb('ar', [128, 32], F32); ai = B.sb('ai', [128, 32], F32); ldt = B.sb('ldt', [128, 32], F32)
            sgn1 = B.sb('sgn1', [128, 1], F32); swp = B.sb('swp', [128, 128], F32); iot = B.sb('iot', [128, 128], F32)
            X1 = B.sb('X1', [128, 32, 16], F32); X2 = B.sb('X2', [128, 32, 16], F32)
            Cc1 = B.sb('Cc1', [128, 4, 128], F32); Cc2 = B.sb('Cc2', [128, 4, 128], F32)
            gws = B.sb('gws', [128, 4, 512], F32)
            kb.dma(ar[:], G['a_re'][l], w=['ar']); kb.dma(ai[:], G['a_im'][l], w=['ai']); kb.dma(ldt[:], G['log_dt'][l], w=['ldt'])
            kb.dma(sgn1[:], cst['sgn1'], w=['sgn1']); kb.dma(swp[:], cst['swap'], w=['swp']); kb.dma(iot[:], cst['iota'], w=['iot'])
            kb.dma(X1[:], G['X1'][l], w=['X1']); kb.dma(X2[:], G['X2'][l], w=['X2'])
            kb.dma(Cc1[:], G['Cc1'][l].rearrange("c p n -> p c n"), w=['Cc1'])
            kb.dma(Cc2[:], G['Cc2'][l].rearrange("c p n -> p c n"), w=['Cc2'])
            kb.dma(gws[:], G['glu_w'][l].rearrange("(kc p) n -> p kc n", p=128), w=['gws'])
            kb.op('pool', lambda e: e.tensor_copy(out=gluw[:], in_=gws[:]), r=['gws'], w=['gluw'])
            S = {}

            def sm(name):
                S[name] = B.sb(name, [128, 32], F32)
                return S[name]

            def tt(o, a, b, op):
                kb.op('dve', lambda e: e.tensor_tensor(out=S[o][:], in0=S[a][:], in1=S[b][:], op=op), r=[a, b], w=[o])
            S['ar'] = ar; S['ai'] = ai; S['ldt'] = ldt
            for n_ in ['dt', 'mag', 'ang', 'phi1', 'abr', 'abi', 'den', 't1', 'fr', 'fi', 'fis', 'frs', 'tmp', 'phi128', 'ssgn']:
                sm(n_)
            kb.op('act', lambda e: e.activation(out=S['dt'][:], in_=ldt[:], func=AF.Exp), r=['ldt'], w=['dt'])
            tt('tmp', 'ar', 'dt', ALU.mult)
            kb.op('act', lambda e: e.activation(out=S['mag'][:], in_=S['tmp'][:], func=AF.Exp), r=['tmp'], w=['mag'])
            tt('ang', 'ai', 'dt', ALU.mult)
            kb.op('dve', lambda e: e.tensor_scalar(out=S['phi1'][:], in0=S['ang'][:], scalar1=1.0 / (2 * np.pi), scalar2=None,
                                                   op0=ALU.mult), r=['ang'], w=['phi1'])
            s1, c1, ks1, kc1 = sincos_turns(kb, B, S['phi1'][:], 32, 'phi1', 'sc1')
            S['s1'] = s1; S['c1'] = c1
            kb.op('dve', lambda e: e.tensor_tensor(out=S['abr'][:], in0=S['mag'][:], in1=c1[:], op=ALU.mult), r=['mag', kc1], w=['abr'])
            kb.op('dve', lambda e: e.tensor_tensor(out=S['abi'][:], in0=S['mag'][:], in1=s1[:], op=ALU.mult), r=['mag', ks1], w=['abi'])
            tt('den', 'ar', 'ar', ALU.mult)
            tt('tmp', 'ai', 'ai', ALU.mult)
            tt('den', 'den', 'tmp', ALU.add)
            kb.op('dve', lambda e: e.reciprocal(out=S['den'][:], in_=S['den'][:]), r=['den'], w=['den'])
            kb.op('dve', lambda e: e.tensor_scalar(out=S['t1'][:], in0=S['abr'][:], scalar1=-1.0, scalar2=None, op0=ALU.add),
                  r=['abr'], w=['t1'])
            tt('fr', 't1', 'ar', ALU.mult)
            tt('tmp', 'abi', 'ai', ALU.mult)
            tt('fr', 'fr', 'tmp', ALU.add)
            tt('fr', 'fr', 'den', ALU.mult)
            tt('fi', 'abi', 'ar', ALU.mult)
            tt('tmp', 't1', 'ai', ALU.mult)
            tt('fi', 'fi', 'tmp', ALU.subtract)
            tt('fi', 'fi', 'den', ALU.mult)
            kb.op('dve', lambda e: e.tensor_scalar(out=S['frs'][:], in0=S['fr'][:], scalar1=sgn1[:, 0:1], scalar2=None, op0=ALU.mult),
                  r=['fr', 'sgn1'], w=['frs'])
            kb.op('dve', lambda e: e.tensor_scalar(out=S['fis'][:], in0=S['fi'][:], scalar1=sgn1[:, 0:1], scalar2=-1.0, op0=ALU.mult,
                                                   op1=ALU.mult), r=['fi', 'sgn1'], w=['fis'])
            kb.op('dve', lambda e: e.tensor_scalar(out=S['phi128'][:], in0=S['phi1'][:], scalar1=128.0, scalar2=None, op0=ALU.mult),
                  r=['phi1'], w=['phi128'])
            s128, c128, ks128, kc128 = sincos_turns(kb, B, S['phi128'][:], 32, 'phi128', 'sc128')
            kb.op('dve', lambda e: e.tensor_scalar(out=S['ssgn'][:], in0=s128[:], scalar1=sgn1[:, 0:1], scalar2=None, op0=ALU.mult),
                  r=[ks128, 'sgn1'], w=['ssgn'])
            for g in range(32):
                kb.op('dve', lambda e: e.tensor_scalar(out=RotL[:, g, :], in0=identf[:], scalar1=c128[:, g:g + 1], scalar2=None,
                                                       op0=ALU.mult), r=['identf', kc128], w=[('RotL', g)])
                kb.op('dve', lambda e: e.scalar_tensor_tensor(out=RotL[:, g, :], in0=swp[:], scalar=S['ssgn'][:, g:g + 1],
                                                              in1=RotL[:, g, :], op0=ALU.mult, op1=ALU.add),
                      r=['swp', 'ssgn', ('RotL', g)], w=[('RotL', g)])
            es3 = ExitStack()
            B3 = Alloc(kb, es3, f's5l{l}u')
            PHI = B3.sb('PHI', [128, 32 * 128], F32)
            for g in range(32):
                kb.op('pool', lambda e: e.tensor_scalar(out=PHI[:, g * 128:(g + 1) * 128], in0=iot[:], scalar1=S['phi1'][:, g:g + 1],
                                                        scalar2=None, op0=ALU.mult), r=['iot', 'phi1'], w=[('PHI', g)])
                kb.op('pool', lambda e: e.tensor_scalar(out=Rtab[:, g, :], in0=iot[:], scalar1=0.0, scalar2=S['mag'][:, g:g + 1],
                                                        op0=ALU.mult, op1=ALU.add), r=['iot', 'mag'], w=[('Rtab', g)])
            phikeys = [('PHI', g) for g in range(32)]
            tq = B3.sb('tq', [128, 32 * 128], F32)
            kq = B3.sb('kq', [128, 32 * 128], F32)
            for (off, dst, kd) in ((0.0, Stab, 'Stab'), (0.25, Ctab, 'Ctab')):
                kb.op('dve', lambda e: e.tensor_scalar(out=tq[:], in0=PHI[:], scalar1=off, scalar2=MAGIC, op0=ALU.add, op1=ALU.add),
                      r=phikeys, w=['tq'])
                kb.op('dve', lambda e: e.tensor_scalar(out=kq[:], in0=tq[:], scalar1=-MAGIC, scalar2=None, op0=ALU.add),
                      r=['tq'], w=['kq'])
                kb.op('dve', lambda e: e.scalar_tensor_tensor(out=tq[:], in0=PHI[:], scalar=off, in1=kq[:], op0=ALU.add,
                                                              op1=ALU.subtract), r=phikeys + ['kq'], w=['tq'])
                kb.op('dve', lambda e: e.tensor_scalar(out=tq[:], in0=tq[:], scalar1=0.5, scalar2=-0.5, op0=ALU.min, op1=ALU.max),
                      r=['tq'], w=['tq'])
                kb.op('act', lambda e: e.activation(out=dst[:].rearrange("p g j -> p (g j)"), in_=tq[:], func=AF.Sin, scale=TWO_PI_S),
                      r=['tq'], w=[kd])
            kb.barrier()
            es3.close()
            Bp1 = B.sb('Bp1', [128, 32, 128], F32)
            Bp2 = B.sb('Bp2', [128, 32, 128], F32)
            kb.op('pool', lambda e: e.memset(Bp1[:], 0.0), w=['Bp1'])
            kb.op('pool', lambda e: e.memset(Bp2[:], 0.0), w=['Bp2'])
            kb.op('pool', lambda e: e.memset(W1[:], 0.0), w=['W1'])
            kb.op('pool', lambda e: e.memset(W2[:], 0.0), w=['W2'])
            for g in range(32):
                c0 = (g % 8) * 16
                kb.op('dve', lambda e: e.tensor_scalar(out=Bp1[:, g, c0:c0 + 16], in0=X1[:, g, :], scalar1=S['fr'][:, g:g + 1],
                                                       scalar2=None, op0=ALU.mult), r=['X1', 'fr', 'Bp1'], w=[('Bp1', g)])
                kb.op('dve', lambda e: e.scalar_tensor_tensor(out=Bp1[:, g, c0:c0 + 16], in0=X2[:, g, :], scalar=S['fis'][:, g:g + 1],
                                                              in1=Bp1[:, g, c0:c0 + 16], op0=ALU.mult, op1=ALU.add),
                      r=['X2', 'fis', ('Bp1', g)], w=[('Bp1', g)])
                kb.op('dve', lambda e: e.tensor_scalar(out=Bp2[:, g, c0:c0 + 16], in0=X2[:, g, :], scalar1=S['frs'][:, g:g + 1],
                                                       scalar2=None, op0=ALU.mult), r=['X2', 'frs', 'Bp2'], w=[('Bp2', g)])
                kb.op('dve', lambda e: e.scalar_tensor_tensor(out=Bp2[:, g, c0:c0 + 16], in0=X1[:, g, :], scalar=S['fi'][:, g:g + 1],
                                                              in1=Bp2[:, g, c0:c0 + 16], op0=ALU.mult, op1=ALU.add),
                      r=['X1', 'fi', ('Bp2', g)], w=[('Bp2', g)])
            pst = [B.ps(f'pst{i}', [128, 4, 128], F32) for i in range(2)]
            pi_ = 0
            for (Bp, Lx, kn, kl) in ((Bp1, L1, 'Bp1', 'L1'), (Bp2, L2, 'Bp2', 'L2')):
                for q in range(8):
                    pt = pst[pi_ % 2]; kpt = ('pst', pi_ % 2)
                    for j in range(4):
                        g = q * 4 + j
                        kb.op('pe', lambda e: e.transpose(out=pt[:, j, :], in_=Bp[:, g, :], identity=identf[:]),
                              r=[(kn, g), 'identf'], w=[kpt])
                    kb.op('act', lambda e: e.activation(out=Lx[:, q * 4:(q + 1) * 4, :], in_=pt[:], func=AF.Copy),
                          r=[kpt], w=[(kl, q)])
                    pi_ += 1
            for (Cc, Wx, kc_, kw_, neg_all) in ((Cc1, W1, 'Cc1', 'W1', False), (Cc2, W2, 'Cc2', 'W2', True)):
                pt = pst[pi_ % 2]; kpt = ('pst', pi_ % 2)
                for c in range(4):
                    kb.op('pe', lambda e: e.transpose(out=pt[:, c, :], in_=Cc[:, c, :], identity=identf[:]),
                          r=[kc_, 'identf'], w=[kpt])
                for g in range(32):
                    c0 = (g % 8) * 16
                    if neg_all:
                        kb.op('dve', lambda e: e.tensor_scalar(out=Wx[:, g, c0:c0 + 16], in0=pt[:, g // 8, c0:c0 + 16], scalar1=-1.0,
                                                               scalar2=None, op0=ALU.mult), r=[kpt, kw_], w=[(kw_, g)])
                    else:
                        kb.op('dve', lambda e: e.tensor_scalar(out=Wx[:, g, c0:c0 + 16], in0=pt[:, g // 8, c0:c0 + 16],
                                                               scalar1=sgn1[:, 0:1], scalar2=None, op0=ALU.mult),
                              r=[kpt, kw_, 'sgn1'], w=[(kw_, g)])
                pi_ += 1
            kb.barrier()
            if G['cfg'].get('dbg_s5'):
                def dump(name, ap, shape, dt=F32):
                    d = nc.dram_tensor('dbg_' + name, list(shape), dt, kind='ExternalOutput').ap()
                    kb.dma(d, ap, s='dbg')
                mode_ = G['cfg'].get('dbg_s5')
                if mode_ == 'one':
                    dump('dt', S['dt'][:], [128, 32])
                for n_ in ['dt', 'mag', 'ang', 'phi1', 'abr', 'abi', 'fr', 'fi', 'ssgn'] if mode_ is True else []:
                    dump(n_, S[n_][:], [128, 32])
                if mode_ is True:
                  dump('s1', S['s1'][:], [128, 32]); dump('c1', S['c1'][:], [128, 32])
                  dump('Stab', Stab[:], [128, 32, 128]); dump('Ctab', Ctab[:], [128, 32, 128]); dump('Rtab', Rtab[:], [128, 32, 128])
                  dump('RotL', RotL[:], [128, 32, 128]); dump('L1', L1[:], [128, 32, 128], BF16); dump('L2', L2[:], [128, 32, 128], BF16)
                  dump('W1', W1[:], [128, 32, 128], BF16); dump('W2', W2[:], [128, 32, 128], BF16)
                kb.barrier()
        uTb = [A.sb(f'uTb{i}', [128, 4, 512], BF16) for i in range(2)]
        zab = [A.sb(f'zab{i}', [128, 4, 512], BF16) for i in range(2)]
        Dt = [A.sb(f'Dt{i}', [128, 512], F32) for i in range(8)]
        Gt = [A.sb(f'Gt{i}', [128, 512], F32) for i in range(8)]
        tmpb = [A.sb(f'tmpb{i}', [128, 512], F32) for i in range(2)]
        P1 = [A.sb(f'P1{i}', [128, 512], BF16) for i in range(2)]
        P2 = [A.sb(f'P2{i}', [128, 512], BF16) for i in range(2)]
        yv = A.sb('yv', [128, 512], F32)
        yt = A.sb('yt', [128, 512], F32)
        gy = A.sb('gy', [128, 4, 512], BF16)
        sg = A.sb('sg', [128, 512], F32)
        yo = [A.sb(f'yo{i}', [128, 512], BF16) for i in range(2)]
        pb1 = [A.ps(f'pb1{i}', [128, 512], F32) for i in range(2)]
        pb2 = [A.ps(f'pb2{i}', [128, 512], F32) for i in range(2)]
        pY = [A.ps(f'pY{i}', [128, 512], F32) for i in range(2)]
        pc = A.ps('pc', [128, 16], F32)
        cj = A.sb('cj', [128, 8], F32)
        pG = A.ps('pG', [128, 512], F32)

        def load(blk):
            for c in range(4):
                kb.dma(uTb[blk % 2][:, c, :], G['uT'][c * 128:(c + 1) * 128, blk * 512:(blk + 1) * 512],
                       r=[(id(G['uT']), c, blk)], w=[('uTb', blk % 2, c)], s='ldu')
                kb.dma(zab[blk % 2][:, c, :], G['zaT'][c * 128:(c + 1) * 128, blk * 512:(blk + 1) * 512],
                       r=[(id(G['zaT']), c, blk)], w=[('zab', blk % 2, c)], s='ldu')

        bc = lambda tab, g: tab[:, g, :].unsqueeze(1).to_broadcast([128, 4, 128])
        v4 = lambda ap: ap.rearrange("p (s j) -> p s j", j=128)
        load(0)
        bi = 0
        yi = 0
        oi = 0
        for blk in range(NBLK):
            if blk + 1 < NBLK:
                load(blk + 1)
            ub = uTb[blk % 2]
            for c in range(4):
                for gg in range(8):
                    g = 8 * c + gg
                    b1 = pb1[bi % 2]; b2 = pb2[bi % 2]; k1 = ('pb1', bi % 2); k2 = ('pb2', bi % 2)
                    tb = tmpb[bi % 2]; ktb = ('tmpb', bi % 2)
                    kb.op('pe', lambda e: e.matmul(b1[:], lhsT=L1[:, g, :], rhs=ub[:, c, :], start=True, stop=True),
                          r=[('uTb', blk % 2, c)], w=[k1])
                    kb.op('pe', lambda e: e.matmul(b2[:], lhsT=L2[:, g, :], rhs=ub[:, c, :], start=True, stop=True),
                          r=[('uTb', blk % 2, c)], w=[k2])
                    kb.op('dve', lambda e: e.tensor_tensor(out=v4(Dt[gg][:]), in0=v4(b1[:]), in1=bc(Ctab, g), op=ALU.mult),
                          r=[k1], w=[('Dt', gg), k1])
                    kb.op('dve', lambda e: e.tensor_tensor(out=v4(tb[:]), in0=v4(b2[:]), in1=bc(Stab, g), op=ALU.mult),
                          r=[k2], w=[ktb, k2])
                    kb.op('pool', lambda e: e.tensor_tensor(out=Dt[gg][:], in0=Dt[gg][:], in1=tb[:], op=ALU.add),
                          r=[('Dt', gg), ktb], w=[('Dt', gg)])
                    bi += 1
                for seg in range(4):
                    for gg in range(8):
                        g = 8 * c + gg
                        sl = slice(seg * 128, (seg + 1) * 128)
                        kb.op('dve', lambda e: e.tensor_tensor_scan(out=Gt[gg][:, sl], data0=Rtab[:, g, :], data1=Dt[gg][:, sl],
                                                                    initial=carry[:, g:g + 1], op0=ALU.mult, op1=ALU.add),
                              r=[('Dt', gg), ('carry', g)], w=[('Gt', gg, seg)])
                        kb.op('pe', lambda e: e.matmul(pc[:, gg:gg + 1], lhsT=RotL[:, g, :],
                                                       rhs=Gt[gg][:, seg * 128 + 127:seg * 128 + 128], start=True, stop=True),
                              r=[('Gt', gg, seg)], w=[('pc', gg)])
                        kb.op('act', lambda e: e.activation(out=carry[:, g:g + 1], in_=pc[:, gg:gg + 1], func=AF.Copy),
                              r=[('pc', gg)], w=[('carry', g)])
                if G['cfg'].get('s5_cut'):
                    break
                py = pY[yi % 2]; kpy = ('pY', yi % 2)
                for gg in range(8):
                    g = 8 * c + gg
                    p1 = P1[gg % 2]; p2 = P2[gg % 2]
                    gk = [('Gt', gg, sg_) for sg_ in range(4)]
                    kb.op('pool', lambda e: e.tensor_tensor(out=v4(p1[:]), in0=v4(Gt[gg][:]), in1=bc(Ctab, g), op=ALU.mult),
                          r=gk, w=[('P1', gg % 2)])
                    kb.op('pool', lambda e: e.tensor_tensor(out=v4(p2[:]), in0=v4(Gt[gg][:]), in1=bc(Stab, g), op=ALU.mult),
                          r=gk, w=[('P2', gg % 2)])
                    kb.op('pe', lambda e: e.matmul(py[:], lhsT=W1[:, g, :], rhs=p1[:], start=(gg == 0), stop=False),
                          r=[('P1', gg % 2)], w=[kpy])
                    kb.op('pe', lambda e: e.matmul(py[:], lhsT=W2[:, g, :], rhs=p2[:], start=False, stop=(gg == 7)),
                          r=[('P2', gg % 2)], w=[kpy])
                kb.op('dve', lambda e: e.scalar_tensor_tensor(out=yv[:], in0=ub[:, c, :], scalar=dsk[:, c:c + 1], in1=py[:],
                                                              op0=ALU.mult, op1=ALU.add),
                      r=[('uTb', blk % 2, c), 'dsk', kpy], w=['yv', kpy])
                kb.op('dve', lambda e: e.tensor_tensor(out=yt[:], in0=yv[:], in1=yv[:], op=ALU.mult), r=['yv'], w=['yt'])
                kb.op('dve', lambda e: e.tensor_scalar(out=yt[:], in0=yt[:], scalar1=GC1, scalar2=1.0, op0=ALU.mult, op1=ALU.add),
                      r=['yt'], w=['yt'])
                kb.op('dve', lambda e: e.tensor_tensor(out=yt[:], in0=yt[:], in1=yv[:], op=ALU.mult), r=['yt', 'yv'], w=['yt'])
                kb.op('act', lambda e: e.activation(out=yt[:], in_=yt[:], func=AF.Sigmoid, scale=GC2), r=['yt'], w=['yt'])
                kb.op('dve', lambda e: e.tensor_tensor(out=gy[:, c, :], in0=yt[:], in1=yv[:], op=ALU.mult),
                      r=['yt', 'yv'], w=[('gy', c)])
                yi += 1
            gyk = [('gy', c) for c in range(4)]
            for oc in range(4 if not G['cfg'].get('s5_cut') else 0):
                for k in range(4):
                    kb.op('pe', lambda e: e.matmul(pG[:], lhsT=gluw[:, k, oc * 128:(oc + 1) * 128], rhs=gy[:, k, :],
                                                   start=(k == 0), stop=(k == 3)), r=gyk + ['gluw'], w=['pG'])
                kb.op('act', lambda e: e.activation(out=sg[:], in_=pG[:], func=AF.Sigmoid, bias=glub[:, oc:oc + 1]),
                      r=['pG', 'glub'], w=['sg', 'pG'])
                yo_ = yo[oi % 2]; kyo = ('yo', oi % 2)
                kb.op('dve', lambda e: e.tensor_tensor(out=sg[:], in0=sg[:], in1=gy[:, oc, :], op=ALU.mult),
                      r=['sg', ('gy', oc)], w=['sg'])
                kb.op('dve', lambda e: e.tensor_tensor(out=yo_[:], in0=sg[:], in1=zab[blk % 2][:, oc, :], op=ALU.mult),
                      r=['sg', ('zab', blk % 2, oc)], w=[kyo])
                kb.dma(G['ysT'][oc * 128:(oc + 1) * 128, blk * 512:(blk + 1) * 512], yo_[:], r=[kyo],
                       w=[('ysT', oc, blk)], s='st2', q=STQ)
                oi += 1
        if G['cfg'].get('dbg_s5b'):
            kb.barrier()
            def dump2(name, ap, shape, dt=F32):
                d = nc.dram_tensor('dbg_' + name, list(shape), dt, kind='ExternalOutput').ap()
                kb.dma(d, ap, s='dbg')
            for i in range(8):
                dump2(f'Dt{i}', Dt[i][:], [128, 512]); dump2(f'Gt{i}', Gt[i][:], [128, 512])
            dump2('carry', carry[:], [128, 32]); dump2('gy', gy[:], [128, 4, 512], BF16)
            dump2('uTb', uTb[0][:], [128, 4, 512], BF16); dump2('zab', zab[0][:], [128, 4, 512], BF16)
            dump2('gluw', gluw[:], [128, 4, 512], BF16); dump2('sg', sg[:], [128, 512])
            dump2('L1b', L1[:], [128, 32, 128], BF16); dump2('Ctabb', Ctab[:], [128, 32, 128]); dump2('Rtabb', Rtab[:], [128, 32, 128])
            dump2('RotLb', RotL[:], [128, 32, 128]); dump2('W1b', W1[:], [128, 32, 128], BF16)
    kb.barrier()


KSCALE = 128.0 ** -0.5


def phase_ml(kb, G, l):
    nc = kb.nc
    cst = G['cst']
    with ExitStack() as es:
        A = Alloc(kb, es, f'mll{l}')
        identf = A.sb('identf', [128, 128], F32)
        ident = A.sb('ident', [128, 128], BF16)
        U = A.sb('U', [128, 128], F32)
        mneg = A.sb('mneg', [128, 128], F32)
        ones = A.sb('ones', [128, 128], F32)
        cw = A.sb('cw', [128, 8, 4], F32)
        cb = A.sb('cb', [128, 8], F32)
        igb = A.sb('igb', [128, 4], F32)
        fgb = A.sb('fgb', [128, 4], F32)
        mhg = A.sb('mhg', [128, 4], F32)
        C = A.sb('C', [128, 4, 129], F32)
        Cbf = A.sb('Cbf', [128, 4, 129], BF16)
        kb.dma(identf[:], cst['ident'], w=['identf'])
        kb.dma(U[:], cst['utri'], w=['U'])
        kb.dma(mneg[:], cst['mneg'], w=['mneg'])
        kb.dma(cw[:], G['conv_w'][l], w=['cw'])
        kb.dma(cb[:], G['conv_b'][l], w=['cb'])
        kb.dma(igb[:], G['igb'][l], w=['igb'])
        kb.dma(fgb[:], G['fgb'][l], w=['fgb'])
        kb.dma(mhg[:], G['mhg'][l], w=['mhg'])
        kb.op('dve', lambda e: e.tensor_copy(out=ident[:], in_=identf[:]), r=['identf'], w=['ident'])
        kb.op('pool', lambda e: e.memset(ones[:], 1.0), w=['ones'])
        kb.op('pool', lambda e: e.memset(C[:], 0.0), w=[('C', h) for h in range(4)])
        kb.op('pool', lambda e: e.memset(Cbf[:], 0.0), w=[('Cbf', h) for h in range(4)])
        xin = [A.sb(f'xin{i}', [128, 8, 515], BF16) for i in range(2)]
        zb = [A.sb(f'zb{i}', [128, 4, 512], BF16) for i in range(2)]
        vaug = [A.sb(f'vaug{i}', [128, 4, 4, 129], BF16) for i in range(2)]
        gat = [A.sb(f'gat{i}', [128, 4, 8], F32) for i in range(2)]
        acc = A.sb('acc', [128, 512], F32)
        qk = A.sb('qk', [128, 8, 512], BF16)
        ig = A.sb('ig', [128, 4, 4], F32)
        lf = A.sb('lf', [128, 4, 4], F32)
        ybo = [A.sb(f'ybo{i}', [128, 4, 512], BF16) for i in range(2)]
        LF = [A.sb(f'LF{i}', [128, 128], F32) for i in range(2)]
        Am = [A.sb(f'Am{i}', [128, 128], F32) for i in range(2)]
        AT = [A.sb(f'AT{i}', [128, 128], F32) for i in range(2)]
        eb = [A.sb(f'eb{i}', [128, 128], F32) for i in range(2)]
        csc = A.sb('csc', [128, 4], F32)
        wcol = [A.sb(f'wcol{i}', [128, 1], F32) for i in range(2)]
        STm = [A.sb(f'STm{i}', [128, 128], BF16) for i in range(2)]
        qs = [A.sb(f'qs{i}', [128, 128], BF16) for i in range(2)]
        sm = [A.sb(f'sm{i}', [128, 8], F32) for i in range(2)]
        junk = A.sb('junk', [128, 128], BF16)
        hn = [A.sb(f'hn{i}', [128, 128], BF16) for i in range(2)]
        kw = [A.sb(f'kw{i}', [128, 128], BF16) for i in range(2)]
        pB = [A.ps(f'pB{i}', [128, 256], F32) for i in range(2)]
        pS = [A.ps(f'pS{i}', [128, 128], F32) for i in range(2)]
        pN = [A.ps(f'pN{i}', [128, 129], F32) for i in range(2)]
        pTK = A.ps('pTK', [128, 2, 128], BF16)
        pD = A.ps('pD', [128, 129], F32)
        for i in range(2):
            kb.op('pool', lambda e: e.memset(vaug[i][:], 1.0), w=[('vaug', i)])
            kb.op('pool', lambda e: e.memset(xin[i][:, :, 0:3], 0.0), w=[('xin', i, 'halo')])

        def load(blk):
            b = blk % 2
            c0 = blk * 512
            for qk_i, src in ((0, G['qbT']), (1, G['kbT'])):
                for h in range(4):
                    ch = qk_i * 4 + h
                    if blk == 0:
                        kb.dma(xin[b][:, ch, 3:515], src[h * 128:(h + 1) * 128, 0:512], w=[('xin', b, ch)], s='ldm')
                    else:
                        kb.dma(xin[b][:, ch, 0:515], src[h * 128:(h + 1) * 128, c0 - 3:c0 + 512],
                               w=[('xin', b, ch), ('xin', b, 'halo')], s='ldm')
            for h in range(4):
                kb.dma(zb[b][:, h, :], G['zbT'][h * 128:(h + 1) * 128, c0:c0 + 512], w=[('zb', b, h)], s='ldm')
            for ci in range(4):
                t0 = c0 + ci * 128
                kb.dma(vaug[b][:, ci, :, 0:128], G['vb'][t0:t0 + 128, :].rearrange("s (h e) -> s h e", h=4),
                       w=[('vaug', b, ci)], r=[('vaug', b)], s='ldm')
                kb.dma(gat[b][:, ci, :], G['ifg'][t0:t0 + 128, :], w=[('gat', b, ci)], s='ldm')

        load(0)
        it = 0
        for blk in range(NBLK):
            b = blk % 2
            if blk + 1 < NBLK:
                load(blk + 1)
            for ch in range(8):
                xk = [('xin', b, ch), ('xin', b, 'halo')]
                kb.op('dve', lambda e: e.tensor_scalar(out=acc[:], in0=xin[b][:, ch, 3:515], scalar1=cw[:, ch, 3:4],
                                                       scalar2=cb[:, ch:ch + 1], op0=ALU.mult, op1=ALU.add),
                      r=xk + ['cw', 'cb'], w=['acc'])
                for j in range(3):
                    kb.op('dve', lambda e: e.scalar_tensor_tensor(out=acc[:], in0=xin[b][:, ch, j:j + 512], scalar=cw[:, ch, j:j + 1],
                                                                  in1=acc[:], op0=ALU.mult, op1=ALU.add),
                          r=xk + ['cw', 'acc'], w=['acc'])
                kb.op('act', lambda e: e.activation(out=qk[:, ch, :], in_=acc[:], func=AF.Silu), r=['acc'], w=[('qk', ch)])
                if ch >= 4:
                    kb.op('pool', lambda e: e.tensor_scalar(out=qk[:, ch, :], in0=qk[:, ch, :], scalar1=KSCALE, scalar2=1.0,
                                                            op0=ALU.mult, op1=ALU.mult), r=[('qk', ch)], w=[('qk', ch)])
            gk = [('gat', b, ci) for ci in range(4)]
            kb.op('dve', lambda e: e.tensor_tensor(out=ig[:], in0=gat[b][:, :, 0:4], in1=igb[:].unsqueeze(1).to_broadcast([128, 4, 4]),
                                                   op=ALU.add), r=gk + ['igb'], w=['ig'])
            kb.op('dve', lambda e: e.tensor_tensor(out=lf[:], in0=gat[b][:, :, 4:8], in1=fgb[:].unsqueeze(1).to_broadcast([128, 4, 4]),
                                                   op=ALU.add), r=gk + ['fgb'], w=['lf'])
            kb.op('act', lambda e: e.activation(out=lf[:], in_=lf[:], func=AF.Exp, scale=-1.0), r=['lf'], w=['lf'])
            kb.op('act', lambda e: e.activation(out=lf[:], in_=lf[:], func=AF.Ln, bias=1.0), r=['lf'], w=['lf'])
            kb.op('dve', lambda e: e.tensor_scalar(out=lf[:], in0=lf[:], scalar1=-1.0, scalar2=None, op0=ALU.mult), r=['lf'], w=['lf'])
            for ci in range(4):
                cs = slice(ci * 128, (ci + 1) * 128)
                for h in range(4):
                    i2 = it % 2
                    pb = pB[i2]; kpb = ('pB', i2)
                    kb.op('dve', lambda e: e.tensor_scalar(out=LF[i2][:], in0=ones[:], scalar1=lf[:, ci, h:h + 1], scalar2=None,
                                                           op0=ALU.mult), r=['ones', 'lf'], w=[('LF', i2)])
                    kb.op('pe', lambda e: e.matmul(pb[:, 0:128], lhsT=LF[i2][:], rhs=U[:], start=True, stop=True),
                          r=[('LF', i2), 'U'], w=[kpb])
                    kb.op('pe', lambda e: e.matmul(pb[:, 128:132], lhsT=U[:], rhs=lf[:, ci, :], start=True, stop=True),
                          r=['lf', 'U'], w=[kpb])
                    kb.op('dve', lambda e: e.tensor_tensor(out=csc[:], in0=ig[:, ci, :], in1=pb[:, 128:132], op=ALU.subtract),
                          r=['ig', kpb], w=['csc', kpb])
                    kb.op('dve', lambda e: e.tensor_tensor(out=Am[i2][:], in0=pb[:, 0:128], in1=mneg[:], op=ALU.add),
                          r=[kpb, 'mneg'], w=[('Am', i2), kpb])
                    kb.op('act', lambda e: e.activation(out=AT[i2][:], in_=Am[i2][:], func=AF.Exp, bias=csc[:, h:h + 1]),
                          r=[('Am', i2), 'csc'], w=[('AT', i2)])
                    kb.op('act', lambda e: e.activation(out=eb[i2][:], in_=pb[:, 0:128], func=AF.Exp), r=[kpb], w=[('eb', i2), kpb])
                    kb.op('act', lambda e: e.activation(out=wcol[i2][:], in_=pb[:, 127:128], func=AF.Exp, bias=csc[:, h:h + 1]),
                          r=[kpb, 'csc'], w=[('wcol', i2), kpb])
                    ps_ = pS[i2]; kps = ('pS', i2)
                    kb.op('pe', lambda e: e.matmul(ps_[:], lhsT=qk[:, 4 + h, cs], rhs=qk[:, h, cs], start=True, stop=True),
                          r=[('qk', 4 + h), ('qk', h)], w=[kps])
                    kb.op('dve', lambda e: e.tensor_tensor(out=STm[i2][:], in0=ps_[:], in1=AT[i2][:], op=ALU.mult),
                          r=[kps, ('AT', i2)], w=[('STm', i2), kps])
                    kb.op('pool', lambda e: e.tensor_tensor(out=qs[i2][:], in0=qk[:, h, cs], in1=eb[i2][:], op=ALU.mult),
                          r=[('qk', h), ('eb', i2)], w=[('qs', i2)])
                    pn = pN[i2]; kpn = ('pN', i2)
                    kb.op('pe', lambda e: e.matmul(pn[:], lhsT=STm[i2][:], rhs=vaug[b][:, ci, h, :], start=True, stop=False),
                          r=[('STm', i2), ('vaug', b, ci)], w=[kpn])
                    kb.op('pe', lambda e: e.matmul(pn[:], lhsT=qs[i2][:], rhs=Cbf[:, h, :], start=False, stop=True),
                          r=[('qs', i2), ('Cbf', h)], w=[kpn])
                    s_ = sm[i2]; ksm = ('sm', i2)
                    kb.op('dve', lambda e: e.tensor_scalar(out=s_[:, 4:5], in0=pn[:, 128:129], scalar1=-1.0, scalar2=None, op0=ALU.mult),
                          r=[kpn], w=[ksm, kpn])
                    kb.op('dve', lambda e: e.scalar_tensor_tensor(out=s_[:, 0:1], in0=pn[:, 128:129], scalar=1.0, in1=s_[:, 4:5],
                                                                  op0=ALU.max, op1=ALU.max), r=[kpn, ksm], w=[ksm, kpn])
                    kb.op('dve', lambda e: e.reciprocal(out=s_[:, 0:1], in_=s_[:, 0:1]), r=[ksm], w=[ksm])
                    kb.op('act', lambda e: e.activation(out=junk[:], in_=pn[:, 0:128], func=AF.Square, accum_out=s_[:, 1:2]),
                          r=[kpn, ksm], w=['junk', ksm, kpn])
                    kb.op('dve', lambda e: e.tensor_tensor(out=s_[:, 2:3], in0=s_[:, 0:1], in1=s_[:, 0:1], op=ALU.mult), r=[ksm], w=[ksm])
                    kb.op('dve', lambda e: e.tensor_tensor(out=s_[:, 2:3], in0=s_[:, 2:3], in1=s_[:, 1:2], op=ALU.mult), r=[ksm], w=[ksm])
                    kb.op('dve', lambda e: e.tensor_scalar(out=s_[:, 2:3], in0=s_[:, 2:3], scalar1=1.0 / 128, scalar2=EPS, op0=ALU.mult,
                                                           op1=ALU.add), r=[ksm], w=[ksm])
                    kb.op('act', lambda e: e.activation(out=s_[:, 2:3], in_=s_[:, 2:3], func=AF.Sqrt), r=[ksm], w=[ksm])
                    kb.op('dve', lambda e: e.reciprocal(out=s_[:, 2:3], in_=s_[:, 2:3]), r=[ksm], w=[ksm])
                    kb.op('dve', lambda e: e.tensor_tensor(out=s_[:, 3:4], in0=s_[:, 2:3], in1=s_[:, 0:1], op=ALU.mult), r=[ksm], w=[ksm])
                    kb.op('dve', lambda e: e.tensor_scalar(out=hn[i2][:], in0=pn[:, 0:128], scalar1=s_[:, 3:4], scalar2=None, op0=ALU.mult),
                          r=[kpn, ksm], w=[('hn', i2), kpn])
                    kb.op('pe', lambda e: e.transpose(out=pTK[:, 0, :], in_=hn[i2][:], identity=ident[:]),
                          r=[('hn', i2), 'ident'], w=[('pTK', 0)])
                    kb.op('dve', lambda e: e.scalar_tensor_tensor(out=ybo[b][:, h, cs], in0=pTK[:, 0, :], scalar=mhg[:, h:h + 1],
                                                                  in1=zb[b][:, h, cs], op0=ALU.mult, op1=ALU.mult),
                          r=[('pTK', 0), 'mhg', ('zb', b, h)], w=[('ybo', b, h, ci), ('pTK', 0)])
                    kb.op('pe', lambda e: e.transpose(out=pTK[:, 1, :], in_=qk[:, 4 + h, cs], identity=ident[:]),
                          r=[('qk', 4 + h), 'ident'], w=[('pTK', 1)])
                    kb.op('dve', lambda e: e.tensor_scalar(out=kw[i2][:], in0=pTK[:, 1, :], scalar1=wcol[i2][:, 0:1], scalar2=None,
                                                           op0=ALU.mult), r=[('pTK', 1), ('wcol', i2)], w=[('kw', i2), ('pTK', 1)])
                    kb.op('pe', lambda e: e.matmul(pD[:], lhsT=kw[i2][:], rhs=vaug[b][:, ci, h, :], start=True, stop=True),
                          r=[('kw', i2), ('vaug', b, ci)], w=['pD'])
                    kb.op('dve', lambda e: e.scalar_tensor_tensor(out=C[:, h, :], in0=C[:, h, :], scalar=eb[i2][:, 127:128], in1=pD[:],
                                                                  op0=ALU.mult, op1=ALU.add),
                          r=[('C', h), ('eb', i2), 'pD'], w=[('C', h), 'pD'])
                    kb.op('act', lambda e: e.activation(out=Cbf[:, h, :], in_=C[:, h, :], func=AF.Copy), r=[('C', h)], w=[('Cbf', h)])
                    it += 1
            for h in range(4):
                kb.dma(G['ysT'][512 + h * 128:512 + (h + 1) * 128, blk * 512:(blk + 1) * 512], ybo[b][:, h, :],
                       r=[('ybo', b, h, ci) for ci in range(4)], w=[('ysTb', h, blk)], s='st2', q=STQ)
    kb.barrier()


ATT_SCALE = 128.0 ** -0.5
IDX_SCALE = (8.0 ** -0.5) * (64.0 ** -0.5)
TOPK = 256
NBIS = 24


def phase_dsa(kb, G, l):
    nc = kb.nc
    cst = G['cst']
    NQ = L // 128
    with ExitStack() as es:
        A = Alloc(kb, es, f'dsl{l}')
        identf = A.sb('identf', [128, 128], F32)
        ident = A.sb('ident', [128, 128], BF16)
        causn = A.sb('causn', [128, 128], F32)
        kiT2 = A.sb('kiT2', [128, L], BF16)
        kcT = A.sb('kcT', [128, L], BF16)
        vaug = A.sb('vaug', [128, NQ, 129], BF16)
        kb.dma(identf[:], cst['ident'], w=['identf'])
        kb.dma(causn[:], cst['causn'], w=['causn'])
        kb.op('dve', lambda e: e.tensor_copy(out=ident[:], in_=identf[:]), r=['identf'], w=['ident'])
        kb.op('pool', lambda e: e.memset(vaug[:], 1.0), w=['vaug'])
        for blk in range(NBLK):
            kb.dma(kcT[:, blk * 512:(blk + 1) * 512], G['kcT'][:, blk * 512:(blk + 1) * 512], w=[('kcT', blk)], s='ldk')
            kb.dma(vaug[:, blk * 4:(blk + 1) * 4, 0:128], G['vc'][blk * 512:(blk + 1) * 512, :].rearrange("(n s) e -> s n e", s=128),
                   r=['vaug'], w=[('vaug', blk)], s='ldk')
        with ExitStack() as es2:
            B = Alloc(kb, es2, f'dsl{l}t')
            onesf = B.sb('onesf', [128, 128], F32)
            sel8 = B.sb('sel8', [8, 4, 128], F32)
            qng = B.sb('qng', [128, 2], F32)
            kng = B.sb('kng', [128, 64], F32)
            knb = B.sb('knb', [128, 64], F32)
            wuq = B.sb('wuq', [128, 2, 512], BF16)
            wqi = B.sb('wqi', [128, 2, 512], BF16)
            wst = B.sb('wst', [128, 2, 512], F32)
            kb.op('pool', lambda e: e.memset(onesf[:], 1.0), w=['onesf'])
            kb.dma(sel8[:], cst['sel8'], w=['sel8'])
            kb.dma(qng[:], G['qng'][l], w=['qng'])
            kb.dma(kng[:], G['kng'][l], w=['kng'])
            kb.dma(knb[:], G['knb'][l], w=['knb'])
            for (src, dst, kd) in ((G['w_uq'][l], wuq, 'wuq'), (G['w_qidx'][l], wqi, 'wqi')):
                kb.dma(wst[:], src.rearrange("(kc p) n -> p kc n", p=128), w=['wst'], s='ld')
                for kc in range(2):
                    kb.op('dve', lambda e: e.tensor_scalar(out=dst[:, kc, :], in0=wst[:, kc, :], scalar1=qng[:, kc:kc + 1], scalar2=None,
                                                           op0=ALU.mult), r=['wst', 'qng'], w=[(kd, kc)])
            cq = [B.sb(f'cq{i}', [128, 2, 512], BF16) for i in range(2)]
            wT = [B.sb(f'wT{i}', [8, 512], F32) for i in range(2)]
            sq = B.sb('sq', [128, 2, 512], F32)
            rr = B.sb('rr', [128, 512], F32)
            tq = B.sb('tq', [128, 512], F32)
            wab = B.sb('wab', [128, 512], F32)
            qo = [B.sb(f'qo{i}', [128, 512], BF16) for i in range(2)]
            kx = [B.sb(f'kx{i}', [128, 64], F32) for i in range(2)]
            kst = [B.sb(f'kst{i}', [128, 4], F32) for i in range(2)]
            kjunk = B.sb('kjunk', [128, 64], F32)
            kn = [B.sb(f'kn{i}', [128, 128], BF16) for i in range(2)]
            pR = B.ps('pR', [128, 512], F32)
            pQ = [B.ps(f'pQ{i}', [128, 512], F32) for i in range(2)]
            pW = B.ps('pW', [128, 512], F32)
            pK = [B.ps(f'pK{i}', [128, 128], BF16) for i in range(2)]
            oi = 0
            ti = 0
            for blk in range(NBLK):
                b = blk % 2
                c0 = blk * 512
                for kc in range(2):
                    kb.dma(cq[b][:, kc, :], G['cqT'][kc * 128:(kc + 1) * 128, c0:c0 + 512], w=[('cq', b, kc)], s='ldc')
                kb.dma(wT[b][:], G['widxT'][:, c0:c0 + 512], w=[('wT', b)], s='ldc')
                cqk = [('cq', b, 0), ('cq', b, 1)]
                kb.op('dve', lambda e: e.tensor_tensor(out=sq[:], in0=cq[b][:], in1=cq[b][:], op=ALU.mult), r=cqk, w=['sq'])
                for kc in range(2):
                    kb.op('pe', lambda e: e.matmul(pR[:], lhsT=onesf[:], rhs=sq[:, kc, :], start=(kc == 0), stop=(kc == 1)),
                          r=['sq', 'onesf'], w=['pR'])
                kb.op('dve', lambda e: e.tensor_scalar(out=rr[:], in0=pR[:], scalar1=1.0 / 256, scalar2=EPS, op0=ALU.mult, op1=ALU.add),
                      r=['pR'], w=['rr', 'pR'])
                kb.op('act', lambda e: e.activation(out=rr[:], in_=rr[:], func=AF.Sqrt), r=['rr'], w=['rr'])
                kb.op('dve', lambda e: e.reciprocal(out=rr[:], in_=rr[:]), r=['rr'], w=['rr'])
                for h in range(4):
                    pq = pQ[oi % 2]; kpq = ('pQ', oi % 2); q_ = qo[oi % 2]; kq = ('qo', oi % 2)
                    for kc in range(2):
                        kb.op('pe', lambda e: e.matmul(pq[:], lhsT=wuq[:, kc, h * 128:(h + 1) * 128], rhs=cq[b][:, kc, :],
                                                       start=(kc == 0), stop=(kc == 1)), r=cqk + [('wuq', kc)], w=[kpq])
                    kb.op('dve', lambda e: e.scalar_tensor_tensor(out=q_[:], in0=pq[:], scalar=ATT_SCALE, in1=rr[:], op0=ALU.mult,
                                                                  op1=ALU.mult), r=[kpq, 'rr'], w=[kq, kpq])
                    kb.dma(G['qT'][h, :, c0:c0 + 512], q_[:], r=[kq], w=[('qT', h, blk)], s='st2', q=STQ)
                    oi += 1
                for pr in range(4):
                    pq = pQ[oi % 2]; kpq = ('pQ', oi % 2); q_ = qo[oi % 2]; kq = ('qo', oi % 2)
                    for kc in range(2):
                        kb.op('pe', lambda e: e.matmul(pq[:], lhsT=wqi[:, kc, pr * 128:(pr + 1) * 128], rhs=cq[b][:, kc, :],
                                                       start=(kc == 0), stop=(kc == 1)), r=cqk + [('wqi', kc)], w=[kpq])
                    kb.op('pe', lambda e: e.matmul(pW[:], lhsT=sel8[:, pr, :], rhs=wT[b][:], start=True, stop=True),
                          r=[('wT', b), 'sel8'], w=['pW'])
                    kb.op('dve', lambda e: e.scalar_tensor_tensor(out=tq[:], in0=pq[:], scalar=IDX_SCALE, in1=rr[:], op0=ALU.mult,
                                                                  op1=ALU.mult), r=[kpq, 'rr'], w=['tq', kpq])
                    kb.op('act', lambda e: e.activation(out=wab[:], in_=pW[:], func=AF.Abs), r=['pW'], w=['wab', 'pW'])
                    kb.op('dve', lambda e: e.tensor_tensor(out=q_[:], in0=tq[:], in1=wab[:], op=ALU.mult), r=['tq', 'wab'], w=[kq])
                    kb.dma(G['qiT'][pr, :, c0:c0 + 512], q_[:], r=[kq], w=[('qiT', pr, blk)], s='st2', q=STQ)
                    oi += 1
                for tt in range(4):
                    t = blk * 4 + tt
                    j = ti % 2
                    x_ = kx[j]; s_ = kst[j]; n_ = kn[j]; kxk = ('kx', j); ksk = ('kst', j); knk = ('kn', j)
                    kb.dma(x_[:], G['kidx'][t * 128:(t + 1) * 128, :], w=[kxk], s='ldc')
                    kb.op('dve', lambda e: e.tensor_reduce(out=s_[:, 0:1], in_=x_[:], axis=AX.X, op=ALU.add), r=[kxk], w=[ksk])
                    kb.op('dve', lambda e: e.tensor_scalar(out=s_[:, 0:1], in0=s_[:, 0:1], scalar1=-1.0 / 64, scalar2=None, op0=ALU.mult),
                          r=[ksk], w=[ksk])
                    kb.op('dve', lambda e: e.tensor_scalar(out=x_[:], in0=x_[:], scalar1=s_[:, 0:1], scalar2=None, op0=ALU.add),
                          r=[kxk, ksk], w=[kxk])
                    kb.op('act', lambda e: e.activation(out=kjunk[:], in_=x_[:], func=AF.Square, accum_out=s_[:, 1:2]),
                          r=[kxk, ksk], w=['kjunk', ksk])
                    kb.op('dve', lambda e: e.tensor_scalar(out=s_[:, 1:2], in0=s_[:, 1:2], scalar1=1.0 / 64, scalar2=EPS, op0=ALU.mult,
                                                           op1=ALU.add), r=[ksk], w=[ksk])
                    kb.op('act', lambda e: e.activation(out=s_[:, 1:2], in_=s_[:, 1:2], func=AF.Sqrt), r=[ksk], w=[ksk])
                    kb.op('dve', lambda e: e.reciprocal(out=s_[:, 1:2], in_=s_[:, 1:2]), r=[ksk], w=[ksk])
                    kb.op('dve', lambda e: e.scalar_tensor_tensor(out=x_[:], in0=x_[:], scalar=s_[:, 1:2], in1=kng[:], op0=ALU.mult,
                                                                  op1=ALU.mult), r=[kxk, ksk, 'kng'], w=[kxk])
                    kb.op('dve', lambda e: e.tensor_tensor(out=n_[:, 0:64], in0=x_[:], in1=knb[:], op=ALU.add), r=[kxk, 'knb'], w=[knk])
                    kb.op('dve', lambda e: e.tensor_tensor(out=n_[:, 64:128], in0=x_[:], in1=knb[:], op=ALU.add), r=[kxk, 'knb', knk], w=[knk])
                    kb.op('pe', lambda e: e.transpose(out=pK[j][:], in_=n_[:], identity=ident[:]), r=[knk, 'ident'], w=[('pK', j)])
                    kb.op('act', lambda e: e.activation(out=kiT2[:, t * 128:(t + 1) * 128], in_=pK[j][:], func=AF.Copy),
                          r=[('pK', j)], w=[('kiT2', t)])
                    ti += 1
            kb.barrier()
        scoreb = [A.sb(f'score{i}', [128, L], F32) for i in range(2)]
        Rb = [A.sb(f'Rb{i}', [128, 8, 512], BF16) for i in range(2)]
        Dg = [A.sb(f'Dg{i}', [128, 8, 128], BF16) for i in range(2)]
        sg8 = A.sb('sg8', [128, 8], F32)
        mask = A.sb('mask', [128, L], BF16)
        maskT = A.sb('maskT', [128, NQ, 128], BF16)
        qib = [A.sb(f'qib{i}', [128, 4, 128], BF16) for i in range(2)]
        qtb = [A.sb(f'qtb{i}', [128, 4, 128], BF16) for i in range(2)]
        zcb = [A.sb(f'zcb{i}', [128, 4, 128], BF16) for i in range(2)]
        wdx = [A.sb(f'wdx{i}', [128, 8], F32) for i in range(2)]
        bs = A.sb('bs', [128, 8], F32)
        tauc = A.sb('tauc', [128, 1], F32)
        Eb = [A.sb(f'Eb{i}', [128, 512], BF16) for i in range(2)]
        Pb = [A.sb(f'Pb{i}', [128, 512], BF16) for i in range(2)]
        osb = A.sb('osb', [128, 4, 128], BF16)
        rec = A.sb('rec', [128, 4], F32)
        ycb = [A.sb(f'ycb{i}', [128, 4, 128], BF16) for i in range(2)]
        pS = [A.ps(f'pS{i}', [128, 512], F32) for i in range(2)]
        pSc = A.ps('pSc', [128, 512], F32)
        pMT = A.ps('pMT', [128, 4, 128], BF16)
        pL = [A.ps(f'pL{i}', [128, 512], F32) for i in range(2)]
        pO = [A.ps(f'pO{i}', [128, 2, 129], F32) for i in range(2)]
        kb.op('pool', lambda e: e.memset(tauc[:], -1.0e29), w=['tauc'])

        def loadq(i):
            b = i % 2
            t0 = i * 128
            kb.dma(qib[b][:], G['qiT'][:, :, t0:t0 + 128].rearrange("r m t -> m r t"), w=[('qib', b)], s='ldq')
            kb.dma(qtb[b][:], G['qT'][:, :, t0:t0 + 128].rearrange("h d t -> d h t"), w=[('qtb', b)], s='ldq')
            kb.dma(zcb[b][:], G['zcT'][:, t0:t0 + 128].rearrange("(h e) t -> e h t", h=4), w=[('zcb', b)], s='ldq')
            kb.dma(wdx[b][:], G['widx'][t0:t0 + 128, :], w=[('wdx', b)], s='ldq')

        loadq(0)
        st = {'si': 0, 'li': 0}
        osf = A.sb('osf', [128, 4, 128], BF16)

        def scoring(i):
            b = i % 2
            nk = (i + 1) * 128
            nkc = (nk + 511) // 512
            score = scoreb[b]
            kb.op('pool', lambda e: e.tensor_scalar(out=sg8[:], in0=wdx[b][:], scalar1=0.0, scalar2=2.0, op0=ALU.is_ge, op1=ALU.mult),
                  r=[('wdx', b)], w=['sg8'])
            kb.op('pool', lambda e: e.tensor_scalar(out=sg8[:], in0=sg8[:], scalar1=-1.0, scalar2=1.0, op0=ALU.add, op1=ALU.mult),
                  r=['sg8'], w=['sg8'])
            for h in range(8):
                kb.op('pool', lambda e: e.tensor_scalar(out=Dg[b][:, h, :], in0=ident[:], scalar1=sg8[:, h:h + 1], scalar2=1.0,
                                                        op0=ALU.mult, op1=ALU.mult), r=['ident', 'sg8'], w=[('Dg', b, h)])

            def head_sum(kc):
                ncol = min(512, nk - kc * 512)
                rb_ = Rb[kc % 2]
                for h in range(8):
                    kb.op('pe', lambda e: e.matmul(pSc[:, 0:ncol], lhsT=Dg[b][:, h, :], rhs=rb_[:, h, 0:ncol], start=(h == 0), stop=(h == 7)),
                          r=[('Dg', b, h), ('Rb', kc % 2, h)], w=['pSc'])
                if kc == nkc - 1:
                    nd = ncol - 128
                    if nd > 0:
                        kb.op('dve', lambda e: e.tensor_copy(out=score[:, kc * 512:kc * 512 + nd], in_=pSc[:, 0:nd]),
                              r=['pSc'], w=[('score', b, kc), 'pSc'])
                    kb.op('dve', lambda e: e.tensor_tensor(out=score[:, i * 128:nk], in0=pSc[:, nd:ncol], in1=causn[:], op=ALU.add),
                          r=['pSc', 'causn'], w=[('score', b, kc), 'pSc'])
                else:
                    kb.op('dve', lambda e: e.tensor_copy(out=score[:, kc * 512:kc * 512 + ncol], in_=pSc[:, 0:ncol]),
                          r=['pSc'], w=[('score', b, kc), 'pSc'])

            for kc in range(nkc):
                ncol = min(512, nk - kc * 512)
                rb_ = Rb[kc % 2]
                for h in range(8):
                    pr = h // 2
                    p0 = 64 * (h % 2)
                    si = st['si']
                    ps_ = pS[si % 2]; kps = ('pS', si % 2)
                    kb.op('pe', lambda e: e.matmul(ps_[:, 0:ncol], lhsT=qib[b][p0:p0 + 64, pr, :], rhs=kiT2[p0:p0 + 64, kc * 512:kc * 512 + ncol],
                                                   start=True, stop=True), r=[('qib', b)], w=[kps])
                    kb.op('act', lambda e: e.activation(out=rb_[:, h, 0:ncol], in_=ps_[:, 0:ncol], func=AF.Relu),
                          r=[kps], w=[('Rb', kc % 2, h), kps])
                    st['si'] += 1
                if kc >= 1:
                    head_sum(kc - 1)
            head_sum(nkc - 1)

        def bisect(i):
            b = i % 2
            nk = (i + 1) * 128
            nkc = (nk + 511) // 512
            score = scoreb[b]
            sck = [('score', b, kc) for kc in range(nkc)]
            if i >= 2:
                nv = i * 128
                kb.op('dve', lambda e: e.tensor_reduce(out=bs[:, 0:1], in_=score[:, 0:nv], axis=AX.X, op=ALU.min), r=sck, w=['bs'])
                kb.op('dve', lambda e: e.tensor_reduce(out=bs[:, 5:6], in_=score[:, 0:nv], axis=AX.X, op=ALU.max), r=sck + ['bs'], w=['bs'])
                kb.op('dve', lambda e: e.tensor_tensor(out=bs[:, 1:2], in0=bs[:, 5:6], in1=bs[:, 0:1], op=ALU.subtract), r=['bs'], w=['bs'])
                kb.op('dve', lambda e: e.tensor_scalar(out=bs[:, 1:2], in0=bs[:, 1:2], scalar1=1.0001, scalar2=1e-6, op0=ALU.mult, op1=ALU.add),
                      r=['bs'], w=['bs'])
                for k in range(NBIS):
                    f = 2.0 ** -(k + 1)
                    kb.op('dve', lambda e: e.tensor_scalar(out=bs[:, 2:3], in0=bs[:, 1:2], scalar1=f, scalar2=bs[:, 0:1], op0=ALU.mult,
                                                           op1=ALU.add), r=['bs'], w=['bs'])
                    kb.op('dve', lambda e: e.tensor_scalar(out=mask[:, 0:nk], in0=score[:, 0:nk], scalar1=bs[:, 2:3], scalar2=None,
                                                           op0=ALU.is_ge, op1=ALU.add, accum_out=bs[:, 3:4]), r=sck + ['bs'], w=['bs', 'mask'])
                    kb.op('dve', lambda e: e.tensor_scalar(out=bs[:, 4:5], in0=bs[:, 3:4], scalar1=TOPK - 0.5, scalar2=bs[:, 1:2],
                                                           op0=ALU.is_ge, op1=ALU.mult), r=['bs'], w=['bs'])
                    kb.op('dve', lambda e: e.scalar_tensor_tensor(out=bs[:, 0:1], in0=bs[:, 4:5], scalar=f, in1=bs[:, 0:1], op0=ALU.mult,
                                                                  op1=ALU.add), r=['bs'], w=['bs'])
                tau = bs[:, 0:1]
                tk = 'bs'
            else:
                tau = tauc[:, 0:1]
                tk = 'tauc'
            kb.op('dve', lambda e: e.tensor_scalar(out=mask[:, 0:nk], in0=score[:, 0:nk], scalar1=tau, scalar2=None, op0=ALU.is_ge),
                  r=sck + [tk], w=['mask'])

        def attend(i):
            b = i % 2
            for q4 in range((i + 4) // 4):
                nb = min(4, i + 1 - q4 * 4)
                for j in range(nb):
                    kbk = q4 * 4 + j
                    kb.op('pe', lambda e: e.transpose(out=pMT[:, j, :], in_=mask[:, kbk * 128:(kbk + 1) * 128], identity=ident[:]),
                          r=['mask', 'ident'], w=['pMT'])
                kb.op('act', lambda e: e.activation(out=maskT[:, q4 * 4:q4 * 4 + nb, :], in_=pMT[:, 0:nb, :], func=AF.Copy),
                      r=['pMT'], w=[('maskT', q4), 'pMT'])
            for kbk in range(i + 1):
                li = st['li']
                pl = pL[li % 2]; kpl = ('pL', li % 2); E_ = Eb[li % 2]; kE = ('Eb', li % 2); P_ = Pb[li % 2]; kP = ('Pb', li % 2)
                kb.op('pe', lambda e: e.matmul(pl[:], lhsT=kcT[:, kbk * 128:(kbk + 1) * 128], rhs=qtb[b][:].rearrange("d h t -> d (h t)"),
                                               start=True, stop=True), r=[('qtb', b)], w=[kpl])
                kb.op('act', lambda e: e.activation(out=E_[:], in_=pl[:], func=AF.Exp), r=[kpl], w=[kE, kpl])
                kb.op('pool', lambda e: e.tensor_tensor(out=P_[:].rearrange("s (h t) -> s h t", h=4), in0=E_[:].rearrange("s (h t) -> s h t", h=4),
                                                        in1=maskT[:, kbk, :].unsqueeze(1).to_broadcast([128, 4, 128]), op=ALU.mult),
                      r=[kE, ('maskT', kbk // 4)], w=[kP])
                for h in range(4):
                    kb.op('pe', lambda e: e.matmul(pO[h // 2][:, h % 2, :], lhsT=P_[:, h * 128:(h + 1) * 128], rhs=vaug[:, kbk, :],
                                                   start=(kbk == 0 and h % 2 == 0), stop=(kbk == i), skip_group_check=True),
                          r=[kP], w=[('pO', h // 2)])
                st['li'] += 1
            for hp in range(2):
                kb.op('act', lambda e: e.activation(out=rec[:, hp * 2:hp * 2 + 2], in_=pO[hp][:, :, 128], func=AF.Ln),
                      r=[('pO', hp)], w=[('rec', hp), ('pO', hp)])
                kb.op('act', lambda e: e.activation(out=rec[:, hp * 2:hp * 2 + 2], in_=rec[:, hp * 2:hp * 2 + 2], func=AF.Exp, scale=-1.0),
                      r=[('rec', hp)], w=[('rec', hp)])
                for h2 in range(2):
                    h = hp * 2 + h2
                    kb.op('act', lambda e: e.activation(out=osb[:, h, :], in_=pO[hp][:, h2, 0:128], func=AF.Copy, scale=rec[:, h:h + 1]),
                          r=[('pO', hp), ('rec', hp)], w=[('osb', h), ('pO', hp)])
            for h in range(4):
                kb.op('pe', lambda e: e.transpose(out=pMT[:, h, :], in_=osb[:, h, :], identity=ident[:]), r=[('osb', h), 'ident'], w=['pMT'])
            kb.op('act', lambda e: e.activation(out=osf[:], in_=pMT[:], func=AF.Copy), r=['pMT'], w=['osf', 'pMT'])
            kb.op('pool', lambda e: e.tensor_tensor(out=ycb[b][:], in0=osf[:], in1=zcb[b][:], op=ALU.mult),
                  r=['osf', ('zcb', b)], w=[('ycb', b)])
            kb.dma(G['ysT'][1024:1536, i * 128:(i + 1) * 128].rearrange("(h e) t -> e h t", h=4), ycb[b][:], r=[('ycb', b)],
                   w=[('ysTc', i)], s='st2', q=STQ)

        if NQ > 1:
            loadq(1)
        scoring(0)
        for i in range(NQ):
            bisect(i)
            if i + 1 < NQ:
                scoring(i + 1)
            attend(i)
            if i + 2 < NQ:
                loadq(i + 2)
    kb.barrier()


def phase_mg(kb, G, l, xsrc):
    nc = kb.nc
    with ExitStack() as es:
        A = Alloc(kb, es, f'mgl{l}')
        wbr = A.sb('wbr', [128, 12, 1024], BF16)
        wo = A.sb('wo', [128, 8, 1024], BF16)
        stg = [A.sb(f'stg{i}', [128, 1024], F32) for i in range(2)]
        i = 0
        for n in range(3):
            src = G['w_branch'][l, n].rearrange("(kc p) d -> p kc d", p=128)
            for k in range(4):
                kb.dma(stg[i % 2][:], src[:, k, :], w=[('stg', i % 2)], s='ld')
                kb.op('dve' if i % 2 == 0 else 'act',
                      (lambda e: e.tensor_copy(out=wbr[:, n * 4 + k, :], in_=stg[i % 2][:])) if i % 2 == 0 else
                      (lambda e: e.activation(out=wbr[:, n * 4 + k, :], in_=stg[i % 2][:], func=AF.Copy)),
                      r=[('stg', i % 2)], w=[('wbr', n * 4 + k)])
                i += 1
        src = G['w_out'][l].rearrange("(kc p) d -> p kc d", p=128)
        for k in range(8):
            kb.dma(stg[i % 2][:], src[:, k, :], w=[('stg', i % 2)], s='ld')
            kb.op('dve' if i % 2 == 0 else 'act',
                  (lambda e: e.tensor_copy(out=wo[:, k, :], in_=stg[i % 2][:])) if i % 2 == 0 else
                  (lambda e: e.activation(out=wo[:, k, :], in_=stg[i % 2][:], func=AF.Copy)),
                  r=[('stg', i % 2)], w=[('wo', k)])
            i += 1
        ys = [A.sb(f'ys{i}', [128, 12, 512], BF16) for i in range(2)]
        gt = [A.sb(f'gt{i}', [128, 24, 512], BF16) for i in range(2)]
        mT = A.sb('mT', [128, 8, 512], BF16)
        t0 = A.sb('t0', [128, 512], F32)
        t1 = A.sb('t1', [128, 512], F32)
        t2 = A.sb('t2', [128, 512], F32)
        xt = [A.sb(f'xt{i}', [128, 1024], F32) for i in range(2)]
        pM = [A.ps(f'pM{i}', [128, 512], F32) for i in range(3)]
        pO = [A.ps(f'pO{i}', [128, 512], F32) for i in range(2)]

        def load(blk):
            b = blk % 2
            for c in range(12):
                kb.dma(ys[b][:, c, :], G['ysT'][c * 128:(c + 1) * 128, blk * 512:(blk + 1) * 512], w=[('ys', b, c)], s='ldy')
            for c in range(24):
                kb.dma(gt[b][:, c, :], G['gateT'][c * 128:(c + 1) * 128, blk * 512:(blk + 1) * 512], w=[('gt', b, c)], s='ldg')

        load(0)
        xi = 0
        for blk in range(NBLK):
            b = blk % 2
            if blk + 1 < NBLK:
                load(blk + 1)
            for oc in range(8):
                for n in range(3):
                    for k in range(4):
                        kb.op('pe', lambda e: e.matmul(pM[n][:], lhsT=wbr[:, n * 4 + k, oc * 128:(oc + 1) * 128], rhs=ys[b][:, n * 4 + k, :],
                                                       start=(k == 0), stop=(k == 3)),
                              r=[('wbr', n * 4 + k), ('ys', b, n * 4 + k)], w=[('pM', n)])
                kb.op('dve', lambda e: e.tensor_tensor(out=t0[:], in0=pM[0][:], in1=gt[b][:, oc, :], op=ALU.mult),
                      r=[('pM', 0), ('gt', b, oc)], w=['t0', ('pM', 0)])
                kb.op('dve', lambda e: e.tensor_tensor(out=t1[:], in0=pM[1][:], in1=gt[b][:, 8 + oc, :], op=ALU.mult),
                      r=[('pM', 1), ('gt', b, 8 + oc)], w=['t1', ('pM', 1)])
                kb.op('dve', lambda e: e.tensor_tensor(out=t2[:], in0=pM[2][:], in1=gt[b][:, 16 + oc, :], op=ALU.mult),
                      r=[('pM', 2), ('gt', b, 16 + oc)], w=['t2', ('pM', 2)])
                kb.op('pool', lambda e: e.tensor_tensor(out=t0[:], in0=t0[:], in1=t1[:], op=ALU.add), r=['t0', 't1'], w=['t0'])
                kb.op('pool', lambda e: e.tensor_tensor(out=mT[:, oc, :], in0=t0[:], in1=t2[:], op=ALU.add), r=['t0', 't2'], w=[('mT', oc)])
            mk = [('mT', oc) for oc in range(8)]
            for tt in range(4):
                t = blk * 4 + tt
                x_t = xt[xi % 2]; kx = ('xt', xi % 2)
                kb.dma(x_t[:], xsrc[t * 128:(t + 1) * 128, :], r=[('x', t)], w=[kx], s='ldx')
                for hf in range(2):
                    po = pO[hf]; kpo = ('pO', hf)
                    for k in range(8):
                        kb.op('pe', lambda e: e.matmul(po[:], lhsT=mT[:, k, tt * 128:(tt + 1) * 128], rhs=wo[:, k, hf * 512:(hf + 1) * 512],
                                                       start=(k == 0), stop=(k == 7)), r=mk + [('wo', k)], w=[kpo])
                    kb.op('dve', lambda e: e.tensor_tensor(out=x_t[:, hf * 512:(hf + 1) * 512], in0=po[:], in1=x_t[:, hf * 512:(hf + 1) * 512],
                                                           op=ALU.add), r=[kpo, kx], w=[kx, kpo])
                kb.dma(G['xres'][t * 128:(t + 1) * 128, :], x_t[:], r=[kx], w=[('x', t)], s='stx', q=STQ)
                xi += 1
    kb.barrier()


def phase_final(kb, G, xsrc):
    nc = kb.nc
    with ExitStack() as es:
        A = Alloc(kb, es, 'fin')
        gf = A.sb('gf', [128, D], F32)
        kb.dma(gf[:], G['final_g'], w=['gf'])
        xt = [A.sb(f'xt{i}', [128, D], F32) for i in range(2)]
        junk = A.sb('junk', [128, D], BF16)
        ss = [A.sb(f'ss{i}', [128, 1], F32) for i in range(2)]
        for t in range(NT):
            x_t = xt[t % 2]; s_t = ss[t % 2]; kx = ('xt', t % 2); ks = ('ss', t % 2)
            kb.dma(x_t[:], xsrc[t * 128:(t + 1) * 128, :], r=[('x', t)], w=[kx], s='ldx')
            kb.op('act', lambda e: e.activation(out=junk[:], in_=x_t[:], func=AF.Square, accum_out=s_t[:]), r=[kx], w=['junk', ks])
            kb.op('dve', lambda e: e.tensor_scalar(out=s_t[:], in0=s_t[:], scalar1=1.0 / D, scalar2=EPS, op0=ALU.mult, op1=ALU.add),
                  r=[ks], w=[ks])
            kb.op('act', lambda e: e.activation(out=s_t[:], in_=s_t[:], func=AF.Sqrt), r=[ks], w=[ks])
            kb.op('dve', lambda e: e.reciprocal(out=s_t[:], in_=s_t[:]), r=[ks], w=[ks])
            kb.op('dve', lambda e: e.scalar_tensor_tensor(out=x_t[:], in0=x_t[:], scalar=s_t[:, 0:1], in1=gf[:], op0=ALU.mult, op1=ALU.mult),
                  r=[kx, ks, 'gf'], w=[kx])
            kb.dma(G['out'][t * 128:(t + 1) * 128, :], x_t[:], r=[kx], w=[('out', t)], s='sto', q=STQ)
    kb.barrier()


def host_consts():
    c = {}
    c['ident'] = np.eye(128, dtype=np.float32)
    c['sgn1'] = np.concatenate([np.ones(64), -np.ones(64)]).astype(np.float32).reshape(128, 1)
    sw = np.zeros((128, 128), np.float32)
    sw[np.arange(128), (np.arange(128) + 64) % 128] = 1.0
    c['swap'] = sw
    c['iota'] = np.tile(np.arange(128, dtype=np.float32)[None, :], (128, 1))
    c['utri'] = np.triu(np.ones((128, 128), np.float32))
    c['causn'] = np.where(np.arange(128)[None, :] <= np.arange(128)[:, None], 0.0, -1.0e30).astype(np.float32)
    s8 = np.zeros((8, 4, 128), np.float32)
    for pr in range(4):
        s8[2 * pr, pr, 0:64] = 1.0
        s8[2 * pr + 1, pr, 64:128] = 1.0
    c['sel8'] = s8
    c['mneg'] = np.where(np.arange(128)[None, :] >= np.arange(128)[:, None], 0.0, -30000.0).astype(np.float32)
    return c


def input_shapes():
    return {
        'x': ([L, D], F32),
        'w_in': ([NL, D, INW], F32),
        'norm_g': ([NL, 128, 8], F32),
        'a_re': ([NL, 128, 32], F32), 'a_im': ([NL, 128, 32], F32), 'log_dt': ([NL, 128, 32], F32),
        'X1': ([NL, 128, 32, 16], F32), 'X2': ([NL, 128, 32, 16], F32),
        'Cc1': ([NL, 4, 128, 128], F32), 'Cc2': ([NL, 4, 128, 128], F32),
        'ssm_d': ([NL, 128, 4], F32), 'glu_b': ([NL, 128, 4], F32), 'glu_w': ([NL, 512, 512], F32),
        'conv_w': ([NL, 128, 8, 4], F32), 'conv_b': ([NL, 128, 8], F32), 'igb': ([NL, 128, 4], F32), 'fgb': ([NL, 128, 4], F32),
        'mhg': ([NL, 128, 4], F32),
        'qng': ([NL, 128, 2], F32), 'kng': ([NL, 128, 64], F32), 'knb': ([NL, 128, 64], F32),
        'w_uq': ([NL, 256, 512], F32), 'w_qidx': ([NL, 256, 512], F32),
        'w_branch': ([NL, 3, 512, 1024], F32), 'w_out': ([NL, 1024, 1024], F32), 'final_g': ([128, D], F32),
    }


def set_len(n):
    global L, NBLK, NT
    L = n
    NBLK = L // 512
    NT = L // 128


def declare(kb, cfg):
    nc = kb.nc
    G = {}
    for name, (shape, dt) in input_shapes().items():
        G[name] = nc.dram_tensor(name, list(shape), dt, kind='ExternalInput').ap()
    cst = {}
    for name, arr in host_consts().items():
        cst[name] = nc.dram_tensor('c_' + name, list(arr.shape), F32, kind='ExternalInput').ap()
    G['cst'] = cst
    scratch = {
        'uT': ([512, L], BF16), 'zaT': ([512, L], BF16), 'qbT': ([512, L], BF16), 'kbT': ([512, L], BF16),
        'zbT': ([512, L], BF16), 'cqT': ([256, L], BF16), 'kcT': ([128, L], BF16), 'zcT': ([512, L], BF16),
        'gateT': ([3072, L], BF16), 'widxT': ([8, L], F32),
        'vb': ([L, 512], BF16), 'ifg': ([L, 8], F32), 'vc': ([L, 128], BF16), 'kidx': ([L, 64], F32),
        'widx': ([L, 8], F32), 'ysT': ([1536, L], BF16), 'xres': ([L, D], F32),
        'qT': ([4, 128, L], BF16), 'qiT': ([4, 128, L], BF16),
    }
    for name, (shape, dt) in scratch.items():
        G[name] = kb.dram(name, shape, dt)
    G['out'] = nc.dram_tensor('out', [L, D], F32, kind='ExternalOutput').ap()
    return G


def build(cfg=None):
    cfg = cfg or {}
    nc = bass.Bass("TRN2", target_bir_lowering=False)
    es = ExitStack()
    kb = KB(nc, es, ext=cfg.get('ext', {}))
    G = declare(kb, cfg)
    G['cfg'] = cfg
    phases = cfg.get('phases', None)
    layers = cfg.get('layers', list(range(NL)))
    for l in layers:
        xsrc = G['x'] if l == 0 else G['xres']
        if phases is None or 'p1' in phases:
            phase1(kb, G, l, xsrc)
        if phases is None or 's5' in phases:
            phase_s5(kb, G, l)
        if phases is None or 'ml' in phases:
            phase_ml(kb, G, l)
        if phases is None or 'dsa' in phases:
            phase_dsa(kb, G, l)
        if phases is None or 'mg' in phases:
            phase_mg(kb, G, l, xsrc)
    if phases is None or 'fin' in phases:
        phase_final(kb, G, G['xres'])
    kb.barrier()
    kb.final_wait()
    es.close()
    return nc, kb


def host_layout(inputs, b):
    m = {}
    m['x'] = np.ascontiguousarray(inputs['x'][b][:L])
    m['w_in'] = np.ascontiguousarray(inputs['w_in'])
    m['norm_g'] = np.ascontiguousarray(inputs['norm_g'].reshape(NL, 8, 128).transpose(0, 2, 1))
    ca = np.ascontiguousarray
    art = inputs['ssm_a_re'].transpose(0, 2, 1)
    m['a_re'] = ca(np.concatenate([art, art], axis=1))
    ait = inputs['ssm_a_im'].transpose(0, 2, 1)
    m['a_im'] = ca(np.concatenate([ait, ait], axis=1))
    m['log_dt'] = ca(np.broadcast_to(inputs['ssm_log_dt'][:, None, :], (NL, 128, 32)))
    br = inputs['ssm_b_re'].transpose(0, 2, 1, 3)
    bi_ = inputs['ssm_b_im'].transpose(0, 2, 1, 3)
    m['X1'] = ca(np.concatenate([br, bi_], axis=1))
    m['X2'] = ca(np.concatenate([bi_, br], axis=1))
    cr = inputs['ssm_c_re'].reshape(NL, 4, 128, 64)
    ci = inputs['ssm_c_im'].reshape(NL, 4, 128, 64)
    m['Cc1'] = ca(np.concatenate([cr, ci], axis=3))
    m['Cc2'] = ca(np.concatenate([ci, cr], axis=3))
    m['ssm_d'] = ca(inputs['ssm_d'].reshape(NL, 4, 128).transpose(0, 2, 1))
    m['glu_b'] = ca(inputs['glu_b'].reshape(NL, 4, 128).transpose(0, 2, 1))
    m['glu_w'] = ca(inputs['glu_w'])
    m['conv_w'] = ca(inputs['qk_conv_w'].transpose(0, 2, 1).reshape(NL, 8, 128, 4).transpose(0, 2, 1, 3))
    m['conv_b'] = ca(inputs['qk_conv_b'].reshape(NL, 8, 128).transpose(0, 2, 1))
    m['igb'] = ca(np.broadcast_to(inputs['igate_b'][:, None, :], (NL, 128, 4)))
    m['fgb'] = ca(np.broadcast_to(inputs['fgate_b'][:, None, :], (NL, 128, 4)))
    m['mhg'] = ca(inputs['mh_norm_g'].reshape(NL, 4, 128).transpose(0, 2, 1))
    m['qng'] = ca(inputs['q_norm_g'].reshape(NL, 2, 128).transpose(0, 2, 1))
    m['kng'] = ca(np.broadcast_to(inputs['kidx_norm_g'][:, None, :], (NL, 128, 64)))
    m['knb'] = ca(np.broadcast_to(inputs['kidx_norm_b'][:, None, :], (NL, 128, 64)))
    m['w_uq'] = ca(inputs['w_uq'])
    m['w_qidx'] = ca(inputs['w_qidx'])
    m['w_branch'] = ca(inputs['w_branch'])
    m['w_out'] = ca(inputs['w_out'])
    m['final_g'] = ca(np.broadcast_to(inputs['final_norm_g'][None, :], (128, D)))
    for k, v in host_consts().items():
        m['c_' + k] = v
    return m


N_CORES = 2


def kernel(**inputs):
    set_len(8192)
    nc, kb = build({})
    in_maps = [host_layout(inputs, c % 2) for c in range(N_CORES)]
    res = run_bass_kernel_spmd(nc, in_maps, core_ids=list(range(N_CORES)))
    out = np.stack([np.asarray(res.results[b]['out'], dtype=np.float32) for b in range(2)], axis=0)
    return out
```

```python
import numpy as np
import concourse.bass as bass
import concourse.mybir as mybir
from concourse.bass_utils import run_bass_kernel_spmd
from contextlib import ExitStack

F32 = mybir.dt.float32
BF16 = mybir.dt.bfloat16
AF = mybir.ActivationFunctionType
ALU = mybir.AluOpType
AX = mybir.AxisListType

L = 8192
D = 1024
NL = 4
NBLK = L // 512
NT = L // 128
EPS = 1e-6
BIG = 1.0e30

C_U = 0; C_ZA = 512; C_QB = 1024; C_KB = 1536; C_VB = 2048; C_IF = 2560; C_ZB = 2568
C_CQ = 3080; C_KC = 3336; C_VC = 3464; C_KIDX = 3592; C_WIDX = 3656; C_ZC = 3664; C_GATE = 4176
INW = 7248
WJ = 1208

SAME_SYNC = True
STQ = 'sp'


class KB:
    def __init__(self, nc, es, ext=None):
        self.nc = nc
        self.es = es
        self.ext = ext or {}
        self.E = {'pe': nc.tensor, 'act': nc.scalar, 'dve': nc.vector, 'pool': nc.gpsimd, 'sp': nc.sync}
        self.sem = {}
        self.cnt = {}
        for e in ['pe', 'act', 'dve', 'pool']:
            self.sem[e] = es.enter_context(nc.semaphore('s_' + e))
            self.cnt[e] = 0
        self.dsem = {}
        self.dcnt = {}
        self.dstream = {}
        self.waited = {e: {} for e in self.E}
        self.lastw = {}
        self.readers = {}
        self.uid = 0
        self.ninstr = 0

    def name(self, base):
        self.uid += 1
        return f"{base}_{self.uid}"

    def dram(self, name, shape, dtype):
        kind = self.ext.get(name, 'Internal')
        return self.nc.dram_tensor(name, list(shape), dtype, kind=kind).ap()

    NS = 6

    def _stream(self, s):
        if s not in self.dstream:
            self.dstream[s] = 0
            for j in range(self.NS):
                self.dsem[(s, j)] = self.es.enter_context(self.nc.semaphore(f'd_{s}{j}'))
                self.dcnt[(s, j)] = 0
        j = self.dstream[s] % self.NS
        self.dstream[s] += 1
        return (s, j)

    def _wait(self, e, src, val):
        if self.waited[e].get(src, 0) >= val:
            return
        sem = self.sem[src[1]] if src[0] == 'e' else self.dsem[src[1]]
        self.E[e].wait_ge(sem, val)
        self.waited[e][src] = val
        self.ninstr += 1

    def _deps(self, e, reads, writes):
        deps = {}

        def add(src, val):
            if deps.get(src, 0) < val:
                deps[src] = val
        for k in reads:
            ev = self.lastw.get(k)
            if ev:
                add(*ev)
        for k in writes:
            ev = self.lastw.get(k)
            if ev:
                add(*ev)
            for src, val in self.readers.get(k, {}).items():
                add(src, val)
        for src, val in deps.items():
            if src == ('e', e) and (e == 'pe' or not SAME_SYNC):
                continue
            self._wait(e, src, val)

    def _commit(self, ev, reads, writes):
        for k in writes:
            self.lastw[k] = ev
            self.readers[k] = {}
        for k in reads:
            r = self.readers.setdefault(k, {})
            if r.get(ev[0], 0) < ev[1]:
                r[ev[0]] = ev[1]

    def op(self, e, fn, r=(), w=(), post=None):
        self._deps(e, r, w)
        ins = fn(self.E[e])
        if post is not None:
            ins = post(self.E[e])
            self.ninstr += 1
        self.cnt[e] += 1
        ins.then_inc(self.sem[e], 1)
        self._commit((('e', e), self.cnt[e]), r, w)
        self.ninstr += 1

    def dma(self, out, in_, r=(), w=(), s='ld', q='sp'):
        sk = self._stream(s)
        if self.dcnt[sk] > 0:
            self._wait(q, ('d', sk), self.dcnt[sk])
        self._deps(q, r, w)
        ins = self.E[q].dma_start(out=out, in_=in_)
        self.dcnt[sk] += 16
        ins.then_inc(self.dsem[sk], 16)
        self._commit((('d', sk), self.dcnt[sk]), r, w)
        self.ninstr += 1

    def barrier(self):
        for e in self.E:
            for p in self.sem:
                if self.cnt[p] > 0 and not (p == e):
                    self._wait(e, ('e', p), self.cnt[p])
            for s in self.dsem:
                if self.dcnt[s] > 0:
                    self._wait(e, ('d', s), self.dcnt[s])
        self.lastw = {}
        self.readers = {}

    def final_wait(self):
        for s in self.dsem:
            if self.dcnt[s] > 0:
                self._wait('sp', ('d', s), self.dcnt[s])


class Alloc:
    def __init__(self, kb, es, tag):
        self.kb = kb
        self.es = es
        self.tag = tag

    def sb(self, name, shape, dt):
        return self.es.enter_context(self.kb.nc.sbuf_tensor(self.kb.name(self.tag + name), list(shape), dt))

    def ps(self, name, shape, dt):
        return self.es.enter_context(self.kb.nc.psum_tensor(self.kb.name(self.tag + name), list(shape), dt))


def wkeys(kc, c0, c1):
    return [('wb', kc, j) for j in range(c0 // WJ, (c1 - 1) // WJ + 1)]


def phase1(kb, G, l, xsrc):
    nc = kb.nc
    cst = G['cst']
    with ExitStack() as es:
        A = Alloc(kb, es, f'p1l{l}')
        wb = A.sb('wb', [128, 8, INW], BF16)
        stg = [A.sb(f'wstg{i}', [128, WJ], F32) for i in range(2)]
        g = A.sb('g', [128, 8], F32)
        ident = A.sb('ident', [128, 128], BF16)
        identf = A.sb('identf', [128, 128], F32)
        kb.dma(g[:], G['norm_g'][l], w=['g'])
        kb.dma(identf[:], cst['ident'], w=['identf'])
        kb.op('dve', lambda e: e.tensor_copy(out=ident[:], in_=identf[:]), r=['identf'], w=['ident'])
        w_in = G['w_in'][l].rearrange("(kc p) n -> p kc n", p=128)
        i = 0
        for kc in range(8):
            for j in range(6):
                s = stg[i % 2]
                kb.dma(s[:], w_in[:, kc, j * WJ:(j + 1) * WJ], w=[('stg', i % 2)])
                if i % 2 == 0:
                    kb.op('dve', lambda e, s=s, kc=kc, j=j: e.tensor_scalar(
                        out=wb[:, kc, j * WJ:(j + 1) * WJ], in0=s[:], scalar1=g[:, kc:kc + 1], scalar2=None,
                        op0=ALU.mult), r=[('stg', 0), 'g'], w=[('wb', kc, j)])
                else:
                    kb.op('act', lambda e, s=s, kc=kc, j=j: e.activation(
                        out=wb[:, kc, j * WJ:(j + 1) * WJ], in_=s[:], func=AF.Copy, scale=g[:, kc:kc + 1]),
                        r=[('stg', 1), 'g'], w=[('wb', kc, j)])
                i += 1

        xt = [A.sb(f'xt{i}', [128, D], F32) for i in range(2)]
        junk = A.sb('junk', [128, D], BF16)
        ss = [A.sb(f'ss{i}', [128, 1], F32) for i in range(2)]
        hb = [A.sb(f'hb{i}', [128, D], BF16) for i in range(2)]
        hT = [A.sb(f'hT{i}', [128, 8, 512], BF16) for i in range(2)]
        pT = [A.ps(f'pT{i}', [128, 8, 128], BF16) for i in range(2)]
        pO = [A.ps(f'pO{i}', [128, 512], F32) for i in range(4)]
        so = [A.sb(f'so{i}', [128, 512], BF16) for i in range(4)]
        sof = [A.sb(f'sof{i}', [128, 256], F32) for i in range(2)]

        fm = [(C_U, 4, AF.Copy, G['uT']), (C_ZA, 4, AF.Silu, G['zaT']), (C_QB, 4, AF.Copy, G['qbT']),
              (C_KB, 4, AF.Copy, G['kbT']), (C_ZB, 4, AF.Silu, G['zbT']), (C_CQ, 2, AF.Copy, G['cqT']),
              (C_KC, 1, AF.Copy, G['kcT']), (C_ZC, 4, AF.Silu, G['zcT']), (C_GATE, 24, AF.Sigmoid, G['gateT'])]
        wst = [A.sb(f'wst{i}', [8, 512], F32) for i in range(2)]
        state = {'ti': 0, 'oi': 0}

        def prep(blk):
            hTb = hT[blk % 2]
            for tt in range(4):
                ti = state['ti']
                t = blk * 4 + tt
                x_t = xt[ti % 2]; s_t = ss[ti % 2]; h_t = hb[ti % 2]; p_t = pT[ti % 2]
                kx = ('xt', ti % 2); ks = ('ss', ti % 2); kh = ('hb', ti % 2); kp = ('pT', ti % 2)
                kb.dma(x_t[:], xsrc[t * 128:(t + 1) * 128, :], r=[('x', t)], w=[kx], s='ldx')
                kb.op('act', lambda e: e.activation(out=junk[:], in_=x_t[:], func=AF.Square, accum_out=s_t[:]),
                      r=[kx], w=['junk', ks])
                kb.op('dve', lambda e: e.tensor_scalar(out=s_t[:], in0=s_t[:], scalar1=1.0 / D, scalar2=EPS,
                                                       op0=ALU.mult, op1=ALU.add), r=[ks], w=[ks])
                kb.op('act', lambda e: e.activation(out=s_t[:], in_=s_t[:], func=AF.Sqrt), r=[ks], w=[ks])
                kb.op('dve', lambda e: e.reciprocal(out=s_t[:], in_=s_t[:]), r=[ks], w=[ks])
                kb.op('dve', lambda e: e.tensor_scalar(out=h_t[:], in0=x_t[:], scalar1=s_t[:, 0:1], scalar2=None,
                                                       op0=ALU.mult), r=[kx, ks], w=[kh])
                for c in range(8):
                    kb.op('pe', lambda e: e.transpose(out=p_t[:, c, :], in_=h_t[:, c * 128:(c + 1) * 128],
                                                      identity=ident[:]), r=[kh, 'ident'], w=[kp])
                kb.op('dve', lambda e: e.tensor_copy(out=hTb[:, :, tt * 128:(tt + 1) * 128], in_=p_t[:]),
                      r=[kp], w=[('hT', blk % 2, tt)])
                state['ti'] += 1

        def tm(blk):
            hTb = hT[blk % 2]
            hkeys = [('hT', blk % 2, tt) for tt in range(4)]
            for tt in range(4):
                t = blk * 4 + tt
                hk = [('hT', blk % 2, tt)]
                lhs = lambda k: hTb[:, k, tt * 128:(tt + 1) * 128]
                oi = state['oi']; po = pO[oi % 4]; kpo = ('pO', oi % 4); st = so[oi % 4]; kst = ('so', oi % 4)
                for k in range(8):
                    kb.op('pe', lambda e: e.matmul(po[:], lhsT=lhs(k), rhs=wb[:, k, C_VB:C_VB + 512],
                                                   start=(k == 0), stop=(k == 7)),
                          r=hk + wkeys(k, C_VB, C_VB + 512), w=[kpo])
                kb.op('dve', lambda e: e.tensor_copy(out=st[:], in_=po[:]), r=[kpo], w=[kst])
                kb.dma(G['vb'][t * 128:(t + 1) * 128, :], st[:], r=[kst], w=[('vb', t)], s='st1')
                state['oi'] += 1
                if 'if' in G.get('cfg', {}).get('tm_skip', ()):
                    continue
                oi = state['oi']; po = pO[oi % 4]; kpo = ('pO', oi % 4); sf = sof[0]
                for k in range(8):
                    kb.op('pe', lambda e: e.matmul(po[:, 0:8], lhsT=lhs(k), rhs=wb[:, k, C_IF:C_IF + 8],
                                                   start=(k == 0), stop=(k == 7)),
                          r=hk + wkeys(k, C_IF, C_IF + 8), w=[kpo])
                kb.op('dve', lambda e: e.tensor_copy(out=sf[:, 0:8], in_=po[:, 0:8]), r=[kpo], w=[('sof', 0)])
                kb.dma(G['ifg'][t * 128:(t + 1) * 128, :], sf[:, 0:8], r=[('sof', 0)], w=[('ifg', t)], s='st1')
                state['oi'] += 1
                if 'vkw' in G.get('cfg', {}).get('tm_skip', ()):
                    continue
                oi = state['oi']; po = pO[oi % 4]; kpo = ('pO', oi % 4); st = so[oi % 4]; kst = ('so', oi % 4); sf = sof[1]
                for k in range(8):
                    kb.op('pe', lambda e: e.matmul(po[:, 0:200], lhsT=lhs(k), rhs=wb[:, k, C_VC:C_VC + 200],
                                                   start=(k == 0), stop=(k == 7)),
                          r=hk + wkeys(k, C_VC, C_VC + 200), w=[kpo])
                kb.op('dve', lambda e: e.tensor_copy(out=st[:, 0:128], in_=po[:, 0:128]), r=[kpo], w=[kst])
                kb.op('dve', lambda e: e.tensor_copy(out=sf[:, 0:72], in_=po[:, 128:200]),
                      r=[kpo], w=[('sof', 1)])
                kb.dma(G['vc'][t * 128:(t + 1) * 128, :], st[:, 0:128], r=[kst], w=[('vc', t)], s='st1')
                kb.dma(G['kidx'][t * 128:(t + 1) * 128, :], sf[:, 0:64], r=[('sof', 1)], w=[('kidx', t)], s='st1')
                kb.dma(G['widx'][t * 128:(t + 1) * 128, :], sf[:, 64:72], r=[('sof', 1)], w=[('widx', t)], s='st1')
                state['oi'] += 1
            if 'wT' in G.get('cfg', {}).get('tm_skip', ()):
                return
            oi = state['oi']; po = pO[oi % 4]; kpo = ('pO', oi % 4); ws = wst[blk % 2]
            for k in range(8):
                kb.op('pe', lambda e: e.matmul(po[:, :], lhsT=wb[:, k, C_WIDX:C_WIDX + 128], rhs=hTb[:, k, :],
                                               start=(k == 0), stop=(k == 7)),
                      r=hkeys + wkeys(k, C_WIDX, C_WIDX + 128), w=[kpo])
            kb.op('dve', lambda e: e.tensor_copy(out=ws[:], in_=po[0:8, :]), r=[kpo], w=[('wst', blk % 2)])
            kb.dma(G['widxT'][:, blk * 512:(blk + 1) * 512], ws[:], r=[('wst', blk % 2)], w=[('widxT', blk)], s='st1')
            state['oi'] += 1

        def fmseg(blk):
            hTb = hT[blk % 2]
            hkeys = [('hT', blk % 2, tt) for tt in range(4)]
            for (c0, nch, func, dst) in fm:
                for ch in range(nch):
                    oi = state['oi']; po = pO[oi % 4]; kpo = ('pO', oi % 4); st = so[oi % 4]; kst = ('so', oi % 4)
                    cc = c0 + ch * 128
                    for k in range(8):
                        kb.op('pe', lambda e: e.matmul(po[:], lhsT=wb[:, k, cc:cc + 128], rhs=hTb[:, k, :],
                                                       start=(k == 0), stop=(k == 7)),
                              r=hkeys + wkeys(k, cc, cc + 128), w=[kpo])
                    kb.op('act', lambda e: e.activation(out=st[:], in_=po[:], func=func), r=[kpo], w=[kst])
                    kb.dma(dst[ch * 128:(ch + 1) * 128, blk * 512:(blk + 1) * 512], st[:], r=[kst],
                           w=[(id(dst), ch, blk)], s='st2', q=STQ)
                    state['oi'] += 1

        stop = G.get('cfg', {}).get('p1_stop', 9)
        if stop >= 1:
            prep(0)
        for blk in range(NBLK):
            if stop >= 2:
                tm(blk)
            if blk + 1 < NBLK and stop >= 1:
                prep(blk + 1)
            if stop >= 3:
                fmseg(blk)
    kb.barrier()


MAGIC = 12582912.0
TWO_PI_S = 6.28318
GC1 = 0.044715
GC2 = 1.5957691216057308


def sincos_turns(kb, A, phi, n, kphi, tag):
    t = A.sb(tag + 't', [128, n], F32)
    k = A.sb(tag + 'k', [128, n], F32)
    sn = A.sb(tag + 'sn', [128, n], F32)
    cs = A.sb(tag + 'cs', [128, n], F32)
    kt, kk, ksn, kcs = tag + 't', tag + 'k', tag + 'sn', tag + 'cs'
    for (off, dst, kd) in ((0.0, sn, ksn), (0.25, cs, kcs)):
        kb.op('dve', lambda e: e.tensor_scalar(out=t[:], in0=phi, scalar1=off, scalar2=MAGIC, op0=ALU.add, op1=ALU.add),
              r=[kphi], w=[kt])
        kb.op('dve', lambda e: e.tensor_scalar(out=k[:], in0=t[:], scalar1=-MAGIC, scalar2=None, op0=ALU.add),
              r=[kt], w=[kk])
        kb.op('dve', lambda e: e.scalar_tensor_tensor(out=t[:], in0=phi, scalar=off, in1=k[:], op0=ALU.add,
                                                      op1=ALU.subtract), r=[kphi, kk], w=[kt])
        kb.op('dve', lambda e: e.tensor_scalar(out=t[:], in0=t[:], scalar1=0.5, scalar2=-0.5, op0=ALU.min, op1=ALU.max),
              r=[kt], w=[kt])
        kb.op('act', lambda e: e.activation(out=dst[:], in_=t[:], func=AF.Sin, scale=TWO_PI_S), r=[kt], w=[kd])
    return sn, cs, ksn, kcs


def phase_s5(kb, G, l):
    nc = kb.nc
    cst = G['cst']
    with ExitStack() as es:
        A = Alloc(kb, es, f's5l{l}')
        Ctab = A.sb('Ctab', [128, 32, 128], F32)
        Stab = A.sb('Stab', [128, 32, 128], F32)
        Rtab = A.sb('Rtab', [128, 32, 128], F32)
        RotL = A.sb('RotL', [128, 32, 128], F32)
        L1 = A.sb('L1', [128, 32, 128], BF16)
        L2 = A.sb('L2', [128, 32, 128], BF16)
        W1 = A.sb('W1', [128, 32, 128], BF16)
        W2 = A.sb('W2', [128, 32, 128], BF16)
        identf = A.sb('identf', [128, 128], F32)
        dsk = A.sb('dsk', [128, 4], F32)
        glub = A.sb('glub', [128, 4], F32)
        gluw = A.sb('gluw', [128, 4, 512], BF16)
        carry = A.sb('carry', [128, 32], F32)
        kb.dma(identf[:], cst['ident'], w=['identf'])
        kb.dma(dsk[:], G['ssm_d'][l], w=['dsk'])
        kb.dma(glub[:], G['glu_b'][l], w=['glub'])
        kb.op('pool', lambda e: e.memset(carry[:], 0.0), w=['carry'])
        with ExitStack() as es2:
            B = Alloc(kb, es2, f's5l{l}t')
            ar = B.sb('ar', [128, 32], F32); ai = B.sb('ai', [128, 32], F32); ldt = B.sb('ldt', [128, 32], F32)
            sgn1 = B.sb('sgn1', [128, 1], F32); swp = B.sb('swp', [128, 128], F32); iot = B.sb('iot', [128, 128], F32)
            X1 = B.sb('X1', [128, 32, 16], F32); X2 = B.sb('X2', [128, 32, 16], F32)
            Cc1 = B.sb('Cc1', [128, 4, 128], F32); Cc2 = B.sb('Cc2', [128, 4, 128], F32)
            gws = B.sb('gws', [128, 4, 512], F32)
            kb.dma(ar[:], G['a_re'][l], w=['ar']); kb.dma(ai[:], G['a_im'][l], w=['ai']); kb.dma(ldt[:], G['log_dt'][l], w=['ldt'])
            kb.dma(sgn1[:], cst['sgn1'], w=['sgn1']); kb.dma(swp[:], cst['swap'], w=['swp']); kb.dma(iot[:], cst['iota'], w=['iot'])
            kb.dma(X1[:], G['X1'][l], w=['X1']); kb.dma(X2[:], G['X2'][l], w=['X2'])
            kb.dma(Cc1[:], G['Cc1'][l].rearrange("c p n -> p c n"), w=['Cc1'])
            kb.dma(Cc2[:], G['Cc2'][l].rearrange("c p n -> p c n"), w=['Cc2'])
            kb.dma(gws[:], G['glu_w'][l].rearrange("(kc p) n -> p kc n", p=128), w=['gws'])
            kb.op('pool', lambda e: e.tensor_copy(out=gluw[:], in_=gws[:]), r=['gws'], w=['gluw'])
            S = {}

            def sm(name):
                S[name] = B.sb(name, [128, 32], F32)
                return S[name]

            def tt(o, a, b, op):
                kb.op('dve', lambda e: e.tensor_tensor(out=S[o][:], in0=S[a][:], in1=S[b][:], op=op), r=[a, b], w=[o])
            S['ar'] = ar; S['ai'] = ai; S['ldt'] = ldt
            for n_ in ['dt', 'mag', 'ang', 'phi1', 'abr', 'abi', 'den', 't1', 'fr', 'fi', 'fis', 'frs', 'tmp', 'phi128', 'ssgn']:
                sm(n_)
            kb.op('act', lambda e: e.activation(out=S['dt'][:], in_=ldt[:], func=AF.Exp), r=['ldt'], w=['dt'])
            tt('tmp', 'ar', 'dt', ALU.mult)
            kb.op('act', lambda e: e.activation(out=S['mag'][:], in_=S['tmp'][:], func=AF.Exp), r=['tmp'], w=['mag'])
            tt('ang', 'ai', 'dt', ALU.mult)
            kb.op('dve', lambda e: e.tensor_scalar(out=S['phi1'][:], in0=S['ang'][:], scalar1=1.0 / (2 * np.pi), scalar2=None,
                                                   op0=ALU.mult), r=['ang'], w=['phi1'])
            s1, c1, ks1, kc1 = sincos_turns(kb, B, S['phi1'][:], 32, 'phi1', 'sc1')
            S['s1'] = s1; S['c1'] = c1
            kb.op('dve', lambda e: e.tensor_tensor(out=S['abr'][:], in0=S['mag'][:], in1=c1[:], op=ALU.mult), r=['mag', kc1], w=['abr'])
            kb.op('dve', lambda e: e.tensor_tensor(out=S['abi'][:], in0=S['mag'][:], in1=s1[:], op=ALU.mult), r=['mag', ks1], w=['abi'])
            tt('den', 'ar', 'ar', ALU.mult)
            tt('tmp', 'ai', 'ai', ALU.mult)
            tt('den', 'den', 'tmp', ALU.add)
            kb.op('dve', lambda e: e.reciprocal(out=S['den'][:], in_=S['den'][:]), r=['den'], w=['den'])
            kb.op('dve', lambda e: e.tensor_scalar(out=S['t1'][:], in0=S['abr'][:], scalar1=-1.0, scalar2=None, op0=ALU.add),
                  r=['abr'], w=['t1'])
            tt('fr', 't1', 'ar', ALU.mult)
            tt('tmp', 'abi', 'ai', ALU.mult)
            tt('fr', 'fr', 'tmp', ALU.add)
            tt('fr', 'fr', 'den', ALU.mult)
            tt('fi', 'abi', 'ar', ALU.mult)
            tt('tmp', 't1', 'ai', ALU.mult)
            tt('fi', 'fi', 'tmp', ALU.subtract)
            tt('fi', 'fi', 'den', ALU.mult)
            kb.op('dve', lambda e: e.tensor_scalar(out=S['frs'][:], in0=S['fr'][:], scalar1=sgn1[:, 0:1], scalar2=None, op0=ALU.mult),
                  r=['fr', 'sgn1'], w=['frs'])
            kb.op('dve', lambda e: e.tensor_scalar(out=S['fis'][:], in0=S['fi'][:], scalar1=sgn1[:, 0:1], scalar2=-1.0, op0=ALU.mult,
                                                   op1=ALU.mult), r=['fi', 'sgn1'], w=['fis'])
            kb.op('dve', lambda e: e.tensor_scalar(out=S['phi128'][:], in0=S['phi1'][:], scalar1=128.0, scalar2=None, op0=ALU.mult),
                  r=['phi1'], w=['phi128'])
            s128, c128, ks128, kc128 = sincos_turns(kb, B, S['phi128'][:], 32, 'phi128', 'sc128')
            kb.op('dve', lambda e: e.tensor_scalar(out=S['ssgn'][:], in0=s128[:], scalar1=sgn1[:, 0:1], scalar2=None, op0=ALU.mult),
                  r=[ks128, 'sgn1'], w=['ssgn'])
            for g in range(32):
                kb.op('dve', lambda e: e.tensor_scalar(out=RotL[:, g, :], in0=identf[:], scalar1=c128[:, g:g + 1], scalar2=None,
                                                       op0=ALU.mult), r=['identf', kc128], w=[('RotL', g)])
                kb.op('dve', lambda e: e.scalar_tensor_tensor(out=RotL[:, g, :], in0=swp[:], scalar=S['ssgn'][:, g:g + 1],
                                                              in1=RotL[:, g, :], op0=ALU.mult, op1=ALU.add),
                      r=['swp', 'ssgn', ('RotL', g)], w=[('RotL', g)])
            es3 = ExitStack()
            B3 = Alloc(kb, es3, f's5l{l}u')
            PHI = B3.sb('PHI', [128, 32 * 128], F32)
            for g in range(32):
                kb.op('pool', lambda e: e.tensor_scalar(out=PHI[:, g * 128:(g + 1) * 128], in0=iot[:], scalar1=S['phi1'][:, g:g + 1],
                                                        scalar2=None, op0=ALU.mult), r=['iot', 'phi1'], w=[('PHI', g)])
                kb.op('pool', lambda e: e.tensor_scalar(out=Rtab[:, g, :], in0=iot[:], scalar1=0.0, scalar2=S['mag'][:, g:g + 1],
                                                        op0=ALU.mult, op1=ALU.add), r=['iot', 'mag'], w=[('Rtab', g)])
            phikeys = [('PHI', g) for g in range(32)]
            tq = B3.sb('tq', [128, 32 * 128], F32)
            kq = B3.sb('kq', [128, 32 * 128], F32)
            for (off, dst, kd) in ((0.0, Stab, 'Stab'), (0.25, Ctab, 'Ctab')):
                kb.op('dve', lambda e: e.tensor_scalar(out=tq[:], in0=PHI[:], scalar1=off, scalar2=MAGIC, op0=ALU.add, op1=ALU.add),
                      r=phikeys, w=['tq'])
                kb.op('dve', lambda e: e.tensor_scalar(out=kq[:], in0=tq[:], scalar1=-MAGIC, scalar2=None, op0=ALU.add),
                      r=['tq'], w=['kq'])
                kb.op('dve', lambda e: e.scalar_tensor_tensor(out=tq[:], in0=PHI[:], scalar=off, in1=kq[:], op0=ALU.add,
                                                              op1=ALU.subtract), r=phikeys + ['kq'], w=['tq'])
                kb.op('dve', lambda e: e.tensor_scalar(out=tq[:], in0=tq[:], scalar1=0.5, scalar2=-0.5, op0=ALU.min, op1=ALU.max),
                      r=['tq'], w=['tq'])
                kb.op('act', lambda e: e.activation(out=dst[:].rearrange("p g j -> p (g j)"), in_=tq[:], func=AF.Sin, scale=TWO_PI_S),
                      r=['tq'], w=[kd])
            kb.barrier()
            es3.close()
            Bp1 = B.sb('Bp1', [128, 32, 128], F32)
            Bp2 = B.sb('Bp2', [128, 32, 128], F32)
            kb.op('pool', lambda e: e.memset(Bp1[:], 0.0), w=['Bp1'])
            kb.op('pool', lambda e: e.memset(Bp2[:], 0.0), w=['Bp2'])
            kb.op('pool', lambda e: e.memset(W1[:], 0.0), w=['W1'])
            kb.op('pool', lambda e: e.memset(W2[:], 0.0), w=['W2'])
            for g in range(32):
                c0 = (g % 8) * 16
                kb.op('dve', lambda e: e.tensor_scalar(out=Bp1[:, g, c0:c0 + 16], in0=X1[:, g, :], scalar1=S['fr'][:, g:g + 1],
                                                       scalar2=None, op0=ALU.mult), r=['X1', 'fr', 'Bp1'], w=[('Bp1', g)])
                kb.op('dve', lambda e: e.scalar_tensor_tensor(out=Bp1[:, g, c0:c0 + 16], in0=X2[:, g, :], scalar=S['fis'][:, g:g + 1],
                                                              in1=Bp1[:, g, c0:c0 + 16], op0=ALU.mult, op1=ALU.add),
                      r=['X2', 'fis', ('Bp1', g)], w=[('Bp1', g)])
                kb.op('dve', lambda e: e.tensor_scalar(out=Bp2[:, g, c0:c0 + 16], in0=X2[:, g, :], scalar1=S['frs'][:, g:g + 1],
                                                       scalar2=None, op0=ALU.mult), r=['X2', 'frs', 'Bp2'], w=[('Bp2', g)])
                kb.op('dve', lambda e: e.scalar_tensor_tensor(out=Bp2[:, g, c0:c0 + 16], in0=X1[:, g, :], scalar=S['fi'][:, g:g + 1],
                                                              in1=Bp2[:, g, c0:c0 + 16], op0=ALU.mult, op1=ALU.add),
                      r=['X1', 'fi', ('Bp2', g)], w=[('Bp2', g)])
            pst = [B.ps(f'pst{i}', [128, 4, 128], F32) for i in range(2)]
            pi_ = 0
            for (Bp, Lx, kn, kl) in ((Bp1, L1, 'Bp1', 'L1'), (Bp2, L2, 'Bp2', 'L2')):
                for q in range(8):
                    pt = pst[pi_ % 2]; kpt = ('pst', pi_ % 2)
                    for j in range(4):
                        g = q * 4 + j
                        kb.op('pe', lambda e: e.transpose(out=pt[:, j, :], in_=Bp[:, g, :], identity=identf[:]),
                              r=[(kn, g), 'identf'], w=[kpt])
                    kb.op('act', lambda e: e.activation(out=Lx[:, q * 4:(q + 1) * 4, :], in_=pt[:], func=AF.Copy),
                          r=[kpt], w=[(kl, q)])
                    pi_ += 1
            for (Cc, Wx, kc_, kw_, neg_all) in ((Cc1, W1, 'Cc1', 'W1', False), (Cc2, W2, 'Cc2', 'W2', True)):
                pt = pst[pi_ % 2]; kpt = ('pst', pi_ % 2)
                for c in range(4):
                    kb.op('pe', lambda e: e.transpose(out=pt[:, c, :], in_=Cc[:, c, :], identity=identf[:]),
                          r=[kc_, 'identf'], w=[kpt])
                for g in range(32):
                    c0 = (g % 8) * 16
                    if neg_all:
                        kb.op('dve', lambda e: e.tensor_scalar(out=Wx[:, g, c0:c0 + 16], in0=pt[:, g // 8, c0:c0 + 16], scalar1=-1.0,
                                                               scalar2=None, op0=ALU.mult), r=[kpt, kw_], w=[(kw_, g)])
                    else:
                        kb.op('dve', lambda e: e.tensor_scalar(out=Wx[:, g, c0:c0 + 16], in0=pt[:, g // 8, c0:c0 + 16],
                                                               scalar1=sgn1[:, 0:1], scalar2=None, op0=ALU.mult),
                              r=[kpt, kw_, 'sgn1'], w=[(kw_, g)])
                pi_ += 1
            kb.barrier()
            if G['cfg'].get('dbg_s5'):
                def dump(name, ap, shape, dt=F32):
                    d = nc.dram_tensor('dbg_' + name, list(shape), dt, kind='ExternalOutput').ap()
                    kb.dma(d, ap, s='dbg')
                mode_ = G['cfg'].get('dbg_s5')
                if mode_ == 'one':
                    dump('dt', S['dt'][:], [128, 32])
                for n_ in ['dt', 'mag', 'ang', 'phi1', 'abr', 'abi', 'fr', 'fi', 'ssgn'] if mode_ is True else []:
                    dump(n_, S[n_][:], [128, 32])
                if mode_ is True:
                  dump('s1', S['s1'][:], [128, 32]); dump('c1', S['c1'][:], [128, 32])
                  dump('Stab', Stab[:], [128, 32, 128]); dump('Ctab', Ctab[:], [128, 32, 128]); dump('Rtab', Rtab[:], [128, 32, 128])
                  dump('RotL', RotL[:], [128, 32, 128]); dump('L1', L1[:], [128, 32, 128], BF16); dump('L2', L2[:], [128, 32, 128], BF16)
                  dump('W1', W1[:], [128, 32, 128], BF16); dump('W2', W2[:], [128, 32, 128], BF16)
                kb.barrier()
        uTb = [A.sb(f'uTb{i}', [128, 4, 512], BF16) for i in range(2)]
        zab = [A.sb(f'zab{i}', [128, 4, 512], BF16) for i in range(2)]
        Dt = [A.sb(f'Dt{i}', [128, 512], F32) for i in range(8)]
        Gt = [A.sb(f'Gt{i}', [128, 512], F32) for i in range(8)]
        tmpb = [A.sb(f'tmpb{i}', [128, 512], F32) for i in range(2)]
        P1 = [A.sb(f'P1{i}', [128, 512], BF16) for i in range(2)]
        P2 = [A.sb(f'P2{i}', [128, 512], BF16) for i in range(2)]
        yv = A.sb('yv', [128, 512], F32)
        yt = A.sb('yt', [128, 512], F32)
        gy = A.sb('gy', [128, 4, 512], BF16)
        sg = A.sb('sg', [128, 512], F32)
        yo = [A.sb(f'yo{i}', [128, 512], BF16) for i in range(2)]
        pb1 = [A.ps(f'pb1{i}', [128, 512], F32) for i in range(2)]
        pb2 = [A.ps(f'pb2{i}', [128, 512], F32) for i in range(2)]
        pY = [A.ps(f'pY{i}', [128, 512], F32) for i in range(2)]
        pc = A.ps('pc', [128, 16], F32)
        cj = A.sb('cj', [128, 8], F32)
        pG = A.ps('pG', [128, 512], F32)

        def load(blk):
            for c in range(4):
                kb.dma(uTb[blk % 2][:, c, :], G['uT'][c * 128:(c + 1) * 128, blk * 512:(blk + 1) * 512],
                       r=[(id(G['uT']), c, blk)], w=[('uTb', blk % 2, c)], s='ldu')
                kb.dma(zab[blk % 2][:, c, :], G['zaT'][c * 128:(c + 1) * 128, blk * 512:(blk + 1) * 512],
                       r=[(id(G['zaT']), c, blk)], w=[('zab', blk % 2, c)], s='ldu')

        bc = lambda tab, g: tab[:, g, :].unsqueeze(1).to_broadcast([128, 4, 128])
        v4 = lambda ap: ap.rearrange("p (s j) -> p s j", j=128)
        load(0)
        bi = 0
        yi = 0
        oi = 0
        for blk in range(NBLK):
            if blk + 1 < NBLK:
                load(blk + 1)
            ub = uTb[blk % 2]
            for c in range(4):
                for gg in range(8):
                    g = 8 * c + gg
                    b1 = pb1[bi % 2]; b2 = pb2[bi % 2]; k1 = ('pb1', bi % 2); k2 = ('pb2', bi % 2)
                    tb = tmpb[bi % 2]; ktb = ('tmpb', bi % 2)
                    kb.op('pe', lambda e: e.matmul(b1[:], lhsT=L1[:, g, :], rhs=ub[:, c, :], start=True, stop=True),
                          r=[('uTb', blk % 2, c)], w=[k1])
                    kb.op('pe', lambda e: e.matmul(b2[:], lhsT=L2[:, g, :], rhs=ub[:, c, :], start=True, stop=True),
                          r=[('uTb', blk % 2, c)], w=[k2])
                    kb.op('dve', lambda e: e.tensor_tensor(out=v4(Dt[gg][:]), in0=v4(b1[:]), in1=bc(Ctab, g), op=ALU.mult),
                          r=[k1], w=[('Dt', gg), k1])
                    kb.op('dve', lambda e: e.tensor_tensor(out=v4(tb[:]), in0=v4(b2[:]), in1=bc(Stab, g), op=ALU.mult),
                          r=[k2], w=[ktb, k2])
                    kb.op('pool', lambda e: e.tensor_tensor(out=Dt[gg][:], in0=Dt[gg][:], in1=tb[:], op=ALU.add),
                          r=[('Dt', gg), ktb], w=[('Dt', gg)])
                    bi += 1
                for seg in range(4):
                    for gg in range(8):
                        g = 8 * c + gg
                        sl = slice(seg * 128, (seg + 1) * 128)
                        kb.op('dve', lambda e: e.tensor_tensor_scan(out=Gt[gg][:, sl], data0=Rtab[:, g, :], data1=Dt[gg][:, sl],
                                                                    initial=carry[:, g:g + 1], op0=ALU.mult, op1=ALU.add),
                              r=[('Dt', gg), ('carry', g)], w=[('Gt', gg, seg)])
                        kb.op('pe', lambda e: e.matmul(pc[:, gg:gg + 1], lhsT=RotL[:, g, :],
                                                       rhs=Gt[gg][:, seg * 128 + 127:seg * 128 + 128], start=True, stop=True),
                              r=[('Gt', gg, seg)], w=[('pc', gg)])
                        kb.op('act', lambda e: e.activation(out=carry[:, g:g + 1], in_=pc[:, gg:gg + 1], func=AF.Copy),
                              r=[('pc', gg)], w=[('carry', g)])
                if G['cfg'].get('s5_cut'):
                    break
                py = pY[yi % 2]; kpy = ('pY', yi % 2)
                for gg in range(8):
                    g = 8 * c + gg
                    p1 = P1[gg % 2]; p2 = P2[gg % 2]
                    gk = [('Gt', gg, sg_) for sg_ in range(4)]
                    kb.op('pool', lambda e: e.tensor_tensor(out=v4(p1[:]), in0=v4(Gt[gg][:]), in1=bc(Ctab, g), op=ALU.mult),
                          r=gk, w=[('P1', gg % 2)])
                    kb.op('pool', lambda e: e.tensor_tensor(out=v4(p2[:]), in0=v4(Gt[gg][:]), in1=bc(Stab, g), op=ALU.mult),
                          r=gk, w=[('P2', gg % 2)])
                    kb.op('pe', lambda e: e.matmul(py[:], lhsT=W1[:, g, :], rhs=p1[:], start=(gg == 0), stop=False),
                          r=[('P1', gg % 2)], w=[kpy])
                    kb.op('pe', lambda e: e.matmul(py[:], lhsT=W2[:, g, :], rhs=p2[:], start=False, stop=(gg == 7)),
                          r=[('P2', gg % 2)], w=[kpy])
                kb.op('dve', lambda e: e.scalar_tensor_tensor(out=yv[:], in0=ub[:, c, :], scalar=dsk[:, c:c + 1], in1=py[:],
                                                              op0=ALU.mult, op1=ALU.add),
                      r=[('uTb', blk % 2, c), 'dsk', kpy], w=['yv', kpy])
                kb.op('dve', lambda e: e.tensor_tensor(out=yt[:], in0=yv[:], in1=yv[:], op=ALU.mult), r=['yv'], w=['yt'])
                kb.op('dve', lambda e: e.tensor_scalar(out=yt[:], in0=yt[:], scalar1=GC1, scalar2=1.0, op0=ALU.mult, op1=ALU.add),
                      r=['yt'], w=['yt'])
                kb.op('dve', lambda e: e.tensor_tensor(out=yt[:], in0=yt[:], in1=yv[:], op=ALU.mult), r=['yt', 'yv'], w=['yt'])
                kb.op('act', lambda e: e.activation(out=yt[:], in_=yt[:], func=AF.Sigmoid, scale=GC2), r=['yt'], w=['yt'])
                kb.op('dve', lambda e: e.tensor_tensor(out=gy[:, c, :], in0=yt[:], in1=yv[:], op=ALU.mult),
                      r=['yt', 'yv'], w=[('gy', c)])
                yi += 1
            gyk = [('gy', c) for c in range(4)]
            for oc in range(4 if not G['cfg'].get('s5_cut') else 0):
                for k in range(4):
                    kb.op('pe', lambda e: e.matmul(pG[:], lhsT=gluw[:, k, oc * 128:(oc + 1) * 128], rhs=gy[:, k, :],
                                                   start=(k == 0), stop=(k == 3)), r=gyk + ['gluw'], w=['pG'])
                kb.op('act', lambda e: e.activation(out=sg[:], in_=pG[:], func=AF.Sigmoid, bias=glub[:, oc:oc + 1]),
                      r=['pG', 'glub'], w=['sg', 'pG'])
                yo_ = yo[oi % 2]; kyo = ('yo', oi % 2)
                kb.op('dve', lambda e: e.tensor_tensor(out=sg[:], in0=sg[:], in1=gy[:, oc, :], op=ALU.mult),
                      r=['sg', ('gy', oc)], w=['sg'])
                kb.op('dve', lambda e: e.tensor_tensor(out=yo_[:], in0=sg[:], in1=zab[blk % 2][:, oc, :], op=ALU.mult),
                      r=['sg', ('zab', blk % 2, oc)], w=[kyo])
                kb.dma(G['ysT'][oc * 128:(oc + 1) * 128, blk * 512:(blk + 1) * 512], yo_[:], r=[kyo],
                       w=[('ysT', oc, blk)], s='st2', q=STQ)
                oi += 1
        if G['cfg'].get('dbg_s5b'):
            kb.barrier()
            def dump2(name, ap, shape, dt=F32):
                d = nc.dram_tensor('dbg_' + name, list(shape), dt, kind='ExternalOutput').ap()
                kb.dma(d, ap, s='dbg')
            for i in range(8):
                dump2(f'Dt{i}', Dt[i][:], [128, 512]); dump2(f'Gt{i}', Gt[i][:], [128, 512])
            dump2('carry', carry[:], [128, 32]); dump2('gy', gy[:], [128, 4, 512], BF16)
            dump2('uTb', uTb[0][:], [128, 4, 512], BF16); dump2('zab', zab[0][:], [128, 4, 512], BF16)
            dump2('gluw', gluw[:], [128, 4, 512], BF16); dump2('sg', sg[:], [128, 512])
            dump2('L1b', L1[:], [128, 32, 128], BF16); dump2('Ctabb', Ctab[:], [128, 32, 128]); dump2('Rtabb', Rtab[:], [128, 32, 128])
            dump2('RotLb', RotL[:], [128, 32, 128]); dump2('W1b', W1[:], [128, 32, 128], BF16)
    kb.barrier()


KSCALE = 128.0 ** -0.5


def phase_ml(kb, G, l):
    nc = kb.nc
    cst = G['cst']
    with ExitStack() as es:
        A = Alloc(kb, es, f'mll{l}')
        identf = A.sb('identf', [128, 128], F32)
        ident = A.sb('ident', [128, 128], BF16)
        U = A.sb('U', [128, 128], F32)
        mneg = A.sb('mneg', [128, 128], F32)
        ones = A.sb('ones', [128, 128], F32)
        cw = A.sb('cw', [128, 8, 4], F32)
        cb = A.sb('cb', [128, 8], F32)
        igb = A.sb('igb', [128, 4], F32)
        fgb = A.sb('fgb', [128, 4], F32)
        mhg = A.sb('mhg', [128, 4], F32)
        C = A.sb('C', [128, 4, 129], F32)
        Cbf = A.sb('Cbf', [128, 4, 129], BF16)
        kb.dma(identf[:], cst['ident'], w=['identf'])
        kb.dma(U[:], cst['utri'], w=['U'])
        kb.dma(mneg[:], cst['mneg'], w=['mneg'])
        kb.dma(cw[:], G['conv_w'][l], w=['cw'])
        kb.dma(cb[:], G['conv_b'][l], w=['cb'])
        kb.dma(igb[:], G['igb'][l], w=['igb'])
        kb.dma(fgb[:], G['fgb'][l], w=['fgb'])
        kb.dma(mhg[:], G['mhg'][l], w=['mhg'])
        kb.op('dve', lambda e: e.tensor_copy(out=ident[:], in_=identf[:]), r=['identf'], w=['ident'])
        kb.op('pool', lambda e: e.memset(ones[:], 1.0), w=['ones'])
        kb.op('pool', lambda e: e.memset(C[:], 0.0), w=[('C', h) for h in range(4)])
        kb.op('pool', lambda e: e.memset(Cbf[:], 0.0), w=[('Cbf', h) for h in range(4)])
        xin = [A.sb(f'xin{i}', [128, 8, 515], BF16) for i in range(2)]
        zb = [A.sb(f'zb{i}', [128, 4, 512], BF16) for i in range(2)]
        vaug = [A.sb(f'vaug{i}', [128, 4, 4, 129], BF16) for i in range(2)]
        gat = [A.sb(f'gat{i}', [128, 4, 8], F32) for i in range(2)]
        acc = A.sb('acc', [128, 512], F32)
        qk = A.sb('qk', [128, 8, 512], BF16)
        ig = A.sb('ig', [128, 4, 4], F32)
        lf = A.sb('lf', [128, 4, 4], F32)
        ybo = [A.sb(f'ybo{i}', [128, 4, 512], BF16) for i in range(2)]
        LF = [A.sb(f'LF{i}', [128, 128], F32) for i in range(2)]
        Am = [A.sb(f'Am{i}', [128, 128], F32) for i in range(2)]
        AT = [A.sb(f'AT{i}', [128, 128], F32) for i in range(2)]
        eb = [A.sb(f'eb{i}', [128, 128], F32) for i in range(2)]
        csc = A.sb('csc', [128, 4], F32)
        wcol = [A.sb(f'wcol{i}', [128, 1], F32) for i in range(2)]
        STm = [A.sb(f'STm{i}', [128, 128], BF16) for i in range(2)]
        qs = [A.sb(f'qs{i}', [128, 128], BF16) for i in range(2)]
        sm = [A.sb(f'sm{i}', [128, 8], F32) for i in range(2)]
        junk = A.sb('junk', [128, 128], BF16)
        hn = [A.sb(f'hn{i}', [128, 128], BF16) for i in range(2)]
        kw = [A.sb(f'kw{i}', [128, 128], BF16) for i in range(2)]
        pB = [A.ps(f'pB{i}', [128, 256], F32) for i in range(2)]
        pS = [A.ps(f'pS{i}', [128, 128], F32) for i in range(2)]
        pN = [A.ps(f'pN{i}', [128, 129], F32) for i in range(2)]
        pTK = A.ps('pTK', [128, 2, 128], BF16)
        pD = A.ps('pD', [128, 129], F32)
        for i in range(2):
            kb.op('pool', lambda e: e.memset(vaug[i][:], 1.0), w=[('vaug', i)])
            kb.op('pool', lambda e: e.memset(xin[i][:, :, 0:3], 0.0), w=[('xin', i, 'halo')])

        def load(blk):
            b = blk % 2
            c0 = blk * 512
            for qk_i, src in ((0, G['qbT']), (1, G['kbT'])):
                for h in range(4):
                    ch = qk_i * 4 + h
                    if blk == 0:
                        kb.dma(xin[b][:, ch, 3:515], src[h * 128:(h + 1) * 128, 0:512], w=[('xin', b, ch)], s='ldm')
                    else:
                        kb.dma(xin[b][:, ch, 0:515], src[h * 128:(h + 1) * 128, c0 - 3:c0 + 512],
                               w=[('xin', b, ch), ('xin', b, 'halo')], s='ldm')
            for h in range(4):
                kb.dma(zb[b][:, h, :], G['zbT'][h * 128:(h + 1) * 128, c0:c0 + 512], w=[('zb', b, h)], s='ldm')
            for ci in range(4):
                t0 = c0 + ci * 128
                kb.dma(vaug[b][:, ci, :, 0:128], G['vb'][t0:t0 + 128, :].rearrange("s (h e) -> s h e", h=4),
                       w=[('vaug', b, ci)], r=[('vaug', b)], s='ldm')
                kb.dma(gat[b][:, ci, :], G['ifg'][t0:t0 + 128, :], w=[('gat', b, ci)], s='ldm')

        load(0)
        it = 0
        for blk in range(NBLK):
            b = blk % 2
            if blk + 1 < NBLK:
                load(blk + 1)
            for ch in range(8):
                xk = [('xin', b, ch), ('xin', b, 'halo')]
                kb.op('dve', lambda e: e.tensor_scalar(out=acc[:], in0=xin[b][:, ch, 3:515], scalar1=cw[:, ch, 3:4],
                                                       scalar2=cb[:, ch:ch + 1], op0=ALU.mult, op1=ALU.add),
                      r=xk + ['cw', 'cb'], w=['acc'])
                for j in range(3):
                    kb.op('dve', lambda e: e.scalar_tensor_tensor(out=acc[:], in0=xin[b][:, ch, j:j + 512], scalar=cw[:, ch, j:j + 1],
                                                                  in1=acc[:], op0=ALU.mult, op1=ALU.add),
                          r=xk + ['cw', 'acc'], w=['acc'])
                kb.op('act', lambda e: e.activation(out=qk[:, ch, :], in_=acc[:], func=AF.Silu), r=['acc'], w=[('qk', ch)])
                if ch >= 4:
                    kb.op('pool', lambda e: e.tensor_scalar(out=qk[:, ch, :], in0=qk[:, ch, :], scalar1=KSCALE, scalar2=1.0,
                                                            op0=ALU.mult, op1=ALU.mult), r=[('qk', ch)], w=[('qk', ch)])
            gk = [('gat', b, ci) for ci in range(4)]
            kb.op('dve', lambda e: e.tensor_tensor(out=ig[:], in0=gat[b][:, :, 0:4], in1=igb[:].unsqueeze(1).to_broadcast([128, 4, 4]),
                                                   op=ALU.add), r=gk + ['igb'], w=['ig'])
            kb.op('dve', lambda e: e.tensor_tensor(out=lf[:], in0=gat[b][:, :, 4:8], in1=fgb[:].unsqueeze(1).to_broadcast([128, 4, 4]),
                                                   op=ALU.add), r=gk + ['fgb'], w=['lf'])
            kb.op('act', lambda e: e.activation(out=lf[:], in_=lf[:], func=AF.Exp, scale=-1.0), r=['lf'], w=['lf'])
            kb.op('act', lambda e: e.activation(out=lf[:], in_=lf[:], func=AF.Ln, bias=1.0), r=['lf'], w=['lf'])
            kb.op('dve', lambda e: e.tensor_scalar(out=lf[:], in0=lf[:], scalar1=-1.0, scalar2=None, op0=ALU.mult), r=['lf'], w=['lf'])
            for ci in range(4):
                cs = slice(ci * 128, (ci + 1) * 128)
                for h in range(4):
                    i2 = it % 2
                    pb = pB[i2]; kpb = ('pB', i2)
                    kb.op('dve', lambda e: e.tensor_scalar(out=LF[i2][:], in0=ones[:], scalar1=lf[:, ci, h:h + 1], scalar2=None,
                                                           op0=ALU.mult), r=['ones', 'lf'], w=[('LF', i2)])
                    kb.op('pe', lambda e: e.matmul(pb[:, 0:128], lhsT=LF[i2][:], rhs=U[:], start=True, stop=True),
                          r=[('LF', i2), 'U'], w=[kpb])
                    kb.op('pe', lambda e: e.matmul(pb[:, 128:132], lhsT=U[:], rhs=lf[:, ci, :], start=True, stop=True),
                          r=['lf', 'U'], w=[kpb])
                    kb.op('dve', lambda e: e.tensor_tensor(out=csc[:], in0=ig[:, ci, :], in1=pb[:, 128:132], op=ALU.subtract),
                          r=['ig', kpb], w=['csc', kpb])
                    kb.op('dve', lambda e: e.tensor_tensor(out=Am[i2][:], in0=pb[:, 0:128], in1=mneg[:], op=ALU.add),
                          r=[kpb, 'mneg'], w=[('Am', i2), kpb])
                    kb.op('act', lambda e: e.activation(out=AT[i2][:], in_=Am[i2][:], func=AF.Exp, bias=csc[:, h:h + 1]),
                          r=[('Am', i2), 'csc'], w=[('AT', i2)])
                    kb.op('act', lambda e: e.activation(out=eb[i2][:], in_=pb[:, 0:128], func=AF.Exp), r=[kpb], w=[('eb', i2), kpb])
                    kb.op('act', lambda e: e.activation(out=wcol[i2][:], in_=pb[:, 127:128], func=AF.Exp, bias=csc[:, h:h + 1]),
                          r=[kpb, 'csc'], w=[('wcol', i2), kpb])
                    ps_ = pS[i2]; kps = ('pS', i2)
                    kb.op('pe', lambda e: e.matmul(ps_[:], lhsT=qk[:, 4 + h, cs], rhs=qk[:, h, cs], start=True, stop=True),
                          r=[('qk', 4 + h), ('qk', h)], w=[kps])
                    kb.op('dve', lambda e: e.tensor_tensor(out=STm[i2][:], in0=ps_[:], in1=AT[i2][:], op=ALU.mult),
                          r=[kps, ('AT', i2)], w=[('STm', i2), kps])
                    kb.op('pool', lambda e: e.tensor_tensor(out=qs[i2][:], in0=qk[:, h, cs], in1=eb[i2][:], op=ALU.mult),
                          r=[('qk', h), ('eb', i2)], w=[('qs', i2)])
                    pn = pN[i2]; kpn = ('pN', i2)
                    kb.op('pe', lambda e: e.matmul(pn[:], lhsT=STm[i2][:], rhs=vaug[b][:, ci, h, :], start=True, stop=False),
                          r=[('STm', i2), ('vaug', b, ci)], w=[kpn])
                    kb.op('pe', lambda e: e.matmul(pn[:], lhsT=qs[i2][:], rhs=Cbf[:, h, :], start=False, stop=True),
                          r=[('qs', i2), ('Cbf', h)], w=[kpn])
                    s_ = sm[i2]; ksm = ('sm', i2)
                    kb.op('dve', lambda e: e.tensor_scalar(out=s_[:, 4:5], in0=pn[:, 128:129], scalar1=-1.0, scalar2=None, op0=ALU.mult),
                          r=[kpn], w=[ksm, kpn])
                    kb.op('dve', lambda e: e.scalar_tensor_tensor(out=s_[:, 0:1], in0=pn[:, 128:129], scalar=1.0, in1=s_[:, 4:5],
                                                                  op0=ALU.max, op1=ALU.max), r=[kpn, ksm], w=[ksm, kpn])
                    kb.op('dve', lambda e: e.reciprocal(out=s_[:, 0:1], in_=s_[:, 0:1]), r=[ksm], w=[ksm])
                    kb.op('act', lambda e: e.activation(out=junk[:], in_=pn[:, 0:128], func=AF.Square, accum_out=s_[:, 1:2]),
                          r=[kpn, ksm], w=['junk', ksm, kpn])
                    kb.op('dve', lambda e: e.tensor_tensor(out=s_[:, 2:3], in0=s_[:, 0:1], in1=s_[:, 0:1], op=ALU.mult), r=[ksm], w=[ksm])
                    kb.op('dve', lambda e: e.tensor_tensor(out=s_[:, 2:3], in0=s_[:, 2:3], in1=s_[:, 1:2], op=ALU.mult), r=[ksm], w=[ksm])
                    kb.op('dve', lambda e: e.tensor_scalar(out=s_[:, 2:3], in0=s_[:, 2:3], scalar1=1.0 / 128, scalar2=EPS, op0=ALU.mult,
                                                           op1=ALU.add), r=[ksm], w=[ksm])
                    kb.op('act', lambda e: e.activation(out=s_[:, 2:3], in_=s_[:, 2:3], func=AF.Sqrt), r=[ksm], w=[ksm])
                    kb.op('dve', lambda e: e.reciprocal(out=s_[:, 2:3], in_=s_[:, 2:3]), r=[ksm], w=[ksm])
                    kb.op('dve', lambda e: e.tensor_tensor(out=s_[:, 3:4], in0=s_[:, 2:3], in1=s_[:, 0:1], op=ALU.mult), r=[ksm], w=[ksm])
                    kb.op('dve', lambda e: e.tensor_scalar(out=hn[i2][:], in0=pn[:, 0:128], scalar1=s_[:, 3:4], scalar2=None, op0=ALU.mult),
                          r=[kpn, ksm], w=[('hn', i2), kpn])
                    kb.op('pe', lambda e: e.transpose(out=pTK[:, 0, :], in_=hn[i2][:], identity=ident[:]),
                          r=[('hn', i2), 'ident'], w=[('pTK', 0)])
                    kb.op('dve', lambda e: e.scalar_tensor_tensor(out=ybo[b][:, h, cs], in0=pTK[:, 0, :], scalar=mhg[:, h:h + 1],
                                                                  in1=zb[b][:, h, cs], op0=ALU.mult, op1=ALU.mult),
                          r=[('pTK', 0), 'mhg', ('zb', b, h)], w=[('ybo', b, h, ci), ('pTK', 0)])
                    kb.op('pe', lambda e: e.transpose(out=pTK[:, 1, :], in_=qk[:, 4 + h, cs], identity=ident[:]),
                          r=[('qk', 4 + h), 'ident'], w=[('pTK', 1)])
                    kb.op('dve', lambda e: e.tensor_scalar(out=kw[i2][:], in0=pTK[:, 1, :], scalar1=wcol[i2][:, 0:1], scalar2=None,
                                                           op0=ALU.mult), r=[('pTK', 1), ('wcol', i2)], w=[('kw', i2), ('pTK', 1)])
                    kb.op('pe', lambda e: e.matmul(pD[:], lhsT=kw[i2][:], rhs=vaug[b][:, ci, h, :], start=True, stop=True),
                          r=[('kw', i2), ('vaug', b, ci)], w=['pD'])
                    kb.op('dve', lambda e: e.scalar_tensor_tensor(out=C[:, h, :], in0=C[:, h, :], scalar=eb[i2][:, 127:128], in1=pD[:],
                                                                  op0=ALU.mult, op1=ALU.add),
                          r=[('C', h), ('eb', i2), 'pD'], w=[('C', h), 'pD'])
                    kb.op('act', lambda e: e.activation(out=Cbf[:, h, :], in_=C[:, h, :], func=AF.Copy), r=[('C', h)], w=[('Cbf', h)])
                    it += 1
            for h in range(4):
                kb.dma(G['ysT'][512 + h * 128:512 + (h + 1) * 128, blk * 512:(blk + 1) * 512], ybo[b][:, h, :],
                       r=[('ybo', b, h, ci) for ci in range(4)], w=[('ysTb', h, blk)], s='st2', q=STQ)
    kb.barrier()


ATT_SCALE = 128.0 ** -0.5
IDX_SCALE = (8.0 ** -0.5) * (64.0 ** -0.5)
TOPK = 256
NBIS = 20


def phase_dsa(kb, G, l):
    nc = kb.nc
    cst = G['cst']
    NQ = L // 128
    with ExitStack() as es:
        A = Alloc(kb, es, f'dsl{l}')
        identf = A.sb('identf', [128, 128], F32)
        ident = A.sb('ident', [128, 128], BF16)
        causn = A.sb('causn', [128, 128], F32)
        kiT2 = A.sb('kiT2', [128, L], BF16)
        kcT = A.sb('kcT', [128, L], BF16)
        vaug = A.sb('vaug', [128, NQ, 129], BF16)
        kb.dma(identf[:], cst['ident'], w=['identf'])
        kb.dma(causn[:], cst['causn'], w=['causn'])
        kb.op('dve', lambda e: e.tensor_copy(out=ident[:], in_=identf[:]), r=['identf'], w=['ident'])
        kb.op('pool', lambda e: e.memset(vaug[:], 1.0), w=['vaug'])
        for blk in range(NBLK):
            kb.dma(kcT[:, blk * 512:(blk + 1) * 512], G['kcT'][:, blk * 512:(blk + 1) * 512], w=[('kcT', blk)], s='ldk')
            kb.dma(vaug[:, blk * 4:(blk + 1) * 4, 0:128], G['vc'][blk * 512:(blk + 1) * 512, :].rearrange("(n s) e -> s n e", s=128),
                   r=['vaug'], w=[('vaug', blk)], s='ldk')
        with ExitStack() as es2:
            B = Alloc(kb, es2, f'dsl{l}t')
            onesf = B.sb('onesf', [128, 128], F32)
            sel8 = B.sb('sel8', [8, 4, 128], F32)
            qng = B.sb('qng', [128, 2], F32)
            kng = B.sb('kng', [128, 64], F32)
            knb = B.sb('knb', [128, 64], F32)
            wuq = B.sb('wuq', [128, 2, 512], BF16)
            wqi = B.sb('wqi', [128, 2, 512], BF16)
            wst = B.sb('wst', [128, 2, 512], F32)
            kb.op('pool', lambda e: e.memset(onesf[:], 1.0), w=['onesf'])
            kb.dma(sel8[:], cst['sel8'], w=['sel8'])
            kb.dma(qng[:], G['qng'][l], w=['qng'])
            kb.dma(kng[:], G['kng'][l], w=['kng'])
            kb.dma(knb[:], G['knb'][l], w=['knb'])
            for (src, dst, kd) in ((G['w_uq'][l], wuq, 'wuq'), (G['w_qidx'][l], wqi, 'wqi')):
                kb.dma(wst[:], src.rearrange("(kc p) n -> p kc n", p=128), w=['wst'], s='ld')
                for kc in range(2):
                    kb.op('dve', lambda e: e.tensor_scalar(out=dst[:, kc, :], in0=wst[:, kc, :], scalar1=qng[:, kc:kc + 1], scalar2=None,
                                                           op0=ALU.mult), r=['wst', 'qng'], w=[(kd, kc)])
            cq = [B.sb(f'cq{i}', [128, 2, 512], BF16) for i in range(2)]
            wT = [B.sb(f'wT{i}', [8, 512], F32) for i in range(2)]
            sq = B.sb('sq', [128, 2, 512], F32)
            rr = B.sb('rr', [128, 512], F32)
            tq = B.sb('tq', [128, 512], F32)
            wab = B.sb('wab', [128, 512], F32)
            qo = [B.sb(f'qo{i}', [128, 512], BF16) for i in range(2)]
            kx = [B.sb(f'kx{i}', [128, 64], F32) for i in range(2)]
            kst = [B.sb(f'kst{i}', [128, 4], F32) for i in range(2)]
            kjunk = B.sb('kjunk', [128, 64], F32)
            kn = [B.sb(f'kn{i}', [128, 128], BF16) for i in range(2)]
            pR = B.ps('pR', [128, 512], F32)
            pQ = [B.ps(f'pQ{i}', [128, 512], F32) for i in range(2)]
            pW = B.ps('pW', [128, 512], F32)
            pK = [B.ps(f'pK{i}', [128, 128], BF16) for i in range(2)]
            oi = 0
            ti = 0
            for blk in range(NBLK):
                b = blk % 2
                c0 = blk * 512
                for kc in range(2):
                    kb.dma(cq[b][:, kc, :], G['cqT'][kc * 128:(kc + 1) * 128, c0:c0 + 512], w=[('cq', b, kc)], s='ldc')
                kb.dma(wT[b][:], G['widxT'][:, c0:c0 + 512], w=[('wT', b)], s='ldc')
                cqk = [('cq', b, 0), ('cq', b, 1)]
                kb.op('dve', lambda e: e.tensor_tensor(out=sq[:], in0=cq[b][:], in1=cq[b][:], op=ALU.mult), r=cqk, w=['sq'])
                for kc in range(2):
                    kb.op('pe', lambda e: e.matmul(pR[:], lhsT=onesf[:], rhs=sq[:, kc, :], start=(kc == 0), stop=(kc == 1)),
                          r=['sq', 'onesf'], w=['pR'])
                kb.op('dve', lambda e: e.tensor_scalar(out=rr[:], in0=pR[:], scalar1=1.0 / 256, scalar2=EPS, op0=ALU.mult, op1=ALU.add),
                      r=['pR'], w=['rr', 'pR'])
                kb.op('act', lambda e: e.activation(out=rr[:], in_=rr[:], func=AF.Sqrt), r=['rr'], w=['rr'])
                kb.op('dve', lambda e: e.reciprocal(out=rr[:], in_=rr[:]), r=['rr'], w=['rr'])
                for h in range(4):
                    pq = pQ[oi % 2]; kpq = ('pQ', oi % 2); q_ = qo[oi % 2]; kq = ('qo', oi % 2)
                    for kc in range(2):
                        kb.op('pe', lambda e: e.matmul(pq[:], lhsT=wuq[:, kc, h * 128:(h + 1) * 128], rhs=cq[b][:, kc, :],
                                                       start=(kc == 0), stop=(kc == 1)), r=cqk + [('wuq', kc)], w=[kpq])
                    kb.op('dve', lambda e: e.scalar_tensor_tensor(out=q_[:], in0=pq[:], scalar=ATT_SCALE, in1=rr[:], op0=ALU.mult,
                                                                  op1=ALU.mult), r=[kpq, 'rr'], w=[kq, kpq])
                    kb.dma(G['qT'][h, :, c0:c0 + 512], q_[:], r=[kq], w=[('qT', h, blk)], s='st2', q=STQ)
                    oi += 1
                for pr in range(4):
                    pq = pQ[oi % 2]; kpq = ('pQ', oi % 2); q_ = qo[oi % 2]; kq = ('qo', oi % 2)
                    for kc in range(2):
                        kb.op('pe', lambda e: e.matmul(pq[:], lhsT=wqi[:, kc, pr * 128:(pr + 1) * 128], rhs=cq[b][:, kc, :],
                                                       start=(kc == 0), stop=(kc == 1)), r=cqk + [('wqi', kc)], w=[kpq])
                    kb.op('pe', lambda e: e.matmul(pW[:], lhsT=sel8[:, pr, :], rhs=wT[b][:], start=True, stop=True),
                          r=[('wT', b), 'sel8'], w=['pW'])
                    kb.op('dve', lambda e: e.scalar_tensor_tensor(out=tq[:], in0=pq[:], scalar=IDX_SCALE, in1=rr[:], op0=ALU.mult,
                                                                  op1=ALU.mult), r=[kpq, 'rr'], w=['tq', kpq])
                    kb.op('act', lambda e: e.activation(out=wab[:], in_=pW[:], func=AF.Abs), r=['pW'], w=['wab', 'pW'])
                    kb.op('dve', lambda e: e.tensor_tensor(out=q_[:], in0=tq[:], in1=wab[:], op=ALU.mult), r=['tq', 'wab'], w=[kq])
                    kb.dma(G['qiT'][pr, :, c0:c0 + 512], q_[:], r=[kq], w=[('qiT', pr, blk)], s='st2', q=STQ)
                    oi += 1
                for tt in range(4):
                    t = blk * 4 + tt
                    j = ti % 2
                    x_ = kx[j]; s_ = kst[j]; n_ = kn[j]; kxk = ('kx', j); ksk = ('kst', j); knk = ('kn', j)
                    kb.dma(x_[:], G['kidx'][t * 128:(t + 1) * 128, :], w=[kxk], s='ldc')
                    kb.op('dve', lambda e: e.tensor_reduce(out=s_[:, 0:1], in_=x_[:], axis=AX.X, op=ALU.add), r=[kxk], w=[ksk])
                    kb.op('dve', lambda e: e.tensor_scalar(out=s_[:, 0:1], in0=s_[:, 0:1], scalar1=-1.0 / 64, scalar2=None, op0=ALU.mult),
                          r=[ksk], w=[ksk])
                    kb.op('dve', lambda e: e.tensor_scalar(out=x_[:], in0=x_[:], scalar1=s_[:, 0:1], scalar2=None, op0=ALU.add),
                          r=[kxk, ksk], w=[kxk])
                    kb.op('act', lambda e: e.activation(out=kjunk[:], in_=x_[:], func=AF.Square, accum_out=s_[:, 1:2]),
                          r=[kxk, ksk], w=['kjunk', ksk])
                    kb.op('dve', lambda e: e.tensor_scalar(out=s_[:, 1:2], in0=s_[:, 1:2], scalar1=1.0 / 64, scalar2=EPS, op0=ALU.mult,
                                                           op1=ALU.add), r=[ksk], w=[ksk])
                    kb.op('act', lambda e: e.activation(out=s_[:, 1:2], in_=s_[:, 1:2], func=AF.Sqrt), r=[ksk], w=[ksk])
                    kb.op('dve', lambda e: e.reciprocal(out=s_[:, 1:2], in_=s_[:, 1:2]), r=[ksk], w=[ksk])
                    kb.op('dve', lambda e: e.scalar_tensor_tensor(out=x_[:], in0=x_[:], scalar=s_[:, 1:2], in1=kng[:], op0=ALU.mult,
                                                                  op1=ALU.mult), r=[kxk, ksk, 'kng'], w=[kxk])
                    kb.op('dve', lambda e: e.tensor_tensor(out=n_[:, 0:64], in0=x_[:], in1=knb[:], op=ALU.add), r=[kxk, 'knb'], w=[knk])
                    kb.op('dve', lambda e: e.tensor_tensor(out=n_[:, 64:128], in0=x_[:], in1=knb[:], op=ALU.add), r=[kxk, 'knb', knk], w=[knk])
                    kb.op('pe', lambda e: e.transpose(out=pK[j][:], in_=n_[:], identity=ident[:]), r=[knk, 'ident'], w=[('pK', j)])
                    kb.op('act', lambda e: e.activation(out=kiT2[:, t * 128:(t + 1) * 128], in_=pK[j][:], func=AF.Copy),
                          r=[('pK', j)], w=[('kiT2', t)])
                    ti += 1
            kb.barrier()
        scoreb = [A.sb(f'score{i}', [128, L], F32) for i in range(2)]
        Rb = [A.sb(f'Rb{i}', [128, 8, 512], BF16) for i in range(2)]
        Dg = [A.sb(f'Dg{i}', [128, 8, 128], BF16) for i in range(2)]
        sg8 = A.sb('sg8', [128, 8], F32)
        mask = A.sb('mask', [128, L], BF16)
        maskT = A.sb('maskT', [128, NQ, 128], BF16)
        qib = [A.sb(f'qib{i}', [128, 4, 128], BF16) for i in range(2)]
        qtb = [A.sb(f'qtb{i}', [128, 4, 128], BF16) for i in range(2)]
        zcb = [A.sb(f'zcb{i}', [128, 4, 128], BF16) for i in range(2)]
        wdx = [A.sb(f'wdx{i}', [128, 8], F32) for i in range(2)]
        bs = A.sb('bs', [128, 8], F32)
        tauc = A.sb('tauc', [128, 1], F32)
        Eb = [A.sb(f'Eb{i}', [128, 512], BF16) for i in range(2)]
        Pb = [A.sb(f'Pb{i}', [128, 512], BF16) for i in range(2)]
        osb = A.sb('osb', [128, 4, 128], BF16)
        rec = A.sb('rec', [128, 4], F32)
        ycb = [A.sb(f'ycb{i}', [128, 4, 128], BF16) for i in range(2)]
        pS = [A.ps(f'pS{i}', [128, 512], F32) for i in range(2)]
        pSc = A.ps('pSc', [128, 512], F32)
        pMT = A.ps('pMT', [128, 4, 128], BF16)
        pL = [A.ps(f'pL{i}', [128, 512], F32) for i in range(2)]
        pO = [A.ps(f'pO{i}', [128, 2, 129], F32) for i in range(2)]
        kb.op('pool', lambda e: e.memset(tauc[:], -1.0e29), w=['tauc'])

        def loadq(i):
            b = i % 2
            t0 = i * 128
            kb.dma(qib[b][:], G['qiT'][:, :, t0:t0 + 128].rearrange("r m t -> m r t"), w=[('qib', b)], s='ldq')
            kb.dma(qtb[b][:], G['qT'][:, :, t0:t0 + 128].rearrange("h d t -> d h t"), w=[('qtb', b)], s='ldq')
            kb.dma(zcb[b][:], G['zcT'][:, t0:t0 + 128].rearrange("(h e) t -> e h t", h=4), w=[('zcb', b)], s='ldq')
            kb.dma(wdx[b][:], G['widx'][t0:t0 + 128, :], w=[('wdx', b)], s='ldq')

        loadq(0)
        st = {'si': 0, 'li': 0}
        osf = A.sb('osf', [128, 4, 128], BF16)

        def scoring(i):
            b = i % 2
            nk = (i + 1) * 128
            nkc = (nk + 511) // 512
            score = scoreb[b]
            kb.op('pool', lambda e: e.tensor_scalar(out=sg8[:], in0=wdx[b][:], scalar1=0.0, scalar2=2.0, op0=ALU.is_ge, op1=ALU.mult),
                  r=[('wdx', b)], w=['sg8'])
            kb.op('pool', lambda e: e.tensor_scalar(out=sg8[:], in0=sg8[:], scalar1=-1.0, scalar2=1.0, op0=ALU.add, op1=ALU.mult),
                  r=['sg8'], w=['sg8'])
            for h in range(8):
                kb.op('pool', lambda e: e.tensor_scalar(out=Dg[b][:, h, :], in0=ident[:], scalar1=sg8[:, h:h + 1], scalar2=1.0,
                                                        op0=ALU.mult, op1=ALU.mult), r=['ident', 'sg8'], w=[('Dg', b, h)])

            def sum_mm(kc, h):
                ncol = min(512, nk - kc * 512)
                rb_ = Rb[kc % 2]
                kb.op('pe', lambda e: e.matmul(pSc[:, 0:ncol], lhsT=Dg[b][:, h, :], rhs=rb_[:, h, 0:ncol], start=(h == 0), stop=(h == 7)),
                      r=[('Dg', b, h), ('Rb', kc % 2, h)], w=['pSc'])
                if h == 7:
                    kb.op('act', lambda e: e.activation(out=score[:, kc * 512:kc * 512 + ncol], in_=pSc[:, 0:ncol], func=AF.Copy),
                          r=['pSc'], w=[('score', b, kc), 'pSc'])

            for kc in range(nkc):
                ncol = min(512, nk - kc * 512)
                rb_ = Rb[kc % 2]
                for h in range(8):
                    pr = h // 2
                    p0 = 64 * (h % 2)
                    si = st['si']
                    ps_ = pS[si % 2]; kps = ('pS', si % 2)
                    kb.op('pe', lambda e: e.matmul(ps_[:, 0:ncol], lhsT=qib[b][p0:p0 + 64, pr, :], rhs=kiT2[p0:p0 + 64, kc * 512:kc * 512 + ncol],
                                                   start=True, stop=True), r=[('qib', b)], w=[kps])
                    kb.op('act', lambda e: e.activation(out=rb_[:, h, 0:ncol], in_=ps_[:, 0:ncol], func=AF.Relu),
                          r=[kps], w=[('Rb', kc % 2, h), kps])
                    st['si'] += 1
                    if kc >= 1:
                        sum_mm(kc - 1, h)
            for h in range(8):
                sum_mm(nkc - 1, h)

        def bisect(i):
            b = i % 2
            nk = (i + 1) * 128
            nkc = (nk + 511) // 512
            score = scoreb[b]
            sck = [('score', b, kc) for kc in range(nkc)]
            kb.op('dve', lambda e: e.tensor_tensor(out=score[:, i * 128:nk], in0=score[:, i * 128:nk], in1=causn[:], op=ALU.add),
                  r=sck + ['causn'], w=sck)
            if i >= 2:
                nv = i * 128
                kb.op('dve', lambda e: e.tensor_reduce(out=bs[:, 0:1], in_=score[:, 0:nv], axis=AX.X, op=ALU.min), r=sck, w=['bs'])
                kb.op('dve', lambda e: e.tensor_reduce(out=bs[:, 5:6], in_=score[:, 0:nv], axis=AX.X, op=ALU.max), r=sck + ['bs'], w=['bs'])
                kb.op('dve', lambda e: e.tensor_tensor(out=bs[:, 1:2], in0=bs[:, 5:6], in1=bs[:, 0:1], op=ALU.subtract), r=['bs'], w=['bs'])
                kb.op('dve', lambda e: e.tensor_scalar(out=bs[:, 1:2], in0=bs[:, 1:2], scalar1=1.0001, scalar2=1e-6, op0=ALU.mult, op1=ALU.add),
                      r=['bs'], w=['bs'])
                for k in range(NBIS):
                    f = 2.0 ** -(k + 1)
                    kb.op('dve', lambda e: e.tensor_scalar(out=bs[:, 2:3], in0=bs[:, 1:2], scalar1=f, scalar2=bs[:, 0:1], op0=ALU.mult,
                                                           op1=ALU.add), r=['bs'], w=['bs'])
                    kb.op('dve', lambda e: e.tensor_scalar(out=mask[:, 0:nk], in0=score[:, 0:nk], scalar1=bs[:, 2:3], scalar2=None,
                                                           op0=ALU.is_ge, op1=ALU.add, accum_out=bs[:, 3:4]), r=sck + ['bs'], w=['bs', 'mask'])
                    kb.op('dve', lambda e: e.tensor_scalar(out=bs[:, 4:5], in0=bs[:, 3:4], scalar1=TOPK - 0.5, scalar2=bs[:, 1:2],
                                                           op0=ALU.is_ge, op1=ALU.mult), r=['bs'], w=['bs'])
                    kb.op('dve', lambda e: e.scalar_tensor_tensor(out=bs[:, 0:1], in0=bs[:, 4:5], scalar=f, in1=bs[:, 0:1], op0=ALU.mult,
                                                                  op1=ALU.add), r=['bs'], w=['bs'])
                tau = bs[:, 0:1]
                tk = 'bs'
            else:
                tau = tauc[:, 0:1]
                tk = 'tauc'
            kb.op('dve', lambda e: e.tensor_scalar(out=mask[:, 0:nk], in0=score[:, 0:nk], scalar1=tau, scalar2=None, op0=ALU.is_ge),
                  r=sck + [tk], w=['mask'])

        def attend(i):
            b = i % 2
            for q4 in range((i + 4) // 4):
                nb = min(4, i + 1 - q4 * 4)
                for j in range(nb):
                    kbk = q4 * 4 + j
                    kb.op('pe', lambda e: e.transpose(out=pMT[:, j, :], in_=mask[:, kbk * 128:(kbk + 1) * 128], identity=ident[:]),
                          r=['mask', 'ident'], w=['pMT'])
                kb.op('act', lambda e: e.activation(out=maskT[:, q4 * 4:q4 * 4 + nb, :], in_=pMT[:, 0:nb, :], func=AF.Copy),
                      r=['pMT'], w=[('maskT', q4), 'pMT'])
            def qk(kbk):
                li = st['li'] + kbk
                pl = pL[li % 2]; kpl = ('pL', li % 2)
                kb.op('pe', lambda e: e.matmul(pl[:], lhsT=kcT[:, kbk * 128:(kbk + 1) * 128], rhs=qtb[b][:].rearrange("d h t -> d (h t)"),
                                               start=True, stop=True), r=[('qtb', b)], w=[kpl])

            qk(0)
            for kbk in range(i + 1):
                li = st['li'] + kbk
                pl = pL[li % 2]; kpl = ('pL', li % 2); E_ = Eb[li % 2]; kE = ('Eb', li % 2); P_ = Pb[li % 2]; kP = ('Pb', li % 2)
                kb.op('act', lambda e: e.activation(out=E_[:], in_=pl[:], func=AF.Exp), r=[kpl], w=[kE, kpl])
                if kbk + 1 <= i:
                    qk(kbk + 1)
                kb.op('pool', lambda e: e.tensor_tensor(out=P_[:].rearrange("s (h t) -> s h t", h=4), in0=E_[:].rearrange("s (h t) -> s h t", h=4),
                                                        in1=maskT[:, kbk, :].unsqueeze(1).to_broadcast([128, 4, 128]), op=ALU.mult),
                      r=[kE, ('maskT', kbk // 4)], w=[kP])
                for h in range(4):
                    kb.op('pe', lambda e: e.matmul(pO[h // 2][:, h % 2, :], lhsT=P_[:, h * 128:(h + 1) * 128], rhs=vaug[:, kbk, :],
                                                   start=(kbk == 0 and h % 2 == 0), stop=(kbk == i), skip_group_check=True),
                          r=[kP], w=[('pO', h // 2)])
            st['li'] += i + 1
            for hp in range(2):
                kb.op('act', lambda e: e.activation(out=rec[:, hp * 2:hp * 2 + 2], in_=pO[hp][:, :, 128], func=AF.Ln),
                      r=[('pO', hp)], w=[('rec', hp), ('pO', hp)])
                kb.op('act', lambda e: e.activation(out=rec[:, hp * 2:hp * 2 + 2], in_=rec[:, hp * 2:hp * 2 + 2], func=AF.Exp, scale=-1.0),
                      r=[('rec', hp)], w=[('rec', hp)])
                for h2 in range(2):
                    h = hp * 2 + h2
                    kb.op('act', lambda e: e.activation(out=osb[:, h, :], in_=pO[hp][:, h2, 0:128], func=AF.Copy, scale=rec[:, h:h + 1]),
                          r=[('pO', hp), ('rec', hp)], w=[('osb', h), ('pO', hp)])
            for h in range(4):
                kb.op('pe', lambda e: e.transpose(out=pMT[:, h, :], in_=osb[:, h, :], identity=ident[:]), r=[('osb', h), 'ident'], w=['pMT'])
            kb.op('act', lambda e: e.activation(out=osf[:], in_=pMT[:], func=AF.Copy), r=['pMT'], w=['osf', 'pMT'])
            kb.op('pool', lambda e: e.tensor_tensor(out=ycb[b][:], in0=osf[:], in1=zcb[b][:], op=ALU.mult),
                  r=['osf', ('zcb', b)], w=[('ycb', b)])
            kb.dma(G['ysT'][1024:1536, i * 128:(i + 1) * 128].rearrange("(h e) t -> e h t", h=4), ycb[b][:], r=[('ycb', b)],
                   w=[('ysTc', i)], s='st2', q=STQ)

        if NQ > 1:
            loadq(1)
        scoring(0)
        for i in range(NQ):
            bisect(i)
            if i + 1 < NQ:
                scoring(i + 1)
            attend(i)
            if i + 2 < NQ:
                loadq(i + 2)
    kb.barrier()


def phase_mg(kb, G, l, xsrc):
    nc = kb.nc
    with ExitStack() as es:
        A = Alloc(kb, es, f'mgl{l}')
        wbr = A.sb('wbr', [128, 12, 1024], BF16)
        wo = A.sb('wo', [128, 8, 1024], BF16)
        stg = [A.sb(f'stg{i}', [128, 1024], F32) for i in range(2)]
        i = 0
        for n in range(3):
            src = G['w_branch'][l, n].rearrange("(kc p) d -> p kc d", p=128)
            for k in range(4):
                kb.dma(stg[i % 2][:], src[:, k, :], w=[('stg', i % 2)], s='ld')
                kb.op('dve' if i % 2 == 0 else 'act',
                      (lambda e: e.tensor_copy(out=wbr[:, n * 4 + k, :], in_=stg[i % 2][:])) if i % 2 == 0 else
                      (lambda e: e.activation(out=wbr[:, n * 4 + k, :], in_=stg[i % 2][:], func=AF.Copy)),
                      r=[('stg', i % 2)], w=[('wbr', n * 4 + k)])
                i += 1
        src = G['w_out'][l].rearrange("(kc p) d -> p kc d", p=128)
        for k in range(8):
            kb.dma(stg[i % 2][:], src[:, k, :], w=[('stg', i % 2)], s='ld')
            kb.op('dve' if i % 2 == 0 else 'act',
                  (lambda e: e.tensor_copy(out=wo[:, k, :], in_=stg[i % 2][:])) if i % 2 == 0 else
                  (lambda e: e.activation(out=wo[:, k, :], in_=stg[i % 2][:], func=AF.Copy)),
                  r=[('stg', i % 2)], w=[('wo', k)])
            i += 1
        ys = [A.sb(f'ys{i}', [128, 12, 512], BF16) for i in range(2)]
        gt = [A.sb(f'gt{i}', [128, 24, 512], BF16) for i in range(2)]
        mT = A.sb('mT', [128, 8, 512], BF16)
        t0 = A.sb('t0', [128, 512], F32)
        t1 = A.sb('t1', [128, 512], F32)
        t2 = A.sb('t2', [128, 512], F32)
        xt = [A.sb(f'xt{i}', [128, 1024], F32) for i in range(2)]
        pM = [A.ps(f'pM{i}', [128, 512], F32) for i in range(3)]
        pO = [A.ps(f'pO{i}', [128, 512], F32) for i in range(2)]

        def load(blk):
            b = blk % 2
            for c in range(12):
                kb.dma(ys[b][:, c, :], G['ysT'][c * 128:(c + 1) * 128, blk * 512:(blk + 1) * 512], w=[('ys', b, c)], s='ldy')
            for c in range(24):
                kb.dma(gt[b][:, c, :], G['gateT'][c * 128:(c + 1) * 128, blk * 512:(blk + 1) * 512], w=[('gt', b, c)], s='ldg')

        load(0)
        xi = 0
        for blk in range(NBLK):
            b = blk % 2
            if blk + 1 < NBLK:
                load(blk + 1)
            for oc in range(8):
                for n in range(3):
                    for k in range(4):
                        kb.op('pe', lambda e: e.matmul(pM[n][:], lhsT=wbr[:, n * 4 + k, oc * 128:(oc + 1) * 128], rhs=ys[b][:, n * 4 + k, :],
                                                       start=(k == 0), stop=(k == 3)),
                              r=[('wbr', n * 4 + k), ('ys', b, n * 4 + k)], w=[('pM', n)])
                kb.op('dve', lambda e: e.tensor_tensor(out=t0[:], in0=pM[0][:], in1=gt[b][:, oc, :], op=ALU.mult),
                      r=[('pM', 0), ('gt', b, oc)], w=['t0', ('pM', 0)])
                kb.op('dve', lambda e: e.tensor_tensor(out=t1[:], in0=pM[1][:], in1=gt[b][:, 8 + oc, :], op=ALU.mult),
                      r=[('pM', 1), ('gt', b, 8 + oc)], w=['t1', ('pM', 1)])
                kb.op('dve', lambda e: e.tensor_tensor(out=t2[:], in0=pM[2][:], in1=gt[b][:, 16 + oc, :], op=ALU.mult),
                      r=[('pM', 2), ('gt', b, 16 + oc)], w=['t2', ('pM', 2)])
                kb.op('pool', lambda e: e.tensor_tensor(out=t0[:], in0=t0[:], in1=t1[:], op=ALU.add), r=['t0', 't1'], w=['t0'])
                kb.op('pool', lambda e: e.tensor_tensor(out=mT[:, oc, :], in0=t0[:], in1=t2[:], op=ALU.add), r=['t0', 't2'], w=[('mT', oc)])
            mk = [('mT', oc) for oc in range(8)]
            for tt in range(4):
                t = blk * 4 + tt
                x_t = xt[xi % 2]; kx = ('xt', xi % 2)
                kb.dma(x_t[:], xsrc[t * 128:(t + 1) * 128, :], r=[('x', t)], w=[kx], s='ldx')
                for hf in range(2):
                    po = pO[hf]; kpo = ('pO', hf)
                    for k in range(8):
                        kb.op('pe', lambda e: e.matmul(po[:], lhsT=mT[:, k, tt * 128:(tt + 1) * 128], rhs=wo[:, k, hf * 512:(hf + 1) * 512],
                                                       start=(k == 0), stop=(k == 7)), r=mk + [('wo', k)], w=[kpo])
                    kb.op('dve', lambda e: e.tensor_tensor(out=x_t[:, hf * 512:(hf + 1) * 512], in0=po[:], in1=x_t[:, hf * 512:(hf + 1) * 512],
                                                           op=ALU.add), r=[kpo, kx], w=[kx, kpo])
                kb.dma(G['xres'][t * 128:(t + 1) * 128, :], x_t[:], r=[kx], w=[('x', t)], s='stx', q=STQ)
                xi += 1
    kb.barrier()


def phase_final(kb, G, xsrc):
    nc = kb.nc
    with ExitStack() as es:
        A = Alloc(kb, es, 'fin')
        gf = A.sb('gf', [128, D], F32)
        kb.dma(gf[:], G['final_g'], w=['gf'])
        xt = [A.sb(f'xt{i}', [128, D], F32) for i in range(2)]
        junk = A.sb('junk', [128, D], BF16)
        ss = [A.sb(f'ss{i}', [128, 1], F32) for i in range(2)]
        for t in range(NT):
            x_t = xt[t % 2]; s_t = ss[t % 2]; kx = ('xt', t % 2); ks = ('ss', t % 2)
            kb.dma(x_t[:], xsrc[t * 128:(t + 1) * 128, :], r=[('x', t)], w=[kx], s='ldx')
            kb.op('act', lambda e: e.activation(out=junk[:], in_=x_t[:], func=AF.Square, accum_out=s_t[:]), r=[kx], w=['junk', ks])
            kb.op('dve', lambda e: e.tensor_scalar(out=s_t[:], in0=s_t[:], scalar1=1.0 / D, scalar2=EPS, op0=ALU.mult, op1=ALU.add),
                  r=[ks], w=[ks])
            kb.op('act', lambda e: e.activation(out=s_t[:], in_=s_t[:], func=AF.Sqrt), r=[ks], w=[ks])
            kb.op('dve', lambda e: e.reciprocal(out=s_t[:], in_=s_t[:]), r=[ks], w=[ks])
            kb.op('dve', lambda e: e.scalar_tensor_tensor(out=x_t[:], in0=x_t[:], scalar=s_t[:, 0:1], in1=gf[:], op0=ALU.mult, op1=ALU.mult),
                  r=[kx, ks, 'gf'], w=[kx])
            kb.dma(G['out'][t * 128:(t + 1) * 128, :], x_t[:], r=[kx], w=[('out', t)], s='sto', q=STQ)
    kb.barrier()


def host_consts():
    c = {}
    c['ident'] = np.eye(128, dtype=np.float32)
    c['sgn1'] = np.concatenate([np.ones(64), -np.ones(64)]).astype(np.float32).reshape(128, 1)
    sw = np.zeros((128, 128), np.float32)
    sw[np.arange(128), (np.arange(128) + 64) % 128] = 1.0
    c['swap'] = sw
    c['iota'] = np.tile(np.arange(128, dtype=np.float32)[None, :], (128, 1))
    c['utri'] = np.triu(np.ones((128, 128), np.float32))
    c['causn'] = np.where(np.arange(128)[None, :] <= np.arange(128)[:, None], 0.0, -1.0e30).astype(np.float32)
    s8 = np.zeros((8, 4, 128), np.float32)
    for pr in range(4):
        s8[2 * pr, pr, 0:64] = 1.0
        s8[2 * pr + 1, pr, 64:128] = 1.0
    c['sel8'] = s8
    c['mneg'] = np.where(np.arange(128)[None, :] >= np.arange(128)[:, None], 0.0, -30000.0).astype(np.float32)
    return c


def input_shapes():
    return {
        'x': ([L, D], F32),
        'w_in': ([NL, D, INW], F32),
        'norm_g': ([NL, 128, 8], F32),
        'a_re': ([NL, 128, 32], F32), 'a_im': ([NL, 128, 32], F32), 'log_dt': ([NL, 128, 32], F32),
        'X1': ([NL, 128, 32, 16], F32), 'X2': ([NL, 128, 32, 16], F32),
        'Cc1': ([NL, 4, 128, 128], F32), 'Cc2': ([NL, 4, 128, 128], F32),
        'ssm_d': ([NL, 128, 4], F32), 'glu_b': ([NL, 128, 4], F32), 'glu_w': ([NL, 512, 512], F32),
        'conv_w': ([NL, 128, 8, 4], F32), 'conv_b': ([NL, 128, 8], F32), 'igb': ([NL, 128, 4], F32), 'fgb': ([NL, 128, 4], F32),
        'mhg': ([NL, 128, 4], F32),
        'qng': ([NL, 128, 2], F32), 'kng': ([NL, 128, 64], F32), 'knb': ([NL, 128, 64], F32),
        'w_uq': ([NL, 256, 512], F32), 'w_qidx': ([NL, 256, 512], F32),
        'w_branch': ([NL, 3, 512, 1024], F32), 'w_out': ([NL, 1024, 1024], F32), 'final_g': ([128, D], F32),
    }


def set_len(n):
    global L, NBLK, NT
    L = n
    NBLK = L // 512
    NT = L // 128


def declare(kb, cfg):
    nc = kb.nc
    G = {}
    for name, (shape, dt) in input_shapes().items():
        G[name] = nc.dram_tensor(name, list(shape), dt, kind='ExternalInput').ap()
    cst = {}
    for name, arr in host_consts().items():
        cst[name] = nc.dram_tensor('c_' + name, list(arr.shape), F32, kind='ExternalInput').ap()
    G['cst'] = cst
    scratch = {
        'uT': ([512, L], BF16), 'zaT': ([512, L], BF16), 'qbT': ([512, L], BF16), 'kbT': ([512, L], BF16),
        'zbT': ([512, L], BF16), 'cqT': ([256, L], BF16), 'kcT': ([128, L], BF16), 'zcT': ([512, L], BF16),
        'gateT': ([3072, L], BF16), 'widxT': ([8, L], F32),
        'vb': ([L, 512], BF16), 'ifg': ([L, 8], F32), 'vc': ([L, 128], BF16), 'kidx': ([L, 64], F32),
        'widx': ([L, 8], F32), 'ysT': ([1536, L], BF16), 'xres': ([L, D], F32),
        'qT': ([4, 128, L], BF16), 'qiT': ([4, 128, L], BF16),
    }
    for name, (shape, dt) in scratch.items():
        G[name] = kb.dram(name, shape, dt)
    G['out'] = nc.dram_tensor('out', [L, D], F32, kind='ExternalOutput').ap()
    return G


def build(cfg=None):
    cfg = cfg or {}
    nc = bass.Bass("TRN2", target_bir_lowering=False)
    es = ExitStack()
    kb = KB(nc, es, ext=cfg.get('ext', {}))
    G = declare(kb, cfg)
    G['cfg'] = cfg
    phases = cfg.get('phases', None)
    layers = cfg.get('layers', list(range(NL)))
    for l in layers:
        xsrc = G['x'] if l == 0 else G['xres']
        if phases is None or 'p1' in phases:
            phase1(kb, G, l, xsrc)
        if phases is None or 's5' in phases:
            phase_s5(kb, G, l)
        if phases is None or 'ml' in phases:
            phase_ml(kb, G, l)
        if phases is None or 'dsa' in phases:
            phase_dsa(kb, G, l)
        if phases is None or 'mg' in phases:
            phase_mg(kb, G, l, xsrc)
    if phases is None or 'fin' in phases:
        phase_final(kb, G, G['xres'])
    kb.barrier()
    kb.final_wait()
    es.close()
    return nc, kb


def host_layout(inputs, b):
    m = {}
    m['x'] = np.ascontiguousarray(inputs['x'][b][:L])
    m['w_in'] = np.ascontiguousarray(inputs['w_in'])
    m['norm_g'] = np.ascontiguousarray(inputs['norm_g'].reshape(NL, 8, 128).transpose(0, 2, 1))
    ca = np.ascontiguousarray
    art = inputs['ssm_a_re'].transpose(0, 2, 1)
    m['a_re'] = ca(np.concatenate([art, art], axis=1))
    ait = inputs['ssm_a_im'].transpose(0, 2, 1)
    m['a_im'] = ca(np.concatenate([ait, ait], axis=1))
    m['log_dt'] = ca(np.broadcast_to(inputs['ssm_log_dt'][:, None, :], (NL, 128, 32)))
    br = inputs['ssm_b_re'].transpose(0, 2, 1, 3)
    bi_ = inputs['ssm_b_im'].transpose(0, 2, 1, 3)
    m['X1'] = ca(np.concatenate([br, bi_], axis=1))
    m['X2'] = ca(np.concatenate([bi_, br], axis=1))
    cr = inputs['ssm_c_re'].reshape(NL, 4, 128, 64)
    ci = inputs['ssm_c_im'].reshape(NL, 4, 128, 64)
    m['Cc1'] = ca(np.concatenate([cr, ci], axis=3))
    m['Cc2'] = ca(np.concatenate([ci, cr], axis=3))
    m['ssm_d'] = ca(inputs['ssm_d'].reshape(NL, 4, 128).transpose(0, 2, 1))
    m['glu_b'] = ca(inputs['glu_b'].reshape(NL, 4, 128).transpose(0, 2, 1))
    m['glu_w'] = ca(inputs['glu_w'])
    m['conv_w'] = ca(inputs['qk_conv_w'].transpose(0, 2, 1).reshape(NL, 8, 128, 4).transpose(0, 2, 1, 3))
    m['conv_b'] = ca(inputs['qk_conv_b'].reshape(NL, 8, 128).transpose(0, 2, 1))
    m['igb'] = ca(np.broadcast_to(inputs['igate_b'][:, None, :], (NL, 128, 4)))
    m['fgb'] = ca(np.broadcast_to(inputs['fgate_b'][:, None, :], (NL, 128, 4)))
    m['mhg'] = ca(inputs['mh_norm_g'].reshape(NL, 4, 128).transpose(0, 2, 1))
    m['qng'] = ca(inputs['q_norm_g'].reshape(NL, 2, 128).transpose(0, 2, 1))
    m['kng'] = ca(np.broadcast_to(inputs['kidx_norm_g'][:, None, :], (NL, 128, 64)))
    m['knb'] = ca(np.broadcast_to(inputs['kidx_norm_b'][:, None, :], (NL, 128, 64)))
    m['w_uq'] = ca(inputs['w_uq'])
    m['w_qidx'] = ca(inputs['w_qidx'])
    m['w_branch'] = ca(inputs['w_branch'])
    m['w_out'] = ca(inputs['w_out'])
    m['final_g'] = ca(np.broadcast_to(inputs['final_norm_g'][None, :], (128, D)))
    for k, v in host_consts().items():
        m['c_' + k] = v
    return m


N_CORES = 2


def kernel(**inputs):
    set_len(8192)
    nc, kb = build({})
    in_maps = [host_layout(inputs, c % 2) for c in range(N_CORES)]
    res = run_bass_kernel_spmd(nc, in_maps, core_ids=list(range(N_CORES)))
    out = np.stack([np.asarray(res.results[b]['out'], dtype=np.float32) for b in range(2)], axis=0)
    return out
```

```python
import numpy as np
import concourse.bass as bass
import concourse.mybir as mybir
from concourse.bass_utils import run_bass_kernel_spmd
from contextlib import ExitStack

F32 = mybir.dt.float32
BF16 = mybir.dt.bfloat16
AF = mybir.ActivationFunctionType
ALU = mybir.AluOpType
AX = mybir.AxisListType

L = 8192
D = 1024
NL = 4
NBLK = L // 512
NT = L // 128
EPS = 1e-6
BIG = 1.0e30

C_U = 0; C_ZA = 512; C_QB = 1024; C_KB = 1536; C_VB = 2048; C_IF = 2560; C_ZB = 2568
C_CQ = 3080; C_KC = 3336; C_VC = 3464; C_KIDX = 3592; C_WIDX = 3656; C_ZC = 3664; C_GATE = 4176
INW = 7248
WJ = 1208

SAME_SYNC = True
STQ = 'sp'


class KB:
    def __init__(self, nc, es, ext=None):
        self.nc = nc
        self.es = es
        self.ext = ext or {}
        self.E = {'pe': nc.tensor, 'act': nc.scalar, 'dve': nc.vector, 'pool': nc.gpsimd, 'sp': nc.sync}
        self.sem = {}
        self.cnt = {}
        for e in ['pe', 'act', 'dve', 'pool']:
            self.sem[e] = es.enter_context(nc.semaphore('s_' + e))
            self.cnt[e] = 0
        self.dsem = {}
        self.dcnt = {}
        self.dstream = {}
        self.waited = {e: {} for e in self.E}
        self.lastw = {}
        self.readers = {}
        self.uid = 0
        self.ninstr = 0

    def name(self, base):
        self.uid += 1
        return f"{base}_{self.uid}"

    def dram(self, name, shape, dtype):
        kind = self.ext.get(name, 'Internal')
        return self.nc.dram_tensor(name, list(shape), dtype, kind=kind).ap()

    NS = 6

    def _stream(self, s):
        if s not in self.dstream:
            self.dstream[s] = 0
            for j in range(self.NS):
                self.dsem[(s, j)] = self.es.enter_context(self.nc.semaphore(f'd_{s}{j}'))
                self.dcnt[(s, j)] = 0
        j = self.dstream[s] % self.NS
        self.dstream[s] += 1
        return (s, j)

    def _wait(self, e, src, val):
        if self.waited[e].get(src, 0) >= val:
            return
        sem = self.sem[src[1]] if src[0] == 'e' else self.dsem[src[1]]
        self.E[e].wait_ge(sem, val)
        self.waited[e][src] = val
        self.ninstr += 1

    def _deps(self, e, reads, writes):
        deps = {}

        def add(src, val):
            if deps.get(src, 0) < val:
                deps[src] = val
        for k in reads:
            ev = self.lastw.get(k)
            if ev:
                add(*ev)
        for k in writes:
            ev = self.lastw.get(k)
            if ev:
                add(*ev)
            for src, val in self.readers.get(k, {}).items():
                add(src, val)
        for src, val in deps.items():
            if src == ('e', e) and (e == 'pe' or not SAME_SYNC):
                continue
            self._wait(e, src, val)

    def _commit(self, ev, reads, writes):
        for k in writes:
            self.lastw[k] = ev
            self.readers[k] = {}
        for k in reads:
            r = self.readers.setdefault(k, {})
            if r.get(ev[0], 0) < ev[1]:
                r[ev[0]] = ev[1]

    def op(self, e, fn, r=(), w=(), post=None):
        self._deps(e, r, w)
        ins = fn(self.E[e])
        if post is not None:
            ins = post(self.E[e])
            self.ninstr += 1
        self.cnt[e] += 1
        ins.then_inc(self.sem[e], 1)
        self._commit((('e', e), self.cnt[e]), r, w)
        self.ninstr += 1

    def dma(self, out, in_, r=(), w=(), s='ld', q='sp'):
        sk = self._stream(s)
        if self.dcnt[sk] > 0:
            self._wait(q, ('d', sk), self.dcnt[sk])
        self._deps(q, r, w)
        ins = self.E[q].dma_start(out=out, in_=in_)
        self.dcnt[sk] += 16
        ins.then_inc(self.dsem[sk], 16)
        self._commit((('d', sk), self.dcnt[sk]), r, w)
        self.ninstr += 1

    def barrier(self):
        for e in self.E:
            for p in self.sem:
                if self.cnt[p] > 0 and not (p == e):
                    self._wait(e, ('e', p), self.cnt[p])
            for s in self.dsem:
                if self.dcnt[s] > 0:
                    self._wait(e, ('d', s), self.dcnt[s])
        self.lastw = {}
        self.readers = {}

    def final_wait(self):
        for s in self.dsem:
            if self.dcnt[s] > 0:
                self._wait('sp', ('d', s), self.dcnt[s])


class Alloc:
    def __init__(self, kb, es, tag):
        self.kb = kb
        self.es = es
        self.tag = tag

    def sb(self, name, shape, dt):
        return self.es.enter_context(self.kb.nc.sbuf_tensor(self.kb.name(self.tag + name), list(shape), dt))

    def ps(self, name, shape, dt):
        return self.es.enter_context(self.kb.nc.psum_tensor(self.kb.name(self.tag + name), list(shape), dt))


def wkeys(kc, c0, c1):
    return [('wb', kc, j) for j in range(c0 // WJ, (c1 - 1) // WJ + 1)]


def phase1(kb, G, l, xsrc):
    nc = kb.nc
    cst = G['cst']
    with ExitStack() as es:
        A = Alloc(kb, es, f'p1l{l}')
        wb = A.sb('wb', [128, 8, INW], BF16)
        stg = [A.sb(f'wstg{i}', [128, WJ], F32) for i in range(2)]
        g = A.sb('g', [128, 8], F32)
        ident = A.sb('ident', [128, 128], BF16)
        identf = A.sb('identf', [128, 128], F32)
        kb.dma(g[:], G['norm_g'][l], w=['g'])
        kb.dma(identf[:], cst['ident'], w=['identf'])
        kb.op('dve', lambda e: e.tensor_copy(out=ident[:], in_=identf[:]), r=['identf'], w=['ident'])
        w_in = G['w_in'][l].rearrange("(kc p) n -> p kc n", p=128)
        i = 0
        for kc in range(8):
            for j in range(6):
                s = stg[i % 2]
                kb.dma(s[:], w_in[:, kc, j * WJ:(j + 1) * WJ], w=[('stg', i % 2)])
                if i % 2 == 0:
                    kb.op('dve', lambda e, s=s, kc=kc, j=j: e.tensor_scalar(
                        out=wb[:, kc, j * WJ:(j + 1) * WJ], in0=s[:], scalar1=g[:, kc:kc + 1], scalar2=None,
                        op0=ALU.mult), r=[('stg', 0), 'g'], w=[('wb', kc, j)])
                else:
                    kb.op('act', lambda e, s=s, kc=kc, j=j: e.activation(
                        out=wb[:, kc, j * WJ:(j + 1) * WJ], in_=s[:], func=AF.Copy, scale=g[:, kc:kc + 1]),
                        r=[('stg', 1), 'g'], w=[('wb', kc, j)])
                i += 1

        xt = [A.sb(f'xt{i}', [128, D], F32) for i in range(2)]
        junk = A.sb('junk', [128, D], BF16)
        ss = [A.sb(f'ss{i}', [128, 1], F32) for i in range(2)]
        hb = [A.sb(f'hb{i}', [128, D], BF16) for i in range(2)]
        hT = [A.sb(f'hT{i}', [128, 8, 512], BF16) for i in range(2)]
        pT = [A.ps(f'pT{i}', [128, 8, 128], BF16) for i in range(2)]
        pO = [A.ps(f'pO{i}', [128, 512], F32) for i in range(4)]
        so = [A.sb(f'so{i}', [128, 512], BF16) for i in range(4)]
        sof = [A.sb(f'sof{i}', [128, 256], F32) for i in range(2)]

        fm = [(C_U, 4, AF.Copy, G['uT']), (C_ZA, 4, AF.Silu, G['zaT']), (C_QB, 4, AF.Copy, G['qbT']),
              (C_KB, 4, AF.Copy, G['kbT']), (C_ZB, 4, AF.Silu, G['zbT']), (C_CQ, 2, AF.Copy, G['cqT']),
              (C_KC, 1, AF.Copy, G['kcT']), (C_ZC, 4, AF.Silu, G['zcT']), (C_GATE, 24, AF.Sigmoid, G['gateT'])]
        wst = [A.sb(f'wst{i}', [8, 512], F32) for i in range(2)]
        state = {'ti': 0, 'oi': 0}

        def prep(blk):
            hTb = hT[blk % 2]
            for tt in range(4):
                ti = state['ti']
                t = blk * 4 + tt
                x_t = xt[ti % 2]; s_t = ss[ti % 2]; h_t = hb[ti % 2]; p_t = pT[ti % 2]
                kx = ('xt', ti % 2); ks = ('ss', ti % 2); kh = ('hb', ti % 2); kp = ('pT', ti % 2)
                kb.dma(x_t[:], xsrc[t * 128:(t + 1) * 128, :], r=[('x', t)], w=[kx], s='ldx')
                kb.op('act', lambda e: e.activation(out=junk[:], in_=x_t[:], func=AF.Square, accum_out=s_t[:]),
                      r=[kx], w=['junk', ks])
                kb.op('dve', lambda e: e.tensor_scalar(out=s_t[:], in0=s_t[:], scalar1=1.0 / D, scalar2=EPS,
                                                       op0=ALU.mult, op1=ALU.add), r=[ks], w=[ks])
                kb.op('act', lambda e: e.activation(out=s_t[:], in_=s_t[:], func=AF.Sqrt), r=[ks], w=[ks])
                kb.op('dve', lambda e: e.reciprocal(out=s_t[:], in_=s_t[:]), r=[ks], w=[ks])
                kb.op('dve', lambda e: e.tensor_scalar(out=h_t[:], in0=x_t[:], scalar1=s_t[:, 0:1], scalar2=None,
                                                       op0=ALU.mult), r=[kx, ks], w=[kh])
                for c in range(8):
                    kb.op('pe', lambda e: e.transpose(out=p_t[:, c, :], in_=h_t[:, c * 128:(c + 1) * 128],
                                                      identity=ident[:]), r=[kh, 'ident'], w=[kp])
                kb.op('dve', lambda e: e.tensor_copy(out=hTb[:, :, tt * 128:(tt + 1) * 128], in_=p_t[:]),
                      r=[kp], w=[('hT', blk % 2, tt)])
                state['ti'] += 1

        def tm(blk):
            hTb = hT[blk % 2]
            hkeys = [('hT', blk % 2, tt) for tt in range(4)]
            for tt in range(4):
                t = blk * 4 + tt
                hk = [('hT', blk % 2, tt)]
                lhs = lambda k: hTb[:, k, tt * 128:(tt + 1) * 128]
                oi = state['oi']; po = pO[oi % 4]; kpo = ('pO', oi % 4); st = so[oi % 4]; kst = ('so', oi % 4)
                for k in range(8):
                    kb.op('pe', lambda e: e.matmul(po[:], lhsT=lhs(k), rhs=wb[:, k, C_VB:C_VB + 512],
                                                   start=(k == 0), stop=(k == 7)),
                          r=hk + wkeys(k, C_VB, C_VB + 512), w=[kpo])
                kb.op('dve', lambda e: e.tensor_copy(out=st[:], in_=po[:]), r=[kpo], w=[kst])
                kb.dma(G['vb'][t * 128:(t + 1) * 128, :], st[:], r=[kst], w=[('vb', t)], s='st1')
                state['oi'] += 1
                if 'if' in G.get('cfg', {}).get('tm_skip', ()):
                    continue
                oi = state['oi']; po = pO[oi % 4]; kpo = ('pO', oi % 4); sf = sof[0]
                for k in range(8):
                    kb.op('pe', lambda e: e.matmul(po[:, 0:8], lhsT=lhs(k), rhs=wb[:, k, C_IF:C_IF + 8],
                                                   start=(k == 0), stop=(k == 7)),
                          r=hk + wkeys(k, C_IF, C_IF + 8), w=[kpo])
                kb.op('dve', lambda e: e.tensor_copy(out=sf[:, 0:8], in_=po[:, 0:8]), r=[kpo], w=[('sof', 0)])
                kb.dma(G['ifg'][t * 128:(t + 1) * 128, :], sf[:, 0:8], r=[('sof', 0)], w=[('ifg', t)], s='st1')
                state['oi'] += 1
                if 'vkw' in G.get('cfg', {}).get('tm_skip', ()):
                    continue
                oi = state['oi']; po = pO[oi % 4]; kpo = ('pO', oi % 4); st = so[oi % 4]; kst = ('so', oi % 4); sf = sof[1]
                for k in range(8):
                    kb.op('pe', lambda e: e.matmul(po[:, 0:200], lhsT=lhs(k), rhs=wb[:, k, C_VC:C_VC + 200],
                                                   start=(k == 0), stop=(k == 7)),
                          r=hk + wkeys(k, C_VC, C_VC + 200), w=[kpo])
                kb.op('dve', lambda e: e.tensor_copy(out=st[:, 0:128], in_=po[:, 0:128]), r=[kpo], w=[kst])
                kb.op('dve', lambda e: e.tensor_copy(out=sf[:, 0:72], in_=po[:, 128:200]),
                      r=[kpo], w=[('sof', 1)])
                kb.dma(G['vc'][t * 128:(t + 1) * 128, :], st[:, 0:128], r=[kst], w=[('vc', t)], s='st1')
                kb.dma(G['kidx'][t * 128:(t + 1) * 128, :], sf[:, 0:64], r=[('sof', 1)], w=[('kidx', t)], s='st1')
                kb.dma(G['widx'][t * 128:(t + 1) * 128, :], sf[:, 64:72], r=[('sof', 1)], w=[('widx', t)], s='st1')
                state['oi'] += 1
            if 'wT' in G.get('cfg', {}).get('tm_skip', ()):
                return
            oi = state['oi']; po = pO[oi % 4]; kpo = ('pO', oi % 4); ws = wst[blk % 2]
            for k in range(8):
                kb.op('pe', lambda e: e.matmul(po[:, :], lhsT=wb[:, k, C_WIDX:C_WIDX + 128], rhs=hTb[:, k, :],
                                               start=(k == 0), stop=(k == 7)),
                      r=hkeys + wkeys(k, C_WIDX, C_WIDX + 128), w=[kpo])
            kb.op('dve', lambda e: e.tensor_copy(out=ws[:], in_=po[0:8, :]), r=[kpo], w=[('wst', blk % 2)])
            kb.dma(G['widxT'][:, blk * 512:(blk + 1) * 512], ws[:], r=[('wst', blk % 2)], w=[('widxT', blk)], s='st1')
            state['oi'] += 1

        def fmseg(blk):
            hTb = hT[blk % 2]
            hkeys = [('hT', blk % 2, tt) for tt in range(4)]
            for (c0, nch, func, dst) in fm:
                for ch in range(nch):
                    oi = state['oi']; po = pO[oi % 4]; kpo = ('pO', oi % 4); st = so[oi % 4]; kst = ('so', oi % 4)
                    cc = c0 + ch * 128
                    for k in range(8):
                        kb.op('pe', lambda e: e.matmul(po[:], lhsT=wb[:, k, cc:cc + 128], rhs=hTb[:, k, :],
                                                       start=(k == 0), stop=(k == 7)),
                              r=hkeys + wkeys(k, cc, cc + 128), w=[kpo])
                    kb.op('act', lambda e: e.activation(out=st[:], in_=po[:], func=func), r=[kpo], w=[kst])
                    kb.dma(dst[ch * 128:(ch + 1) * 128, blk * 512:(blk + 1) * 512], st[:], r=[kst],
                           w=[(id(dst), ch, blk)], s='st2', q=STQ)
                    state['oi'] += 1

        stop = G.get('cfg', {}).get('p1_stop', 9)
        if stop >= 1:
            prep(0)
        for blk in range(NBLK):
            if stop >= 2:
                tm(blk)
            if blk + 1 < NBLK and stop >= 1:
                prep(blk + 1)
            if stop >= 3:
                fmseg(blk)
    kb.barrier()


MAGIC = 12582912.0
TWO_PI_S = 6.28318
GC1 = 0.044715
GC2 = 1.5957691216057308


def sincos_turns(kb, A, phi, n, kphi, tag):
    t = A.sb(tag + 't', [128, n], F32)
    k = A.sb(tag + 'k', [128, n], F32)
    sn = A.sb(tag + 'sn', [128, n], F32)
    cs = A.sb(tag + 'cs', [128, n], F32)
    kt, kk, ksn, kcs = tag + 't', tag + 'k', tag + 'sn', tag + 'cs'
    for (off, dst, kd) in ((0.0, sn, ksn), (0.25, cs, kcs)):
        kb.op('dve', lambda e: e.tensor_scalar(out=t[:], in0=phi, scalar1=off, scalar2=MAGIC, op0=ALU.add, op1=ALU.add),
              r=[kphi], w=[kt])
        kb.op('dve', lambda e: e.tensor_scalar(out=k[:], in0=t[:], scalar1=-MAGIC, scalar2=None, op0=ALU.add),
              r=[kt], w=[kk])
        kb.op('dve', lambda e: e.scalar_tensor_tensor(out=t[:], in0=phi, scalar=off, in1=k[:], op0=ALU.add,
                                                      op1=ALU.subtract), r=[kphi, kk], w=[kt])
        kb.op('dve', lambda e: e.tensor_scalar(out=t[:], in0=t[:], scalar1=0.5, scalar2=-0.5, op0=ALU.min, op1=ALU.max),
              r=[kt], w=[kt])
        kb.op('act', lambda e: e.activation(out=dst[:], in_=t[:], func=AF.Sin, scale=TWO_PI_S), r=[kt], w=[kd])
    return sn, cs, ksn, kcs


def phase_s5(kb, G, l):
    nc = kb.nc
    cst = G['cst']
    with ExitStack() as es:
        A = Alloc(kb, es, f's5l{l}')
        Ctab = A.sb('Ctab', [128, 32, 128], F32)
        Stab = A.sb('Stab', [128, 32, 128], F32)
        Rtab = A.sb('Rtab', [128, 32, 128], F32)
        RotL = A.sb('RotL', [128, 32, 128], F32)
        L1 = A.sb('L1', [128, 32, 128], BF16)
        L2 = A.sb('L2', [128, 32, 128], BF16)
        W1 = A.sb('W1', [128, 32, 128], BF16)
        W2 = A.sb('W2', [128, 32, 128], BF16)
        identf = A.sb('identf', [128, 128], F32)
        dsk = A.sb('dsk', [128, 4], F32)
        glub = A.sb('glub', [128, 4], F32)
        gluw = A.sb('gluw', [128, 4, 512], BF16)
        carry = A.sb('carry', [128, 32], F32)
        kb.dma(identf[:], cst['ident'], w=['identf'])
        kb.dma(dsk[:], G['ssm_d'][l], w=['dsk'])
        kb.dma(glub[:], G['glu_b'][l], w=['glub'])
        kb.op('pool', lambda e: e.memset(carry[:], 0.0), w=['carry'])
        with ExitStack() as es2:
            B = Alloc(kb, es2, f's5l{l}t')
            ar = B.sb('ar', [128, 32], F32); ai = B.sb('ai', [128, 32], F32); ldt = B.sb('ldt', [128, 32], F32)
            sgn1 = B.sb('sgn1', [128, 1], F32); swp = B.sb('swp', [128, 128], F32); iot = B.sb('iot', [128, 128], F32)
            X1 = B.sb('X1', [128, 32, 16], F32); X2 = B.sb('X2', [128, 32, 16], F32)
            Cc1 = B.sb('Cc1', [128, 4, 128], F32); Cc2 = B.sb('Cc2', [128, 4, 128], F32)
            gws = B.sb('gws', [128, 4, 512], F32)
            kb.dma(ar[:], G['a_re'][l], w=['ar']); kb.dma(ai[:], G['a_im'][l], w=['ai']); kb.dma(ldt[:], G['log_dt'][l], w=['ldt'])
            kb.dma(sgn1[:], cst['sgn1'], w=['sgn1']); kb.dma(swp[:], cst['swap'], w=['swp']); kb.dma(iot[:], cst['iota'], w=['iot'])
            kb.dma(X1[:], G['X1'][l], w=['X1']); kb.dma(X2[:], G['X2'][l], w=['X2'])
            kb.dma(Cc1[:], G['Cc1'][l].rearrange("c p n -> p c n"), w=['Cc1'])
            kb.dma(Cc2[:], G['Cc2'][l].rearrange("c p n -> p c n"), w=['Cc2'])
            kb.dma(gws[:], G['glu_w'][l].rearrange("(kc p) n -> p kc n", p=128), w=['gws'])
            kb.op('pool', lambda e: e.tensor_copy(out=gluw[:], in_=gws[:]), r=['gws'], w=['gluw'])
            S = {}

            def sm(name):
                S[name] = B.sb(name, [128, 32], F32)
                return S[name]

            def tt(o, a, b, op):
                kb.op('dve', lambda e: e.tensor_tensor(out=S[o][:], in0=S[a][:], in1=S[b][:], op=op), r=[a, b], w=[o])
            S['ar'] = ar; S['ai'] = ai; S['ldt'] = ldt
            for n_ in ['dt', 'mag', 'ang', 'phi1', 'abr', 'abi', 'den', 't1', 'fr', 'fi', 'fis', 'frs', 'tmp', 'phi128', 'ssgn']:
                sm(n_)
            kb.op('act', lambda e: e.activation(out=S['dt'][:], in_=ldt[:], func=AF.Exp), r=['ldt'], w=['dt'])
            tt('tmp', 'ar', 'dt', ALU.mult)
            kb.op('act', lambda e: e.activation(out=S['mag'][:], in_=S['tmp'][:], func=AF.Exp), r=['tmp'], w=['mag'])
            tt('ang', 'ai', 'dt', ALU.mult)
            kb.op('dve', lambda e: e.tensor_scalar(out=S['phi1'][:], in0=S['ang'][:], scalar1=1.0 / (2 * np.pi), scalar2=None,
                                                   op0=ALU.mult), r=['ang'], w=['phi1'])
            s1, c1, ks1, kc1 = sincos_turns(kb, B, S['phi1'][:], 32, 'phi1', 'sc1')
            S['s1'] = s1; S['c1'] = c1
            kb.op('dve', lambda e: e.tensor_tensor(out=S['abr'][:], in0=S['mag'][:], in1=c1[:], op=ALU.mult), r=['mag', kc1], w=['abr'])
            kb.op('dve', lambda e: e.tensor_tensor(out=S['abi'][:], in0=S['mag'][:], in1=s1[:], op=ALU.mult), r=['mag', ks1], w=['abi'])
            tt('den', 'ar', 'ar', ALU.mult)
            tt('tmp', 'ai', 'ai', ALU.mult)
            tt('den', 'den', 'tmp', ALU.add)
            kb.op('dve', lambda e: e.reciprocal(out=S['den'][:], in_=S['den'][:]), r=['den'], w=['den'])
            kb.op('dve', lambda e: e.tensor_scalar(out=S['t1'][:], in0=S['abr'][:], scalar1=-1.0, scalar2=None, op0=ALU.add),
                  r=['abr'], w=['t1'])
            tt('fr', 't1', 'ar', ALU.mult)
            tt('tmp', 'abi', 'ai', ALU.mult)
            tt('fr', 'fr', 'tmp', ALU.add)
            tt('fr', 'fr', 'den', ALU.mult)
            tt('fi', 'abi', 'ar', ALU.mult)
            tt('tmp', 't1', 'ai', ALU.mult)
            tt('fi', 'fi', 'tmp', ALU.subtract)
            tt('fi', 'fi', 'den', ALU.mult)
            kb.op('dve', lambda e: e.tensor_scalar(out=S['frs'][:], in0=S['fr'][:], scalar1=sgn1[:, 0:1], scalar2=None, op0=ALU.mult),
                  r=['fr', 'sgn1'], w=['frs'])
            kb.op('dve', lambda e: e.tensor_scalar(out=S['fis'][:], in0=S['fi'][:], scalar1=sgn1[:, 0:1], scalar2=-1.0, op0=ALU.mult,
                                                   op1=ALU.mult), r=['fi', 'sgn1'], w=['fis'])
            kb.op('dve', lambda e: e.tensor_scalar(out=S['phi128'][:], in0=S['phi1'][:], scalar1=128.0, scalar2=None, op0=ALU.mult),
                  r=['phi1'], w=['phi128'])
            s128, c128, ks128, kc128 = sincos_turns(kb, B, S['phi128'][:], 32, 'phi128', 'sc128')
            kb.op('dve', lambda e: e.tensor_scalar(out=S['ssgn'][:], in0=s128[:], scalar1=sgn1[:, 0:1], scalar2=None, op0=ALU.mult),
                  r=[ks128, 'sgn1'], w=['ssgn'])
            for g in range(32):
                kb.op('dve', lambda e: e.tensor_scalar(out=RotL[:, g, :], in0=identf[:], scalar1=c128[:, g:g + 1], scalar2=None,
                                                       op0=ALU.mult), r=['identf', kc128], w=[('RotL', g)])
                kb.op('dve', lambda e: e.scalar_tensor_tensor(out=RotL[:, g, :], in0=swp[:], scalar=S['ssgn'][:, g:g + 1],
                                                              in1=RotL[:, g, :], op0=ALU.mult, op1=ALU.add),
                      r=['swp', 'ssgn', ('RotL', g)], w=[('RotL', g)])
            es3 = ExitStack()
            B3 = Alloc(kb, es3, f's5l{l}u')
            PHI = B3.sb('PHI', [128, 32 * 128], F32)
            for g in range(32):
                kb.op('pool', lambda e: e.tensor_scalar(out=PHI[:, g * 128:(g + 1) * 128], in0=iot[:], scalar1=S['phi1'][:, g:g + 1],
                                                        scalar2=None, op0=ALU.mult), r=['iot', 'phi1'], w=[('PHI', g)])
                kb.op('pool', lambda e: e.tensor_scalar(out=Rtab[:, g, :], in0=iot[:], scalar1=0.0, scalar2=S['mag'][:, g:g + 1],
                                                        op0=ALU.mult, op1=ALU.add), r=['iot', 'mag'], w=[('Rtab', g)])
            phikeys = [('PHI', g) for g in range(32)]
            tq = B3.sb('tq', [128, 32 * 128], F32)
            kq = B3.sb('kq', [128, 32 * 128], F32)
            for (off, dst, kd) in ((0.0, Stab, 'Stab'), (0.25, Ctab, 'Ctab')):
                kb.op('dve', lambda e: e.tensor_scalar(out=tq[:], in0=PHI[:], scalar1=off, scalar2=MAGIC, op0=ALU.add, op1=ALU.add),
                      r=phikeys, w=['tq'])
                kb.op('dve', lambda e: e.tensor_scalar(out=kq[:], in0=tq[:], scalar1=-MAGIC, scalar2=None, op0=ALU.add),
                      r=['tq'], w=['kq'])
                kb.op('dve', lambda e: e.scalar_tensor_tensor(out=tq[:], in0=PHI[:], scalar=off, in1=kq[:], op0=ALU.add,
                                                              op1=ALU.subtract), r=phikeys + ['kq'], w=['tq'])
                kb.op('dve', lambda e: e.tensor_scalar(out=tq[:], in0=tq[:], scalar1=0.5, scalar2=-0.5, op0=ALU.min, op1=ALU.max),
                      r=['tq'], w=['tq'])
                kb.op('act', lambda e: e.activation(out=dst[:].rearrange("p g j -> p (g j)"), in_=tq[:], func=AF.Sin, scale=TWO_PI_S),
                      r=['tq'], w=[kd])
            kb.barrier()
            es3.close()
            Bp1 = B.sb('Bp1', [128, 32, 128], F32)
            Bp2 = B.sb('Bp2', [128, 32, 128], F32)
            kb.op('pool', lambda e: e.memset(Bp1[:], 0.0), w=['Bp1'])
            kb.op('pool', lambda e: e.memset(Bp2[:], 0.0), w=['Bp2'])
            kb.op('pool', lambda e: e.memset(W1[:], 0.0), w=['W1'])
            kb.op('pool', lambda e: e.memset(W2[:], 0.0), w=['W2'])
            for g in range(32):
                c0 = (g % 8) * 16
                kb.op('dve', lambda e: e.tensor_scalar(out=Bp1[:, g, c0:c0 + 16], in0=X1[:, g, :], scalar1=S['fr'][:, g:g + 1],
                                                       scalar2=None, op0=ALU.mult), r=['X1', 'fr', 'Bp1'], w=[('Bp1', g)])
                kb.op('dve', lambda e: e.scalar_tensor_tensor(out=Bp1[:, g, c0:c0 + 16], in0=X2[:, g, :], scalar=S['fis'][:, g:g + 1],
                                                              in1=Bp1[:, g, c0:c0 + 16], op0=ALU.mult, op1=ALU.add),
                      r=['X2', 'fis', ('Bp1', g)], w=[('Bp1', g)])
                kb.op('dve', lambda e: e.tensor_scalar(out=Bp2[:, g, c0:c0 + 16], in0=X2[:, g, :], scalar1=S['frs'][:, g:g + 1],
                                                       scalar2=None, op0=ALU.mult), r=['X2', 'frs', 'Bp2'], w=[('Bp2', g)])
                kb.op('dve', lambda e: e.scalar_tensor_tensor(out=Bp2[:, g, c0:c0 + 16], in0=X1[:, g, :], scalar=S['fi'][:, g:g + 1],
                                                              in1=Bp2[:, g, c0:c0 + 16], op0=ALU.mult, op1=ALU.add),
                      r=['X1', 'fi', ('Bp2', g)], w=[('Bp2', g)])
            pst = [B.ps(f'pst{i}', [128, 4, 128], F32) for i in range(2)]
            pi_ = 0
            for (Bp, Lx, kn, kl) in ((Bp1, L1, 'Bp1', 'L1'), (Bp2, L2, 'Bp2', 'L2')):
                for q in range(8):
                    pt = pst[pi_ % 2]; kpt = ('pst', pi_ % 2)
                    for j in range(4):
                        g = q * 4 + j
                        kb.op('pe', lambda e: e.transpose(out=pt[:, j, :], in_=Bp[:, g, :], identity=identf[:]),
                              r=[(kn, g), 'identf'], w=[kpt])
                    kb.op('act', lambda e: e.activation(out=Lx[:, q * 4:(q + 1) * 4, :], in_=pt[:], func=AF.Copy),
                          r=[kpt], w=[(kl, q)])
                    pi_ += 1
            for (Cc, Wx, kc_, kw_, neg_all) in ((Cc1, W1, 'Cc1', 'W1', False), (Cc2, W2, 'Cc2', 'W2', True)):
                pt = pst[pi_ % 2]; kpt = ('pst', pi_ % 2)
                for c in range(4):
                    kb.op('pe', lambda e: e.transpose(out=pt[:, c, :], in_=Cc[:, c, :], identity=identf[:]),
                          r=[kc_, 'identf'], w=[kpt])
                for g in range(32):
                    c0 = (g % 8) * 16
                    if neg_all:
                        kb.op('dve', lambda e: e.tensor_scalar(out=Wx[:, g, c0:c0 + 16], in0=pt[:, g // 8, c0:c0 + 16], scalar1=-1.0,
                                                               scalar2=None, op0=ALU.mult), r=[kpt, kw_], w=[(kw_, g)])
                    else:
                        kb.op('dve', lambda e: e.tensor_scalar(out=Wx[:, g, c0:c0 + 16], in0=pt[:, g // 8, c0:c0 + 16],
                                                               scalar1=sgn1[:, 0:1], scalar2=None, op0=ALU.mult),
                              r=[kpt, kw_, 'sgn1'], w=[(kw_, g)])
                pi_ += 1
            kb.barrier()
            if G['cfg'].get('dbg_s5'):
                def dump(name, ap, shape, dt=F32):
                    d = nc.dram_tensor('dbg_' + name, list(shape), dt, kind='ExternalOutput').ap()
                    kb.dma(d, ap, s='dbg')
                mode_ = G['cfg'].get('dbg_s5')
                if mode_ == 'one':
                    dump('dt', S['dt'][:], [128, 32])
                for n_ in ['dt', 'mag', 'ang', 'phi1', 'abr', 'abi', 'fr', 'fi', 'ssgn'] if mode_ is True else []:
                    dump(n_, S[n_][:], [128, 32])
                if mode_ is True:
                  dump('s1', S['s1'][:], [128, 32]); dump('c1', S['c1'][:], [128, 32])
                  dump('Stab', Stab[:], [128, 32, 128]); dump('Ctab', Ctab[:], [128, 32, 128]); dump('Rtab', Rtab[:], [128, 32, 128])
                  dump('RotL', RotL[:], [128, 32, 128]); dump('L1', L1[:], [128, 32, 128], BF16); dump('L2', L2[:], [128, 32, 128], BF16)
                  dump('W1', W1[:], [128, 32, 128], BF16); dump('W2', W2[:], [128, 32, 128], BF16)
                kb.barrier()
        uTb = [A.sb(f'uTb{i}', [128, 4, 512], BF16) for i in range(2)]
        zab = [A.sb(f'zab{i}', [128, 4, 512], BF16) for i in range(2)]
        Dt = [A.sb(f'Dt{i}', [128, 512], F32) for i in range(8)]
        Gt = [A.sb(f'Gt{i}', [128, 512], F32) for i in range(8)]
        tmpb = [A.sb(f'tmpb{i}', [128, 512], F32) for i in range(2)]
        P1 = [A.sb(f'P1{i}', [128, 512], BF16) for i in range(2)]
        P2 = [A.sb(f'P2{i}', [128, 512], BF16) for i in range(2)]
        yv = A.sb('yv', [128, 512], F32)
        yt = A.sb('yt', [128, 512], F32)
        gy = A.sb('gy', [128, 4, 512], BF16)
        sg = A.sb('sg', [128, 512], F32)
        yo = [A.sb(f'yo{i}', [128, 512], BF16) for i in range(2)]
        pb1 = [A.ps(f'pb1{i}', [128, 512], F32) for i in range(2)]
        pb2 = [A.ps(f'pb2{i}', [128, 512], F32) for i in range(2)]
        pY = [A.ps(f'pY{i}', [128, 512], F32) for i in range(2)]
        pc = A.ps('pc', [128, 16], F32)
        cj = A.sb('cj', [128, 8], F32)
        pG = A.ps('pG', [128, 512], F32)

        def load(blk):
            for c in range(4):
                kb.dma(uTb[blk % 2][:, c, :], G['uT'][c * 128:(c + 1) * 128, blk * 512:(blk + 1) * 512],
                       r=[(id(G['uT']), c, blk)], w=[('uTb', blk % 2, c)], s='ldu')
                kb.dma(zab[blk % 2][:, c, :], G['zaT'][c * 128:(c + 1) * 128, blk * 512:(blk + 1) * 512],
                       r=[(id(G['zaT']), c, blk)], w=[('zab', blk % 2, c)], s='ldu')

        bc = lambda tab, g: tab[:, g, :].unsqueeze(1).to_broadcast([128, 4, 128])
        v4 = lambda ap: ap.rearrange("p (s j) -> p s j", j=128)
        load(0)
        bi = 0
        yi = 0
        oi = 0
        for blk in range(NBLK):
            if blk + 1 < NBLK:
                load(blk + 1)
            ub = uTb[blk % 2]
            for c in range(4):
                for gg in range(8):
                    g = 8 * c + gg
                    b1 = pb1[bi % 2]; b2 = pb2[bi % 2]; k1 = ('pb1', bi % 2); k2 = ('pb2', bi % 2)
                    tb = tmpb[bi % 2]; ktb = ('tmpb', bi % 2)
                    kb.op('pe', lambda e: e.matmul(b1[:], lhsT=L1[:, g, :], rhs=ub[:, c, :], start=True, stop=True),
                          r=[('uTb', blk % 2, c)], w=[k1])
                    kb.op('pe', lambda e: e.matmul(b2[:], lhsT=L2[:, g, :], rhs=ub[:, c, :], start=True, stop=True),
                          r=[('uTb', blk % 2, c)], w=[k2])
                    kb.op('dve', lambda e: e.tensor_tensor(out=v4(Dt[gg][:]), in0=v4(b1[:]), in1=bc(Ctab, g), op=ALU.mult),
                          r=[k1], w=[('Dt', gg), k1])
                    kb.op('dve', lambda e: e.tensor_tensor(out=v4(tb[:]), in0=v4(b2[:]), in1=bc(Stab, g), op=ALU.mult),
                          r=[k2], w=[ktb, k2])
                    kb.op('pool', lambda e: e.tensor_tensor(out=Dt[gg][:], in0=Dt[gg][:], in1=tb[:], op=ALU.add),
                          r=[('Dt', gg), ktb], w=[('Dt', gg)])
                    bi += 1
                for seg in range(4):
                    for gg in range(8):
                        g = 8 * c + gg
                        sl = slice(seg * 128, (seg + 1) * 128)
                        kb.op('dve', lambda e: e.tensor_tensor_scan(out=Gt[gg][:, sl], data0=Rtab[:, g, :], data1=Dt[gg][:, sl],
                                                                    initial=carry[:, g:g + 1], op0=ALU.mult, op1=ALU.add),
                              r=[('Dt', gg), ('carry', g)], w=[('Gt', gg, seg)])
                        kb.op('pe', lambda e: e.matmul(pc[:, gg:gg + 1], lhsT=RotL[:, g, :],
                                                       rhs=Gt[gg][:, seg * 128 + 127:seg * 128 + 128], start=True, stop=True),
                              r=[('Gt', gg, seg)], w=[('pc', gg)])
                        kb.op('act', lambda e: e.activation(out=carry[:, g:g + 1], in_=pc[:, gg:gg + 1], func=AF.Copy),
                              r=[('pc', gg)], w=[('carry', g)])
                if G['cfg'].get('s5_cut'):
                    break
                py = pY[yi % 2]; kpy = ('pY', yi % 2)
                for gg in range(8):
                    g = 8 * c + gg
                    p1 = P1[gg % 2]; p2 = P2[gg % 2]
                    gk = [('Gt', gg, sg_) for sg_ in range(4)]
                    kb.op('pool', lambda e: e.tensor_tensor(out=v4(p1[:]), in0=v4(Gt[gg][:]), in1=bc(Ctab, g), op=ALU.mult),
                          r=gk, w=[('P1', gg % 2)])
                    kb.op('pool', lambda e: e.tensor_tensor(out=v4(p2[:]), in0=v4(Gt[gg][:]), in1=bc(Stab, g), op=ALU.mult),
                          r=gk, w=[('P2', gg % 2)])
                    kb.op('pe', lambda e: e.matmul(py[:], lhsT=W1[:, g, :], rhs=p1[:], start=(gg == 0), stop=False),
                          r=[('P1', gg % 2)], w=[kpy])
                    kb.op('pe', lambda e: e.matmul(py[:], lhsT=W2[:, g, :], rhs=p2[:], start=False, stop=(gg == 7)),
                          r=[('P2', gg % 2)], w=[kpy])
                kb.op('dve', lambda e: e.scalar_tensor_tensor(out=yv[:], in0=ub[:, c, :], scalar=dsk[:, c:c + 1], in1=py[:],
                                                              op0=ALU.mult, op1=ALU.add),
                      r=[('uTb', blk % 2, c), 'dsk', kpy], w=['yv', kpy])
                kb.op('dve', lambda e: e.tensor_tensor(out=yt[:], in0=yv[:], in1=yv[:], op=ALU.mult), r=['yv'], w=['yt'])
                kb.op('dve', lambda e: e.tensor_scalar(out=yt[:], in0=yt[:], scalar1=GC1, scalar2=1.0, op0=ALU.mult, op1=ALU.add),
                      r=['yt'], w=['yt'])
                kb.op('dve', lambda e: e.tensor_tensor(out=yt[:], in0=yt[:], in1=yv[:], op=ALU.mult), r=['yt', 'yv'], w=['yt'])
                kb.op('act', lambda e: e.activation(out=yt[:], in_=yt[:], func=AF.Sigmoid, scale=GC2), r=['yt'], w=['yt'])
                kb.op('dve', lambda e: e.tensor_tensor(out=gy[:, c, :], in0=yt[:], in1=yv[:], op=ALU.mult),
                      r=['yt', 'yv'], w=[('gy', c)])
                yi += 1
            gyk = [('gy', c) for c in range(4)]
            for oc in range(4 if not G['cfg'].get('s5_cut') else 0):
                for k in range(4):
                    kb.op('pe', lambda e: e.matmul(pG[:], lhsT=gluw[:, k, oc * 128:(oc + 1) * 128], rhs=gy[:, k, :],
                                                   start=(k == 0), stop=(k == 3)), r=gyk + ['gluw'], w=['pG'])
                kb.op('act', lambda e: e.activation(out=sg[:], in_=pG[:], func=AF.Sigmoid, bias=glub[:, oc:oc + 1]),
                      r=['pG', 'glub'], w=['sg', 'pG'])
                yo_ = yo[oi % 2]; kyo = ('yo', oi % 2)
                kb.op('dve', lambda e: e.tensor_tensor(out=sg[:], in0=sg[:], in1=gy[:, oc, :], op=ALU.mult),
                      r=['sg', ('gy', oc)], w=['sg'])
                kb.op('dve', lambda e: e.tensor_tensor(out=yo_[:], in0=sg[:], in1=zab[blk % 2][:, oc, :], op=ALU.mult),
                      r=['sg', ('zab', blk % 2, oc)], w=[kyo])
                kb.dma(G['ysT'][oc * 128:(oc + 1) * 128, blk * 512:(blk + 1) * 512], yo_[:], r=[kyo],
                       w=[('ysT', oc, blk)], s='st2', q=STQ)
                oi += 1
        if G['cfg'].get('dbg_s5b'):
            kb.barrier()
            def dump2(name, ap, shape, dt=F32):
                d = nc.dram_tensor('dbg_' + name, list(shape), dt, kind='ExternalOutput').ap()
                kb.dma(d, ap, s='dbg')
            for i in range(8):
                dump2(f'Dt{i}', Dt[i][:], [128, 512]); dump2(f'Gt{i}', Gt[i][:], [128, 512])
            dump2('carry', carry[:], [128, 32]); dump2('gy', gy[:], [128, 4, 512], BF16)
            dump2('uTb', uTb[0][:], [128, 4, 512], BF16); dump2('zab', zab[0][:], [128, 4, 512], BF16)
            dump2('gluw', gluw[:], [128, 4, 512], BF16); dump2('sg', sg[:], [128, 512])
            dump2('L1b', L1[:], [128, 32, 128], BF16); dump2('Ctabb', Ctab[:], [128, 32, 128]); dump2('Rtabb', Rtab[:], [128, 32, 128])
            dump2('RotLb', RotL[:], [128, 32, 128]); dump2('W1b', W1[:], [128, 32, 128], BF16)
    kb.barrier()


KSCALE = 128.0 ** -0.5


def phase_ml(kb, G, l):
    nc = kb.nc
    cst = G['cst']
    with ExitStack() as es:
        A = Alloc(kb, es, f'mll{l}')
        identf = A.sb('identf', [128, 128], F32)
        ident = A.sb('ident', [128, 128], BF16)
        U = A.sb('U', [128, 128], F32)
        mneg = A.sb('mneg', [128, 128], F32)
        ones = A.sb('ones', [128, 128], F32)
        cw = A.sb('cw', [128, 8, 4], F32)
        cb = A.sb('cb', [128, 8], F32)
        igb = A.sb('igb', [128, 4], F32)
        fgb = A.sb('fgb', [128, 4], F32)
        mhg = A.sb('mhg', [128, 4], F32)
        C = A.sb('C', [128, 4, 129], F32)
        Cbf = A.sb('Cbf', [128, 4, 129], BF16)
        kb.dma(identf[:], cst['ident'], w=['identf'])
        kb.dma(U[:], cst['utri'], w=['U'])
        kb.dma(mneg[:], cst['mneg'], w=['mneg'])
        kb.dma(cw[:], G['conv_w'][l], w=['cw'])
        kb.dma(cb[:], G['conv_b'][l], w=['cb'])
        kb.dma(igb[:], G['igb'][l], w=['igb'])
        kb.dma(fgb[:], G['fgb'][l], w=['fgb'])
        kb.dma(mhg[:], G['mhg'][l], w=['mhg'])
        kb.op('dve', lambda e: e.tensor_copy(out=ident[:], in_=identf[:]), r=['identf'], w=['ident'])
        kb.op('pool', lambda e: e.memset(ones[:], 1.0), w=['ones'])
        kb.op('pool', lambda e: e.memset(C[:], 0.0), w=[('C', h) for h in range(4)])
        kb.op('pool', lambda e: e.memset(Cbf[:], 0.0), w=[('Cbf', h) for h in range(4)])
        xin = [A.sb(f'xin{i}', [128, 8, 515], BF16) for i in range(2)]
        zb = [A.sb(f'zb{i}', [128, 4, 512], BF16) for i in range(2)]
        vaug = [A.sb(f'vaug{i}', [128, 4, 4, 129], BF16) for i in range(2)]
        gat = [A.sb(f'gat{i}', [128, 4, 8], F32) for i in range(2)]
        acc = A.sb('acc', [128, 512], F32)
        qk = A.sb('qk', [128, 8, 512], BF16)
        ig = A.sb('ig', [128, 4, 4], F32)
        lf = A.sb('lf', [128, 4, 4], F32)
        ybo = [A.sb(f'ybo{i}', [128, 4, 512], BF16) for i in range(2)]
        LF = [A.sb(f'LF{i}', [128, 128], F32) for i in range(4)]
        Am = [A.sb(f'Am{i}', [128, 128], F32) for i in range(4)]
        AT = [A.sb(f'AT{i}', [128, 128], F32) for i in range(4)]
        eb = [A.sb(f'eb{i}', [128, 128], F32) for i in range(4)]
        csc = A.sb('csc', [128, 4], F32)
        wcol = [A.sb(f'wcol{i}', [128, 1], F32) for i in range(4)]
        STm = [A.sb(f'STm{i}', [128, 128], BF16) for i in range(4)]
        qs = [A.sb(f'qs{i}', [128, 128], BF16) for i in range(4)]
        sm = [A.sb(f'sm{i}', [128, 8], F32) for i in range(4)]
        junk = A.sb('junk', [128, 128], BF16)
        hn = [A.sb(f'hn{i}', [128, 128], BF16) for i in range(4)]
        kw = [A.sb(f'kw{i}', [128, 128], BF16) for i in range(4)]
        pH = [A.ps(f'pH{i}', [128, 512], F32) for i in range(4)]
        pcol = A.ps('pcol', [128, 8], F32)
        pTK = A.ps('pTK', [128, 4, 128], BF16)
        pDa = [A.ps(f'pD{i}', [128, 2, 129], F32) for i in range(2)]
        for i in range(2):
            kb.op('pool', lambda e: e.memset(vaug[i][:], 1.0), w=[('vaug', i)])
            kb.op('pool', lambda e: e.memset(xin[i][:, :, 0:3], 0.0), w=[('xin', i, 'halo')])

        def load(blk):
            b = blk % 2
            c0 = blk * 512
            for qk_i, src in ((0, G['qbT']), (1, G['kbT'])):
                for h in range(4):
                    ch = qk_i * 4 + h
                    if blk == 0:
                        kb.dma(xin[b][:, ch, 3:515], src[h * 128:(h + 1) * 128, 0:512], w=[('xin', b, ch)], s='ldm')
                    else:
                        kb.dma(xin[b][:, ch, 0:515], src[h * 128:(h + 1) * 128, c0 - 3:c0 + 512],
                               w=[('xin', b, ch), ('xin', b, 'halo')], s='ldm')
            for h in range(4):
                kb.dma(zb[b][:, h, :], G['zbT'][h * 128:(h + 1) * 128, c0:c0 + 512], w=[('zb', b, h)], s='ldm')
            for ci in range(4):
                t0 = c0 + ci * 128
                kb.dma(vaug[b][:, ci, :, 0:128], G['vb'][t0:t0 + 128, :].rearrange("s (h e) -> s h e", h=4),
                       w=[('vaug', b, ci)], r=[('vaug', b)], s='ldm')
                kb.dma(gat[b][:, ci, :], G['ifg'][t0:t0 + 128, :], w=[('gat', b, ci)], s='ldm')

        load(0)
        it = 0
        for blk in range(NBLK):
            b = blk % 2
            if blk + 1 < NBLK:
                load(blk + 1)
            for ch in range(8):
                xk = [('xin', b, ch), ('xin', b, 'halo')]
                kb.op('dve', lambda e: e.tensor_scalar(out=acc[:], in0=xin[b][:, ch, 3:515], scalar1=cw[:, ch, 3:4],
                                                       scalar2=cb[:, ch:ch + 1], op0=ALU.mult, op1=ALU.add),
                      r=xk + ['cw', 'cb'], w=['acc'])
                for j in range(3):
                    kb.op('dve', lambda e: e.scalar_tensor_tensor(out=acc[:], in0=xin[b][:, ch, j:j + 512], scalar=cw[:, ch, j:j + 1],
                                                                  in1=acc[:], op0=ALU.mult, op1=ALU.add),
                          r=xk + ['cw', 'acc'], w=['acc'])
                kb.op('act', lambda e: e.activation(out=qk[:, ch, :], in_=acc[:], func=AF.Silu), r=['acc'], w=[('qk', ch)])
                if ch >= 4:
                    kb.op('pool', lambda e: e.tensor_scalar(out=qk[:, ch, :], in0=qk[:, ch, :], scalar1=KSCALE, scalar2=1.0,
                                                            op0=ALU.mult, op1=ALU.mult), r=[('qk', ch)], w=[('qk', ch)])
            gk = [('gat', b, ci) for ci in range(4)]
            kb.op('dve', lambda e: e.tensor_tensor(out=ig[:], in0=gat[b][:, :, 0:4], in1=igb[:].unsqueeze(1).to_broadcast([128, 4, 4]),
                                                   op=ALU.add), r=gk + ['igb'], w=['ig'])
            kb.op('dve', lambda e: e.tensor_tensor(out=lf[:], in0=gat[b][:, :, 4:8], in1=fgb[:].unsqueeze(1).to_broadcast([128, 4, 4]),
                                                   op=ALU.add), r=gk + ['fgb'], w=['lf'])
            kb.op('act', lambda e: e.activation(out=lf[:], in_=lf[:], func=AF.Exp, scale=-1.0), r=['lf'], w=['lf'])
            kb.op('act', lambda e: e.activation(out=lf[:], in_=lf[:], func=AF.Ln, bias=1.0), r=['lf'], w=['lf'])
            kb.op('dve', lambda e: e.tensor_scalar(out=lf[:], in0=lf[:], scalar1=-1.0, scalar2=None, op0=ALU.mult), r=['lf'], w=['lf'])
            for ci in range(4):
                cs = slice(ci * 128, (ci + 1) * 128)
                H = range(4)
                kb.op('pe', lambda e: e.matmul(pcol[:, 0:4], lhsT=U[:], rhs=lf[:, ci, :], start=True, stop=True), r=['lf', 'U'], w=['pcol'])
                kb.op('dve', lambda e: e.tensor_tensor(out=csc[:], in0=ig[:, ci, :], in1=pcol[:, 0:4], op=ALU.subtract),
                      r=['ig', 'pcol'], w=['csc', 'pcol'])
                for h in H:
                    kb.op('dve', lambda e: e.tensor_scalar(out=LF[h][:], in0=ones[:], scalar1=lf[:, ci, h:h + 1], scalar2=None,
                                                           op0=ALU.mult), r=['ones', 'lf'], w=[('LF', h)])
                for h in H:
                    kb.op('pe', lambda e: e.matmul(pH[h][:, 0:128], lhsT=LF[h][:], rhs=U[:], start=True, stop=True),
                          r=[('LF', h), 'U'], w=[('pH', h)])
                for h in H:
                    kb.op('pe', lambda e: e.matmul(pH[h][:, 128:256], lhsT=qk[:, 4 + h, cs], rhs=qk[:, h, cs], start=True, stop=True),
                          r=[('qk', 4 + h), ('qk', h)], w=[('pH', h)])
                for h in H:
                    kb.op('dve', lambda e: e.tensor_tensor(out=Am[h][:], in0=pH[h][:, 0:128], in1=mneg[:], op=ALU.add),
                          r=[('pH', h), 'mneg'], w=[('Am', h), ('pH', h)])
                for h in H:
                    kb.op('act', lambda e: e.activation(out=AT[h][:], in_=Am[h][:], func=AF.Exp, bias=csc[:, h:h + 1]),
                          r=[('Am', h), 'csc'], w=[('AT', h)])
                    kb.op('act', lambda e: e.activation(out=eb[h][:], in_=pH[h][:, 0:128], func=AF.Exp), r=[('pH', h)], w=[('eb', h), ('pH', h)])
                    kb.op('act', lambda e: e.activation(out=wcol[h][:], in_=pH[h][:, 127:128], func=AF.Exp, bias=csc[:, h:h + 1]),
                          r=[('pH', h), 'csc'], w=[('wcol', h), ('pH', h)])
                for h in H:
                    kb.op('dve', lambda e: e.tensor_tensor(out=STm[h][:], in0=pH[h][:, 128:256], in1=AT[h][:], op=ALU.mult),
                          r=[('pH', h), ('AT', h)], w=[('STm', h), ('pH', h)])
                    kb.op('pool', lambda e: e.tensor_tensor(out=qs[h][:], in0=qk[:, h, cs], in1=eb[h][:], op=ALU.mult),
                          r=[('qk', h), ('eb', h)], w=[('qs', h)])
                for h in H:
                    kb.op('pe', lambda e: e.matmul(pH[h][:, 256:385], lhsT=STm[h][:], rhs=vaug[b][:, ci, h, :], start=True, stop=False),
                          r=[('STm', h), ('vaug', b, ci)], w=[('pH', h)])
                    kb.op('pe', lambda e: e.matmul(pH[h][:, 256:385], lhsT=qs[h][:], rhs=Cbf[:, h, :], start=False, stop=True),
                          r=[('qs', h), ('Cbf', h)], w=[('pH', h)])
                for h in H:
                    kb.op('pe', lambda e: e.transpose(out=pTK[:, h, :], in_=qk[:, 4 + h, cs], identity=ident[:]),
                          r=[('qk', 4 + h), 'ident'], w=['pTK'])
                for h in H:
                    kb.op('dve', lambda e: e.tensor_scalar(out=kw[h][:], in0=pTK[:, h, :], scalar1=wcol[h][:, 0:1], scalar2=None,
                                                           op0=ALU.mult), r=['pTK', ('wcol', h)], w=[('kw', h), 'pTK'])
                for h in H:
                    s_ = sm[h]; ksm = ('sm', h); pn = pH[h][:, 256:385]; kpn = ('pH', h)
                    kb.op('dve', lambda e: e.tensor_scalar(out=s_[:, 4:5], in0=pn[:, 128:129], scalar1=-1.0, scalar2=None, op0=ALU.mult),
                          r=[kpn], w=[ksm, kpn])
                    kb.op('dve', lambda e: e.scalar_tensor_tensor(out=s_[:, 0:1], in0=pn[:, 128:129], scalar=1.0, in1=s_[:, 4:5],
                                                                  op0=ALU.max, op1=ALU.max), r=[kpn, ksm], w=[ksm, kpn])
                    kb.op('dve', lambda e: e.reciprocal(out=s_[:, 0:1], in_=s_[:, 0:1]), r=[ksm], w=[ksm])
                    kb.op('act', lambda e: e.activation(out=junk[:], in_=pn[:, 0:128], func=AF.Square, accum_out=s_[:, 1:2]),
                          r=[kpn, ksm], w=['junk', ksm, kpn])
                for h in H:
                    s_ = sm[h]; ksm = ('sm', h)
                    kb.op('dve', lambda e: e.tensor_tensor(out=s_[:, 2:3], in0=s_[:, 0:1], in1=s_[:, 0:1], op=ALU.mult), r=[ksm], w=[ksm])
                    kb.op('dve', lambda e: e.tensor_tensor(out=s_[:, 2:3], in0=s_[:, 2:3], in1=s_[:, 1:2], op=ALU.mult), r=[ksm], w=[ksm])
                    kb.op('dve', lambda e: e.tensor_scalar(out=s_[:, 2:3], in0=s_[:, 2:3], scalar1=1.0 / 128, scalar2=EPS, op0=ALU.mult,
                                                           op1=ALU.add), r=[ksm], w=[ksm])
                for h in H:
                    s_ = sm[h]; ksm = ('sm', h)
                    kb.op('act', lambda e: e.activation(out=s_[:, 2:3], in_=s_[:, 2:3], func=AF.Ln), r=[ksm], w=[ksm])
                    kb.op('act', lambda e: e.activation(out=s_[:, 2:3], in_=s_[:, 2:3], func=AF.Exp, scale=-0.5), r=[ksm], w=[ksm])
                for h in H:
                    s_ = sm[h]; ksm = ('sm', h); pn = pH[h][:, 256:385]; kpn = ('pH', h)
                    kb.op('dve', lambda e: e.tensor_tensor(out=s_[:, 3:4], in0=s_[:, 2:3], in1=s_[:, 0:1], op=ALU.mult), r=[ksm], w=[ksm])
                    kb.op('dve', lambda e: e.tensor_scalar(out=hn[h][:], in0=pn[:, 0:128], scalar1=s_[:, 3:4], scalar2=None, op0=ALU.mult),
                          r=[kpn, ksm], w=[('hn', h), kpn])
                for h in H:
                    kb.op('pe', lambda e: e.matmul(pDa[h // 2][:, h % 2, :], lhsT=kw[h][:], rhs=vaug[b][:, ci, h, :], start=True, stop=True),
                          r=[('kw', h), ('vaug', b, ci)], w=[('pD', h // 2)])
                for h in H:
                    kb.op('dve', lambda e: e.scalar_tensor_tensor(out=C[:, h, :], in0=C[:, h, :], scalar=eb[h][:, 127:128], in1=pDa[h // 2][:, h % 2, :],
                                                                  op0=ALU.mult, op1=ALU.add),
                          r=[('C', h), ('eb', h), ('pD', h // 2)], w=[('C', h), ('pD', h // 2)])
                    kb.op('act', lambda e: e.activation(out=Cbf[:, h, :], in_=C[:, h, :], func=AF.Copy), r=[('C', h)], w=[('Cbf', h)])
                for h in H:
                    kb.op('pe', lambda e: e.transpose(out=pTK[:, h, :], in_=hn[h][:], identity=ident[:]),
                          r=[('hn', h), 'ident'], w=['pTK'])
                for h in H:
                    kb.op('dve', lambda e: e.scalar_tensor_tensor(out=ybo[b][:, h, cs], in0=pTK[:, h, :], scalar=mhg[:, h:h + 1],
                                                                  in1=zb[b][:, h, cs], op0=ALU.mult, op1=ALU.mult),
                          r=['pTK', 'mhg', ('zb', b, h)], w=[('ybo', b, h, ci), 'pTK'])
            for h in range(4):
                kb.dma(G['ysT'][512 + h * 128:512 + (h + 1) * 128, blk * 512:(blk + 1) * 512], ybo[b][:, h, :],
                       r=[('ybo', b, h, ci) for ci in range(4)], w=[('ysTb', h, blk)], s='st2', q=STQ)
    kb.barrier()


ATT_SCALE = 128.0 ** -0.5
IDX_SCALE = (8.0 ** -0.5) * (64.0 ** -0.5)
TOPK = 256
NBIS = 20


def phase_dsa(kb, G, l):
    nc = kb.nc
    cst = G['cst']
    NQ = L // 128
    with ExitStack() as es:
        A = Alloc(kb, es, f'dsl{l}')
        identf = A.sb('identf', [128, 128], F32)
        ident = A.sb('ident', [128, 128], BF16)
        causn = A.sb('causn', [128, 128], F32)
        kiT2 = A.sb('kiT2', [128, L], BF16)
        kcT = A.sb('kcT', [128, L], BF16)
        vaug = A.sb('vaug', [128, NQ, 129], BF16)
        kb.dma(identf[:], cst['ident'], w=['identf'])
        kb.dma(causn[:], cst['causn'], w=['causn'])
        kb.op('dve', lambda e: e.tensor_copy(out=ident[:], in_=identf[:]), r=['identf'], w=['ident'])
        kb.op('pool', lambda e: e.memset(vaug[:], 1.0), w=['vaug'])
        for blk in range(NBLK):
            kb.dma(kcT[:, blk * 512:(blk + 1) * 512], G['kcT'][:, blk * 512:(blk + 1) * 512], w=[('kcT', blk)], s='ldk')
            kb.dma(vaug[:, blk * 4:(blk + 1) * 4, 0:128], G['vc'][blk * 512:(blk + 1) * 512, :].rearrange("(n s) e -> s n e", s=128),
                   r=['vaug'], w=[('vaug', blk)], s='ldk')
        with ExitStack() as es2:
            B = Alloc(kb, es2, f'dsl{l}t')
            onesf = B.sb('onesf', [128, 128], F32)
            sel8 = B.sb('sel8', [8, 4, 128], F32)
            qng = B.sb('qng', [128, 2], F32)
            kng = B.sb('kng', [128, 64], F32)
            knb = B.sb('knb', [128, 64], F32)
            wuq = B.sb('wuq', [128, 2, 512], BF16)
            wqi = B.sb('wqi', [128, 2, 512], BF16)
            wst = B.sb('wst', [128, 2, 512], F32)
            kb.op('pool', lambda e: e.memset(onesf[:], 1.0), w=['onesf'])
            kb.dma(sel8[:], cst['sel8'], w=['sel8'])
            kb.dma(qng[:], G['qng'][l], w=['qng'])
            kb.dma(kng[:], G['kng'][l], w=['kng'])
            kb.dma(knb[:], G['knb'][l], w=['knb'])
            for (src, dst, kd) in ((G['w_uq'][l], wuq, 'wuq'), (G['w_qidx'][l], wqi, 'wqi')):
                kb.dma(wst[:], src.rearrange("(kc p) n -> p kc n", p=128), w=['wst'], s='ld')
                for kc in range(2):
                    kb.op('dve', lambda e: e.tensor_scalar(out=dst[:, kc, :], in0=wst[:, kc, :], scalar1=qng[:, kc:kc + 1], scalar2=None,
                                                           op0=ALU.mult), r=['wst', 'qng'], w=[(kd, kc)])
            cq = [B.sb(f'cq{i}', [128, 2, 512], BF16) for i in range(2)]
            wT = [B.sb(f'wT{i}', [8, 512], F32) for i in range(2)]
            sq = B.sb('sq', [128, 2, 512], F32)
            rr = B.sb('rr', [128, 512], F32)
            tq = B.sb('tq', [128, 512], F32)
            wab = B.sb('wab', [128, 512], F32)
            qo = [B.sb(f'qo{i}', [128, 512], BF16) for i in range(2)]
            kx = [B.sb(f'kx{i}', [128, 64], F32) for i in range(2)]
            kst = [B.sb(f'kst{i}', [128, 4], F32) for i in range(2)]
            kjunk = B.sb('kjunk', [128, 64], F32)
            kn = [B.sb(f'kn{i}', [128, 128], BF16) for i in range(2)]
            pR = B.ps('pR', [128, 512], F32)
            pQ = [B.ps(f'pQ{i}', [128, 512], F32) for i in range(2)]
            pW = B.ps('pW', [128, 512], F32)
            pK = [B.ps(f'pK{i}', [128, 128], BF16) for i in range(2)]
            oi = 0
            ti = 0
            for blk in range(NBLK):
                b = blk % 2
                c0 = blk * 512
                for kc in range(2):
                    kb.dma(cq[b][:, kc, :], G['cqT'][kc * 128:(kc + 1) * 128, c0:c0 + 512], w=[('cq', b, kc)], s='ldc')
                kb.dma(wT[b][:], G['widxT'][:, c0:c0 + 512], w=[('wT', b)], s='ldc')
                cqk = [('cq', b, 0), ('cq', b, 1)]
                kb.op('dve', lambda e: e.tensor_tensor(out=sq[:], in0=cq[b][:], in1=cq[b][:], op=ALU.mult), r=cqk, w=['sq'])
                for kc in range(2):
                    kb.op('pe', lambda e: e.matmul(pR[:], lhsT=onesf[:], rhs=sq[:, kc, :], start=(kc == 0), stop=(kc == 1)),
                          r=['sq', 'onesf'], w=['pR'])
                kb.op('dve', lambda e: e.tensor_scalar(out=rr[:], in0=pR[:], scalar1=1.0 / 256, scalar2=EPS, op0=ALU.mult, op1=ALU.add),
                      r=['pR'], w=['rr', 'pR'])
                kb.op('act', lambda e: e.activation(out=rr[:], in_=rr[:], func=AF.Sqrt), r=['rr'], w=['rr'])
                kb.op('dve', lambda e: e.reciprocal(out=rr[:], in_=rr[:]), r=['rr'], w=['rr'])
                for h in range(4):
                    pq = pQ[oi % 2]; kpq = ('pQ', oi % 2); q_ = qo[oi % 2]; kq = ('qo', oi % 2)
                    for kc in range(2):
                        kb.op('pe', lambda e: e.matmul(pq[:], lhsT=wuq[:, kc, h * 128:(h + 1) * 128], rhs=cq[b][:, kc, :],
                                                       start=(kc == 0), stop=(kc == 1)), r=cqk + [('wuq', kc)], w=[kpq])
                    kb.op('dve', lambda e: e.scalar_tensor_tensor(out=q_[:], in0=pq[:], scalar=ATT_SCALE, in1=rr[:], op0=ALU.mult,
                                                                  op1=ALU.mult), r=[kpq, 'rr'], w=[kq, kpq])
                    kb.dma(G['qT'][h, :, c0:c0 + 512], q_[:], r=[kq], w=[('qT', h, blk)], s='st2', q=STQ)
                    oi += 1
                for pr in range(4):
                    pq = pQ[oi % 2]; kpq = ('pQ', oi % 2); q_ = qo[oi % 2]; kq = ('qo', oi % 2)
                    for kc in range(2):
                        kb.op('pe', lambda e: e.matmul(pq[:], lhsT=wqi[:, kc, pr * 128:(pr + 1) * 128], rhs=cq[b][:, kc, :],
                                                       start=(kc == 0), stop=(kc == 1)), r=cqk + [('wqi', kc)], w=[kpq])
                    kb.op('pe', lambda e: e.matmul(pW[:], lhsT=sel8[:, pr, :], rhs=wT[b][:], start=True, stop=True),
                          r=[('wT', b), 'sel8'], w=['pW'])
                    kb.op('dve', lambda e: e.scalar_tensor_tensor(out=tq[:], in0=pq[:], scalar=IDX_SCALE, in1=rr[:], op0=ALU.mult,
                                                                  op1=ALU.mult), r=[kpq, 'rr'], w=['tq', kpq])
                    kb.op('act', lambda e: e.activation(out=wab[:], in_=pW[:], func=AF.Abs), r=['pW'], w=['wab', 'pW'])
                    kb.op('dve', lambda e: e.tensor_tensor(out=q_[:], in0=tq[:], in1=wab[:], op=ALU.mult), r=['tq', 'wab'], w=[kq])
                    kb.dma(G['qiT'][pr, :, c0:c0 + 512], q_[:], r=[kq], w=[('qiT', pr, blk)], s='st2', q=STQ)
                    oi += 1
                for tt in range(4):
                    t = blk * 4 + tt
                    j = ti % 2
                    x_ = kx[j]; s_ = kst[j]; n_ = kn[j]; kxk = ('kx', j); ksk = ('kst', j); knk = ('kn', j)
                    kb.dma(x_[:], G['kidx'][t * 128:(t + 1) * 128, :], w=[kxk], s='ldc')
                    kb.op('dve', lambda e: e.tensor_reduce(out=s_[:, 0:1], in_=x_[:], axis=AX.X, op=ALU.add), r=[kxk], w=[ksk])
                    kb.op('dve', lambda e: e.tensor_scalar(out=s_[:, 0:1], in0=s_[:, 0:1], scalar1=-1.0 / 64, scalar2=None, op0=ALU.mult),
                          r=[ksk], w=[ksk])
                    kb.op('dve', lambda e: e.tensor_scalar(out=x_[:], in0=x_[:], scalar1=s_[:, 0:1], scalar2=None, op0=ALU.add),
                          r=[kxk, ksk], w=[kxk])
                    kb.op('act', lambda e: e.activation(out=kjunk[:], in_=x_[:], func=AF.Square, accum_out=s_[:, 1:2]),
                          r=[kxk, ksk], w=['kjunk', ksk])
                    kb.op('dve', lambda e: e.tensor_scalar(out=s_[:, 1:2], in0=s_[:, 1:2], scalar1=1.0 / 64, scalar2=EPS, op0=ALU.mult,
                                                           op1=ALU.add), r=[ksk], w=[ksk])
                    kb.op('act', lambda e: e.activation(out=s_[:, 1:2], in_=s_[:, 1:2], func=AF.Sqrt), r=[ksk], w=[ksk])
                    kb.op('dve', lambda e: e.reciprocal(out=s_[:, 1:2], in_=s_[:, 1:2]), r=[ksk], w=[ksk])
                    kb.op('dve', lambda e: e.scalar_tensor_tensor(out=x_[:], in0=x_[:], scalar=s_[:, 1:2], in1=kng[:], op0=ALU.mult,
                                                                  op1=ALU.mult), r=[kxk, ksk, 'kng'], w=[kxk])
                    kb.op('dve', lambda e: e.tensor_tensor(out=n_[:, 0:64], in0=x_[:], in1=knb[:], op=ALU.add), r=[kxk, 'knb'], w=[knk])
                    kb.op('dve', lambda e: e.tensor_tensor(out=n_[:, 64:128], in0=x_[:], in1=knb[:], op=ALU.add), r=[kxk, 'knb', knk], w=[knk])
                    kb.op('pe', lambda e: e.transpose(out=pK[j][:], in_=n_[:], identity=ident[:]), r=[knk, 'ident'], w=[('pK', j)])
                    kb.op('act', lambda e: e.activation(out=kiT2[:, t * 128:(t + 1) * 128], in_=pK[j][:], func=AF.Copy),
                          r=[('pK', j)], w=[('kiT2', t)])
                    ti += 1
            kb.barrier()
        scoreb = [A.sb(f'score{i}', [128, L], F32) for i in range(2)]
        Rb = [A.sb(f'Rb{i}', [128, 8, 512], BF16) for i in range(2)]
        Dg = [A.sb(f'Dg{i}', [128, 8, 128], BF16) for i in range(2)]
        sg8 = A.sb('sg8', [128, 8], F32)
        mask = A.sb('mask', [128, L], BF16)
        maskT = A.sb('maskT', [128, NQ, 128], BF16)
        qib = [A.sb(f'qib{i}', [128, 4, 128], BF16) for i in range(2)]
        qtb = [A.sb(f'qtb{i}', [128, 4, 128], BF16) for i in range(2)]
        zcb = [A.sb(f'zcb{i}', [128, 4, 128], BF16) for i in range(2)]
        wdx = [A.sb(f'wdx{i}', [128, 8], F32) for i in range(2)]
        bs = A.sb('bs', [128, 8], F32)
        tauc = A.sb('tauc', [128, 1], F32)
        Eb = [A.sb(f'Eb{i}', [128, 512], BF16) for i in range(2)]
        Pb = [A.sb(f'Pb{i}', [128, 512], BF16) for i in range(2)]
        osb = A.sb('osb', [128, 4, 128], BF16)
        rec = A.sb('rec', [128, 4], F32)
        ycb = [A.sb(f'ycb{i}', [128, 4, 128], BF16) for i in range(2)]
        pS = [A.ps(f'pS{i}', [128, 512], F32) for i in range(2)]
        pSc = A.ps('pSc', [128, 512], F32)
        pMT = A.ps('pMT', [128, 4, 128], BF16)
        pL = [A.ps(f'pL{i}', [128, 512], F32) for i in range(2)]
        pO = [A.ps(f'pO{i}', [128, 2, 129], F32) for i in range(2)]
        kb.op('pool', lambda e: e.memset(tauc[:], -1.0e29), w=['tauc'])

        def loadq(i):
            b = i % 2
            t0 = i * 128
            kb.dma(qib[b][:], G['qiT'][:, :, t0:t0 + 128].rearrange("r m t -> m r t"), w=[('qib', b)], s='ldq')
            kb.dma(qtb[b][:], G['qT'][:, :, t0:t0 + 128].rearrange("h d t -> d h t"), w=[('qtb', b)], s='ldq')
            kb.dma(zcb[b][:], G['zcT'][:, t0:t0 + 128].rearrange("(h e) t -> e h t", h=4), w=[('zcb', b)], s='ldq')
            kb.dma(wdx[b][:], G['widx'][t0:t0 + 128, :], w=[('wdx', b)], s='ldq')

        loadq(0)
        st = {'si': 0, 'li': 0}
        osf = A.sb('osf', [128, 4, 128], BF16)

        def scoring(i):
            b = i % 2
            nk = (i + 1) * 128
            nkc = (nk + 511) // 512
            score = scoreb[b]
            kb.op('pool', lambda e: e.tensor_scalar(out=sg8[:], in0=wdx[b][:], scalar1=0.0, scalar2=2.0, op0=ALU.is_ge, op1=ALU.mult),
                  r=[('wdx', b)], w=['sg8'])
            kb.op('pool', lambda e: e.tensor_scalar(out=sg8[:], in0=sg8[:], scalar1=-1.0, scalar2=1.0, op0=ALU.add, op1=ALU.mult),
                  r=['sg8'], w=['sg8'])
            for h in range(8):
                kb.op('pool', lambda e: e.tensor_scalar(out=Dg[b][:, h, :], in0=ident[:], scalar1=sg8[:, h:h + 1], scalar2=1.0,
                                                        op0=ALU.mult, op1=ALU.mult), r=['ident', 'sg8'], w=[('Dg', b, h)])

            def sum_mm(kc, h):
                ncol = min(512, nk - kc * 512)
                rb_ = Rb[kc % 2]
                kb.op('pe', lambda e: e.matmul(pSc[:, 0:ncol], lhsT=Dg[b][:, h, :], rhs=rb_[:, h, 0:ncol], start=(h == 0), stop=(h == 7)),
                      r=[('Dg', b, h), ('Rb', kc % 2, h)], w=['pSc'])
                if h == 7:
                    kb.op('act', lambda e: e.activation(out=score[:, kc * 512:kc * 512 + ncol], in_=pSc[:, 0:ncol], func=AF.Copy),
                          r=['pSc'], w=[('score', b, kc), 'pSc'])

            for kc in range(nkc):
                ncol = min(512, nk - kc * 512)
                rb_ = Rb[kc % 2]
                for h in range(8):
                    pr = h // 2
                    p0 = 64 * (h % 2)
                    si = st['si']
                    ps_ = pS[si % 2]; kps = ('pS', si % 2)
                    kb.op('pe', lambda e: e.matmul(ps_[:, 0:ncol], lhsT=qib[b][p0:p0 + 64, pr, :], rhs=kiT2[p0:p0 + 64, kc * 512:kc * 512 + ncol],
                                                   start=True, stop=True), r=[('qib', b)], w=[kps])
                    kb.op('act', lambda e: e.activation(out=rb_[:, h, 0:ncol], in_=ps_[:, 0:ncol], func=AF.Relu),
                          r=[kps], w=[('Rb', kc % 2, h), kps])
                    st['si'] += 1
                    if kc >= 1:
                        sum_mm(kc - 1, h)
            for h in range(8):
                sum_mm(nkc - 1, h)

        def bisect(i):
            b = i % 2
            nk = (i + 1) * 128
            nkc = (nk + 511) // 512
            score = scoreb[b]
            sck = [('score', b, kc) for kc in range(nkc)]
            kb.op('dve', lambda e: e.tensor_tensor(out=score[:, i * 128:nk], in0=score[:, i * 128:nk], in1=causn[:], op=ALU.add),
                  r=sck + ['causn'], w=sck)
            if i >= 2:
                nv = i * 128
                kb.op('dve', lambda e: e.tensor_reduce(out=bs[:, 0:1], in_=score[:, 0:nv], axis=AX.X, op=ALU.min), r=sck, w=['bs'])
                kb.op('dve', lambda e: e.tensor_reduce(out=bs[:, 5:6], in_=score[:, 0:nv], axis=AX.X, op=ALU.max), r=sck + ['bs'], w=['bs'])
                kb.op('dve', lambda e: e.tensor_tensor(out=bs[:, 1:2], in0=bs[:, 5:6], in1=bs[:, 0:1], op=ALU.subtract), r=['bs'], w=['bs'])
                kb.op('dve', lambda e: e.tensor_scalar(out=bs[:, 1:2], in0=bs[:, 1:2], scalar1=1.0001, scalar2=1e-6, op0=ALU.mult, op1=ALU.add),
                      r=['bs'], w=['bs'])
                for k in range(NBIS):
                    f = 2.0 ** -(k + 1)
                    kb.op('dve', lambda e: e.tensor_scalar(out=bs[:, 2:3], in0=bs[:, 1:2], scalar1=f, scalar2=bs[:, 0:1], op0=ALU.mult,
                                                           op1=ALU.add), r=['bs'], w=['bs'])
                    kb.op('dve', lambda e: e.tensor_scalar(out=mask[:, 0:nk], in0=score[:, 0:nk], scalar1=bs[:, 2:3], scalar2=None,
                                                           op0=ALU.is_ge, op1=ALU.add, accum_out=bs[:, 3:4]), r=sck + ['bs'], w=['bs', 'mask'])
                    kb.op('dve', lambda e: e.tensor_scalar(out=bs[:, 4:5], in0=bs[:, 3:4], scalar1=TOPK - 0.5, scalar2=bs[:, 1:2],
                                                           op0=ALU.is_ge, op1=ALU.mult), r=['bs'], w=['bs'])
                    kb.op('dve', lambda e: e.scalar_tensor_tensor(out=bs[:, 0:1], in0=bs[:, 4:5], scalar=f, in1=bs[:, 0:1], op0=ALU.mult,
                                                                  op1=ALU.add), r=['bs'], w=['bs'])
                tau = bs[:, 0:1]
                tk = 'bs'
            else:
                tau = tauc[:, 0:1]
                tk = 'tauc'
            kb.op('dve', lambda e: e.tensor_scalar(out=mask[:, 0:nk], in0=score[:, 0:nk], scalar1=tau, scalar2=None, op0=ALU.is_ge),
                  r=sck + [tk], w=['mask'])

        def attend(i):
            b = i % 2
            for q4 in range((i + 4) // 4):
                nb = min(4, i + 1 - q4 * 4)
                for j in range(nb):
                    kbk = q4 * 4 + j
                    kb.op('pe', lambda e: e.transpose(out=pMT[:, j, :], in_=mask[:, kbk * 128:(kbk + 1) * 128], identity=ident[:]),
                          r=['mask', 'ident'], w=['pMT'])
                kb.op('act', lambda e: e.activation(out=maskT[:, q4 * 4:q4 * 4 + nb, :], in_=pMT[:, 0:nb, :], func=AF.Copy),
                      r=['pMT'], w=[('maskT', q4), 'pMT'])
            def qk(kbk):
                li = st['li'] + kbk
                pl = pL[li % 2]; kpl = ('pL', li % 2)
                kb.op('pe', lambda e: e.matmul(pl[:], lhsT=kcT[:, kbk * 128:(kbk + 1) * 128], rhs=qtb[b][:].rearrange("d h t -> d (h t)"),
                                               start=True, stop=True), r=[('qtb', b)], w=[kpl])

            qk(0)
            for kbk in range(i + 1):
                li = st['li'] + kbk
                pl = pL[li % 2]; kpl = ('pL', li % 2); E_ = Eb[li % 2]; kE = ('Eb', li % 2); P_ = Pb[li % 2]; kP = ('Pb', li % 2)
                kb.op('act', lambda e: e.activation(out=E_[:], in_=pl[:], func=AF.Exp), r=[kpl], w=[kE, kpl])
                if kbk + 1 <= i:
                    qk(kbk + 1)
                kb.op('pool', lambda e: e.tensor_tensor(out=P_[:].rearrange("s (h t) -> s h t", h=4), in0=E_[:].rearrange("s (h t) -> s h t", h=4),
                                                        in1=maskT[:, kbk, :].unsqueeze(1).to_broadcast([128, 4, 128]), op=ALU.mult),
                      r=[kE, ('maskT', kbk // 4)], w=[kP])
                for h in range(4):
                    kb.op('pe', lambda e: e.matmul(pO[h // 2][:, h % 2, :], lhsT=P_[:, h * 128:(h + 1) * 128], rhs=vaug[:, kbk, :],
                                                   start=(kbk == 0 and h % 2 == 0), stop=(kbk == i), skip_group_check=True),
                          r=[kP], w=[('pO', h // 2)])
            st['li'] += i + 1
            for hp in range(2):
                kb.op('act', lambda e: e.activation(out=rec[:, hp * 2:hp * 2 + 2], in_=pO[hp][:, :, 128], func=AF.Ln),
                      r=[('pO', hp)], w=[('rec', hp), ('pO', hp)])
                kb.op('act', lambda e: e.activation(out=rec[:, hp * 2:hp * 2 + 2], in_=rec[:, hp * 2:hp * 2 + 2], func=AF.Exp, scale=-1.0),
                      r=[('rec', hp)], w=[('rec', hp)])
                for h2 in range(2):
                    h = hp * 2 + h2
                    kb.op('act', lambda e: e.activation(out=osb[:, h, :], in_=pO[hp][:, h2, 0:128], func=AF.Copy, scale=rec[:, h:h + 1]),
                          r=[('pO', hp), ('rec', hp)], w=[('osb', h), ('pO', hp)])
            for h in range(4):
                kb.op('pe', lambda e: e.transpose(out=pMT[:, h, :], in_=osb[:, h, :], identity=ident[:]), r=[('osb', h), 'ident'], w=['pMT'])
            kb.op('act', lambda e: e.activation(out=osf[:], in_=pMT[:], func=AF.Copy), r=['pMT'], w=['osf', 'pMT'])
            kb.op('pool', lambda e: e.tensor_tensor(out=ycb[b][:], in0=osf[:], in1=zcb[b][:], op=ALU.mult),
                  r=['osf', ('zcb', b)], w=[('ycb', b)])
            kb.dma(G['ysT'][1024:1536, i * 128:(i + 1) * 128].rearrange("(h e) t -> e h t", h=4), ycb[b][:], r=[('ycb', b)],
                   w=[('ysTc', i)], s='st2', q=STQ)

        if NQ > 1:
            loadq(1)
        scoring(0)
        for i in range(NQ):
            bisect(i)
            if i + 1 < NQ:
                scoring(i + 1)
            attend(i)
            if i + 2 < NQ:
                loadq(i + 2)
    kb.barrier()


def phase_mg(kb, G, l, xsrc):
    nc = kb.nc
    with ExitStack() as es:
        A = Alloc(kb, es, f'mgl{l}')
        wbr = A.sb('wbr', [128, 12, 1024], BF16)
        wo = A.sb('wo', [128, 8, 1024], BF16)
        stg = [A.sb(f'stg{i}', [128, 1024], F32) for i in range(2)]
        i = 0
        for n in range(3):
            src = G['w_branch'][l, n].rearrange("(kc p) d -> p kc d", p=128)
            for k in range(4):
                kb.dma(stg[i % 2][:], src[:, k, :], w=[('stg', i % 2)], s='ld')
                kb.op('dve' if i % 2 == 0 else 'act',
                      (lambda e: e.tensor_copy(out=wbr[:, n * 4 + k, :], in_=stg[i % 2][:])) if i % 2 == 0 else
                      (lambda e: e.activation(out=wbr[:, n * 4 + k, :], in_=stg[i % 2][:], func=AF.Copy)),
                      r=[('stg', i % 2)], w=[('wbr', n * 4 + k)])
                i += 1
        src = G['w_out'][l].rearrange("(kc p) d -> p kc d", p=128)
        for k in range(8):
            kb.dma(stg[i % 2][:], src[:, k, :], w=[('stg', i % 2)], s='ld')
            kb.op('dve' if i % 2 == 0 else 'act',
                  (lambda e: e.tensor_copy(out=wo[:, k, :], in_=stg[i % 2][:])) if i % 2 == 0 else
                  (lambda e: e.activation(out=wo[:, k, :], in_=stg[i % 2][:], func=AF.Copy)),
                  r=[('stg', i % 2)], w=[('wo', k)])
            i += 1
        ys = [A.sb(f'ys{i}', [128, 12, 512], BF16) for i in range(2)]
        gt = [A.sb(f'gt{i}', [128, 24, 512], BF16) for i in range(2)]
        mT = A.sb('mT', [128, 8, 512], BF16)
        t0 = A.sb('t0', [128, 512], F32)
        t1 = A.sb('t1', [128, 512], F32)
        t2 = A.sb('t2', [128, 512], F32)
        xt = [A.sb(f'xt{i}', [128, 1024], F32) for i in range(2)]
        pM = [A.ps(f'pM{i}', [128, 512], F32) for i in range(3)]
        pO = [A.ps(f'pO{i}', [128, 512], F32) for i in range(2)]

        def load(blk):
            b = blk % 2
            for c in range(12):
                kb.dma(ys[b][:, c, :], G['ysT'][c * 128:(c + 1) * 128, blk * 512:(blk + 1) * 512], w=[('ys', b, c)], s='ldy')
            for c in range(24):
                kb.dma(gt[b][:, c, :], G['gateT'][c * 128:(c + 1) * 128, blk * 512:(blk + 1) * 512], w=[('gt', b, c)], s='ldg')

        load(0)
        xi = 0
        for blk in range(NBLK):
            b = blk % 2
            if blk + 1 < NBLK:
                load(blk + 1)
            for oc in range(8):
                for n in range(3):
                    for k in range(4):
                        kb.op('pe', lambda e: e.matmul(pM[n][:], lhsT=wbr[:, n * 4 + k, oc * 128:(oc + 1) * 128], rhs=ys[b][:, n * 4 + k, :],
                                                       start=(k == 0), stop=(k == 3)),
                              r=[('wbr', n * 4 + k), ('ys', b, n * 4 + k)], w=[('pM', n)])
                kb.op('dve', lambda e: e.tensor_tensor(out=t0[:], in0=pM[0][:], in1=gt[b][:, oc, :], op=ALU.mult),
                      r=[('pM', 0), ('gt', b, oc)], w=['t0', ('pM', 0)])
                kb.op('dve', lambda e: e.tensor_tensor(out=t1[:], in0=pM[1][:], in1=gt[b][:, 8 + oc, :], op=ALU.mult),
                      r=[('pM', 1), ('gt', b, 8 + oc)], w=['t1', ('pM', 1)])
                kb.op('dve', lambda e: e.tensor_tensor(out=t2[:], in0=pM[2][:], in1=gt[b][:, 16 + oc, :], op=ALU.mult),
                      r=[('pM', 2), ('gt', b, 16 + oc)], w=['t2', ('pM', 2)])
                kb.op('pool', lambda e: e.tensor_tensor(out=t0[:], in0=t0[:], in1=t1[:], op=ALU.add), r=['t0', 't1'], w=['t0'])
                kb.op('pool', lambda e: e.tensor_tensor(out=mT[:, oc, :], in0=t0[:], in1=t2[:], op=ALU.add), r=['t0', 't2'], w=[('mT', oc)])
            mk = [('mT', oc) for oc in range(8)]
            for tt in range(4):
                t = blk * 4 + tt
                x_t = xt[xi % 2]; kx = ('xt', xi % 2)
                kb.dma(x_t[:], xsrc[t * 128:(t + 1) * 128, :], r=[('x', t)], w=[kx], s='ldx')
                for hf in range(2):
                    po = pO[hf]; kpo = ('pO', hf)
                    for k in range(8):
                        kb.op('pe', lambda e: e.matmul(po[:], lhsT=mT[:, k, tt * 128:(tt + 1) * 128], rhs=wo[:, k, hf * 512:(hf + 1) * 512],
                                                       start=(k == 0), stop=(k == 7)), r=mk + [('wo', k)], w=[kpo])
                    kb.op('dve', lambda e: e.tensor_tensor(out=x_t[:, hf * 512:(hf + 1) * 512], in0=po[:], in1=x_t[:, hf * 512:(hf + 1) * 512],
                                                           op=ALU.add), r=[kpo, kx], w=[kx, kpo])
                kb.dma(G['xres'][t * 128:(t + 1) * 128, :], x_t[:], r=[kx], w=[('x', t)], s='stx', q=STQ)
                xi += 1
    kb.barrier()


def phase_final(kb, G, xsrc):
    nc = kb.nc
    with ExitStack() as es:
        A = Alloc(kb, es, 'fin')
        gf = A.sb('gf', [128, D], F32)
        kb.dma(gf[:], G['final_g'], w=['gf'])
        xt = [A.sb(f'xt{i}', [128, D], F32) for i in range(2)]
        junk = A.sb('junk', [128, D], BF16)
        ss = [A.sb(f'ss{i}', [128, 1], F32) for i in range(2)]
        for t in range(NT):
            x_t = xt[t % 2]; s_t = ss[t % 2]; kx = ('xt', t % 2); ks = ('ss', t % 2)
            kb.dma(x_t[:], xsrc[t * 128:(t + 1) * 128, :], r=[('x', t)], w=[kx], s='ldx')
            kb.op('act', lambda e: e.activation(out=junk[:], in_=x_t[:], func=AF.Square, accum_out=s_t[:]), r=[kx], w=['junk', ks])
            kb.op('dve', lambda e: e.tensor_scalar(out=s_t[:], in0=s_t[:], scalar1=1.0 / D, scalar2=EPS, op0=ALU.mult, op1=ALU.add),
                  r=[ks], w=[ks])
            kb.op('act', lambda e: e.activation(out=s_t[:], in_=s_t[:], func=AF.Sqrt), r=[ks], w=[ks])
            kb.op('dve', lambda e: e.reciprocal(out=s_t[:], in_=s_t[:]), r=[ks], w=[ks])
            kb.op('dve', lambda e: e.scalar_tensor_tensor(out=x_t[:], in0=x_t[:], scalar=s_t[:, 0:1], in1=gf[:], op0=ALU.mult, op1=ALU.mult),
                  r=[kx, ks, 'gf'], w=[kx])
            kb.dma(G['out'][t * 128:(t + 1) * 128, :], x_t[:], r=[kx], w=[('out', t)], s='sto', q=STQ)
    kb.barrier()


def host_consts():
    c = {}
    c['ident'] = np.eye(128, dtype=np.float32)
    c['sgn1'] = np.concatenate([np.ones(64), -np.ones(64)]).astype(np.float32).reshape(128, 1)
    sw = np.zeros((128, 128), np.float32)
    sw[np.arange(128), (np.arange(128) + 64) % 128] = 1.0
    c['swap'] = sw
    c['iota'] = np.tile(np.arange(128, dtype=np.float32)[None, :], (128, 1))
    c['utri'] = np.triu(np.ones((128, 128), np.float32))
    c['causn'] = np.where(np.arange(128)[None, :] <= np.arange(128)[:, None], 0.0, -1.0e30).astype(np.float32)
    s8 = np.zeros((8, 4, 128), np.float32)
    for pr in range(4):
        s8[2 * pr, pr, 0:64] = 1.0
        s8[2 * pr + 1, pr, 64:128] = 1.0
    c['sel8'] = s8
    c['mneg'] = np.where(np.arange(128)[None, :] >= np.arange(128)[:, None], 0.0, -30000.0).astype(np.float32)
    return c


def input_shapes():
    return {
        'x': ([L, D], F32),
        'w_in': ([NL, D, INW], F32),
        'norm_g': ([NL, 128, 8], F32),
        'a_re': ([NL, 128, 32], F32), 'a_im': ([NL, 128, 32], F32), 'log_dt': ([NL, 128, 32], F32),
        'X1': ([NL, 128, 32, 16], F32), 'X2': ([NL, 128, 32, 16], F32),
        'Cc1': ([NL, 4, 128, 128], F32), 'Cc2': ([NL, 4, 128, 128], F32),
        'ssm_d': ([NL, 128, 4], F32), 'glu_b': ([NL, 128, 4], F32), 'glu_w': ([NL, 512, 512], F32),
        'conv_w': ([NL, 128, 8, 4], F32), 'conv_b': ([NL, 128, 8], F32), 'igb': ([NL, 128, 4], F32), 'fgb': ([NL, 128, 4], F32),
        'mhg': ([NL, 128, 4], F32),
        'qng': ([NL, 128, 2], F32), 'kng': ([NL, 128, 64], F32), 'knb': ([NL, 128, 64], F32),
        'w_uq': ([NL, 256, 512], F32), 'w_qidx': ([NL, 256, 512], F32),
        'w_branch': ([NL, 3, 512, 1024], F32), 'w_out': ([NL, 1024, 1024], F32), 'final_g': ([128, D], F32),
    }


def set_len(n):
    global L, NBLK, NT
    L = n
    NBLK = L // 512
    NT = L // 128


def declare(kb, cfg):
    nc = kb.nc
    G = {}
    for name, (shape, dt) in input_shapes().items():
        G[name] = nc.dram_tensor(name, list(shape), dt, kind='ExternalInput').ap()
    cst = {}
    for name, arr in host_consts().items():
        cst[name] = nc.dram_tensor('c_' + name, list(arr.shape), F32, kind='ExternalInput').ap()
    G['cst'] = cst
    scratch = {
        'uT': ([512, L], BF16), 'zaT': ([512, L], BF16), 'qbT': ([512, L], BF16), 'kbT': ([512, L], BF16),
        'zbT': ([512, L], BF16), 'cqT': ([256, L], BF16), 'kcT': ([128, L], BF16), 'zcT': ([512, L], BF16),
        'gateT': ([3072, L], BF16), 'widxT': ([8, L], F32),
        'vb': ([L, 512], BF16), 'ifg': ([L, 8], F32), 'vc': ([L, 128], BF16), 'kidx': ([L, 64], F32),
        'widx': ([L, 8], F32), 'ysT': ([1536, L], BF16), 'xres': ([L, D], F32),
        'qT': ([4, 128, L], BF16), 'qiT': ([4, 128, L], BF16),
    }
    for name, (shape, dt) in scratch.items():
        G[name] = kb.dram(name, shape, dt)
    G['out'] = nc.dram_tensor('out', [L, D], F32, kind='ExternalOutput').ap()
    return G


def build(cfg=None):
    cfg = cfg or {}
    nc = bass.Bass("TRN2", target_bir_lowering=False)
    es = ExitStack()
    kb = KB(nc, es, ext=cfg.get('ext', {}))
    G = declare(kb, cfg)
    G['cfg'] = cfg
    phases = cfg.get('phases', None)
    layers = cfg.get('layers', list(range(NL)))
    for l in layers:
        xsrc = G['x'] if l == 0 else G['xres']
        if phases is None or 'p1' in phases:
            phase1(kb, G, l, xsrc)
        if phases is None or 's5' in phases:
            phase_s5(kb, G, l)
        if phases is None or 'ml' in phases:
            phase_ml(kb, G, l)
        if phases is None or 'dsa' in phases:
            phase_dsa(kb, G, l)
        if phases is None or 'mg' in phases:
            phase_mg(kb, G, l, xsrc)
    if phases is None or 'fin' in phases:
        phase_final(kb, G, G['xres'])
    kb.barrier()
    kb.final_wait()
    es.close()
    return nc, kb


def host_layout(inputs, b):
    m = {}
    m['x'] = np.ascontiguousarray(inputs['x'][b][:L])
    m['w_in'] = np.ascontiguousarray(inputs['w_in'])
    m['norm_g'] = np.ascontiguousarray(inputs['norm_g'].reshape(NL, 8, 128).transpose(0, 2, 1))
    ca = np.ascontiguousarray
    art = inputs['ssm_a_re'].transpose(0, 2, 1)
    m['a_re'] = ca(np.concatenate([art, art], axis=1))
    ait = inputs['ssm_a_im'].transpose(0, 2, 1)
    m['a_im'] = ca(np.concatenate([ait, ait], axis=1))
    m['log_dt'] = ca(np.broadcast_to(inputs['ssm_log_dt'][:, None, :], (NL, 128, 32)))
    br = inputs['ssm_b_re'].transpose(0, 2, 1, 3)
    bi_ = inputs['ssm_b_im'].transpose(0, 2, 1, 3)
    m['X1'] = ca(np.concatenate([br, bi_], axis=1))
    m['X2'] = ca(np.concatenate([bi_, br], axis=1))
    cr = inputs['ssm_c_re'].reshape(NL, 4, 128, 64)
    ci = inputs['ssm_c_im'].reshape(NL, 4, 128, 64)
    m['Cc1'] = ca(np.concatenate([cr, ci], axis=3))
    m['Cc2'] = ca(np.concatenate([ci, cr], axis=3))
    m['ssm_d'] = ca(inputs['ssm_d'].reshape(NL, 4, 128).transpose(0, 2, 1))
    m['glu_b'] = ca(inputs['glu_b'].reshape(NL, 4, 128).transpose(0, 2, 1))
    m['glu_w'] = ca(inputs['glu_w'])
    m['conv_w'] = ca(inputs['qk_conv_w'].transpose(0, 2, 1).reshape(NL, 8, 128, 4).transpose(0, 2, 1, 3))
    m['conv_b'] = ca(inputs['qk_conv_b'].reshape(NL, 8, 128).transpose(0, 2, 1))
    m['igb'] = ca(np.broadcast_to(inputs['igate_b'][:, None, :], (NL, 128, 4)))
    m['fgb'] = ca(np.broadcast_to(inputs['fgate_b'][:, None, :], (NL, 128, 4)))
    m['mhg'] = ca(inputs['mh_norm_g'].reshape(NL, 4, 128).transpose(0, 2, 1))
    m['qng'] = ca(inputs['q_norm_g'].reshape(NL, 2, 128).transpose(0, 2, 1))
    m['kng'] = ca(np.broadcast_to(inputs['kidx_norm_g'][:, None, :], (NL, 128, 64)))
    m['knb'] = ca(np.broadcast_to(inputs['kidx_norm_b'][:, None, :], (NL, 128, 64)))
    m['w_uq'] = ca(inputs['w_uq'])
    m['w_qidx'] = ca(inputs['w_qidx'])
    m['w_branch'] = ca(inputs['w_branch'])
    m['w_out'] = ca(inputs['w_out'])
    m['final_g'] = ca(np.broadcast_to(inputs['final_norm_g'][None, :], (128, D)))
    for k, v in host_consts().items():
        m['c_' + k] = v
    return m


N_CORES = 2


def kernel(**inputs):
    set_len(8192)
    nc, kb = build({})
    in_maps = [host_layout(inputs, c % 2) for c in range(N_CORES)]
    res = run_bass_kernel_spmd(nc, in_maps, core_ids=list(range(N_CORES)))
    out = np.stack([np.asarray(res.results[b]['out'], dtype=np.float32) for b in range(2)], axis=0)
    return out
```
